# Optimizing a Trainium2 kernel written in Bass

```python
import math
import jax, jax.numpy as jnp
from jax import lax
import numpy as np

D_MODEL = 1024
BATCH = 8
SEQ = 8192
DEPTH = 1

ATTN_HEADS = 8
HEAD_DIM = 64
ATTN_WIDTH = ATTN_HEADS * HEAD_DIM
Q_BLOCK = 128
SSM_GROUP_CH = 16
SSM_GROUPS = 32
SSM_WIDTH = SSM_GROUPS * SSM_GROUP_CH
SSM_STATE = 64
DT_MIN = 1e-3
DT_MAX = 1e-1
NORM_EPS = 1e-6
MASK_VALUE = -1e30
SPLIT_SIZES = (ATTN_WIDTH, ATTN_WIDTH, ATTN_WIDTH, ATTN_HEADS, ATTN_WIDTH,
               SSM_WIDTH, SSM_WIDTH, D_MODEL, D_MODEL)
IN_COLS = 5 * ATTN_WIDTH + ATTN_HEADS + 2 * SSM_WIDTH + 2 * D_MODEL

kernel_name = "fox_s5_gated_hybrid_block"


def rms_norm(x, gain):
    xf = x.astype(jnp.float32)
    y = xf * lax.rsqrt(jnp.mean(xf * xf, axis=-1, keepdims=True) + NORM_EPS)
    return (y * gain.astype(jnp.float32)).astype(x.dtype)


def split_columns(proj):
    parts = []
    start = 0
    for size in SPLIT_SIZES:
        parts.append(proj[..., start:start + size])
        start += size
    return parts


def forgetting_attention(q, k, v, f_logit):
    b, l, _ = q.shape
    nb = l // Q_BLOCK
    to_heads = lambda t: t.reshape(b, l, ATTN_HEADS, HEAD_DIM).transpose(0, 2, 1, 3)
    q, k, v = to_heads(q), to_heads(k), to_heads(v)
    log_f = jax.nn.log_sigmoid(f_logit.astype(jnp.float32))
    cum = jnp.cumsum(log_f, axis=1).transpose(0, 2, 1)
    scale = 1.0 / math.sqrt(HEAD_DIM)
    q_blocks = q.reshape(b, ATTN_HEADS, nb, Q_BLOCK, HEAD_DIM).transpose(2, 0, 1, 3, 4)
    c_blocks = cum.reshape(b, ATTN_HEADS, nb, Q_BLOCK).transpose(2, 0, 1, 3)
    starts = jnp.arange(nb, dtype=jnp.int32) * Q_BLOCK
    k_pos = jnp.arange(l, dtype=jnp.int32)

    def one_block(args):
        q_blk, c_blk, start = args
        s = jnp.einsum('bhqd,bhkd->bhqk', q_blk, k).astype(jnp.float32)
        logits = s * scale + c_blk[..., None] - cum[:, :, None, :]
        q_pos = start + jnp.arange(Q_BLOCK, dtype=jnp.int32)
        causal = k_pos[None, :] <= q_pos[:, None]
        logits = jnp.where(causal[None, None], logits, jnp.float32(MASK_VALUE))
        p = jax.nn.softmax(logits, axis=-1)
        return jnp.einsum('bhqk,bhkd->bhqd', p.astype(v.dtype), v)

    out = lax.map(one_block, (q_blocks, c_blocks, starts))
    return out.transpose(1, 0, 3, 2, 4).reshape(b, l, ATTN_WIDTH)


def _complex_combine(e1, e2):
    a1r, a1i, b1r, b1i = e1
    a2r, a2i, b2r, b2i = e2
    return (a2r * a1r - a2i * a1i,
            a2r * a1i + a2i * a1r,
            a2r * b1r - a2i * b1i + b2r,
            a2r * b1i + a2i * b1r + b2i)


def s5_ssm(u, lam_re, lam_im, log_dt, b_re, b_im, c_re, c_im, d_skip):
    bsz, l, _ = u.shape
    uf = u.astype(jnp.float32).reshape(bsz, l, SSM_GROUPS, SSM_GROUP_CH)
    lr = lam_re.astype(jnp.float32)
    li = lam_im.astype(jnp.float32)
    dt = jnp.exp(log_dt.astype(jnp.float32))[:, None]
    mag = jnp.exp(lr * dt)
    ab_re = mag * jnp.cos(li * dt)
    ab_im = mag * jnp.sin(li * dt)
    den = lr * lr + li * li
    n_re = ab_re - 1.0
    fac_re = (n_re * lr + ab_im * li) / den
    fac_im = (ab_im * lr - n_re * li) / den
    br = b_re.astype(jnp.float32)
    bi = b_im.astype(jnp.float32)
    bb_re = fac_re[..., None] * br - fac_im[..., None] * bi
    bb_im = fac_re[..., None] * bi + fac_im[..., None] * br
    bu_re = jnp.einsum('blgc,gpc->blgp', uf, bb_re)
    bu_im = jnp.einsum('blgc,gpc->blgp', uf, bb_im)
    a_re = jnp.broadcast_to(ab_re, bu_re.shape)
    a_im = jnp.broadcast_to(ab_im, bu_im.shape)
    _, _, s_re, s_im = lax.associative_scan(_complex_combine, (a_re, a_im, bu_re, bu_im), axis=1)
    y = (jnp.einsum('blgp,gcp->blgc', s_re, c_re.astype(jnp.float32))
         - jnp.einsum('blgp,gcp->blgc', s_im, c_im.astype(jnp.float32))
         + d_skip.astype(jnp.float32) * uf)
    return y.reshape(bsz, l, SSM_WIDTH).astype(u.dtype)


def hybrid_layer(x, norm_pre, w_in, b_forget, lam_re, lam_im, log_dt, b_re, b_im,
                 c_re, c_im, d_skip, w_glu, b_glu, w_branch_a, w_branch_s, w_out, norm_post):
    h = rms_norm(x, norm_pre)
    proj = h @ w_in
    q, k, v, f_logit, gate_a, u, gate_s, mix_a, mix_s = split_columns(proj)
    y_a = forgetting_attention(q, k, v, f_logit + b_forget) * jax.nn.silu(gate_a)
    y_s = jax.nn.gelu(s5_ssm(u, lam_re, lam_im, log_dt, b_re, b_im, c_re, c_im, d_skip))
    y_s = y_s * jax.nn.sigmoid(y_s @ w_glu + b_glu)
    y_s = y_s * jax.nn.silu(gate_s)
    merged = jax.nn.sigmoid(mix_a) * (y_a @ w_branch_a) + jax.nn.sigmoid(mix_s) * (y_s @ w_branch_s)
    out = merged @ w_out
    return x + rms_norm(out, norm_post)


def setup_inputs(seed: int = 0) -> dict:
    key = jax.random.key(seed)
    ks = jax.random.split(key, 20)
    f32 = jnp.float32
    nrm = lambda k, shape, s: jax.random.normal(k, shape, f32) * s
    G, P, C = SSM_GROUPS, SSM_STATE, SSM_GROUP_CH
    x = jax.random.normal(ks[0], (BATCH, SEQ, D_MODEL), f32)
    norm_pre = 1.0 + nrm(ks[1], (DEPTH, D_MODEL), 0.01)
    w_in = nrm(ks[2], (DEPTH, D_MODEL, IN_COLS), D_MODEL ** -0.5)
    b_forget = 2.0 + nrm(ks[3], (DEPTH, ATTN_HEADS), 0.5)
    n_idx = jnp.arange(P, dtype=f32)
    lam_re = -0.5 + nrm(ks[4], (DEPTH, G, P), 0.01)
    lam_im = jnp.pi * n_idx + nrm(ks[5], (DEPTH, G, P), 0.01)
    log_dt = jax.random.uniform(ks[6], (DEPTH, G), f32, math.log(DT_MIN), math.log(DT_MAX))
    b_re = nrm(ks[7], (DEPTH, G, P, C), (2 * C) ** -0.5)
    b_im = nrm(ks[8], (DEPTH, G, P, C), (2 * C) ** -0.5)
    c_re = nrm(ks[9], (DEPTH, G, C, P), P ** -0.5)
    c_im = nrm(ks[10], (DEPTH, G, C, P), P ** -0.5)
    d_skip = nrm(ks[11], (DEPTH, G, C), 1.0)
    w_glu = nrm(ks[12], (DEPTH, SSM_WIDTH, SSM_WIDTH), SSM_WIDTH ** -0.5)
    b_glu = nrm(ks[13], (DEPTH, SSM_WIDTH), 0.01)
    w_branch_a = nrm(ks[14], (DEPTH, ATTN_WIDTH, D_MODEL), ATTN_WIDTH ** -0.5)
    w_branch_s = nrm(ks[15], (DEPTH, SSM_WIDTH, D_MODEL), SSM_WIDTH ** -0.5)
    w_out = nrm(ks[16], (DEPTH, D_MODEL, D_MODEL), D_MODEL ** -0.5)
    norm_post = 1.0 + nrm(ks[17], (DEPTH, D_MODEL), 0.01)
    return {"x": x, "norm_pre": norm_pre, "w_in": w_in, "b_forget": b_forget,
            "lam_re": lam_re, "lam_im": lam_im, "log_dt": log_dt, "b_re": b_re, "b_im": b_im,
            "c_re": c_re, "c_im": c_im, "d_skip": d_skip, "w_glu": w_glu, "b_glu": b_glu,
            "w_branch_a": w_branch_a, "w_branch_s": w_branch_s, "w_out": w_out,
            "norm_post": norm_post}


def reference(x, norm_pre, w_in, b_forget, lam_re, lam_im, log_dt, b_re, b_im, c_re, c_im,
              d_skip, w_glu, b_glu, w_branch_a, w_branch_s, w_out, norm_post):
    for l in range(DEPTH):
        x = hybrid_layer(x, norm_pre[l], w_in[l], b_forget[l], lam_re[l], lam_im[l], log_dt[l],
                         b_re[l], b_im[l], c_re[l], c_im[l], d_skip[l], w_glu[l], b_glu[l],
                         w_branch_a[l], w_branch_s[l], w_out[l], norm_post[l])
    return x
```

```python
import contextlib
import math
import numpy as np
import concourse.bass as bass
import concourse.mybir as mybir
from concourse.bass_utils import run_bass_kernel_spmd

F32 = mybir.dt.float32
BF16 = mybir.dt.bfloat16
I32 = mybir.dt.int32
AF = mybir.ActivationFunctionType
ALU = mybir.AluOpType

D = 1024
SEQ = 8192
NCORES = 8
INC = 5640
USED = 5128
H = 8
HD = 64
EPS = 1e-6
CH = 512
NCH = SEQ // CH


class Ctx:
    pass


def build(last_phase=99, debug=False):
    nc = bass.Bass("TRN2", target_bir_lowering=False)
    es = contextlib.ExitStack()
    g = Ctx()
    g.nc = nc
    g.debug = debug

    def din(name, shape):
        return nc.dram_tensor(name, list(shape), F32, kind="ExternalInput").ap()

    g.x = din("x", [SEQ, D])
    g.norm_pre = din("norm_pre", [1, D])
    g.w_in = din("w_in", [D, INC])
    g.b_forget = din("b_forget", [1, H])
    g.lam_re = din("lam_re", [32, 64])
    g.lam_im = din("lam_im", [32, 64])
    g.log_dt = din("log_dt", [1, 32])
    g.b_re = din("b_re", [32, 64, 16])
    g.b_im = din("b_im", [32, 64, 16])
    g.c_re = din("c_re", [32, 16, 64])
    g.c_im = din("c_im", [32, 16, 64])
    g.d_skip = din("d_skip", [32, 16])
    g.w_glu = din("w_glu", [512, 512])
    g.b_glu = din("b_glu", [1, 512])
    g.w_branch_a = din("w_branch_a", [512, D])
    g.w_branch_s = din("w_branch_s", [512, D])
    g.w_out = din("w_out", [D, D])
    g.norm_post = din("norm_post", [1, D])
    g.out = nc.dram_tensor("out", [SEQ, D], F32, kind="ExternalOutput").ap()

    def scratch(name, shape, dt=BF16):
        if debug:
            return nc.dram_tensor(name, list(shape), dt, kind="ExternalOutput").ap()
        return nc.dram_tensor(name, list(shape), dt).ap()

    g.qT = scratch("qT", [512, SEQ])
    g.kT = scratch("kT", [512, SEQ])
    g.vtok = scratch("vtok", [SEQ, 512])
    g.fT = scratch("fT", [8, SEQ], F32)
    g.sgaT = scratch("sgaT", [512, SEQ])
    g.uT = scratch("uT", [512, SEQ])
    g.sgsT = scratch("sgsT", [512, SEQ])
    g.smaT = scratch("smaT", [D, SEQ])
    g.smsT = scratch("smsT", [D, SEQ])

    g.ident = es.enter_context(nc.sbuf_tensor("ident", [128, 128], BF16))
    g.ones_f = es.enter_context(nc.sbuf_tensor("ones_f", [128, 128], F32))
    S0 = Steps(nc, "init")

    def i0(p, raw):
        p.memset(g.ones_f[:], 1.0)
        p.affine_select(out=g.ident[:], in_=g.ones_f[:], pattern=[[-1, 128]],
                        compare_op=ALU.is_equal, fill=0.0, base=0, channel_multiplier=1)
    S0.run(gpsimd=i0)

    g.crk = scratch("crk", [3, H, SEQ])
    g.crq = scratch("crq", [3, H, SEQ])
    g.yaT = scratch("yaT", [512, SEQ])
    g.zeros_b = es.enter_context(nc.sbuf_tensor("zeros_b", [128, 128], BF16))
    g.maskT = es.enter_context(nc.sbuf_tensor("maskT", [128, 128], BF16))
    def i1(p, raw):
        p.memset(g.zeros_b[:], 0.0)
        p.affine_select(out=g.maskT[:], in_=g.zeros_b[:], pattern=[[1, 128]],
                        compare_op=ALU.is_ge, fill=-65536.0, base=0, channel_multiplier=-1)
    S0.run(gpsimd=i1)

    if last_phase >= 1:
        phase_inproj(g)
    if last_phase >= 2:
        phase_forget(g)
    g.wg = es.enter_context(nc.sbuf_tensor("fin_wg", [128, 4, 512], BF16))
    g.wa = es.enter_context(nc.sbuf_tensor("fin_wa", [128, 4, D], BF16))
    g.wsr = es.enter_context(nc.sbuf_tensor("fin_wsr", [128, 4, D], BF16))
    g.wo = es.enter_context(nc.sbuf_tensor("fin_wo", [128, 8, D], BF16))
    g.bg = es.enter_context(nc.sbuf_tensor("fin_bg", [128, 4], F32))
    g.gpost = es.enter_context(nc.sbuf_tensor("fin_gpost", [128, D], F32))
    if last_phase >= 3:
        phase_attn(g)
    g.ys1T = scratch("ys1T", [512, SEQ])
    if last_phase >= 4:
        phase_ssm(g)
    if last_phase >= 5:
        phase_final(g)
    es.close()
    return nc


def phase_inproj(g):
    nc = g.nc
    with contextlib.ExitStack() as es:
        wsb = es.enter_context(nc.sbuf_tensor("wsb", [128, 8, USED], BF16))
        gain = es.enter_context(nc.sbuf_tensor("gain", [128, 8], F32))
        with contextlib.ExitStack() as es0:
            wtmp = es0.enter_context(nc.sbuf_tensor("wtmp", [128, 2, USED], F32))
            s_w = [nc.alloc_semaphore(f"p0_w{i}") for i in range(2)]
            s_g = nc.alloc_semaphore("p0_g")
            s_done = [nc.alloc_semaphore(f"p0_d{i}") for i in range(3)]
            cuts = [0, 1536, 4864, USED]
            with nc.Block() as b:
                @b.sync
                def _(sp):
                    sp.dma_start(out=gain[:], in_=g.norm_pre.rearrange("o (k p) -> p (o k)", p=128),
                                 allow_slow_non_contiguous=True).then_inc(s_g, 16)
                    for dk in range(8):
                        if dk >= 2:
                            for e in range(3):
                                sp.wait_ge(s_done[e], dk - 1)
                        sp.dma_start(out=wtmp[:, dk % 2, :],
                                     in_=g.w_in[dk * 128:(dk + 1) * 128, 0:USED]).then_inc(s_w[dk % 2], 16)

                def conv(eng, e, kind):
                    eng.wait_ge(s_g, 16)
                    for dk in range(8):
                        eng.wait_ge(s_w[dk % 2], 16 * (dk // 2 + 1))
                        c0, c1 = cuts[e], cuts[e + 1]
                        if kind == "act":
                            eng.activation(out=wsb[:, dk, c0:c1], in_=wtmp[:, dk % 2, c0:c1],
                                           func=AF.Copy, scale=gain[:, dk:dk + 1]).then_inc(s_done[e], 1)
                        else:
                            eng.tensor_scalar(out=wsb[:, dk, c0:c1], in0=wtmp[:, dk % 2, c0:c1],
                                              scalar1=gain[:, dk:dk + 1], scalar2=None,
                                              op0=ALU.mult).then_inc(s_done[e], 1)

                @b.vector
                def _(e):
                    conv(e, 0, "dve")

                @b.scalar
                def _(e):
                    conv(e, 1, "act")

                @b.gpsimd
                def _(e):
                    conv(e, 2, "pool")

        xs = es.enter_context(nc.sbuf_tensor("xs", [128, 2, 4, D], F32))
        hn = es.enter_context(nc.sbuf_tensor("hn", [128, 2, 4, D], BF16))
        hT = es.enter_context(nc.sbuf_tensor("hT", [128, 2, 8, CH], BF16))
        junk = es.enter_context(nc.sbuf_tensor("junk", [128, 4, D], BF16))
        ss = es.enter_context(nc.sbuf_tensor("ss", [128, 2, 4], F32))
        sd = es.enter_context(nc.sbuf_tensor("sd", [128, 2, 4], F32))
        rstd = es.enter_context(nc.sbuf_tensor("rstd", [128, 2, 4], F32))
        NS = 6
        stage = es.enter_context(nc.sbuf_tensor("stage", [128, NS, CH], BF16))
        stagef = es.enter_context(nc.sbuf_tensor("stagef", [8, 2, CH], F32))
        ps = es.enter_context(nc.psum_tensor("ps", [128, 4, CH], F32))
        tp = es.enter_context(nc.psum_tensor("tp", [128, 2, 1024], BF16))

        blks = []

        def add(kind, col0, m, func, eng, dest, row0):
            blks.append(dict(kind=kind, col0=col0, m=m, func=func, eng=eng, dest=dest, row0=row0))

        for j in range(4):
            add("fm", 0 + 128 * j, 128, AF.Copy, "dve", g.qT, 128 * j)
        for j in range(4):
            add("fm", 512 + 128 * j, 128, AF.Copy, "dve", g.kT, 128 * j)
        for j in range(4):
            add("fm", 2056 + 128 * j, 128, AF.Copy, "dve", g.uT, 128 * j)
        for tt in range(4):
            add("v", 1024, 128, AF.Copy, "dve", g.vtok, tt)
        add("f", 1536, 8, AF.Copy, "dve", g.fT, 0)
        for j in range(4):
            add("fm", 1544 + 128 * j, 128, AF.Silu, "act", g.sgaT, 128 * j)
        for j in range(4):
            add("fm", 2568 + 128 * j, 128, AF.Silu, "act", g.sgsT, 128 * j)
        for j in range(8):
            add("fm", 3080 + 128 * j, 128, AF.Sigmoid, "act", g.smaT, 128 * j)
        for j in range(8):
            add("fm", 4104 + 128 * j, 128, AF.Sigmoid, "act", g.smsT, 128 * j)
        NB = len(blks)
        seq = []
        cnt = {"act": 0, "dve": 0}
        nstage = 0
        for c in range(NCH):
            for j, bk in enumerate(blks):
                cnt[bk["eng"]] += 1
                d = dict(bk)
                d.update(c=c, n=len(seq), eidx=cnt[bk["eng"]])
                if bk["kind"] == "f":
                    d["slot"] = None
                else:
                    d["slot"] = nstage % NS
                    d["suse"] = nstage // NS
                    nstage += 1
                seq.append(d)

        s_x = [nc.alloc_semaphore(f"p1_x{i}") for i in range(2)]
        s_sd = nc.alloc_semaphore("p1_sd")
        s_ss = nc.alloc_semaphore("p1_ss")
        s_ln = nc.alloc_semaphore("p1_ln")
        s_rstd = nc.alloc_semaphore("p1_rstd")
        s_hn = nc.alloc_semaphore("p1_hn")
        s_tp = nc.alloc_semaphore("p1_tp")
        s_hT = nc.alloc_semaphore("p1_hT")
        s_mm = nc.alloc_semaphore("p1_mm")
        s_ev = {"act": nc.alloc_semaphore("p1_eva"), "dve": nc.alloc_semaphore("p1_evd")}
        s_out = [nc.alloc_semaphore(f"p1_o{i}") for i in range(NS)]
        s_outf = [nc.alloc_semaphore(f"p1_of{i}") for i in range(2)]

        def stage_ap(d):
            if d["kind"] == "f":
                return stagef[0:8, d["c"] % 2, :]
            return stage[:, d["slot"], :]

        def dest_ap(d):
            c = d["c"]
            if d["kind"] == "v":
                r0 = c * CH + d["row0"] * 128
                return d["dest"][r0:r0 + 128, :]
            if d["kind"] == "f":
                return d["dest"][0:8, c * CH:(c + 1) * CH]
            return d["dest"][d["row0"]:d["row0"] + 128, c * CH:(c + 1) * CH]

        with nc.Block() as b:
            @b.sync
            def _(sp):
                def load(c):
                    if c >= 2:
                        sp.wait_ge(s_hn, c - 1)
                    sp.dma_start(out=xs[:, c % 2, :, :],
                                 in_=g.x[c * CH:(c + 1) * CH, :].rearrange("(t p) d -> p t d", p=128)
                                 ).then_inc(s_x[c % 2], 16)
                load(0)
                load(1)
                for c in range(NCH):
                    if c + 2 < NCH:
                        load(c + 2)
                    for d in seq[c * NB:(c + 1) * NB]:
                        sp.wait_ge(s_ev[d["eng"]], d["eidx"])
                        so = s_outf[c % 2] if d["kind"] == "f" else s_out[d["slot"]]
                        sp.dma_start(out=dest_ap(d), in_=stage_ap(d)).then_inc(so, 16)
                for i in range(NS):
                    uses = len([d for d in seq if d["slot"] == i])
                    sp.wait_ge(s_out[i], 16 * uses)
                for i in range(2):
                    sp.wait_ge(s_outf[i], 16 * (NCH // 2))

            def evac(eng, d, is_act):
                eng.wait_ge(s_mm, d["n"] + 1)
                if d["kind"] == "f":
                    if d["c"] >= 2:
                        eng.wait_ge(s_outf[d["c"] % 2], 16 * (d["c"] // 2))
                    src = ps[0:8, d["n"] % 4, :]
                else:
                    if d["suse"] >= 1:
                        eng.wait_ge(s_out[d["slot"]], 16 * d["suse"])
                    src = ps[:, d["n"] % 4, :]
                if is_act:
                    eng.activation(out=stage_ap(d), in_=src, func=d["func"]).then_inc(s_ev["act"], 1)
                else:
                    eng.tensor_copy(out=stage_ap(d), in_=src).then_inc(s_ev["dve"], 1)

            @b.scalar
            def _(act):
                def stats(c):
                    sl = c % 2
                    act.wait_ge(s_x[sl], 16 * (c // 2 + 1))
                    for tt in range(4):
                        act.activation(out=junk[:, tt, :], in_=xs[:, sl, tt, :], func=AF.Square,
                                       accum_out=ss[:, sl, tt:tt + 1]).then_inc(s_ss, 1)
                    act.wait_ge(s_ss, 4 * (c + 1))
                    act.activation(out=ss[:, sl, :], in_=ss[:, sl, :], func=AF.Ln,
                                   scale=1.0 / D, bias=EPS).then_inc(s_ln, 1)
                    act.wait_ge(s_ln, c + 1)
                    act.activation(out=sd[:, sl, :], in_=ss[:, sl, :], func=AF.Exp,
                                   scale=-0.5).then_inc(s_sd, 1)
                    act.wait_ge(s_sd, c + 1)
                    if c >= 2:
                        act.wait_ge(s_tp, 8 * (c - 1))
                    for tt in range(4):
                        ins = act.activation(out=hn[:, sl, tt, :], in_=xs[:, sl, tt, :], func=AF.Copy,
                                             scale=sd[:, sl, tt:tt + 1])
                    ins.then_inc(s_hn, 1)
                stats(0)
                stats(1)
                for c in range(NCH):
                    if c + 2 < NCH:
                        stats(c + 2)
                    for d in seq[c * NB:(c + 1) * NB]:
                        if d["eng"] == "act":
                            evac(act, d, True)

            @b.vector
            def _(dve):
                def pro(c):
                    sl = c % 2
                    if c >= 2:
                        dve.wait_ge(s_mm, (c - 1) * NB)
                    for dk in range(8):
                        dve.wait_ge(s_tp, c * 8 + dk + 1)
                        dve.tensor_copy(out=hT[:, sl, dk, :], in_=tp[:, dk % 2, 0:CH]).then_inc(s_hT, 1)
                pro(0)
                for c in range(NCH):
                    if c + 1 < NCH:
                        pro(c + 1)
                    for d in seq[c * NB:(c + 1) * NB]:
                        if d["eng"] == "dve":
                            evac(dve, d, False)


            @b.tensor
            def _(pe):
                def trans(c):
                    sl = c % 2
                    pe.wait_ge(s_hn, c + 1)
                    for dk in range(8):
                        gi = c * 8 + dk
                        if gi >= 2:
                            pe.wait_ge(s_hT, gi - 1)
                        for tt in range(4):
                            ins = pe.transpose(out=tp[:, dk % 2, tt * 128:(tt + 1) * 128],
                                               in_=hn[:, sl, tt, dk * 128:(dk + 1) * 128], identity=g.ident[:])
                        ins.then_inc(s_tp, 1)
                trans(0)
                for c in range(NCH):
                    sl = c % 2
                    if c + 1 < NCH:
                        trans(c + 1)
                    pe.wait_ge(s_hT, 8 * (c + 1))
                    for d in seq[c * NB:(c + 1) * NB]:
                        n = d["n"]
                        if n >= 4:
                            pd = seq[n - 4]
                            pe.wait_ge(s_ev[pd["eng"]], pd["eidx"])
                        for dk in range(8):
                            if d["kind"] == "v":
                                tt = d["row0"]
                                ins = pe.matmul(ps[:, n % 4, :], lhsT=hT[:, sl, dk, tt * 128:(tt + 1) * 128],
                                                rhs=wsb[:, dk, 1024:1536], start=(dk == 0), stop=(dk == 7))
                            else:
                                m = d["m"]
                                ins = pe.matmul(ps[0:m, n % 4, :], lhsT=wsb[:, dk, d["col0"]:d["col0"] + m],
                                                rhs=hT[:, sl, dk, :], start=(dk == 0), stop=(dk == 7))
                        ins.then_inc(s_mm, 1)


def phase_forget(g):
    nc = g.nc
    SEG = 16
    SL = SEQ // SEG
    with contextlib.ExitStack() as es:
        ft = es.enter_context(nc.sbuf_tensor("ft", [128, SL], F32))
        t1 = es.enter_context(nc.sbuf_tensor("fg_t1", [128, SL], F32))
        t2 = es.enter_context(nc.sbuf_tensor("fg_t2", [128, SL], F32))
        ones = es.enter_context(nc.sbuf_tensor("fg_ones", [128, SL], F32))
        rows = es.enter_context(nc.sbuf_tensor("fg_rows", [128, 6, SL], BF16))
        bfn = es.enter_context(nc.sbuf_tensor("bfn", [128, 1], F32))
        M = es.enter_context(nc.sbuf_tensor("fg_M", [128, 128], F32))
        tot = es.enter_context(nc.sbuf_tensor("fg_tot", [128, 2], F32))
        off = es.enter_context(nc.sbuf_tensor("fg_off", [128, 2], F32))
        offp = es.enter_context(nc.psum_tensor("fg_offp", [128, 2], F32))
        S = Steps(nc, "p2")

        def ld(sp, raw):
            L = [raw.dma_start(out=ft[:], in_=g.fT.rearrange("h (s t) -> (h s) t", s=SEG))]
            for h in range(H):
                L.append(raw.dma_start(out=bfn[SEG * h:SEG * (h + 1), :],
                                       in_=bass.AP(g.b_forget.tensor, h, [[0, SEG], [1, 1]]),
                                       allow_slow_non_contiguous=True))
            sp.many(L, 16)

        def mk(p, raw):
            p.affine_select(out=M[:], in_=g.ones_f[:], pattern=[[1, 128]], compare_op=ALU.is_gt, fill=0.0,
                            base=0, channel_multiplier=-1)
            m3 = M[:].rearrange("p (h s) -> p h s", s=SEG)
            p.affine_select(out=m3, in_=m3, pattern=[[-SEG, H], [0, SEG]], compare_op=ALU.is_ge, fill=0.0,
                            base=0, channel_multiplier=1)
            p.affine_select(out=m3, in_=m3, pattern=[[SEG, H], [0, SEG]], compare_op=ALU.is_ge, fill=0.0,
                            base=SEG - 1, channel_multiplier=-1)
            p.memset(ones[:], 1.0)
        S.run(sync=ld, gpsimd=mk)
        S.run(vector=lambda v, raw: v.tensor_scalar(out=bfn[:], in0=bfn[:], scalar1=-1.0, scalar2=None, op0=ALU.mult))

        def a(act, raw):
            act.activation(out=t1[:], in_=ft[:], func=AF.Exp, scale=-1.0, bias=bfn[:, 0:1])
            act.activation(out=t2[:], in_=t1[:], func=AF.Ln, bias=1.0, scale=1.0)
        S.run(scalar=a)

        def d1(dve, raw):
            dve.tensor_tensor_scan(out=t1[:], data0=ones[:], data1=t2[:], initial=0.0, op0=ALU.mult, op1=ALU.add)
            dve.tensor_copy(out=tot[:, 0:1], in_=t1[:, SL - 1:SL])
            dve.tensor_copy(out=tot[:, 1:2], in_=t1[:, SL - 1:SL])
        S.run(vector=d1)
        S.run(tensor=lambda pe, raw: pe.matmul(offp[:, :], lhsT=M[:], rhs=tot[:], start=True, stop=True))

        def d2(dve, raw):
            dve.tensor_copy(out=off[:], in_=offp[:, :])
            dve.tensor_scalar(out=t1[:], in0=t1[:], scalar1=off[:, 0:1], scalar2=8.0, op0=ALU.add, op1=ALU.mult)
            dve.tensor_copy(out=rows[:, 0, :], in_=t1[:])
            dve.tensor_tensor(out=t2[:], in0=t1[:], in1=rows[:, 0, :], op=ALU.subtract)
            dve.tensor_copy(out=rows[:, 1, :], in_=t2[:])
            dve.tensor_tensor(out=t1[:], in0=t2[:], in1=rows[:, 1, :], op=ALU.subtract)
            dve.tensor_copy(out=rows[:, 2, :], in_=t1[:])
            dve.tensor_scalar(out=rows[:, 3:6, :], in0=rows[:, 0:3, :], scalar1=-1.0, scalar2=None, op0=ALU.mult)
        S.run(vector=d2)

        def st(sp, raw):
            L = []
            for j in range(3):
                L.append(raw.dma_start(out=g.crk[j].rearrange("h (s t) -> (h s) t", s=SEG), in_=rows[:, j, :]))
                L.append(raw.dma_start(out=g.crq[j].rearrange("h (s t) -> (h s) t", s=SEG), in_=rows[:, 3 + j, :]))
            sp.many(L, 16)
        S.run(sync=st)


def phase_attn(g):
    nc = g.nc
    NQ = SEQ // CH
    NKT = SEQ // 128
    with contextlib.ExitStack() as es:
        kTa = es.enter_context(nc.sbuf_tensor("kTa", [70, 2, SEQ], BF16))
        qTa = es.enter_context(nc.sbuf_tensor("qTa", [70, 2, SEQ], BF16))
        vsb = es.enter_context(nc.sbuf_tensor("vsb", [128, 2, NKT, 128], BF16))
        sga = es.enter_context(nc.sbuf_tensor("sga", [64, 2, SEQ], BF16))
        pT = es.enter_context(nc.sbuf_tensor("pT", [128, 3, 3, CH], BF16))
        rl = es.enter_context(nc.sbuf_tensor("rl", [64, CH], F32))
        yt = es.enter_context(nc.sbuf_tensor("yt", [64, CH], F32))
        ystage = es.enter_context(nc.sbuf_tensor("ystage", [64, 2, CH], BF16))
        sp_ps = es.enter_context(nc.psum_tensor("sp_ps", [128, 2, 3, CH], F32))
        o_ps = es.enter_context(nc.psum_tensor("o_ps", [128, 2, CH], F32))
        s_ms = nc.alloc_semaphore("p3_ms")
        s_ms2 = nc.alloc_semaphore("p3_ms2")
        s_pw = nc.alloc_semaphore("p3_pw")
        s_pc = nc.alloc_semaphore("p3_pc")
        s_pd = [nc.alloc_semaphore(f"p3_pd{i}") for i in range(2)]
        wtmpP = es.enter_context(nc.sbuf_tensor("wtmpP", [128, 2, D], F32))
        s_dv = nc.alloc_semaphore("p3_dv")
        s_ld = [nc.alloc_semaphore(f"p3_ld{i}") for i in range(2)]
        s_S = nc.alloc_semaphore("p3_S")
        s_exp = nc.alloc_semaphore("p3_exp")
        s_pv = nc.alloc_semaphore("p3_pv")
        s_fin = nc.alloc_semaphore("p3_fin")
        s_yo = [nc.alloc_semaphore(f"p3_yo{i}") for i in range(2)]

        GMAX = 3
        groups = []
        qlast = {}
        for h in range(H):
            for Q in range(NQ):
                nk = 4 * Q + 4
                full = [(kt, kt >= 4 * Q) for kt in range(4 * Q + 1)]
                cur = []
                glist = []
                for t in full:
                    cur.append(t)
                    if len(cur) == GMAX:
                        glist.append((0, cur)); cur = []
                if cur:
                    glist.append((0, cur))
                for kt in range(4 * Q + 1, nk):
                    glist.append(((kt - 4 * Q) * 128, [(kt, True)]))
                for gi_, (n0, tl) in enumerate(glist):
                    groups.append(dict(h=h, Q=Q, n0=n0, tiles=tl, first=(gi_ == 0), last=(gi_ == len(glist) - 1),
                                       i=len(groups)))
                qlast[(h, Q)] = len(groups) - 1
        head_first = {h: min(t["i"] for t in groups if t["h"] == h) for h in range(H)}
        head_last = {h: max(t["i"] for t in groups if t["h"] == h) for h in range(H)}
        NLD = 13

        with nc.Block() as b:
            @b.gpsimd
            def _(p):
                for k, ap in enumerate((kTa[64:70, 0, :], kTa[64:70, 1, :], vsb[:, 0, :, 64:128])):
                    p.memset(ap, 1.0).then_inc(s_ms, 1)
                    p.wait_ge(s_ms, k + 1)
                p.dma_start(out=g.bg[:], in_=g.b_glu.rearrange("o (j p) -> p (o j)", p=128),
                            allow_slow_non_contiguous=True).then_inc(s_pw, 16)
                p.dma_start(out=g.gpost[:], in_=bass.AP(g.norm_post.tensor, 0, [[0, 128], [1, D]])).then_inc(s_pw, 16)
                p.wait_ge(s_pw, 32)
                jobs = []
                for (src, dst, nk, nco) in ((g.w_glu, g.wg, 4, 512), (g.w_branch_a, g.wa, 4, D),
                                            (g.w_branch_s, g.wsr, 4, D), (g.w_out, g.wo, 8, D)):
                    for k in range(nk):
                        jobs.append((src[128 * k:128 * (k + 1), :], dst[:, k, :], nco))

                def pdma(i):
                    srcap, _, nco = jobs[i]
                    p.dma_start(out=wtmpP[:, i % 2, 0:nco], in_=srcap).then_inc(s_pd[i % 2], 16)
                pdma(0)
                pdma(1)
                for i, (srcap, dstap, nco) in enumerate(jobs):
                    p.wait_ge(s_pd[i % 2], 16 * (i // 2 + 1))
                    p.tensor_copy(out=dstap, in_=wtmpP[:, i % 2, 0:nco]).then_inc(s_pc, 1)
                    p.wait_ge(s_pc, i + 1)
                    if i + 2 < len(jobs):
                        pdma(i + 2)

            @b.sync
            def _(sp):
                def load(h, first=False):
                    sl = h % 2
                    if h >= 2:
                        sp.wait_ge(s_pv, head_last[h - 2] + 1)
                        sp.wait_ge(s_fin, NQ * (h - 1))
                    sp.dma_start(out=kTa[0:64, sl, :], in_=g.kT[h * 64:(h + 1) * 64, :]).then_inc(s_ld[sl], 16)
                    sp.dma_start(out=qTa[0:64, sl, :], in_=g.qT[h * 64:(h + 1) * 64, :]).then_inc(s_ld[sl], 16)
                    vsrc = g.vtok[:, h * 64:(h + 1) * 64].rearrange("(kt p) d -> p kt d", p=128)
                    for part in range(8):
                        sp.dma_start(out=vsb[:, sl, part * 8:(part + 1) * 8, 0:64],
                                     in_=vsrc[:, part * 8:(part + 1) * 8, :]).then_inc(s_ld[sl], 16)
                    sp.dma_start(out=sga[:, sl, :], in_=g.sgaT[h * 64:(h + 1) * 64, :]).then_inc(s_ld[sl], 16)
                    if first:
                        sp.wait_ge(s_ms, 3)
                        sp.wait_ge(s_ms2, 3)
                    sp.dma_start(out=kTa[67:70, sl, :], in_=g.crk[:, h, :]).then_inc(s_ld[sl], 16)
                    sp.dma_start(out=qTa[64:67, sl, :], in_=g.crq[:, h, :]).then_inc(s_ld[sl], 16)
                load(0, True)
                load(1)
                for h in range(H):
                    for Q in range(NQ):
                        qi = h * NQ + Q
                        sp.wait_ge(s_fin, qi + 1)
                        sp.dma_start(out=g.yaT[h * 64:(h + 1) * 64, Q * CH:(Q + 1) * CH],
                                     in_=ystage[:, qi % 2, :]).then_inc(s_yo[qi % 2], 16)
                    if h + 2 < H:
                        load(h + 2)
                for i in range(2):
                    sp.wait_ge(s_yo[i], 16 * (H * NQ // 2))

            @b.tensor
            def _(pe):
                def S(t):
                    i = t["i"]
                    sl = t["h"] % 2
                    if i == head_first[t["h"]]:
                        pe.wait_ge(s_ld[sl], 16 * NLD * (t["h"] // 2 + 1))
                    if i >= 2:
                        pe.wait_ge(s_exp, i - 1)
                    n0 = t["n0"]
                    q0 = t["Q"] * CH
                    for j, (kt, diag) in enumerate(t["tiles"]):
                        ins = pe.matmul(sp_ps[:, i % 2, j, n0:CH], lhsT=kTa[0:70, sl, kt * 128:(kt + 1) * 128],
                                        rhs=qTa[0:70, sl, q0 + n0:q0 + CH], start=True, stop=not diag)
                        if diag:
                            ins = pe.matmul(sp_ps[:, i % 2, j, n0:n0 + 128], lhsT=g.ident[:], rhs=g.maskT[:],
                                            start=False, stop=True)
                    ins.then_inc(s_S, 1)

                def PV(t):
                    i = t["i"]
                    sl = t["h"] % 2
                    qi = t["h"] * NQ + t["Q"]
                    pe.wait_ge(s_exp, i + 1)
                    if t["first"] and qi >= 2:
                        pe.wait_ge(s_fin, qi - 1)
                    n0 = t["n0"]
                    nt = len(t["tiles"])
                    for j, (kt, diag) in enumerate(t["tiles"]):
                        ins = pe.matmul(o_ps[:, qi % 2, n0:CH], lhsT=vsb[:, sl, kt, :], rhs=pT[:, i % 3, j, n0:CH],
                                        start=(t["first"] and j == 0), stop=(t["last"] and j == nt - 1))
                    ins.then_inc(s_pv, 1)

                n = len(groups)
                S(groups[0])
                S(groups[1])
                for i in range(n):
                    if i + 2 < n:
                        S(groups[i + 2])
                    PV(groups[i])

            @b.scalar
            def _(act):
                for t in groups:
                    i = t["i"]
                    act.wait_ge(s_S, i + 1)
                    if i >= 3:
                        act.wait_ge(s_pv, i - 2)
                    n0 = t["n0"]
                    nt = len(t["tiles"])
                    act.activation(out=pT[:, i % 3, 0:nt, n0:CH], in_=sp_ps[:, i % 2, 0:nt, n0:CH], func=AF.Exp,
                                   scale=0.125).then_inc(s_exp, 1)

            @b.vector
            def _(dve):
                for k, ap in enumerate((qTa[64:70, 0, :], qTa[64:70, 1, :], vsb[:, 1, :, 64:128])):
                    dve.memset(ap, 1.0).then_inc(s_ms2, 1)
                    dve.wait_ge(s_ms2, k + 1)
                for h in range(H):
                    sl = h % 2
                    dve.wait_ge(s_ld[sl], 16 * NLD * (h // 2 + 1))
                    for Q in range(NQ):
                        qi = h * NQ + Q
                        dve.wait_ge(s_pv, qlast[(h, Q)] + 1)
                        if qi >= 1:
                            dve.wait_ge(s_fin, qi)
                        if qi >= 2:
                            dve.wait_ge(s_yo[qi % 2], 16 * (qi // 2))
                        dve.reciprocal(out=rl[:], in_=o_ps[64:128, qi % 2, :]).then_inc(s_dv, 1)
                        dve.wait_ge(s_dv, 2 * qi + 1)
                        dve.tensor_tensor(out=yt[:], in0=o_ps[0:64, qi % 2, :], in1=rl[:], op=ALU.mult).then_inc(s_dv, 1)
                        dve.wait_ge(s_dv, 2 * qi + 2)
                        dve.tensor_tensor(out=ystage[:, qi % 2, :], in0=yt[:], in1=sga[:, sl, Q * CH:(Q + 1) * CH],
                                          op=ALU.mult).then_inc(s_fin, 1)


class Ser:
    def __init__(self, eng, st):
        self.e = eng
        self.st = st

    def done(self, ins, inc=1):
        ins.then_inc(self.st["sem"], inc)
        self.st["n"] += inc
        self.e.wait_ge(self.st["sem"], self.st["n"])
        return ins

    def many(self, instrs, inc=1):
        for ins in instrs:
            ins.then_inc(self.st["sem"], inc)
            self.st["n"] += inc
        self.e.wait_ge(self.st["sem"], self.st["n"])

    def __getattr__(self, name):
        f = getattr(self.e, name)
        inc = 16 if name == "dma_start" else 1

        def w(*a, **k):
            return self.done(f(*a, **k), inc)
        return w


class Steps:
    def __init__(self, nc, tag):
        self.nc = nc
        self.st = {e: {"sem": nc.alloc_semaphore(f"{tag}_{e}"), "n": 0}
                   for e in ("sync", "vector", "scalar", "gpsimd", "tensor")}

    def run(self, **fns):
        with self.nc.Block() as b:
            for ename, fn in fns.items():
                def mk(fn, ename):
                    def body(e):
                        fn(Ser(e, self.st[ename]), e)
                    return body
                getattr(b, ename)(mk(fn, ename))


TWO_PI = 6.28318
HALF_PI = 1.5707963


def phase_ssm(g):
    nc = g.nc
    NK = SEQ // 16
    with contextlib.ExitStack() as es:
        def sb(name, shape, dt=F32):
            return es.enter_context(nc.sbuf_tensor("ssm_" + name, list(shape), dt))
        S = Steps(nc, "p4")
        identf = sb("identf", [128, 128])
        pm = sb("pm", [128, 2])
        bm = sb("bm", [128, 8])
        kidx_i = sb("kidx_i", [128, NK], I32)
        kidx = sb("kidx", [128, NK])
        midx = sb("midx", [128, 17])
        LR = sb("LR", [128, 4]); LI = sb("LI", [128, 4]); LDT = sb("LDT", [128, 4])
        BR = sb("BR", [128, 4, 16]); BI = sb("BI", [128, 4, 16])
        CNr = sb("CNr", [64, 128]); CNi = sb("CNi", [64, 128])
        Dcol = sb("Dcol", [128, 1])
        dt_ = sb("dt", [128, 4]); lrdt = sb("lrdt", [128, 4]); lidt = sb("lidt", [128, 4]); phi = sb("phi", [128, 4])
        tA = sb("tA", [128, 4, 17]); tB = sb("tB", [128, 4, 17]); tC = sb("tC", [128, 4, 17]); tI = sb("tI", [128, 4, 17], I32)
        EAr = sb("EAr", [128, 4, 17]); EAi = sb("EAi", [128, 4, 17])
        s1 = sb("s1", [128, 4]); s2 = sb("s2", [128, 4]); s3 = sb("s3", [128, 4]); s4 = sb("s4", [128, 4])
        fr = sb("fr", [128, 4]); fi = sb("fi", [128, 4]); sI = sb("sI", [128, 4], I32)
        Bbr = sb("Bbr", [128, 4, 16]); Bbi = sb("Bbi", [128, 4, 16]); b1 = sb("b1", [128, 4, 16])
        Gr = sb("Gr", [128, 4, 16, 16]); Gi = sb("Gi", [128, 4, 16, 16]); G1 = sb("G1", [128, 4, 16, 16])
        Gx = sb("Gx", [128, 16, 2, 4, 2, 16], BF16)
        CTr = sb("CTr", [128, 4, 16]); CTi = sb("CTi", [128, 4, 16])
        CTrb = sb("CTrb", [128, 4, 16], BF16); CTnib = sb("CTnib", [128, 4, 16], BF16)
        Wc = sb("Wc", [128, 4, 16, 2, 2, 16], BF16)
        Wb = sb("Wb", [128, 16, 2, 128], BF16)
        Kst = sb("Kst", [128, 16, 8, 16], BF16)
        cosT = sb("cosT", [128, 4, NK], BF16); sinT = sb("sinT", [128, 4, NK], BF16)
        rho = sb("rho", [128, 4]); phr = sb("phr", [128, 4])
        uN = sb("uN", [128, SEQ], BF16); uP = sb("uP", [128, 16, NK], BF16)
        Sp = sb("Sp", [128, 4, 2, NK], BF16)
        w1 = sb("w1", [128, 4, NK]); w2 = sb("w2", [128, 4, NK]); w3 = sb("w3", [128, 4, NK]); w4 = sb("w4", [128, 4, NK])
        kA = w1; kB = w2; kIv = w3[:].bitcast(I32)
        ystage = sb("ystage", [128, SEQ], BF16)
        yperm = sb("yperm", [128, 16, NK], BF16)

        def bc(ap, shape, axis):
            return ap.unsqueeze(axis).to_broadcast(list(shape))

        def c0(p, raw):
            p.memset(identf[:], 0.0)
            p.affine_select(out=identf[:], in_=g.ones_f[:], pattern=[[-1, 128]], compare_op=ALU.is_equal,
                            fill=0.0, base=0, channel_multiplier=1)
            p.affine_select(out=pm[:], in_=g.ones_f[:, 0:2], pattern=[[-64, 2]], compare_op=ALU.is_ge,
                            fill=0.0, base=0, channel_multiplier=1)
            p.affine_select(out=pm[:], in_=pm[:], pattern=[[64, 2]], compare_op=ALU.is_ge,
                            fill=0.0, base=63, channel_multiplier=-1)
            p.affine_select(out=bm[:], in_=g.ones_f[:, 0:8], pattern=[[-16, 8]], compare_op=ALU.is_ge,
                            fill=0.0, base=0, channel_multiplier=1)
            p.affine_select(out=bm[:], in_=bm[:], pattern=[[16, 8]], compare_op=ALU.is_ge,
                            fill=0.0, base=15, channel_multiplier=-1)
            p.iota(kidx_i[:], pattern=[[1, NK]], base=0, channel_multiplier=0)
            p.memset(Sp[:], 0.0)
        S.run(gpsimd=c0)

        def c1(v, raw):
            v.tensor_copy(out=kidx[:], in_=kidx_i[:])
            v.tensor_copy(out=midx[:], in_=kidx[:, 0:17])
        S.run(vector=c1)

        lamr = g.lam_re.rearrange("(q t) p -> (t p) q", t=2)
        lami = g.lam_im.rearrange("(q t) p -> (t p) q", t=2)
        bre = g.b_re.rearrange("(q t) p c -> (t p) q c", t=2)
        bim = g.b_im.rearrange("(q t) p c -> (t p) q c", t=2)

        for r in range(4):
            def ld(sp, raw, r=r):
                L = []
                L.append(raw.dma_start(out=LR[:], in_=lamr[:, 4 * r:4 * r + 4], allow_slow_non_contiguous=True))
                L.append(raw.dma_start(out=LI[:], in_=lami[:, 4 * r:4 * r + 4], allow_slow_non_contiguous=True))
                for t in range(2):
                    L.append(raw.dma_start(out=LDT[64 * t:64 * t + 64, :],
                                           in_=bass.AP(g.log_dt.tensor, t + 8 * r, [[0, 64], [2, 4]]),
                                           allow_slow_non_contiguous=True))
                L.append(raw.dma_start(out=BR[:], in_=bre[:, 4 * r:4 * r + 4, :]))
                L.append(raw.dma_start(out=BI[:], in_=bim[:, 4 * r:4 * r + 4, :]))
                for qq in range(4):
                    q = 4 * r + qq
                    L.append(raw.dma_start(out=CNr[16 * qq:16 * qq + 16, :].rearrange("c (t p) -> c t p", t=2),
                                           in_=g.c_re[2 * q:2 * q + 2, :, :].rearrange("t c p -> c t p")))
                    L.append(raw.dma_start(out=CNi[16 * qq:16 * qq + 16, :].rearrange("c (t p) -> c t p", t=2),
                                           in_=g.c_im[2 * q:2 * q + 2, :, :].rearrange("t c p -> c t p")))
                L.append(raw.dma_start(out=Dcol[:],
                                       in_=g.d_skip[8 * r:8 * r + 8, :].rearrange("g (c o) -> (g c) o", o=1)))
                L.append(raw.dma_start(out=uN[:], in_=g.uT[128 * r:128 * r + 128, :]))
                sp.many(L, 16)
            def make_ld(rr):
                return lambda sp, raw: ld(sp, raw, rr)
            if r == 0:
                S.run(sync=ld)

            if r == 0:
                S.run(scalar=lambda a, raw: a.activation(out=dt_[:], in_=LDT[:], func=AF.Exp))
            else:
                S.run(scalar=lambda a, raw: a.activation(out=dt_[:], in_=LDT[:], func=AF.Exp),
                      sync=lambda sp, raw: sp.dma_start(out=g.ys1T[128 * (r - 1):128 * r, :], in_=ystage[:]))

            def a1(v, raw):
                v.tensor_tensor(out=lrdt[:], in0=LR[:], in1=dt_[:], op=ALU.mult)
                v.tensor_tensor(out=lidt[:], in0=LI[:], in1=dt_[:], op=ALU.mult)
                v.tensor_scalar(out=phi[:], in0=lidt[:], scalar1=1.0 / (2 * math.pi), scalar2=None, op0=ALU.mult)
                v.tensor_tensor(out=tA[:], in0=bc(phi[:], [128, 4, 17], 2), in1=bc(midx[:], [128, 4, 17], 1), op=ALU.mult)
                v.tensor_tensor(out=tB[:], in0=bc(lrdt[:], [128, 4, 17], 2), in1=bc(midx[:], [128, 4, 17], 1), op=ALU.mult)
                v.tensor_copy(out=tI[:], in_=tA[:])
                v.tensor_copy(out=tC[:], in_=tI[:])
                v.tensor_tensor(out=tA[:], in0=tA[:], in1=tC[:], op=ALU.subtract)
                v.tensor_scalar(out=tC[:], in0=tA[:], scalar1=-1.0, scalar2=None, op0=ALU.mult)
                v.tensor_tensor(out=tC[:], in0=tA[:], in1=tC[:], op=ALU.max)
                v.tensor_scalar(out=s1[:], in0=phi[:], scalar1=16.0, scalar2=None, op0=ALU.mult)
                v.tensor_copy(out=sI[:], in_=s1[:])
                v.tensor_copy(out=s2[:], in_=sI[:])
                v.tensor_tensor(out=phr[:], in0=s1[:], in1=s2[:], op=ALU.subtract)
            S.run(vector=a1)

            def a2(a, raw):
                a.activation(out=EAi[:], in_=tA[:], func=AF.Sin, scale=TWO_PI)
                a.activation(out=EAr[:], in_=tC[:], func=AF.Sin, scale=-TWO_PI, bias=HALF_PI)
                a.activation(out=tB[:], in_=tB[:], func=AF.Exp)
                a.activation(out=rho[:], in_=lrdt[:], func=AF.Exp, scale=16.0)
            S.run(scalar=a2)

            def a3(v, raw):
                v.tensor_tensor(out=EAr[:], in0=EAr[:], in1=tB[:], op=ALU.mult)
                v.tensor_tensor(out=EAi[:], in0=EAi[:], in1=tB[:], op=ALU.mult)
                v.tensor_scalar(out=s1[:], in0=EAr[:, :, 1], scalar1=-1.0, scalar2=None, op0=ALU.add)
                v.tensor_tensor(out=s2[:], in0=LR[:], in1=LR[:], op=ALU.mult)
                v.tensor_tensor(out=s3[:], in0=LI[:], in1=LI[:], op=ALU.mult)
                v.tensor_tensor(out=s2[:], in0=s2[:], in1=s3[:], op=ALU.add)
                v.reciprocal(out=s2[:], in_=s2[:])
                v.tensor_tensor(out=s3[:], in0=s1[:], in1=LR[:], op=ALU.mult)
                v.tensor_tensor(out=s4[:], in0=EAi[:, :, 1], in1=LI[:], op=ALU.mult)
                v.tensor_tensor(out=s3[:], in0=s3[:], in1=s4[:], op=ALU.add)
                v.tensor_tensor(out=fr[:], in0=s3[:], in1=s2[:], op=ALU.mult)
                v.tensor_tensor(out=s3[:], in0=EAi[:, :, 1], in1=LR[:], op=ALU.mult)
                v.tensor_tensor(out=s4[:], in0=s1[:], in1=LI[:], op=ALU.mult)
                v.tensor_tensor(out=s3[:], in0=s3[:], in1=s4[:], op=ALU.subtract)
                v.tensor_tensor(out=fi[:], in0=s3[:], in1=s2[:], op=ALU.mult)
                frb = bc(fr[:], [128, 4, 16], 2); fib = bc(fi[:], [128, 4, 16], 2)
                v.tensor_tensor(out=Bbr[:], in0=BR[:], in1=frb, op=ALU.mult)
                v.tensor_tensor(out=b1[:], in0=BI[:], in1=fib, op=ALU.mult)
                v.tensor_tensor(out=Bbr[:], in0=Bbr[:], in1=b1[:], op=ALU.subtract)
                v.tensor_tensor(out=Bbi[:], in0=BI[:], in1=frb, op=ALU.mult)
                v.tensor_tensor(out=b1[:], in0=BR[:], in1=fib, op=ALU.mult)
                v.tensor_tensor(out=Bbi[:], in0=Bbi[:], in1=b1[:], op=ALU.add)
                sh = [128, 4, 16, 16]
                ear = bc(EAr[:, :, 0:16], sh, 3); eai = bc(EAi[:, :, 0:16], sh, 3)
                bbr = bc(Bbr[:], sh, 2); bbi = bc(Bbi[:], sh, 2)
                v.tensor_tensor(out=Gr[:], in0=ear, in1=bbr, op=ALU.mult)
                v.tensor_tensor(out=G1[:], in0=eai, in1=bbi, op=ALU.mult)
                v.tensor_tensor(out=Gr[:], in0=Gr[:], in1=G1[:], op=ALU.subtract)
                v.tensor_tensor(out=Gi[:], in0=ear, in1=bbi, op=ALU.mult)
                v.tensor_tensor(out=G1[:], in0=eai, in1=bbr, op=ALU.mult)
                v.tensor_tensor(out=Gi[:], in0=Gi[:], in1=G1[:], op=ALU.add)
                for x, Gsrc in enumerate((Gr, Gi)):
                    for g2 in range(2):
                        v.tensor_scalar(out=Gx[:, :, x, :, g2, :].rearrange("p m q c -> p q m c"), in0=Gsrc[:],
                                        scalar1=pm[:, g2:g2 + 1], scalar2=None, op0=ALU.mult)
            S.run(vector=a3)

            with nc.psum_tensor(f"ssm_ctp{r}", [128, 2, 64], F32) as ctp, \
                    nc.psum_tensor(f"ssm_wbp{r}", [128, 4, 8, 128], BF16) as wbp, \
                    nc.psum_tensor(f"ssm_kc{r}", [128, 16, 16], F32) as kc:
                def t1(pe, raw):
                    raw.transpose(out=ctp[:, 0, :], in_=CNr[:, :], identity=identf[0:64, 0:64])
                    raw.transpose(out=ctp[:, 1, :], in_=CNi[:, :], identity=identf[0:64, 0:64])
                    for m in range(16):
                        for x in range(2):
                            idx = m * 2 + x
                            ins = raw.transpose(out=wbp[:, idx // 8, idx % 8, :],
                                                in_=Gx[:, m, x, :, :, :].rearrange("p q t c -> p (q t c)"),
                                                identity=g.ident[:])
                    pe.done(ins)
                S.run(tensor=t1)

                def t2(v, raw):
                    v.tensor_copy(out=CTr[:].rearrange("p q c -> p (q c)"), in_=ctp[:, 0, :])
                    v.tensor_copy(out=CTi[:].rearrange("p q c -> p (q c)"), in_=ctp[:, 1, :])
                    v.tensor_copy(out=CTrb[:], in_=CTr[:])
                    v.tensor_scalar(out=CTnib[:], in0=CTi[:], scalar1=-1.0, scalar2=None, op0=ALU.mult)
                    for bk in range(4):
                        v.tensor_copy(out=Wb[:, 4 * bk:4 * bk + 4, :, :].rearrange("p m x c -> p (m x) c"),
                                      in_=wbp[:, bk, :, :])
                    sh = [128, 4, 16, 16]
                    ear = bc(EAr[:, :, 1:17], sh, 3); eai = bc(EAi[:, :, 1:17], sh, 3)
                    ctr = bc(CTr[:], sh, 2); cti = bc(CTi[:], sh, 2)
                    v.tensor_tensor(out=Gr[:], in0=ear, in1=ctr, op=ALU.mult)
                    v.tensor_tensor(out=G1[:], in0=eai, in1=cti, op=ALU.mult)
                    v.tensor_tensor(out=Gr[:], in0=Gr[:], in1=G1[:], op=ALU.subtract)
                    v.tensor_tensor(out=Gi[:], in0=eai, in1=ctr, op=ALU.mult)
                    v.tensor_tensor(out=G1[:], in0=ear, in1=cti, op=ALU.mult)
                    v.tensor_tensor(out=Gi[:], in0=Gi[:], in1=G1[:], op=ALU.add)
                    v.tensor_scalar(out=Gi[:], in0=Gi[:], scalar1=-1.0, scalar2=None, op0=ALU.mult)
                    for x, Csrc in enumerate((Gr, Gi)):
                        for g2 in range(2):
                            v.tensor_scalar(out=Wc[:, :, :, x, g2, :], in0=Csrc[:], scalar1=pm[:, g2:g2 + 1],
                                            scalar2=None, op0=ALU.mult)
                S.run(vector=t2)

                def t3(pe, raw):
                    for lag in range(16):
                        for qq in range(4):
                            raw.matmul(kc[32 * qq:32 * qq + 32, lag, :],
                                       lhsT=Gx[:, lag, 0, qq, :, :].rearrange("p t c -> p (t c)"), rhs=CTrb[:, qq, :],
                                       start=True, stop=False, tile_position=(0, 32 * qq), skip_group_check=True)
                            ins = raw.matmul(kc[32 * qq:32 * qq + 32, lag, :],
                                             lhsT=Gx[:, lag, 1, qq, :, :].rearrange("p t c -> p (t c)"),
                                             rhs=CTnib[:, qq, :], start=False, stop=True,
                                             tile_position=(0, 32 * qq), skip_group_check=True)
                    pe.done(ins)
                S.run(tensor=t3)

                def t4(v, raw):
                    for gi in range(8):
                        v.tensor_scalar(out=Kst[:, :, gi, :], in0=kc[:, :, :], scalar1=bm[:, gi:gi + 1], scalar2=None,
                                        op0=ALU.mult)
                    k0 = Kst[:, 0, :, :].rearrange("p a c -> p (a c)")
                    v.scalar_tensor_tensor(out=k0, in0=identf[:], scalar=Dcol[:, 0:1], in1=k0, op0=ALU.mult, op1=ALU.add)
                    shk = [128, 4, NK]
                    v.tensor_tensor(out=kA[:], in0=bc(phr[:], shk, 2), in1=bc(kidx[:], shk, 1), op=ALU.mult)
                    v.tensor_copy(out=kIv, in_=kA[:])
                    v.tensor_copy(out=kB[:], in_=kIv)
                    v.tensor_tensor(out=kA[:], in0=kA[:], in1=kB[:], op=ALU.subtract)
                    v.tensor_scalar(out=kB[:], in0=kA[:], scalar1=-1.0, scalar2=None, op0=ALU.mult)
                    v.tensor_tensor(out=kB[:], in0=kA[:], in1=kB[:], op=ALU.max)
                S.run(vector=t4)

            def t5(a, raw):
                a.activation(out=sinT[:], in_=kA[:], func=AF.Sin, scale=TWO_PI)
                a.activation(out=cosT[:], in_=kB[:], func=AF.Sin, scale=-TWO_PI, bias=HALF_PI)
            S.run(scalar=t5, vector=lambda p, raw: p.tensor_copy(
                out=uP[:], in_=uN[:].rearrange("p (k i) -> p i k", i=16)))

            with nc.psum_tensor(f"ssm_bb{r}", [128, 4, 2, NK], F32) as bb:
                def m1(pe, raw):
                    for x in range(2):
                        for j in range(16):
                            for qq in range(4):
                                ins = raw.matmul(bb[:, qq, x, :], lhsT=Wb[32 * qq:32 * qq + 32, 15 - j, x, :],
                                                 rhs=uP[32 * qq:32 * qq + 32, j, :], start=(j == 0), stop=(j == 15),
                                                 tile_position=(32 * qq, 0))
                    pe.done(ins)
                S.run(tensor=m1)

                def m2(v, raw):
                    br = bb[:, :, 0, :]; bi = bb[:, :, 1, :]
                    cs = cosT[:, :, :]; sn = sinT[:, :, :]
                    v.tensor_tensor(out=w1[:], in0=br, in1=cs, op=ALU.mult)
                    v.tensor_tensor(out=w2[:], in0=bi, in1=sn, op=ALU.mult)
                    v.tensor_tensor(out=w1[:], in0=w1[:], in1=w2[:], op=ALU.add)
                    v.tensor_tensor(out=w2[:], in0=bi, in1=cs, op=ALU.mult)
                    v.tensor_tensor(out=w3[:], in0=br, in1=sn, op=ALU.mult)
                    v.tensor_tensor(out=w2[:], in0=w2[:], in1=w3[:], op=ALU.subtract)
                    for qq in range(4):
                        rb = rho[:, qq:qq + 1].to_broadcast([128, NK])
                        v.tensor_tensor_scan(out=w3[:, qq, :], data0=rb, data1=w1[:, qq, :], initial=0.0,
                                             op0=ALU.mult, op1=ALU.add)
                        v.tensor_tensor_scan(out=w4[:, qq, :], data0=rb, data1=w2[:, qq, :], initial=0.0,
                                             op0=ALU.mult, op1=ALU.add)
                    v.tensor_tensor(out=w1[:], in0=w3[:], in1=cs, op=ALU.mult)
                    v.tensor_tensor(out=w2[:], in0=w4[:], in1=sn, op=ALU.mult)
                    v.tensor_tensor(out=Sp[:, :, 0, 1:NK], in0=w1[:, :, 0:NK - 1], in1=w2[:, :, 0:NK - 1],
                                    op=ALU.subtract)
                    v.tensor_tensor(out=w1[:], in0=w3[:], in1=sn, op=ALU.mult)
                    v.tensor_tensor(out=w2[:], in0=w4[:], in1=cs, op=ALU.mult)
                    v.tensor_tensor(out=Sp[:, :, 1, 1:NK], in0=w1[:, :, 0:NK - 1], in1=w2[:, :, 0:NK - 1], op=ALU.add)
                if r + 1 < 4:
                    S.run(vector=m2, sync=make_ld(r + 1))
                else:
                    S.run(vector=m2)

            ys = yperm
            with nc.psum_tensor(f"ssm_yb{r}", [128, 8, NK], F32) as yb:
                def mk_f1(qtr):
                    def f1(pe, raw):
                        for ii in range(4):
                            i = qtr * 4 + ii
                            bank = (qtr % 2) * 4 + ii
                            for lag in range(i + 1):
                                raw.matmul(yb[:, bank, :], lhsT=Kst[:, lag, :, :].rearrange("p a c -> p (a c)"),
                                           rhs=uP[:, i - lag, :], start=(lag == 0), stop=False)
                            for qq in range(4):
                                for x in range(2):
                                    ins = raw.matmul(yb[32 * qq:32 * qq + 32, bank, :],
                                                     lhsT=Wc[:, qq, i, x, :, :].rearrange("p t c -> p (t c)"),
                                                     rhs=Sp[:, qq, x, :], start=False, stop=(x == 1),
                                                     tile_position=(0, 32 * qq), skip_group_check=True)
                        pe.done(ins)
                    return f1

                def mk_f2(qtr):
                    def f2(a, raw):
                        for ii in range(4):
                            i = qtr * 4 + ii
                            bank = (qtr % 2) * 4 + ii
                            a.activation(out=ys[:, i, :], in_=yb[:, bank, :], func=AF.Gelu_apprx_tanh)
                    return f2

                S.run(tensor=mk_f1(0))
                for qtr in range(1, 4):
                    S.run(tensor=mk_f1(qtr), scalar=mk_f2(qtr - 1))
                S.run(scalar=mk_f2(3))
            S.run(vector=lambda v, raw: v.tensor_copy(out=ystage[:].rearrange("p (k i) -> p k i", i=16),
                                                      in_=yperm[:].rearrange("p i k -> p k i")))
        S.run(sync=lambda sp, raw: sp.dma_start(out=g.ys1T[128 * 3:128 * 4, :], in_=ystage[:]))


def phase_final(g):
    nc = g.nc
    with contextlib.ExitStack() as es:
        def sb(name, shape, dt=F32, stack=es):
            return stack.enter_context(nc.sbuf_tensor("fin_" + name, list(shape), dt))
        wg, wa, wsr, wo, bg, gpost = g.wg, g.wa, g.wsr, g.wo, g.bg, g.gpost
        ys1 = sb("ys1", [128, 2, 4, CH], BF16); sgs = sb("sgs", [128, 2, 4, CH], BF16); ya = sb("ya", [128, 2, 4, CH], BF16)
        sma = sb("sma", [128, 2, 8, CH], BF16); sms = sb("sms", [128, 2, 8, CH], BF16)
        xin = sb("xin", [128, 2, 4, D])
        sg = sb("sg", [128, 4, CH], BF16); y3 = sb("y3", [128, 4, CH], BF16)
        ma = sb("ma", [128, 8, CH]); mg = sb("mg", [128, 8, CH], BF16)
        tS = sb("tS", [128, 2, CH])
        tF = sb("tF", [128, 2, D])
        ost = sb("ost", [128, 2, D])
        junk = sb("junk", [128, 2, D], BF16)
        ss2 = sb("ss2", [128, 2]); sd2 = sb("sd2", [128, 2]); rs2 = sb("rs2", [128, 2])
        ps = es.enter_context(nc.psum_tensor("fin_ps", [128, 4, CH], F32))
        po = es.enter_context(nc.psum_tensor("fin_po", [128, 2, 2, CH], F32))

        s_in = [nc.alloc_semaphore(f"p5_in{i}") for i in range(2)]
        s_evA = nc.alloc_semaphore("p5_evA")
        s_evD = nc.alloc_semaphore("p5_evD")
        s_dd = nc.alloc_semaphore("p5_dd")
        s_y3 = nc.alloc_semaphore("p5_y3")
        s_mm = nc.alloc_semaphore("p5_mm")
        s_o = nc.alloc_semaphore("p5_o")
        s_sq = nc.alloc_semaphore("p5_sq")
        s_rs = nc.alloc_semaphore("p5_rs")
        s_fin = nc.alloc_semaphore("p5_fin")
        s_st = [nc.alloc_semaphore(f"p5_st{i}") for i in range(2)]

        fmseq = [("z", 0, j) for j in range(4)]
        for c in range(NCH):
            fmseq += [("a", c, j) for j in range(8)] + [("s", c, j) for j in range(8)]
            if c + 1 < NCH:
                fmseq += [("z", c + 1, j) for j in range(4)]
        nidx = {k: n for n, k in enumerate(fmseq)}

        def ev_info(n):
            kind, c, j = fmseq[n]
            if kind == "z":
                return "A", c * 4 + j + 1
            return "D", c * 16 + (j if kind == "a" else 8 + j) + 1

        dd = [0]

        with nc.Block() as b:
            @b.sync
            def _(sp):
                def load(c):
                    sl = c % 2
                    tok = slice(c * CH, (c + 1) * CH)
                    if c >= 2:
                        sp.wait_ge(s_fin, 4 * (c - 1))
                    for dst, src in ((ys1, g.ys1T), (sgs, g.sgsT), (ya, g.yaT), (sma, g.smaT), (sms, g.smsT)):
                        sp.dma_start(out=dst[:, sl], in_=src[:, tok].rearrange("(j p) t -> p j t", p=128)
                                     ).then_inc(s_in[sl], 16)
                    sp.dma_start(out=xin[:, sl], in_=g.x[tok, :].rearrange("(t p) d -> p t d", p=128)
                                 ).then_inc(s_in[sl], 16)
                load(0)
                load(1)
                for c in range(NCH):
                    for tt in range(4):
                        ti = 4 * c + tt
                        sp.wait_ge(s_fin, ti + 1)
                        r0 = c * CH + tt * 128
                        sp.dma_start(out=g.out[r0:r0 + 128, :], in_=ost[:, ti % 2, :]).then_inc(s_st[ti % 2], 16)
                    if c + 2 < NCH:
                        load(c + 2)
                for i in range(2):
                    sp.wait_ge(s_st[i], 16 * (4 * NCH // 2))

            @b.tensor
            def _(pe):
                loaded = set()

                def fmblock(n):
                    kind, c, j = fmseq[n]
                    sl = c % 2
                    if c not in loaded:
                        pe.wait_ge(s_in[sl], 96 * (c // 2 + 1))
                        loaded.add(c)
                    if n >= 4:
                        e, cnt = ev_info(n - 4)
                        pe.wait_ge(s_evA if e == "A" else s_evD, cnt)
                    if kind == "s" and j == 0:
                        pe.wait_ge(s_y3, c + 1)
                    w, src = {"z": (wg, ys1[:, sl]), "a": (wa, ya[:, sl]), "s": (wsr, y3)}[kind]
                    for kk in range(4):
                        ins = pe.matmul(ps[:, n % 4, :], lhsT=w[:, kk, 128 * j:128 * j + 128], rhs=src[:, kk, :],
                                        start=(kk == 0), stop=(kk == 3))
                    ins.then_inc(s_mm, 1)

                def outproj(c):
                    pe.wait_ge(s_evD, 16 * (c + 1))
                    for tt in range(4):
                        ti = 4 * c + tt
                        if ti >= 2:
                            pe.wait_ge(s_fin, ti - 1)
                        for hf in range(2):
                            for kk in range(8):
                                ins = pe.matmul(po[:, ti % 2, hf, :], lhsT=mg[:, kk, 128 * tt:128 * tt + 128],
                                                rhs=wo[:, kk, 512 * hf:512 * hf + 512], start=(kk == 0), stop=(kk == 7))
                        ins.then_inc(s_o, 1)

                for n, (kind, c, j) in enumerate(fmseq):
                    fmblock(n)
                    last_of_chunk = (kind == "z" and j == 3 and c >= 1) or (kind == "s" and j == 7 and c == NCH - 1)
                    if last_of_chunk:
                        outproj(c - 1 if kind == "z" else c)

            @b.scalar
            def _(act):
                def sig(c):
                    for j in range(4):
                        n = nidx[("z", c, j)]
                        act.wait_ge(s_mm, n + 1)
                        if j == 0 and c >= 1:
                            act.wait_ge(s_y3, c)
                        act.activation(out=sg[:, j, :], in_=ps[:, n % 4, :], func=AF.Sigmoid,
                                       bias=bg[:, j:j + 1]).then_inc(s_evA, 1)

                def stats(c):
                    for tt in range(4):
                        ti = 4 * c + tt
                        act.wait_ge(s_o, ti + 1)
                        if ti >= 2:
                            act.wait_ge(s_fin, ti - 1)
                        act.activation(out=junk[:, ti % 2, :], in_=po[:, ti % 2, :, :].rearrange("p a c -> p (a c)"),
                                       func=AF.Square, accum_out=ss2[:, ti % 2:ti % 2 + 1]).then_inc(s_sq, 1)
                        act.wait_ge(s_sq, 2 * ti + 1)
                        act.activation(out=sd2[:, ti % 2:ti % 2 + 1], in_=ss2[:, ti % 2:ti % 2 + 1], func=AF.Ln,
                                       scale=1.0 / D, bias=EPS).then_inc(s_sq, 1)
                        act.wait_ge(s_sq, 2 * ti + 2)
                        act.activation(out=rs2[:, ti % 2:ti % 2 + 1], in_=sd2[:, ti % 2:ti % 2 + 1], func=AF.Exp,
                                       scale=-0.5).then_inc(s_rs, 1)

                sig(0)
                for c in range(NCH):
                    if c + 1 < NCH:
                        sig(c + 1)
                    stats(c)

            @b.vector
            def _(dve):
                def chain(ins):
                    ins.then_inc(s_dd, 1)
                    dd[0] += 1
                    dve.wait_ge(s_dd, dd[0])

                def y3f(c):
                    sl = c % 2
                    dve.wait_ge(s_in[sl], 96 * (c // 2 + 1))
                    dve.wait_ge(s_evA, 4 * (c + 1))
                    if c >= 1:
                        dve.wait_ge(s_mm, nidx[("s", c - 1, 7)] + 1)
                    chain(dve.tensor_tensor(out=sg[:], in0=sg[:], in1=ys1[:, sl], op=ALU.mult))
                    dve.tensor_tensor(out=y3[:], in0=sg[:], in1=sgs[:, sl], op=ALU.mult).then_inc(s_y3, 1)

                def evacs(c):
                    sl = c % 2
                    dve.wait_ge(s_in[sl], 96 * (c // 2 + 1))
                    for kind in ("a", "s"):
                        for j in range(8):
                            n = nidx[(kind, c, j)]
                            dve.wait_ge(s_mm, n + 1)
                            if kind == "a":
                                if j == 0 and c >= 1:
                                    dve.wait_ge(s_evD, 16 * c)
                                dve.tensor_tensor(out=ma[:, j, :], in0=ps[:, n % 4, :], in1=sma[:, sl, j, :],
                                                  op=ALU.mult).then_inc(s_evD, 1)
                            else:
                                if j == 0 and c >= 1:
                                    dve.wait_ge(s_o, 4 * c)
                                chain(dve.tensor_tensor(out=tS[:, j % 2, :], in0=ps[:, n % 4, :],
                                                        in1=sms[:, sl, j, :], op=ALU.mult))
                                dve.wait_ge(s_evD, 16 * c + j + 1)
                                dve.tensor_tensor(out=mg[:, j, :], in0=tS[:, j % 2, :], in1=ma[:, j, :],
                                                  op=ALU.add).then_inc(s_evD, 1)

                def fin(c, tt):
                    sl = c % 2
                    ti = 4 * c + tt
                    dve.wait_ge(s_rs, ti + 1)
                    if ti >= 2:
                        dve.wait_ge(s_st[ti % 2], 16 * (ti // 2))
                    chain(dve.scalar_tensor_tensor(out=tF[:, ti % 2, :],
                                                   in0=po[:, ti % 2, :, :].rearrange("p a c -> p (a c)"),
                                                   scalar=rs2[:, ti % 2:ti % 2 + 1], in1=gpost[:],
                                                   op0=ALU.mult, op1=ALU.mult))
                    dve.tensor_tensor(out=ost[:, ti % 2, :], in0=tF[:, ti % 2, :], in1=xin[:, sl, tt, :],
                                      op=ALU.add).then_inc(s_fin, 1)

                y3f(0)
                for c in range(NCH):
                    evacs(c)
                    fin(c, 0)
                    fin(c, 1)
                    if c + 1 < NCH:
                        y3f(c + 1)
                    fin(c, 2)
                    fin(c, 3)


def kernel(**inputs):
    nc = build()
    x = np.ascontiguousarray(inputs["x"], dtype=np.float32)
    shared = {}
    for k, v in inputs.items():
        if k == "x":
            continue
        a = np.ascontiguousarray(v, dtype=np.float32)
        shared[k] = a.reshape(_SHAPES[k])
    in_maps = []
    for c in range(NCORES):
        m = dict(shared)
        m["x"] = x[c]
        in_maps.append(m)
    res = run_bass_kernel_spmd(nc, in_maps, core_ids=list(range(NCORES)))
    return np.stack([r["out"] for r in res.results], axis=0).astype(np.float32)


_SHAPES = {
    "norm_pre": (1, D), "w_in": (D, INC), "b_forget": (1, H), "lam_re": (32, 64), "lam_im": (32, 64),
    "log_dt": (1, 32), "b_re": (32, 64, 16), "b_im": (32, 64, 16), "c_re": (32, 16, 64),
    "c_im": (32, 16, 64), "d_skip": (32, 16), "w_glu": (512, 512), "b_glu": (1, 512),
    "w_branch_a": (512, D), "w_branch_s": (512, D), "w_out": (D, D), "norm_post": (1, D),
}
```

```python
import contextlib
import math
import numpy as np
import concourse.bass as bass
import concourse.mybir as mybir
from concourse.bass_utils import run_bass_kernel_spmd

F32 = mybir.dt.float32
BF16 = mybir.dt.bfloat16
I32 = mybir.dt.int32
AF = mybir.ActivationFunctionType
ALU = mybir.AluOpType

D = 1024
SEQ = 8192
NCORES = 8
INC = 5640
USED = 5128
H = 8
HD = 64
EPS = 1e-6
CH = 512
NCH = SEQ // CH


class Ctx:
    pass


def build(last_phase=99, debug=False):
    nc = bass.Bass("TRN2", target_bir_lowering=False)
    es = contextlib.ExitStack()
    g = Ctx()
    g.nc = nc
    g.debug = debug

    def din(name, shape):
        return nc.dram_tensor(name, list(shape), F32, kind="ExternalInput").ap()

    g.x = din("x", [SEQ, D])
    g.norm_pre = din("norm_pre", [1, D])
    g.w_in = din("w_in", [D, INC])
    g.b_forget = din("b_forget", [1, H])
    g.lam_re = din("lam_re", [32, 64])
    g.lam_im = din("lam_im", [32, 64])
    g.log_dt = din("log_dt", [1, 32])
    g.b_re = din("b_re", [32, 64, 16])
    g.b_im = din("b_im", [32, 64, 16])
    g.c_re = din("c_re", [32, 16, 64])
    g.c_im = din("c_im", [32, 16, 64])
    g.d_skip = din("d_skip", [32, 16])
    g.w_glu = din("w_glu", [512, 512])
    g.b_glu = din("b_glu", [1, 512])
    g.w_branch_a = din("w_branch_a", [512, D])
    g.w_branch_s = din("w_branch_s", [512, D])
    g.w_out = din("w_out", [D, D])
    g.norm_post = din("norm_post", [1, D])
    g.out = nc.dram_tensor("out", [SEQ, D], F32, kind="ExternalOutput").ap()

    def scratch(name, shape, dt=BF16):
        if debug:
            return nc.dram_tensor(name, list(shape), dt, kind="ExternalOutput").ap()
        return nc.dram_tensor(name, list(shape), dt).ap()

    g.qT = scratch("qT", [512, SEQ])
    g.kT = scratch("kT", [512, SEQ])
    g.vtok = scratch("vtok", [SEQ, 512])
    g.fT = scratch("fT", [8, SEQ], F32)
    g.sgaT = scratch("sgaT", [512, SEQ])
    g.uT = scratch("uT", [512, SEQ])
    g.sgsT = scratch("sgsT", [512, SEQ])
    g.smaT = scratch("smaT", [D, SEQ])
    g.smsT = scratch("smsT", [D, SEQ])

    g.ident = es.enter_context(nc.sbuf_tensor("ident", [128, 128], BF16))
    g.ones_f = es.enter_context(nc.sbuf_tensor("ones_f", [128, 128], F32))
    S0 = Steps(nc, "init")

    def i0(p, raw):
        p.memset(g.ones_f[:], 1.0)
        p.affine_select(out=g.ident[:], in_=g.ones_f[:], pattern=[[-1, 128]],
                        compare_op=ALU.is_equal, fill=0.0, base=0, channel_multiplier=1)
    S0.run(gpsimd=i0)

    g.crk = scratch("crk", [3, H, SEQ])
    g.crq = scratch("crq", [3, H, SEQ])
    g.yaT = scratch("yaT", [512, SEQ])
    g.zeros_b = es.enter_context(nc.sbuf_tensor("zeros_b", [128, 128], BF16))
    g.maskT = es.enter_context(nc.sbuf_tensor("maskT", [128, 128], BF16))
    def i1(p, raw):
        p.memset(g.zeros_b[:], 0.0)
        p.affine_select(out=g.maskT[:], in_=g.zeros_b[:], pattern=[[1, 128]],
                        compare_op=ALU.is_ge, fill=-65536.0, base=0, channel_multiplier=-1)
    S0.run(gpsimd=i1)

    if last_phase >= 1:
        phase_inproj(g)
    if last_phase >= 2:
        phase_forget(g)
    g.wg = es.enter_context(nc.sbuf_tensor("fin_wg", [128, 4, 512], BF16))
    g.wa = es.enter_context(nc.sbuf_tensor("fin_wa", [128, 4, D], BF16))
    g.wsr = es.enter_context(nc.sbuf_tensor("fin_wsr", [128, 4, D], BF16))
    g.wo = es.enter_context(nc.sbuf_tensor("fin_wo", [128, 8, D], BF16))
    g.bg = es.enter_context(nc.sbuf_tensor("fin_bg", [128, 4], F32))
    g.gpost = es.enter_context(nc.sbuf_tensor("fin_gpost", [128, D], F32))
    if last_phase >= 3:
        phase_attn(g)
    g.ys1T = scratch("ys1T", [512, SEQ])
    if last_phase >= 4:
        phase_ssm(g)
    if last_phase >= 5:
        phase_final(g)
    es.close()
    return nc


def phase_inproj(g):
    nc = g.nc
    with contextlib.ExitStack() as es:
        wsb = es.enter_context(nc.sbuf_tensor("wsb", [128, 8, USED], BF16))
        gain = es.enter_context(nc.sbuf_tensor("gain", [128, 8], F32))
        with contextlib.ExitStack() as es0:
            wtmp = es0.enter_context(nc.sbuf_tensor("wtmp", [128, 2, USED], F32))
            s_w = [nc.alloc_semaphore(f"p0_w{i}") for i in range(2)]
            s_g = nc.alloc_semaphore("p0_g")
            s_done = [nc.alloc_semaphore(f"p0_d{i}") for i in range(3)]
            cuts = [0, 1536, 4864, USED]
            with nc.Block() as b:
                @b.sync
                def _(sp):
                    sp.dma_start(out=gain[:], in_=g.norm_pre.rearrange("o (k p) -> p (o k)", p=128),
                                 allow_slow_non_contiguous=True).then_inc(s_g, 16)
                    for dk in range(8):
                        if dk >= 2:
                            for e in range(3):
                                sp.wait_ge(s_done[e], dk - 1)
                        sp.dma_start(out=wtmp[:, dk % 2, :],
                                     in_=g.w_in[dk * 128:(dk + 1) * 128, 0:USED]).then_inc(s_w[dk % 2], 16)

                def conv(eng, e, kind):
                    eng.wait_ge(s_g, 16)
                    for dk in range(8):
                        eng.wait_ge(s_w[dk % 2], 16 * (dk // 2 + 1))
                        c0, c1 = cuts[e], cuts[e + 1]
                        if kind == "act":
                            eng.activation(out=wsb[:, dk, c0:c1], in_=wtmp[:, dk % 2, c0:c1],
                                           func=AF.Copy, scale=gain[:, dk:dk + 1]).then_inc(s_done[e], 1)
                        else:
                            eng.tensor_scalar(out=wsb[:, dk, c0:c1], in0=wtmp[:, dk % 2, c0:c1],
                                              scalar1=gain[:, dk:dk + 1], scalar2=None,
                                              op0=ALU.mult).then_inc(s_done[e], 1)

                @b.vector
                def _(e):
                    conv(e, 0, "dve")

                @b.scalar
                def _(e):
                    conv(e, 1, "act")

                @b.gpsimd
                def _(e):
                    conv(e, 2, "pool")

        xs = es.enter_context(nc.sbuf_tensor("xs", [128, 2, 4, D], F32))
        hn = es.enter_context(nc.sbuf_tensor("hn", [128, 2, 4, D], BF16))
        hT = es.enter_context(nc.sbuf_tensor("hT", [128, 2, 8, CH], BF16))
        junk = es.enter_context(nc.sbuf_tensor("junk", [128, 4, D], BF16))
        ss = es.enter_context(nc.sbuf_tensor("ss", [128, 2, 4], F32))
        sd = es.enter_context(nc.sbuf_tensor("sd", [128, 2, 4], F32))
        rstd = es.enter_context(nc.sbuf_tensor("rstd", [128, 2, 4], F32))
        NS = 6
        stage = es.enter_context(nc.sbuf_tensor("stage", [128, NS, CH], BF16))
        stagef = es.enter_context(nc.sbuf_tensor("stagef", [8, 2, CH], F32))
        ps = es.enter_context(nc.psum_tensor("ps", [128, 4, CH], F32))
        tp = es.enter_context(nc.psum_tensor("tp", [128, 2, 1024], BF16))

        blks = []

        def add(kind, col0, m, func, eng, dest, row0):
            blks.append(dict(kind=kind, col0=col0, m=m, func=func, eng=eng, dest=dest, row0=row0))

        for j in range(4):
            add("fm", 0 + 128 * j, 128, AF.Copy, "dve", g.qT, 128 * j)
        for j in range(4):
            add("fm", 512 + 128 * j, 128, AF.Copy, "dve", g.kT, 128 * j)
        for j in range(4):
            add("fm", 2056 + 128 * j, 128, AF.Copy, "dve", g.uT, 128 * j)
        for tt in range(4):
            add("v", 1024, 128, AF.Copy, "dve", g.vtok, tt)
        add("f", 1536, 8, AF.Copy, "dve", g.fT, 0)
        for j in range(4):
            add("fm", 1544 + 128 * j, 128, AF.Silu, "act", g.sgaT, 128 * j)
        for j in range(4):
            add("fm", 2568 + 128 * j, 128, AF.Silu, "act", g.sgsT, 128 * j)
        for j in range(8):
            add("fm", 3080 + 128 * j, 128, AF.Sigmoid, "act", g.smaT, 128 * j)
        for j in range(8):
            add("fm", 4104 + 128 * j, 128, AF.Sigmoid, "act", g.smsT, 128 * j)
        NB = len(blks)
        seq = []
        cnt = {"act": 0, "dve": 0}
        nstage = 0
        for c in range(NCH):
            for j, bk in enumerate(blks):
                cnt[bk["eng"]] += 1
                d = dict(bk)
                d.update(c=c, n=len(seq), eidx=cnt[bk["eng"]])
                if bk["kind"] == "f":
                    d["slot"] = None
                else:
                    d["slot"] = nstage % NS
                    d["suse"] = nstage // NS
                    nstage += 1
                seq.append(d)

        s_x = [nc.alloc_semaphore(f"p1_x{i}") for i in range(2)]
        s_sd = nc.alloc_semaphore("p1_sd")
        s_ss = nc.alloc_semaphore("p1_ss")
        s_ln = nc.alloc_semaphore("p1_ln")
        s_rstd = nc.alloc_semaphore("p1_rstd")
        s_hn = nc.alloc_semaphore("p1_hn")
        s_tp = nc.alloc_semaphore("p1_tp")
        s_hT = nc.alloc_semaphore("p1_hT")
        s_mm = nc.alloc_semaphore("p1_mm")
        s_ev = {"act": nc.alloc_semaphore("p1_eva"), "dve": nc.alloc_semaphore("p1_evd")}
        s_out = [nc.alloc_semaphore(f"p1_o{i}") for i in range(NS)]
        s_outf = [nc.alloc_semaphore(f"p1_of{i}") for i in range(2)]

        def stage_ap(d):
            if d["kind"] == "f":
                return stagef[0:8, d["c"] % 2, :]
            return stage[:, d["slot"], :]

        def dest_ap(d):
            c = d["c"]
            if d["kind"] == "v":
                r0 = c * CH + d["row0"] * 128
                return d["dest"][r0:r0 + 128, :]
            if d["kind"] == "f":
                return d["dest"][0:8, c * CH:(c + 1) * CH]
            return d["dest"][d["row0"]:d["row0"] + 128, c * CH:(c + 1) * CH]

        with nc.Block() as b:
            @b.sync
            def _(sp):
                def load(c):
                    if c >= 2:
                        sp.wait_ge(s_hn, c - 1)
                    sp.dma_start(out=xs[:, c % 2, :, :],
                                 in_=g.x[c * CH:(c + 1) * CH, :].rearrange("(t p) d -> p t d", p=128)
                                 ).then_inc(s_x[c % 2], 16)
                load(0)
                load(1)
                for c in range(NCH):
                    if c + 2 < NCH:
                        load(c + 2)
                    for d in seq[c * NB:(c + 1) * NB]:
                        sp.wait_ge(s_ev[d["eng"]], d["eidx"])
                        so = s_outf[c % 2] if d["kind"] == "f" else s_out[d["slot"]]
                        sp.dma_start(out=dest_ap(d), in_=stage_ap(d)).then_inc(so, 16)
                for i in range(NS):
                    uses = len([d for d in seq if d["slot"] == i])
                    sp.wait_ge(s_out[i], 16 * uses)
                for i in range(2):
                    sp.wait_ge(s_outf[i], 16 * (NCH // 2))

            def evac(eng, d, is_act):
                eng.wait_ge(s_mm, d["n"] + 1)
                if d["kind"] == "f":
                    if d["c"] >= 2:
                        eng.wait_ge(s_outf[d["c"] % 2], 16 * (d["c"] // 2))
                    src = ps[0:8, d["n"] % 4, :]
                else:
                    if d["suse"] >= 1:
                        eng.wait_ge(s_out[d["slot"]], 16 * d["suse"])
                    src = ps[:, d["n"] % 4, :]
                if is_act:
                    eng.activation(out=stage_ap(d), in_=src, func=d["func"]).then_inc(s_ev["act"], 1)
                else:
                    eng.tensor_copy(out=stage_ap(d), in_=src).then_inc(s_ev["dve"], 1)

            @b.scalar
            def _(act):
                def stats(c):
                    sl = c % 2
                    act.wait_ge(s_x[sl], 16 * (c // 2 + 1))
                    for tt in range(4):
                        act.activation(out=junk[:, tt, :], in_=xs[:, sl, tt, :], func=AF.Square,
                                       accum_out=ss[:, sl, tt:tt + 1]).then_inc(s_ss, 1)
                    act.wait_ge(s_ss, 4 * (c + 1))
                    act.activation(out=ss[:, sl, :], in_=ss[:, sl, :], func=AF.Ln,
                                   scale=1.0 / D, bias=EPS).then_inc(s_ln, 1)
                    act.wait_ge(s_ln, c + 1)
                    act.activation(out=sd[:, sl, :], in_=ss[:, sl, :], func=AF.Exp,
                                   scale=-0.5).then_inc(s_sd, 1)
                    act.wait_ge(s_sd, c + 1)
                    if c >= 2:
                        act.wait_ge(s_tp, 8 * (c - 1))
                    for tt in range(4):
                        ins = act.activation(out=hn[:, sl, tt, :], in_=xs[:, sl, tt, :], func=AF.Copy,
                                             scale=sd[:, sl, tt:tt + 1])
                    ins.then_inc(s_hn, 1)
                stats(0)
                stats(1)
                for c in range(NCH):
                    if c + 2 < NCH:
                        stats(c + 2)
                    for d in seq[c * NB:(c + 1) * NB]:
                        if d["eng"] == "act":
                            evac(act, d, True)

            @b.vector
            def _(dve):
                def pro(c):
                    sl = c % 2
                    if c >= 2:
                        dve.wait_ge(s_mm, (c - 1) * NB)
                    for dk in range(8):
                        dve.wait_ge(s_tp, c * 8 + dk + 1)
                        dve.tensor_copy(out=hT[:, sl, dk, :], in_=tp[:, dk % 2, 0:CH]).then_inc(s_hT, 1)
                pro(0)
                for c in range(NCH):
                    if c + 1 < NCH:
                        pro(c + 1)
                    for d in seq[c * NB:(c + 1) * NB]:
                        if d["eng"] == "dve":
                            evac(dve, d, False)


            @b.tensor
            def _(pe):
                def trans(c):
                    sl = c % 2
                    pe.wait_ge(s_hn, c + 1)
                    for dk in range(8):
                        gi = c * 8 + dk
                        if gi >= 2:
                            pe.wait_ge(s_hT, gi - 1)
                        for tt in range(4):
                            ins = pe.transpose(out=tp[:, dk % 2, tt * 128:(tt + 1) * 128],
                                               in_=hn[:, sl, tt, dk * 128:(dk + 1) * 128], identity=g.ident[:])
                        ins.then_inc(s_tp, 1)
                trans(0)
                for c in range(NCH):
                    sl = c % 2
                    if c + 1 < NCH:
                        trans(c + 1)
                    pe.wait_ge(s_hT, 8 * (c + 1))
                    for d in seq[c * NB:(c + 1) * NB]:
                        n = d["n"]
                        if n >= 4:
                            pd = seq[n - 4]
                            pe.wait_ge(s_ev[pd["eng"]], pd["eidx"])
                        for dk in range(8):
                            if d["kind"] == "v":
                                tt = d["row0"]
                                ins = pe.matmul(ps[:, n % 4, :], lhsT=hT[:, sl, dk, tt * 128:(tt + 1) * 128],
                                                rhs=wsb[:, dk, 1024:1536], start=(dk == 0), stop=(dk == 7))
                            else:
                                m = d["m"]
                                ins = pe.matmul(ps[0:m, n % 4, :], lhsT=wsb[:, dk, d["col0"]:d["col0"] + m],
                                                rhs=hT[:, sl, dk, :], start=(dk == 0), stop=(dk == 7))
                        ins.then_inc(s_mm, 1)


def phase_forget(g):
    nc = g.nc
    SEG = 16
    SL = SEQ // SEG
    with contextlib.ExitStack() as es:
        ft = es.enter_context(nc.sbuf_tensor("ft", [128, SL], F32))
        t1 = es.enter_context(nc.sbuf_tensor("fg_t1", [128, SL], F32))
        t2 = es.enter_context(nc.sbuf_tensor("fg_t2", [128, SL], F32))
        ones = es.enter_context(nc.sbuf_tensor("fg_ones", [128, SL], F32))
        rows = es.enter_context(nc.sbuf_tensor("fg_rows", [128, 6, SL], BF16))
        bfn = es.enter_context(nc.sbuf_tensor("bfn", [128, 1], F32))
        M = es.enter_context(nc.sbuf_tensor("fg_M", [128, 128], F32))
        tot = es.enter_context(nc.sbuf_tensor("fg_tot", [128, 2], F32))
        off = es.enter_context(nc.sbuf_tensor("fg_off", [128, 2], F32))
        offp = es.enter_context(nc.psum_tensor("fg_offp", [128, 2], F32))
        S = Steps(nc, "p2")

        def ld(sp, raw):
            L = [raw.dma_start(out=ft[:], in_=g.fT.rearrange("h (s t) -> (h s) t", s=SEG))]
            for h in range(H):
                L.append(raw.dma_start(out=bfn[SEG * h:SEG * (h + 1), :],
                                       in_=bass.AP(g.b_forget.tensor, h, [[0, SEG], [1, 1]]),
                                       allow_slow_non_contiguous=True))
            sp.many(L, 16)

        def mk(p, raw):
            p.affine_select(out=M[:], in_=g.ones_f[:], pattern=[[1, 128]], compare_op=ALU.is_gt, fill=0.0,
                            base=0, channel_multiplier=-1)
            m3 = M[:].rearrange("p (h s) -> p h s", s=SEG)
            p.affine_select(out=m3, in_=m3, pattern=[[-SEG, H], [0, SEG]], compare_op=ALU.is_ge, fill=0.0,
                            base=0, channel_multiplier=1)
            p.affine_select(out=m3, in_=m3, pattern=[[SEG, H], [0, SEG]], compare_op=ALU.is_ge, fill=0.0,
                            base=SEG - 1, channel_multiplier=-1)
            p.memset(ones[:], 1.0)
        S.run(sync=ld, gpsimd=mk)
        S.run(vector=lambda v, raw: v.tensor_scalar(out=bfn[:], in0=bfn[:], scalar1=-1.0, scalar2=None, op0=ALU.mult))

        def a(act, raw):
            act.activation(out=t1[:], in_=ft[:], func=AF.Exp, scale=-1.0, bias=bfn[:, 0:1])
            act.activation(out=t2[:], in_=t1[:], func=AF.Ln, bias=1.0, scale=1.0)
        S.run(scalar=a)

        def d1(dve, raw):
            dve.tensor_tensor_scan(out=t1[:], data0=ones[:], data1=t2[:], initial=0.0, op0=ALU.mult, op1=ALU.add)
            dve.tensor_copy(out=tot[:, 0:1], in_=t1[:, SL - 1:SL])
            dve.tensor_copy(out=tot[:, 1:2], in_=t1[:, SL - 1:SL])
        S.run(vector=d1)
        S.run(tensor=lambda pe, raw: pe.matmul(offp[:, :], lhsT=M[:], rhs=tot[:], start=True, stop=True))

        def d2(dve, raw):
            dve.tensor_copy(out=off[:], in_=offp[:, :])
            dve.tensor_scalar(out=t1[:], in0=t1[:], scalar1=off[:, 0:1], scalar2=8.0, op0=ALU.add, op1=ALU.mult)
            dve.tensor_copy(out=rows[:, 0, :], in_=t1[:])
            dve.tensor_tensor(out=t2[:], in0=t1[:], in1=rows[:, 0, :], op=ALU.subtract)
            dve.tensor_copy(out=rows[:, 1, :], in_=t2[:])
            dve.tensor_tensor(out=t1[:], in0=t2[:], in1=rows[:, 1, :], op=ALU.subtract)
            dve.tensor_copy(out=rows[:, 2, :], in_=t1[:])
            dve.tensor_scalar(out=rows[:, 3:6, :], in0=rows[:, 0:3, :], scalar1=-1.0, scalar2=None, op0=ALU.mult)
        S.run(vector=d2)

        def st(sp, raw):
            L = []
            for j in range(3):
                L.append(raw.dma_start(out=g.crk[j].rearrange("h (s t) -> (h s) t", s=SEG), in_=rows[:, j, :]))
                L.append(raw.dma_start(out=g.crq[j].rearrange("h (s t) -> (h s) t", s=SEG), in_=rows[:, 3 + j, :]))
            sp.many(L, 16)
        S.run(sync=st)


def phase_attn(g):
    nc = g.nc
    NQ = SEQ // CH
    NKT = SEQ // 128
    with contextlib.ExitStack() as es:
        kTa = es.enter_context(nc.sbuf_tensor("kTa", [70, 2, SEQ], BF16))
        qTa = es.enter_context(nc.sbuf_tensor("qTa", [70, 2, SEQ], BF16))
        vsb = es.enter_context(nc.sbuf_tensor("vsb", [128, 2, NKT, 128], BF16))
        sga = es.enter_context(nc.sbuf_tensor("sga", [64, 2, SEQ], BF16))
        pT = es.enter_context(nc.sbuf_tensor("pT", [128, 3, 3, CH], BF16))
        rl = es.enter_context(nc.sbuf_tensor("rl", [64, CH], F32))
        yt = es.enter_context(nc.sbuf_tensor("yt", [64, CH], F32))
        ystage = es.enter_context(nc.sbuf_tensor("ystage", [64, 2, CH], BF16))
        sp_ps = es.enter_context(nc.psum_tensor("sp_ps", [128, 2, 3, CH], F32))
        o_ps = es.enter_context(nc.psum_tensor("o_ps", [128, 2, CH], F32))
        s_ms = nc.alloc_semaphore("p3_ms")
        s_ms2 = nc.alloc_semaphore("p3_ms2")
        s_pw = nc.alloc_semaphore("p3_pw")
        s_pc = nc.alloc_semaphore("p3_pc")
        s_pd = [nc.alloc_semaphore(f"p3_pd{i}") for i in range(2)]
        wtmpP = es.enter_context(nc.sbuf_tensor("wtmpP", [128, 2, D], F32))
        s_dv = nc.alloc_semaphore("p3_dv")
        s_ld = [nc.alloc_semaphore(f"p3_ld{i}") for i in range(2)]
        s_S = nc.alloc_semaphore("p3_S")
        s_exp = nc.alloc_semaphore("p3_exp")
        s_pv = nc.alloc_semaphore("p3_pv")
        s_fin = nc.alloc_semaphore("p3_fin")
        s_yo = [nc.alloc_semaphore(f"p3_yo{i}") for i in range(2)]

        GMAX = 3
        groups = []
        qlast = {}
        for h in range(H):
            for Q in range(NQ):
                nk = 4 * Q + 4
                full = [(kt, kt >= 4 * Q) for kt in range(4 * Q + 1)]
                cur = []
                glist = []
                for t in full:
                    cur.append(t)
                    if len(cur) == GMAX:
                        glist.append((0, cur)); cur = []
                if cur:
                    glist.append((0, cur))
                for kt in range(4 * Q + 1, nk):
                    glist.append(((kt - 4 * Q) * 128, [(kt, True)]))
                for gi_, (n0, tl) in enumerate(glist):
                    groups.append(dict(h=h, Q=Q, n0=n0, tiles=tl, first=(gi_ == 0), last=(gi_ == len(glist) - 1),
                                       i=len(groups)))
                qlast[(h, Q)] = len(groups) - 1
        head_first = {h: min(t["i"] for t in groups if t["h"] == h) for h in range(H)}
        head_last = {h: max(t["i"] for t in groups if t["h"] == h) for h in range(H)}
        NLD = 13

        with nc.Block() as b:
            @b.gpsimd
            def _(p):
                for k, ap in enumerate((kTa[64:70, 0, :], kTa[64:70, 1, :], vsb[:, 0, :, 64:128])):
                    p.memset(ap, 1.0).then_inc(s_ms, 1)
                    p.wait_ge(s_ms, k + 1)
                p.dma_start(out=g.bg[:], in_=g.b_glu.rearrange("o (j p) -> p (o j)", p=128),
                            allow_slow_non_contiguous=True).then_inc(s_pw, 16)
                p.dma_start(out=g.gpost[:], in_=bass.AP(g.norm_post.tensor, 0, [[0, 128], [1, D]])).then_inc(s_pw, 16)
                p.wait_ge(s_pw, 32)
                jobs = []
                for (src, dst, nk, nco) in ((g.w_glu, g.wg, 4, 512), (g.w_branch_a, g.wa, 4, D),
                                            (g.w_branch_s, g.wsr, 4, D), (g.w_out, g.wo, 8, D)):
                    for k in range(nk):
                        jobs.append((src[128 * k:128 * (k + 1), :], dst[:, k, :], nco))

                def pdma(i):
                    srcap, _, nco = jobs[i]
                    p.dma_start(out=wtmpP[:, i % 2, 0:nco], in_=srcap).then_inc(s_pd[i % 2], 16)
                pdma(0)
                pdma(1)
                for i, (srcap, dstap, nco) in enumerate(jobs):
                    p.wait_ge(s_pd[i % 2], 16 * (i // 2 + 1))
                    p.tensor_copy(out=dstap, in_=wtmpP[:, i % 2, 0:nco]).then_inc(s_pc, 1)
                    p.wait_ge(s_pc, i + 1)
                    if i + 2 < len(jobs):
                        pdma(i + 2)

            @b.sync
            def _(sp):
                def load(h, first=False):
                    sl = h % 2
                    if h >= 2:
                        sp.wait_ge(s_pv, head_last[h - 2] + 1)
                        sp.wait_ge(s_fin, NQ * (h - 1))
                    sp.dma_start(out=kTa[0:64, sl, :], in_=g.kT[h * 64:(h + 1) * 64, :]).then_inc(s_ld[sl], 16)
                    sp.dma_start(out=qTa[0:64, sl, :], in_=g.qT[h * 64:(h + 1) * 64, :]).then_inc(s_ld[sl], 16)
                    vsrc = g.vtok[:, h * 64:(h + 1) * 64].rearrange("(kt p) d -> p kt d", p=128)
                    for part in range(8):
                        sp.dma_start(out=vsb[:, sl, part * 8:(part + 1) * 8, 0:64],
                                     in_=vsrc[:, part * 8:(part + 1) * 8, :]).then_inc(s_ld[sl], 16)
                    sp.dma_start(out=sga[:, sl, :], in_=g.sgaT[h * 64:(h + 1) * 64, :]).then_inc(s_ld[sl], 16)
                    if first:
                        sp.wait_ge(s_ms, 3)
                        sp.wait_ge(s_ms2, 3)
                    sp.dma_start(out=kTa[67:70, sl, :], in_=g.crk[:, h, :]).then_inc(s_ld[sl], 16)
                    sp.dma_start(out=qTa[64:67, sl, :], in_=g.crq[:, h, :]).then_inc(s_ld[sl], 16)
                load(0, True)
                load(1)
                for h in range(H):
                    for Q in range(NQ):
                        qi = h * NQ + Q
                        sp.wait_ge(s_fin, qi + 1)
                        sp.dma_start(out=g.yaT[h * 64:(h + 1) * 64, Q * CH:(Q + 1) * CH],
                                     in_=ystage[:, qi % 2, :]).then_inc(s_yo[qi % 2], 16)
                    if h + 2 < H:
                        load(h + 2)
                for i in range(2):
                    sp.wait_ge(s_yo[i], 16 * (H * NQ // 2))

            @b.tensor
            def _(pe):
                def S(t):
                    i = t["i"]
                    sl = t["h"] % 2
                    if i == head_first[t["h"]]:
                        pe.wait_ge(s_ld[sl], 16 * NLD * (t["h"] // 2 + 1))
                    if i >= 2:
                        pe.wait_ge(s_exp, i - 1)
                    n0 = t["n0"]
                    q0 = t["Q"] * CH
                    for j, (kt, diag) in enumerate(t["tiles"]):
                        ins = pe.matmul(sp_ps[:, i % 2, j, n0:CH], lhsT=kTa[0:70, sl, kt * 128:(kt + 1) * 128],
                                        rhs=qTa[0:70, sl, q0 + n0:q0 + CH], start=True, stop=not diag)
                        if diag:
                            ins = pe.matmul(sp_ps[:, i % 2, j, n0:n0 + 128], lhsT=g.ident[:], rhs=g.maskT[:],
                                            start=False, stop=True)
                    ins.then_inc(s_S, 1)

                def PV(t):
                    i = t["i"]
                    sl = t["h"] % 2
                    qi = t["h"] * NQ + t["Q"]
                    pe.wait_ge(s_exp, i + 1)
                    if t["first"] and qi >= 2:
                        pe.wait_ge(s_fin, qi - 1)
                    n0 = t["n0"]
                    nt = len(t["tiles"])
                    for j, (kt, diag) in enumerate(t["tiles"]):
                        ins = pe.matmul(o_ps[:, qi % 2, n0:CH], lhsT=vsb[:, sl, kt, :], rhs=pT[:, i % 3, j, n0:CH],
                                        start=(t["first"] and j == 0), stop=(t["last"] and j == nt - 1))
                    ins.then_inc(s_pv, 1)

                n = len(groups)
                S(groups[0])
                S(groups[1])
                for i in range(n):
                    if i + 2 < n:
                        S(groups[i + 2])
                    PV(groups[i])

            @b.scalar
            def _(act):
                for t in groups:
                    i = t["i"]
                    act.wait_ge(s_S, i + 1)
                    if i >= 3:
                        act.wait_ge(s_pv, i - 2)
                    n0 = t["n0"]
                    nt = len(t["tiles"])
                    act.activation(out=pT[:, i % 3, 0:nt, n0:CH], in_=sp_ps[:, i % 2, 0:nt, n0:CH], func=AF.Exp,
                                   scale=0.125).then_inc(s_exp, 1)

            @b.vector
            def _(dve):
                for k, ap in enumerate((qTa[64:70, 0, :], qTa[64:70, 1, :], vsb[:, 1, :, 64:128])):
                    dve.memset(ap, 1.0).then_inc(s_ms2, 1)
                    dve.wait_ge(s_ms2, k + 1)
                for h in range(H):
                    sl = h % 2
                    dve.wait_ge(s_ld[sl], 16 * NLD * (h // 2 + 1))
                    for Q in range(NQ):
                        qi = h * NQ + Q
                        dve.wait_ge(s_pv, qlast[(h, Q)] + 1)
                        if qi >= 1:
                            dve.wait_ge(s_fin, qi)
                        if qi >= 2:
                            dve.wait_ge(s_yo[qi % 2], 16 * (qi // 2))
                        dve.reciprocal(out=rl[:], in_=o_ps[64:128, qi % 2, :]).then_inc(s_dv, 1)
                        dve.wait_ge(s_dv, 2 * qi + 1)
                        dve.tensor_tensor(out=yt[:], in0=o_ps[0:64, qi % 2, :], in1=rl[:], op=ALU.mult).then_inc(s_dv, 1)
                        dve.wait_ge(s_dv, 2 * qi + 2)
                        dve.tensor_tensor(out=ystage[:, qi % 2, :], in0=yt[:], in1=sga[:, sl, Q * CH:(Q + 1) * CH],
                                          op=ALU.mult).then_inc(s_fin, 1)


class Ser:
    def __init__(self, eng, st):
        self.e = eng
        self.st = st

    def done(self, ins, inc=1):
        ins.then_inc(self.st["sem"], inc)
        self.st["n"] += inc
        self.e.wait_ge(self.st["sem"], self.st["n"])
        return ins

    def many(self, instrs, inc=1):
        for ins in instrs:
            ins.then_inc(self.st["sem"], inc)
            self.st["n"] += inc
        self.e.wait_ge(self.st["sem"], self.st["n"])

    def __getattr__(self, name):
        f = getattr(self.e, name)
        inc = 16 if name == "dma_start" else 1

        def w(*a, **k):
            return self.done(f(*a, **k), inc)
        return w


class Steps:
    def __init__(self, nc, tag):
        self.nc = nc
        self.st = {e: {"sem": nc.alloc_semaphore(f"{tag}_{e}"), "n": 0}
                   for e in ("sync", "vector", "scalar", "gpsimd", "tensor")}

    def run(self, **fns):
        with self.nc.Block() as b:
            for ename, fn in fns.items():
                def mk(fn, ename):
                    def body(e):
                        fn(Ser(e, self.st[ename]), e)
                    return body
                getattr(b, ename)(mk(fn, ename))


TWO_PI = 6.28318
HALF_PI = 1.5707963


def phase_ssm(g):
    nc = g.nc
    NK = SEQ // 16
    with contextlib.ExitStack() as es:
        def sb(name, shape, dt=F32):
            return es.enter_context(nc.sbuf_tensor("ssm_" + name, list(shape), dt))
        S = Steps(nc, "p4")
        identf = sb("identf", [128, 128])
        pm = sb("pm", [128, 2])
        bm = sb("bm", [128, 8])
        kidx_i = sb("kidx_i", [128, NK], I32)
        kidx = sb("kidx", [128, NK])
        midx = sb("midx", [128, 17])
        LR = sb("LR", [128, 4]); LI = sb("LI", [128, 4]); LDT = sb("LDT", [128, 4])
        BR = sb("BR", [128, 4, 16]); BI = sb("BI", [128, 4, 16])
        CNr = sb("CNr", [64, 128]); CNi = sb("CNi", [64, 128])
        Dcol = sb("Dcol", [128, 1])
        dt_ = sb("dt", [128, 4]); lrdt = sb("lrdt", [128, 4]); lidt = sb("lidt", [128, 4]); phi = sb("phi", [128, 4])
        tA = sb("tA", [128, 4, 17]); tB = sb("tB", [128, 4, 17]); tC = sb("tC", [128, 4, 17]); tI = sb("tI", [128, 4, 17], I32)
        EAr = sb("EAr", [128, 4, 17]); EAi = sb("EAi", [128, 4, 17])
        s1 = sb("s1", [128, 4]); s2 = sb("s2", [128, 4]); s3 = sb("s3", [128, 4]); s4 = sb("s4", [128, 4])
        fr = sb("fr", [128, 4]); fi = sb("fi", [128, 4]); sI = sb("sI", [128, 4], I32)
        Bbr = sb("Bbr", [128, 4, 16]); Bbi = sb("Bbi", [128, 4, 16]); b1 = sb("b1", [128, 4, 16])
        Gr = sb("Gr", [128, 4, 16, 16]); Gi = sb("Gi", [128, 4, 16, 16]); G1 = sb("G1", [128, 4, 16, 16])
        Gx = sb("Gx", [128, 16, 2, 4, 2, 16], BF16)
        CTr = sb("CTr", [128, 4, 16]); CTi = sb("CTi", [128, 4, 16])
        CTrb = sb("CTrb", [128, 4, 16], BF16); CTnib = sb("CTnib", [128, 4, 16], BF16)
        Wc = sb("Wc", [128, 4, 16, 2, 2, 16], BF16)
        Wb = sb("Wb", [128, 16, 2, 128], BF16)
        Kst = sb("Kst", [128, 16, 8, 16], BF16)
        cosT = sb("cosT", [128, 4, NK], BF16); sinT = sb("sinT", [128, 4, NK], BF16)
        rho = sb("rho", [128, 4]); phr = sb("phr", [128, 4])
        uN = sb("uN", [128, SEQ], BF16); uP = sb("uP", [128, 16, NK], BF16)
        Sp = sb("Sp", [128, 4, 2, NK], BF16)
        w1 = sb("w1", [128, 4, NK]); w2 = sb("w2", [128, 4, NK]); w3 = sb("w3", [128, 4, NK]); w4 = sb("w4", [128, 4, NK])
        kA = w1; kB = w2; kIv = w3[:].bitcast(I32)
        ystage = sb("ystage", [128, SEQ], BF16)
        yperm = sb("yperm", [128, 16, NK], BF16)

        def bc(ap, shape, axis):
            return ap.unsqueeze(axis).to_broadcast(list(shape))

        def c0(p, raw):
            p.memset(identf[:], 0.0)
            p.affine_select(out=identf[:], in_=g.ones_f[:], pattern=[[-1, 128]], compare_op=ALU.is_equal,
                            fill=0.0, base=0, channel_multiplier=1)
            p.affine_select(out=pm[:], in_=g.ones_f[:, 0:2], pattern=[[-64, 2]], compare_op=ALU.is_ge,
                            fill=0.0, base=0, channel_multiplier=1)
            p.affine_select(out=pm[:], in_=pm[:], pattern=[[64, 2]], compare_op=ALU.is_ge,
                            fill=0.0, base=63, channel_multiplier=-1)
            p.affine_select(out=bm[:], in_=g.ones_f[:, 0:8], pattern=[[-16, 8]], compare_op=ALU.is_ge,
                            fill=0.0, base=0, channel_multiplier=1)
            p.affine_select(out=bm[:], in_=bm[:], pattern=[[16, 8]], compare_op=ALU.is_ge,
                            fill=0.0, base=15, channel_multiplier=-1)
            p.iota(kidx_i[:], pattern=[[1, NK]], base=0, channel_multiplier=0)
            p.memset(Sp[:], 0.0)
        S.run(gpsimd=c0)

        def c1(v, raw):
            v.tensor_copy(out=kidx[:], in_=kidx_i[:])
            v.tensor_copy(out=midx[:], in_=kidx[:, 0:17])
        S.run(vector=c1)

        lamr = g.lam_re.rearrange("(q t) p -> (t p) q", t=2)
        lami = g.lam_im.rearrange("(q t) p -> (t p) q", t=2)
        bre = g.b_re.rearrange("(q t) p c -> (t p) q c", t=2)
        bim = g.b_im.rearrange("(q t) p c -> (t p) q c", t=2)

        for r in range(4):
            def ld(sp, raw, r=r):
                L = []
                L.append(raw.dma_start(out=LR[:], in_=lamr[:, 4 * r:4 * r + 4], allow_slow_non_contiguous=True))
                L.append(raw.dma_start(out=LI[:], in_=lami[:, 4 * r:4 * r + 4], allow_slow_non_contiguous=True))
                for t in range(2):
                    L.append(raw.dma_start(out=LDT[64 * t:64 * t + 64, :],
                                           in_=bass.AP(g.log_dt.tensor, t + 8 * r, [[0, 64], [2, 4]]),
                                           allow_slow_non_contiguous=True))
                L.append(raw.dma_start(out=BR[:], in_=bre[:, 4 * r:4 * r + 4, :]))
                L.append(raw.dma_start(out=BI[:], in_=bim[:, 4 * r:4 * r + 4, :]))
                for qq in range(4):
                    q = 4 * r + qq
                    L.append(raw.dma_start(out=CNr[16 * qq:16 * qq + 16, :].rearrange("c (t p) -> c t p", t=2),
                                           in_=g.c_re[2 * q:2 * q + 2, :, :].rearrange("t c p -> c t p")))
                    L.append(raw.dma_start(out=CNi[16 * qq:16 * qq + 16, :].rearrange("c (t p) -> c t p", t=2),
                                           in_=g.c_im[2 * q:2 * q + 2, :, :].rearrange("t c p -> c t p")))
                L.append(raw.dma_start(out=Dcol[:],
                                       in_=g.d_skip[8 * r:8 * r + 8, :].rearrange("g (c o) -> (g c) o", o=1)))
                L.append(raw.dma_start(out=uN[:], in_=g.uT[128 * r:128 * r + 128, :]))
                sp.many(L, 16)
            def make_ld(rr):
                return lambda sp, raw: ld(sp, raw, rr)
            if r == 0:
                S.run(sync=ld)

            if r == 0:
                S.run(scalar=lambda a, raw: a.activation(out=dt_[:], in_=LDT[:], func=AF.Exp))
            else:
                S.run(scalar=lambda a, raw: a.activation(out=dt_[:], in_=LDT[:], func=AF.Exp),
                      sync=lambda sp, raw: sp.dma_start(out=g.ys1T[128 * (r - 1):128 * r, :], in_=ystage[:]))

            def a1(v, raw):
                v.tensor_tensor(out=lrdt[:], in0=LR[:], in1=dt_[:], op=ALU.mult)
                v.tensor_tensor(out=lidt[:], in0=LI[:], in1=dt_[:], op=ALU.mult)
                v.tensor_scalar(out=phi[:], in0=lidt[:], scalar1=1.0 / (2 * math.pi), scalar2=None, op0=ALU.mult)
                v.tensor_tensor(out=tA[:], in0=bc(phi[:], [128, 4, 17], 2), in1=bc(midx[:], [128, 4, 17], 1), op=ALU.mult)
                v.tensor_tensor(out=tB[:], in0=bc(lrdt[:], [128, 4, 17], 2), in1=bc(midx[:], [128, 4, 17], 1), op=ALU.mult)
                v.tensor_copy(out=tI[:], in_=tA[:])
                v.tensor_copy(out=tC[:], in_=tI[:])
                v.tensor_tensor(out=tA[:], in0=tA[:], in1=tC[:], op=ALU.subtract)
                v.tensor_scalar(out=tC[:], in0=tA[:], scalar1=-1.0, scalar2=None, op0=ALU.mult)
                v.tensor_tensor(out=tC[:], in0=tA[:], in1=tC[:], op=ALU.max)
                v.tensor_scalar(out=s1[:], in0=phi[:], scalar1=16.0, scalar2=None, op0=ALU.mult)
                v.tensor_copy(out=sI[:], in_=s1[:])
                v.tensor_copy(out=s2[:], in_=sI[:])
                v.tensor_tensor(out=phr[:], in0=s1[:], in1=s2[:], op=ALU.subtract)
            S.run(vector=a1)

            def a2(a, raw):
                a.activation(out=EAi[:], in_=tA[:], func=AF.Sin, scale=TWO_PI)
                a.activation(out=EAr[:], in_=tC[:], func=AF.Sin, scale=-TWO_PI, bias=HALF_PI)
                a.activation(out=tB[:], in_=tB[:], func=AF.Exp)
                a.activation(out=rho[:], in_=lrdt[:], func=AF.Exp, scale=16.0)
            S.run(scalar=a2)

            def a3(v, raw):
                v.tensor_tensor(out=EAr[:], in0=EAr[:], in1=tB[:], op=ALU.mult)
                v.tensor_tensor(out=EAi[:], in0=EAi[:], in1=tB[:], op=ALU.mult)
                v.tensor_scalar(out=s1[:], in0=EAr[:, :, 1], scalar1=-1.0, scalar2=None, op0=ALU.add)
                v.tensor_tensor(out=s2[:], in0=LR[:], in1=LR[:], op=ALU.mult)
                v.tensor_tensor(out=s3[:], in0=LI[:], in1=LI[:], op=ALU.mult)
                v.tensor_tensor(out=s2[:], in0=s2[:], in1=s3[:], op=ALU.add)
                v.reciprocal(out=s2[:], in_=s2[:])
                v.tensor_tensor(out=s3[:], in0=s1[:], in1=LR[:], op=ALU.mult)
                v.tensor_tensor(out=s4[:], in0=EAi[:, :, 1], in1=LI[:], op=ALU.mult)
                v.tensor_tensor(out=s3[:], in0=s3[:], in1=s4[:], op=ALU.add)
                v.tensor_tensor(out=fr[:], in0=s3[:], in1=s2[:], op=ALU.mult)
                v.tensor_tensor(out=s3[:], in0=EAi[:, :, 1], in1=LR[:], op=ALU.mult)
                v.tensor_tensor(out=s4[:], in0=s1[:], in1=LI[:], op=ALU.mult)
                v.tensor_tensor(out=s3[:], in0=s3[:], in1=s4[:], op=ALU.subtract)
                v.tensor_tensor(out=fi[:], in0=s3[:], in1=s2[:], op=ALU.mult)
                frb = bc(fr[:], [128, 4, 16], 2); fib = bc(fi[:], [128, 4, 16], 2)
                v.tensor_tensor(out=Bbr[:], in0=BR[:], in1=frb, op=ALU.mult)
                v.tensor_tensor(out=b1[:], in0=BI[:], in1=fib, op=ALU.mult)
                v.tensor_tensor(out=Bbr[:], in0=Bbr[:], in1=b1[:], op=ALU.subtract)
                v.tensor_tensor(out=Bbi[:], in0=BI[:], in1=frb, op=ALU.mult)
                v.tensor_tensor(out=b1[:], in0=BR[:], in1=fib, op=ALU.mult)
                v.tensor_tensor(out=Bbi[:], in0=Bbi[:], in1=b1[:], op=ALU.add)
                sh = [128, 4, 16, 16]
                ear = bc(EAr[:, :, 0:16], sh, 3); eai = bc(EAi[:, :, 0:16], sh, 3)
                bbr = bc(Bbr[:], sh, 2); bbi = bc(Bbi[:], sh, 2)
                v.tensor_tensor(out=Gr[:], in0=ear, in1=bbr, op=ALU.mult)
                v.tensor_tensor(out=G1[:], in0=eai, in1=bbi, op=ALU.mult)
                v.tensor_tensor(out=Gr[:], in0=Gr[:], in1=G1[:], op=ALU.subtract)
                v.tensor_tensor(out=Gi[:], in0=ear, in1=bbi, op=ALU.mult)
                v.tensor_tensor(out=G1[:], in0=eai, in1=bbr, op=ALU.mult)
                v.tensor_tensor(out=Gi[:], in0=Gi[:], in1=G1[:], op=ALU.add)
                for x, Gsrc in enumerate((Gr, Gi)):
                    for g2 in range(2):
                        v.tensor_scalar(out=Gx[:, :, x, :, g2, :].rearrange("p m q c -> p q m c"), in0=Gsrc[:],
                                        scalar1=pm[:, g2:g2 + 1], scalar2=None, op0=ALU.mult)
            S.run(vector=a3)

            with nc.psum_tensor(f"ssm_ctp{r}", [128, 2, 64], F32) as ctp, \
                    nc.psum_tensor(f"ssm_wbp{r}", [128, 4, 8, 128], BF16) as wbp, \
                    nc.psum_tensor(f"ssm_kc{r}", [128, 16, 16], F32) as kc:
                def t1(pe, raw):
                    raw.transpose(out=ctp[:, 0, :], in_=CNr[:, :], identity=identf[0:64, 0:64])
                    raw.transpose(out=ctp[:, 1, :], in_=CNi[:, :], identity=identf[0:64, 0:64])
                    for m in range(16):
                        for x in range(2):
                            idx = m * 2 + x
                            ins = raw.transpose(out=wbp[:, idx // 8, idx % 8, :],
                                                in_=Gx[:, m, x, :, :, :].rearrange("p q t c -> p (q t c)"),
                                                identity=g.ident[:])
                    pe.done(ins)
                S.run(tensor=t1)

                def t2(v, raw):
                    v.tensor_copy(out=CTr[:].rearrange("p q c -> p (q c)"), in_=ctp[:, 0, :])
                    v.tensor_copy(out=CTi[:].rearrange("p q c -> p (q c)"), in_=ctp[:, 1, :])
                    v.tensor_copy(out=CTrb[:], in_=CTr[:])
                    v.tensor_scalar(out=CTnib[:], in0=CTi[:], scalar1=-1.0, scalar2=None, op0=ALU.mult)
                    for bk in range(4):
                        v.tensor_copy(out=Wb[:, 4 * bk:4 * bk + 4, :, :].rearrange("p m x c -> p (m x) c"),
                                      in_=wbp[:, bk, :, :])
                    sh = [128, 4, 16, 16]
                    ear = bc(EAr[:, :, 1:17], sh, 3); eai = bc(EAi[:, :, 1:17], sh, 3)
                    ctr = bc(CTr[:], sh, 2); cti = bc(CTi[:], sh, 2)
                    v.tensor_tensor(out=Gr[:], in0=ear, in1=ctr, op=ALU.mult)
                    v.tensor_tensor(out=G1[:], in0=eai, in1=cti, op=ALU.mult)
                    v.tensor_tensor(out=Gr[:], in0=Gr[:], in1=G1[:], op=ALU.subtract)
                    v.tensor_tensor(out=Gi[:], in0=eai, in1=ctr, op=ALU.mult)
                    v.tensor_tensor(out=G1[:], in0=ear, in1=cti, op=ALU.mult)
                    v.tensor_tensor(out=Gi[:], in0=Gi[:], in1=G1[:], op=ALU.add)
                    v.tensor_scalar(out=Gi[:], in0=Gi[:], scalar1=-1.0, scalar2=None, op0=ALU.mult)
                    for x, Csrc in enumerate((Gr, Gi)):
                        for g2 in range(2):
                            v.tensor_scalar(out=Wc[:, :, :, x, g2, :], in0=Csrc[:], scalar1=pm[:, g2:g2 + 1],
                                            scalar2=None, op0=ALU.mult)
                S.run(vector=t2)

                def t3(pe, raw):
                    for lag in range(16):
                        for qq in range(4):
                            raw.matmul(kc[32 * qq:32 * qq + 32, lag, :],
                                       lhsT=Gx[:, lag, 0, qq, :, :].rearrange("p t c -> p (t c)"), rhs=CTrb[:, qq, :],
                                       start=True, stop=False, tile_position=(0, 32 * qq), skip_group_check=True)
                            ins = raw.matmul(kc[32 * qq:32 * qq + 32, lag, :],
                                             lhsT=Gx[:, lag, 1, qq, :, :].rearrange("p t c -> p (t c)"),
                                             rhs=CTnib[:, qq, :], start=False, stop=True,
                                             tile_position=(0, 32 * qq), skip_group_check=True)
                    pe.done(ins)
                S.run(tensor=t3)

                def t4(v, raw):
                    for gi in range(8):
                        v.tensor_scalar(out=Kst[:, :, gi, :], in0=kc[:, :, :], scalar1=bm[:, gi:gi + 1], scalar2=None,
                                        op0=ALU.mult)
                    k0 = Kst[:, 0, :, :].rearrange("p a c -> p (a c)")
                    v.scalar_tensor_tensor(out=k0, in0=identf[:], scalar=Dcol[:, 0:1], in1=k0, op0=ALU.mult, op1=ALU.add)
                    shk = [128, 4, NK]
                    v.tensor_tensor(out=kA[:], in0=bc(phr[:], shk, 2), in1=bc(kidx[:], shk, 1), op=ALU.mult)
                    v.tensor_copy(out=kIv, in_=kA[:])
                    v.tensor_copy(out=kB[:], in_=kIv)
                    v.tensor_tensor(out=kA[:], in0=kA[:], in1=kB[:], op=ALU.subtract)
                    v.tensor_scalar(out=kB[:], in0=kA[:], scalar1=-1.0, scalar2=None, op0=ALU.mult)
                    v.tensor_tensor(out=kB[:], in0=kA[:], in1=kB[:], op=ALU.max)
                S.run(vector=t4)

            def t5(a, raw):
                a.activation(out=sinT[:], in_=kA[:], func=AF.Sin, scale=TWO_PI)
                a.activation(out=cosT[:], in_=kB[:], func=AF.Sin, scale=-TWO_PI, bias=HALF_PI)
            S.run(scalar=t5, vector=lambda p, raw: p.tensor_copy(
                out=uP[:], in_=uN[:].rearrange("p (k i) -> p i k", i=16)))

            with nc.psum_tensor(f"ssm_bb{r}", [128, 4, 2, NK], F32) as bb:
                def m1(pe, raw):
                    for x in range(2):
                        for j in range(16):
                            for qq in range(4):
                                ins = raw.matmul(bb[:, qq, x, :], lhsT=Wb[32 * qq:32 * qq + 32, 15 - j, x, :],
                                                 rhs=uP[32 * qq:32 * qq + 32, j, :], start=(j == 0), stop=(j == 15),
                                                 tile_position=(32 * qq, 0))
                    pe.done(ins)
                S.run(tensor=m1)

                def m2(v, raw):
                    br = bb[:, :, 0, :]; bi = bb[:, :, 1, :]
                    cs = cosT[:, :, :]; sn = sinT[:, :, :]
                    v.tensor_tensor(out=w1[:], in0=br, in1=cs, op=ALU.mult)
                    v.tensor_tensor(out=w2[:], in0=bi, in1=sn, op=ALU.mult)
                    v.tensor_tensor(out=w1[:], in0=w1[:], in1=w2[:], op=ALU.add)
                    v.tensor_tensor(out=w2[:], in0=bi, in1=cs, op=ALU.mult)
                    v.tensor_tensor(out=w3[:], in0=br, in1=sn, op=ALU.mult)
                    v.tensor_tensor(out=w2[:], in0=w2[:], in1=w3[:], op=ALU.subtract)
                    for qq in range(4):
                        rb = rho[:, qq:qq + 1].to_broadcast([128, NK])
                        v.tensor_tensor_scan(out=w3[:, qq, :], data0=rb, data1=w1[:, qq, :], initial=0.0,
                                             op0=ALU.mult, op1=ALU.add)
                        v.tensor_tensor_scan(out=w4[:, qq, :], data0=rb, data1=w2[:, qq, :], initial=0.0,
                                             op0=ALU.mult, op1=ALU.add)
                    v.tensor_tensor(out=w1[:], in0=w3[:], in1=cs, op=ALU.mult)
                    v.tensor_tensor(out=w2[:], in0=w4[:], in1=sn, op=ALU.mult)
                    v.tensor_tensor(out=Sp[:, :, 0, 1:NK], in0=w1[:, :, 0:NK - 1], in1=w2[:, :, 0:NK - 1],
                                    op=ALU.subtract)
                    v.tensor_tensor(out=w1[:], in0=w3[:], in1=sn, op=ALU.mult)
                    v.tensor_tensor(out=w2[:], in0=w4[:], in1=cs, op=ALU.mult)
                    v.tensor_tensor(out=Sp[:, :, 1, 1:NK], in0=w1[:, :, 0:NK - 1], in1=w2[:, :, 0:NK - 1], op=ALU.add)
                if r + 1 < 4:
                    S.run(vector=m2, sync=make_ld(r + 1))
                else:
                    S.run(vector=m2)

            ys = yperm
            with nc.psum_tensor(f"ssm_yb{r}", [128, 8, NK], F32) as yb:
                def mk_f1(qtr):
                    def f1(pe, raw):
                        for ii in range(4):
                            i = qtr * 4 + ii
                            bank = (qtr % 2) * 4 + ii
                            for lag in range(i + 1):
                                raw.matmul(yb[:, bank, :], lhsT=Kst[:, lag, :, :].rearrange("p a c -> p (a c)"),
                                           rhs=uP[:, i - lag, :], start=(lag == 0), stop=False)
                            for qq in range(4):
                                for x in range(2):
                                    ins = raw.matmul(yb[32 * qq:32 * qq + 32, bank, :],
                                                     lhsT=Wc[:, qq, i, x, :, :].rearrange("p t c -> p (t c)"),
                                                     rhs=Sp[:, qq, x, :], start=False, stop=(x == 1),
                                                     tile_position=(0, 32 * qq), skip_group_check=True)
                        pe.done(ins)
                    return f1

                def mk_f2(qtr):
                    def f2(a, raw):
                        for ii in range(4):
                            i = qtr * 4 + ii
                            bank = (qtr % 2) * 4 + ii
                            a.activation(out=ys[:, i, :], in_=yb[:, bank, :], func=AF.Gelu_apprx_tanh)
                    return f2

                S.run(tensor=mk_f1(0))
                for qtr in range(1, 4):
                    S.run(tensor=mk_f1(qtr), scalar=mk_f2(qtr - 1))
                S.run(scalar=mk_f2(3))
            S.run(vector=lambda v, raw: v.tensor_copy(out=ystage[:].rearrange("p (k i) -> p k i", i=16),
                                                      in_=yperm[:].rearrange("p i k -> p k i")))
        S.run(sync=lambda sp, raw: sp.dma_start(out=g.ys1T[128 * 3:128 * 4, :], in_=ystage[:]))


def phase_final(g):
    nc = g.nc
    with contextlib.ExitStack() as es:
        def sb(name, shape, dt=F32, stack=es):
            return stack.enter_context(nc.sbuf_tensor("fin_" + name, list(shape), dt))
        wg, wa, wsr, wo, bg, gpost = g.wg, g.wa, g.wsr, g.wo, g.bg, g.gpost
        ys1 = sb("ys1", [128, 2, 4, CH], BF16); sgs = sb("sgs", [128, 2, 4, CH], BF16); ya = sb("ya", [128, 2, 4, CH], BF16)
        sma = sb("sma", [128, 2, 8, CH], BF16); sms = sb("sms", [128, 2, 8, CH], BF16)
        xin = sb("xin", [128, 2, 4, D])
        sg = sb("sg", [128, 4, CH], BF16); y3 = sb("y3", [128, 4, CH], BF16)
        ma = sb("ma", [128, 8, CH]); mg = sb("mg", [128, 8, CH], BF16)
        tS = sb("tS", [128, 2, CH])
        tF = sb("tF", [128, 2, D])
        ost = sb("ost", [128, 2, D])
        junk = sb("junk", [128, 2, D], BF16)
        ss2 = sb("ss2", [128, 2]); sd2 = sb("sd2", [128, 2]); rs2 = sb("rs2", [128, 2])
        ps = es.enter_context(nc.psum_tensor("fin_ps", [128, 4, CH], F32))
        po = es.enter_context(nc.psum_tensor("fin_po", [128, 2, 2, CH], F32))

        s_in = [nc.alloc_semaphore(f"p5_in{i}") for i in range(2)]
        s_evA = nc.alloc_semaphore("p5_evA")
        s_evD = nc.alloc_semaphore("p5_evD")
        s_dd = nc.alloc_semaphore("p5_dd")
        s_y3 = nc.alloc_semaphore("p5_y3")
        s_mm = nc.alloc_semaphore("p5_mm")
        s_o = nc.alloc_semaphore("p5_o")
        s_sq = nc.alloc_semaphore("p5_sq")
        s_rs = nc.alloc_semaphore("p5_rs")
        s_fin = nc.alloc_semaphore("p5_fin")
        s_st = [nc.alloc_semaphore(f"p5_st{i}") for i in range(2)]

        fmseq = [("z", 0, j) for j in range(4)]
        for c in range(NCH):
            for j in range(8):
                fmseq += [("a", c, j), ("s", c, j)]
            if c + 1 < NCH:
                fmseq += [("z", c + 1, j) for j in range(4)]
        nidx = {k: n for n, k in enumerate(fmseq)}

        def ev_info(n):
            kind, c, j = fmseq[n]
            if kind == "z":
                return "A", c * 4 + j + 1
            return "D", c * 16 + 2 * j + (0 if kind == "a" else 1) + 1

        dd = [0]

        with nc.Block() as b:
            @b.sync
            def _(sp):
                def load(c):
                    sl = c % 2
                    tok = slice(c * CH, (c + 1) * CH)
                    if c >= 2:
                        sp.wait_ge(s_fin, 4 * (c - 1))
                    for dst, src in ((ys1, g.ys1T), (sgs, g.sgsT), (ya, g.yaT), (sma, g.smaT), (sms, g.smsT)):
                        sp.dma_start(out=dst[:, sl], in_=src[:, tok].rearrange("(j p) t -> p j t", p=128)
                                     ).then_inc(s_in[sl], 16)
                    sp.dma_start(out=xin[:, sl], in_=g.x[tok, :].rearrange("(t p) d -> p t d", p=128)
                                 ).then_inc(s_in[sl], 16)
                load(0)
                load(1)
                for c in range(NCH):
                    for tt in range(4):
                        ti = 4 * c + tt
                        sp.wait_ge(s_fin, ti + 1)
                        r0 = c * CH + tt * 128
                        sp.dma_start(out=g.out[r0:r0 + 128, :], in_=ost[:, ti % 2, :]).then_inc(s_st[ti % 2], 16)
                    if c + 2 < NCH:
                        load(c + 2)
                for i in range(2):
                    sp.wait_ge(s_st[i], 16 * (4 * NCH // 2))

            @b.tensor
            def _(pe):
                loaded = set()

                def fmblock(n):
                    kind, c, j = fmseq[n]
                    sl = c % 2
                    if c not in loaded:
                        pe.wait_ge(s_in[sl], 96 * (c // 2 + 1))
                        loaded.add(c)
                    if n >= 4:
                        e, cnt = ev_info(n - 4)
                        pe.wait_ge(s_evA if e == "A" else s_evD, cnt)
                    if kind == "s" and j == 0:
                        pe.wait_ge(s_y3, c + 1)
                    w, src = {"z": (wg, ys1[:, sl]), "a": (wa, ya[:, sl]), "s": (wsr, y3)}[kind]
                    for kk in range(4):
                        ins = pe.matmul(ps[:, n % 4, :], lhsT=w[:, kk, 128 * j:128 * j + 128], rhs=src[:, kk, :],
                                        start=(kk == 0), stop=(kk == 3))
                    ins.then_inc(s_mm, 1)

                def outproj(c):
                    pe.wait_ge(s_evD, 16 * (c + 1))
                    for tt in range(4):
                        ti = 4 * c + tt
                        if ti >= 2:
                            pe.wait_ge(s_fin, ti - 1)
                        for hf in range(2):
                            for kk in range(8):
                                ins = pe.matmul(po[:, ti % 2, hf, :], lhsT=mg[:, kk, 128 * tt:128 * tt + 128],
                                                rhs=wo[:, kk, 512 * hf:512 * hf + 512], start=(kk == 0), stop=(kk == 7))
                        ins.then_inc(s_o, 1)

                for n, (kind, c, j) in enumerate(fmseq):
                    fmblock(n)
                    last_of_chunk = (kind == "z" and j == 3 and c >= 1) or (kind == "s" and j == 7 and c == NCH - 1)
                    if last_of_chunk:
                        outproj(c - 1 if kind == "z" else c)

            @b.scalar
            def _(act):
                def sig(c):
                    for j in range(4):
                        n = nidx[("z", c, j)]
                        act.wait_ge(s_mm, n + 1)
                        if j == 0 and c >= 1:
                            act.wait_ge(s_y3, c)
                        act.activation(out=sg[:, j, :], in_=ps[:, n % 4, :], func=AF.Sigmoid,
                                       bias=bg[:, j:j + 1]).then_inc(s_evA, 1)

                def stats(c):
                    for tt in range(4):
                        ti = 4 * c + tt
                        act.wait_ge(s_o, ti + 1)
                        if ti >= 2:
                            act.wait_ge(s_fin, ti - 1)
                        act.activation(out=junk[:, ti % 2, :], in_=po[:, ti % 2, :, :].rearrange("p a c -> p (a c)"),
                                       func=AF.Square, accum_out=ss2[:, ti % 2:ti % 2 + 1]).then_inc(s_sq, 1)
                        act.wait_ge(s_sq, 2 * ti + 1)
                        act.activation(out=sd2[:, ti % 2:ti % 2 + 1], in_=ss2[:, ti % 2:ti % 2 + 1], func=AF.Ln,
                                       scale=1.0 / D, bias=EPS).then_inc(s_sq, 1)
                        act.wait_ge(s_sq, 2 * ti + 2)
                        act.activation(out=rs2[:, ti % 2:ti % 2 + 1], in_=sd2[:, ti % 2:ti % 2 + 1], func=AF.Exp,
                                       scale=-0.5).then_inc(s_rs, 1)

                sig(0)
                for c in range(NCH):
                    if c + 1 < NCH:
                        sig(c + 1)
                    stats(c)

            @b.vector
            def _(dve):
                def chain(ins):
                    ins.then_inc(s_dd, 1)
                    dd[0] += 1
                    dve.wait_ge(s_dd, dd[0])

                def y3f(c):
                    sl = c % 2
                    dve.wait_ge(s_in[sl], 96 * (c // 2 + 1))
                    dve.wait_ge(s_evA, 4 * (c + 1))
                    if c >= 1:
                        dve.wait_ge(s_mm, nidx[("s", c - 1, 7)] + 1)
                    chain(dve.tensor_tensor(out=sg[:], in0=sg[:], in1=ys1[:, sl], op=ALU.mult))
                    dve.tensor_tensor(out=y3[:], in0=sg[:], in1=sgs[:, sl], op=ALU.mult).then_inc(s_y3, 1)

                def evacs(c):
                    sl = c % 2
                    dve.wait_ge(s_in[sl], 96 * (c // 2 + 1))
                    for j in range(8):
                        for kind in ("a", "s"):
                            n = nidx[(kind, c, j)]
                            dve.wait_ge(s_mm, n + 1)
                            if kind == "a":
                                if j == 0 and c >= 1:
                                    dve.wait_ge(s_evD, 16 * c)
                                dve.tensor_tensor(out=ma[:, j, :], in0=ps[:, n % 4, :], in1=sma[:, sl, j, :],
                                                  op=ALU.mult).then_inc(s_evD, 1)
                            else:
                                if j == 0 and c >= 1:
                                    dve.wait_ge(s_o, 4 * c)
                                chain(dve.tensor_tensor(out=tS[:, j % 2, :], in0=ps[:, n % 4, :],
                                                        in1=sms[:, sl, j, :], op=ALU.mult))
                                dve.wait_ge(s_evD, 16 * c + 2 * j + 1)
                                dve.tensor_tensor(out=mg[:, j, :], in0=tS[:, j % 2, :], in1=ma[:, j, :],
                                                  op=ALU.add).then_inc(s_evD, 1)

                def fin(c, tt):
                    sl = c % 2
                    ti = 4 * c + tt
                    dve.wait_ge(s_rs, ti + 1)
                    if ti >= 2:
                        dve.wait_ge(s_st[ti % 2], 16 * (ti // 2))
                    chain(dve.scalar_tensor_tensor(out=tF[:, ti % 2, :],
                                                   in0=po[:, ti % 2, :, :].rearrange("p a c -> p (a c)"),
                                                   scalar=rs2[:, ti % 2:ti % 2 + 1], in1=gpost[:],
                                                   op0=ALU.mult, op1=ALU.mult))
                    dve.tensor_tensor(out=ost[:, ti % 2, :], in0=tF[:, ti % 2, :], in1=xin[:, sl, tt, :],
                                      op=ALU.add).then_inc(s_fin, 1)

                y3f(0)
                for c in range(NCH):
                    evacs(c)
                    fin(c, 0)
                    fin(c, 1)
                    if c + 1 < NCH:
                        y3f(c + 1)
                    fin(c, 2)
                    fin(c, 3)


def kernel(**inputs):
    nc = build()
    x = np.ascontiguousarray(inputs["x"], dtype=np.float32)
    shared = {}
    for k, v in inputs.items():
        if k == "x":
            continue
        a = np.ascontiguousarray(v, dtype=np.float32)
        shared[k] = a.reshape(_SHAPES[k])
    in_maps = []
    for c in range(NCORES):
        m = dict(shared)
        m["x"] = x[c]
        in_maps.append(m)
    res = run_bass_kernel_spmd(nc, in_maps, core_ids=list(range(NCORES)))
    return np.stack([r["out"] for r in res.results], axis=0).astype(np.float32)


_SHAPES = {
    "norm_pre": (1, D), "w_in": (D, INC), "b_forget": (1, H), "lam_re": (32, 64), "lam_im": (32, 64),
    "log_dt": (1, 32), "b_re": (32, 64, 16), "b_im": (32, 64, 16), "c_re": (32, 16, 64),
    "c_im": (32, 16, 64), "d_skip": (32, 16), "w_glu": (512, 512), "b_glu": (1, 512),
    "w_branch_a": (512, D), "w_branch_s": (512, D), "w_out": (D, D), "norm_post": (1, D),
}
```

```python
import contextlib
import math
import numpy as np
import concourse.bass as bass
import concourse.mybir as mybir
from concourse.bass_utils import run_bass_kernel_spmd

F32 = mybir.dt.float32
BF16 = mybir.dt.bfloat16
I32 = mybir.dt.int32
AF = mybir.ActivationFunctionType
ALU = mybir.AluOpType

D = 1024
SEQ = 8192
NCORES = 8
INC = 5640
USED = 5128
H = 8
HD = 64
EPS = 1e-6
CH = 512
NCH = SEQ // CH


class Ctx:
    pass


def build(last_phase=99, debug=False):
    nc = bass.Bass("TRN2", target_bir_lowering=False)
    es = contextlib.ExitStack()
    g = Ctx()
    g.nc = nc
    g.debug = debug

    def din(name, shape):
        return nc.dram_tensor(name, list(shape), F32, kind="ExternalInput").ap()

    g.x = din("x", [SEQ, D])
    g.norm_pre = din("norm_pre", [1, D])
    g.w_in = din("w_in", [D, INC])
    g.b_forget = din("b_forget", [1, H])
    g.lam_re = din("lam_re", [32, 64])
    g.lam_im = din("lam_im", [32, 64])
    g.log_dt = din("log_dt", [1, 32])
    g.b_re = din("b_re", [32, 64, 16])
    g.b_im = din("b_im", [32, 64, 16])
    g.c_re = din("c_re", [32, 16, 64])
    g.c_im = din("c_im", [32, 16, 64])
    g.d_skip = din("d_skip", [32, 16])
    g.w_glu = din("w_glu", [512, 512])
    g.b_glu = din("b_glu", [1, 512])
    g.w_branch_a = din("w_branch_a", [512, D])
    g.w_branch_s = din("w_branch_s", [512, D])
    g.w_out = din("w_out", [D, D])
    g.norm_post = din("norm_post", [1, D])
    g.out = nc.dram_tensor("out", [SEQ, D], F32, kind="ExternalOutput").ap()

    def scratch(name, shape, dt=BF16):
        if debug:
            return nc.dram_tensor(name, list(shape), dt, kind="ExternalOutput").ap()
        return nc.dram_tensor(name, list(shape), dt).ap()

    g.qT = scratch("qT", [512, SEQ])
    g.kT = scratch("kT", [512, SEQ])
    g.vtok = scratch("vtok", [SEQ, 512])
    g.fT = scratch("fT", [8, SEQ], F32)
    g.sgaT = scratch("sgaT", [512, SEQ])
    g.uT = scratch("uT", [512, SEQ])
    g.sgsT = scratch("sgsT", [512, SEQ])
    g.smaT = scratch("smaT", [D, SEQ])
    g.smsT = scratch("smsT", [D, SEQ])

    g.ident = es.enter_context(nc.sbuf_tensor("ident", [128, 128], BF16))
    g.ones_f = es.enter_context(nc.sbuf_tensor("ones_f", [128, 128], F32))
    S0 = Steps(nc, "init")

    def i0(p, raw):
        p.memset(g.ones_f[:], 1.0)
        p.affine_select(out=g.ident[:], in_=g.ones_f[:], pattern=[[-1, 128]],
                        compare_op=ALU.is_equal, fill=0.0, base=0, channel_multiplier=1)
    S0.run(gpsimd=i0)

    g.crk = scratch("crk", [3, H, SEQ])
    g.crq = scratch("crq", [3, H, SEQ])
    g.yaT = scratch("yaT", [512, SEQ])
    g.zeros_b = es.enter_context(nc.sbuf_tensor("zeros_b", [128, 128], BF16))
    g.maskT = es.enter_context(nc.sbuf_tensor("maskT", [128, 128], BF16))
    def i1(p, raw):
        p.memset(g.zeros_b[:], 0.0)
        p.affine_select(out=g.maskT[:], in_=g.zeros_b[:], pattern=[[1, 128]],
                        compare_op=ALU.is_ge, fill=-65536.0, base=0, channel_multiplier=-1)
    S0.run(gpsimd=i1)

    if last_phase >= 1:
        phase_inproj(g)
    if last_phase >= 2:
        phase_forget(g)
    g.wg = es.enter_context(nc.sbuf_tensor("fin_wg", [128, 4, 512], BF16))
    g.wa = es.enter_context(nc.sbuf_tensor("fin_wa", [128, 4, D], BF16))
    g.wsr = es.enter_context(nc.sbuf_tensor("fin_wsr", [128, 4, D], BF16))
    g.wo = es.enter_context(nc.sbuf_tensor("fin_wo", [128, 8, D], BF16))
    g.bg = es.enter_context(nc.sbuf_tensor("fin_bg", [128, 4], F32))
    g.gpost = es.enter_context(nc.sbuf_tensor("fin_gpost", [128, D], F32))
    if last_phase >= 3:
        phase_attn(g)
    g.ys1T = scratch("ys1T", [512, SEQ])
    if last_phase >= 4:
        phase_ssm(g)
    if last_phase >= 5:
        phase_final(g)
    es.close()
    return nc


def phase_inproj(g):
    nc = g.nc
    with contextlib.ExitStack() as es:
        wsb = es.enter_context(nc.sbuf_tensor("wsb", [128, 8, USED], BF16))
        gain = es.enter_context(nc.sbuf_tensor("gain", [128, 8], F32))
        with contextlib.ExitStack() as es0:
            wtmp = es0.enter_context(nc.sbuf_tensor("wtmp", [128, 2, USED], F32))
            s_w = [nc.alloc_semaphore(f"p0_w{i}") for i in range(2)]
            s_g = nc.alloc_semaphore("p0_g")
            s_done = [nc.alloc_semaphore(f"p0_d{i}") for i in range(3)]
            cuts = [0, 1536, 4864, USED]
            with nc.Block() as b:
                @b.sync
                def _(sp):
                    sp.dma_start(out=gain[:], in_=g.norm_pre.rearrange("o (k p) -> p (o k)", p=128),
                                 allow_slow_non_contiguous=True).then_inc(s_g, 16)
                    for dk in range(8):
                        if dk >= 2:
                            for e in range(3):
                                sp.wait_ge(s_done[e], dk - 1)
                        sp.dma_start(out=wtmp[:, dk % 2, :],
                                     in_=g.w_in[dk * 128:(dk + 1) * 128, 0:USED]).then_inc(s_w[dk % 2], 16)

                def conv(eng, e, kind):
                    eng.wait_ge(s_g, 16)
                    for dk in range(8):
                        eng.wait_ge(s_w[dk % 2], 16 * (dk // 2 + 1))
                        c0, c1 = cuts[e], cuts[e + 1]
                        if kind == "act":
                            eng.activation(out=wsb[:, dk, c0:c1], in_=wtmp[:, dk % 2, c0:c1],
                                           func=AF.Copy, scale=gain[:, dk:dk + 1]).then_inc(s_done[e], 1)
                        else:
                            eng.tensor_scalar(out=wsb[:, dk, c0:c1], in0=wtmp[:, dk % 2, c0:c1],
                                              scalar1=gain[:, dk:dk + 1], scalar2=None,
                                              op0=ALU.mult).then_inc(s_done[e], 1)

                @b.vector
                def _(e):
                    conv(e, 0, "dve")

                @b.scalar
                def _(e):
                    conv(e, 1, "act")

                @b.gpsimd
                def _(e):
                    conv(e, 2, "pool")

        xs = es.enter_context(nc.sbuf_tensor("xs", [128, 2, 4, D], F32))
        hn = es.enter_context(nc.sbuf_tensor("hn", [128, 2, 4, D], BF16))
        hT = es.enter_context(nc.sbuf_tensor("hT", [128, 2, 8, CH], BF16))
        junk = es.enter_context(nc.sbuf_tensor("junk", [128, 4, D], BF16))
        ss = es.enter_context(nc.sbuf_tensor("ss", [128, 2, 4], F32))
        sd = es.enter_context(nc.sbuf_tensor("sd", [128, 2, 4], F32))
        rstd = es.enter_context(nc.sbuf_tensor("rstd", [128, 2, 4], F32))
        NS = 6
        stage = es.enter_context(nc.sbuf_tensor("stage", [128, NS, CH], BF16))
        stagef = es.enter_context(nc.sbuf_tensor("stagef", [8, 2, CH], F32))
        ps = es.enter_context(nc.psum_tensor("ps", [128, 4, CH], F32))
        tp = es.enter_context(nc.psum_tensor("tp", [128, 2, 1024], BF16))

        blks = []

        def add(kind, col0, m, func, eng, dest, row0):
            blks.append(dict(kind=kind, col0=col0, m=m, func=func, eng=eng, dest=dest, row0=row0))

        for j in range(4):
            add("fm", 0 + 128 * j, 128, AF.Copy, "dve", g.qT, 128 * j)
        for j in range(4):
            add("fm", 512 + 128 * j, 128, AF.Copy, "dve", g.kT, 128 * j)
        for j in range(4):
            add("fm", 2056 + 128 * j, 128, AF.Copy, "dve", g.uT, 128 * j)
        for tt in range(4):
            add("v", 1024, 128, AF.Copy, "dve", g.vtok, tt)
        add("f", 1536, 8, AF.Copy, "dve", g.fT, 0)
        for j in range(4):
            add("fm", 1544 + 128 * j, 128, AF.Silu, "act", g.sgaT, 128 * j)
        for j in range(4):
            add("fm", 2568 + 128 * j, 128, AF.Silu, "act", g.sgsT, 128 * j)
        for j in range(8):
            add("fm", 3080 + 128 * j, 128, AF.Sigmoid, "act", g.smaT, 128 * j)
        for j in range(8):
            add("fm", 4104 + 128 * j, 128, AF.Sigmoid, "act", g.smsT, 128 * j)
        NB = len(blks)
        seq = []
        cnt = {"act": 0, "dve": 0}
        nstage = 0
        for c in range(NCH):
            for j, bk in enumerate(blks):
                cnt[bk["eng"]] += 1
                d = dict(bk)
                d.update(c=c, n=len(seq), eidx=cnt[bk["eng"]])
                if bk["kind"] == "f":
                    d["slot"] = None
                else:
                    d["slot"] = nstage % NS
                    d["suse"] = nstage // NS
                    nstage += 1
                seq.append(d)

        s_x = [nc.alloc_semaphore(f"p1_x{i}") for i in range(2)]
        s_sd = nc.alloc_semaphore("p1_sd")
        s_ss = nc.alloc_semaphore("p1_ss")
        s_ln = nc.alloc_semaphore("p1_ln")
        s_rstd = nc.alloc_semaphore("p1_rstd")
        s_hn = nc.alloc_semaphore("p1_hn")
        s_tp = nc.alloc_semaphore("p1_tp")
        s_hT = nc.alloc_semaphore("p1_hT")
        s_mm = nc.alloc_semaphore("p1_mm")
        s_ev = {"act": nc.alloc_semaphore("p1_eva"), "dve": nc.alloc_semaphore("p1_evd")}
        s_out = [nc.alloc_semaphore(f"p1_o{i}") for i in range(NS)]
        s_outf = [nc.alloc_semaphore(f"p1_of{i}") for i in range(2)]

        def stage_ap(d):
            if d["kind"] == "f":
                return stagef[0:8, d["c"] % 2, :]
            return stage[:, d["slot"], :]

        def dest_ap(d):
            c = d["c"]
            if d["kind"] == "v":
                r0 = c * CH + d["row0"] * 128
                return d["dest"][r0:r0 + 128, :]
            if d["kind"] == "f":
                return d["dest"][0:8, c * CH:(c + 1) * CH]
            return d["dest"][d["row0"]:d["row0"] + 128, c * CH:(c + 1) * CH]

        with nc.Block() as b:
            @b.sync
            def _(sp):
                def load(c):
                    if c >= 2:
                        sp.wait_ge(s_hn, c - 1)
                    sp.dma_start(out=xs[:, c % 2, :, :],
                                 in_=g.x[c * CH:(c + 1) * CH, :].rearrange("(t p) d -> p t d", p=128)
                                 ).then_inc(s_x[c % 2], 16)
                load(0)
                load(1)
                for c in range(NCH):
                    if c + 2 < NCH:
                        load(c + 2)
                    for d in seq[c * NB:(c + 1) * NB]:
                        sp.wait_ge(s_ev[d["eng"]], d["eidx"])
                        so = s_outf[c % 2] if d["kind"] == "f" else s_out[d["slot"]]
                        sp.dma_start(out=dest_ap(d), in_=stage_ap(d)).then_inc(so, 16)
                for i in range(NS):
                    uses = len([d for d in seq if d["slot"] == i])
                    sp.wait_ge(s_out[i], 16 * uses)
                for i in range(2):
                    sp.wait_ge(s_outf[i], 16 * (NCH // 2))

            def evac(eng, d, is_act):
                eng.wait_ge(s_mm, d["n"] + 1)
                if d["kind"] == "f":
                    if d["c"] >= 2:
                        eng.wait_ge(s_outf[d["c"] % 2], 16 * (d["c"] // 2))
                    src = ps[0:8, d["n"] % 4, :]
                else:
                    if d["suse"] >= 1:
                        eng.wait_ge(s_out[d["slot"]], 16 * d["suse"])
                    src = ps[:, d["n"] % 4, :]
                if is_act:
                    eng.activation(out=stage_ap(d), in_=src, func=d["func"]).then_inc(s_ev["act"], 1)
                else:
                    eng.tensor_copy(out=stage_ap(d), in_=src).then_inc(s_ev["dve"], 1)

            @b.scalar
            def _(act):
                def stats(c):
                    sl = c % 2
                    act.wait_ge(s_x[sl], 16 * (c // 2 + 1))
                    for tt in range(4):
                        act.activation(out=junk[:, tt, :], in_=xs[:, sl, tt, :], func=AF.Square,
                                       accum_out=ss[:, sl, tt:tt + 1]).then_inc(s_ss, 1)
                    act.wait_ge(s_ss, 4 * (c + 1))
                    act.activation(out=ss[:, sl, :], in_=ss[:, sl, :], func=AF.Ln,
                                   scale=1.0 / D, bias=EPS).then_inc(s_ln, 1)
                    act.wait_ge(s_ln, c + 1)
                    act.activation(out=sd[:, sl, :], in_=ss[:, sl, :], func=AF.Exp,
                                   scale=-0.5).then_inc(s_sd, 1)
                    act.wait_ge(s_sd, c + 1)
                    if c >= 2:
                        act.wait_ge(s_tp, 8 * (c - 1))
                    for tt in range(4):
                        ins = act.activation(out=hn[:, sl, tt, :], in_=xs[:, sl, tt, :], func=AF.Copy,
                                             scale=sd[:, sl, tt:tt + 1])
                    ins.then_inc(s_hn, 1)
                stats(0)
                stats(1)
                for c in range(NCH):
                    if c + 2 < NCH:
                        stats(c + 2)
                    for d in seq[c * NB:(c + 1) * NB]:
                        if d["eng"] == "act":
                            evac(act, d, True)

            @b.vector
            def _(dve):
                def pro(c):
                    sl = c % 2
                    if c >= 2:
                        dve.wait_ge(s_mm, (c - 1) * NB)
                    for dk in range(8):
                        dve.wait_ge(s_tp, c * 8 + dk + 1)
                        dve.tensor_copy(out=hT[:, sl, dk, :], in_=tp[:, dk % 2, 0:CH]).then_inc(s_hT, 1)
                pro(0)
                for c in range(NCH):
                    if c + 1 < NCH:
                        pro(c + 1)
                    for d in seq[c * NB:(c + 1) * NB]:
                        if d["eng"] == "dve":
                            evac(dve, d, False)


            @b.tensor
            def _(pe):
                def trans(c):
                    sl = c % 2
                    pe.wait_ge(s_hn, c + 1)
                    for dk in range(8):
                        gi = c * 8 + dk
                        if gi >= 2:
                            pe.wait_ge(s_hT, gi - 1)
                        for tt in range(4):
                            ins = pe.transpose(out=tp[:, dk % 2, tt * 128:(tt + 1) * 128],
                                               in_=hn[:, sl, tt, dk * 128:(dk + 1) * 128], identity=g.ident[:])
                        ins.then_inc(s_tp, 1)
                trans(0)
                for c in range(NCH):
                    sl = c % 2
                    if c + 1 < NCH:
                        trans(c + 1)
                    pe.wait_ge(s_hT, 8 * (c + 1))
                    for d in seq[c * NB:(c + 1) * NB]:
                        n = d["n"]
                        if n >= 4:
                            pd = seq[n - 4]
                            pe.wait_ge(s_ev[pd["eng"]], pd["eidx"])
                        for dk in range(8):
                            if d["kind"] == "v":
                                tt = d["row0"]
                                ins = pe.matmul(ps[:, n % 4, :], lhsT=hT[:, sl, dk, tt * 128:(tt + 1) * 128],
                                                rhs=wsb[:, dk, 1024:1536], start=(dk == 0), stop=(dk == 7))
                            else:
                                m = d["m"]
                                ins = pe.matmul(ps[0:m, n % 4, :], lhsT=wsb[:, dk, d["col0"]:d["col0"] + m],
                                                rhs=hT[:, sl, dk, :], start=(dk == 0), stop=(dk == 7))
                        ins.then_inc(s_mm, 1)


def phase_forget(g):
    nc = g.nc
    SEG = 16
    SL = SEQ // SEG
    with contextlib.ExitStack() as es:
        ft = es.enter_context(nc.sbuf_tensor("ft", [128, SL], F32))
        t1 = es.enter_context(nc.sbuf_tensor("fg_t1", [128, SL], F32))
        t2 = es.enter_context(nc.sbuf_tensor("fg_t2", [128, SL], F32))
        ones = es.enter_context(nc.sbuf_tensor("fg_ones", [128, SL], F32))
        rows = es.enter_context(nc.sbuf_tensor("fg_rows", [128, 6, SL], BF16))
        bfn = es.enter_context(nc.sbuf_tensor("bfn", [128, 1], F32))
        M = es.enter_context(nc.sbuf_tensor("fg_M", [128, 128], F32))
        tot = es.enter_context(nc.sbuf_tensor("fg_tot", [128, 2], F32))
        off = es.enter_context(nc.sbuf_tensor("fg_off", [128, 2], F32))
        offp = es.enter_context(nc.psum_tensor("fg_offp", [128, 2], F32))
        S = Steps(nc, "p2")

        def ld(sp, raw):
            L = [raw.dma_start(out=ft[:], in_=g.fT.rearrange("h (s t) -> (h s) t", s=SEG))]
            for h in range(H):
                L.append(raw.dma_start(out=bfn[SEG * h:SEG * (h + 1), :],
                                       in_=bass.AP(g.b_forget.tensor, h, [[0, SEG], [1, 1]]),
                                       allow_slow_non_contiguous=True))
            sp.many(L, 16)

        def mk(p, raw):
            p.affine_select(out=M[:], in_=g.ones_f[:], pattern=[[1, 128]], compare_op=ALU.is_gt, fill=0.0,
                            base=0, channel_multiplier=-1)
            m3 = M[:].rearrange("p (h s) -> p h s", s=SEG)
            p.affine_select(out=m3, in_=m3, pattern=[[-SEG, H], [0, SEG]], compare_op=ALU.is_ge, fill=0.0,
                            base=0, channel_multiplier=1)
            p.affine_select(out=m3, in_=m3, pattern=[[SEG, H], [0, SEG]], compare_op=ALU.is_ge, fill=0.0,
                            base=SEG - 1, channel_multiplier=-1)
            p.memset(ones[:], 1.0)
        S.run(sync=ld, gpsimd=mk)
        S.run(vector=lambda v, raw: v.tensor_scalar(out=bfn[:], in0=bfn[:], scalar1=-1.0, scalar2=None, op0=ALU.mult))

        def a(act, raw):
            act.activation(out=t1[:], in_=ft[:], func=AF.Exp, scale=-1.0, bias=bfn[:, 0:1])
            act.activation(out=t2[:], in_=t1[:], func=AF.Ln, bias=1.0, scale=1.0)
        S.run(scalar=a)

        def d1(dve, raw):
            dve.tensor_tensor_scan(out=t1[:], data0=ones[:], data1=t2[:], initial=0.0, op0=ALU.mult, op1=ALU.add)
            dve.tensor_copy(out=tot[:, 0:1], in_=t1[:, SL - 1:SL])
            dve.tensor_copy(out=tot[:, 1:2], in_=t1[:, SL - 1:SL])
        S.run(vector=d1)
        S.run(tensor=lambda pe, raw: pe.matmul(offp[:, :], lhsT=M[:], rhs=tot[:], start=True, stop=True))

        def d2(dve, raw):
            dve.tensor_copy(out=off[:], in_=offp[:, :])
            dve.tensor_scalar(out=t1[:], in0=t1[:], scalar1=off[:, 0:1], scalar2=8.0, op0=ALU.add, op1=ALU.mult)
            dve.tensor_copy(out=rows[:, 0, :], in_=t1[:])
            dve.tensor_tensor(out=t2[:], in0=t1[:], in1=rows[:, 0, :], op=ALU.subtract)
            dve.tensor_copy(out=rows[:, 1, :], in_=t2[:])
            dve.tensor_tensor(out=t1[:], in0=t2[:], in1=rows[:, 1, :], op=ALU.subtract)
            dve.tensor_copy(out=rows[:, 2, :], in_=t1[:])
            dve.tensor_scalar(out=rows[:, 3:6, :], in0=rows[:, 0:3, :], scalar1=-1.0, scalar2=None, op0=ALU.mult)
        S.run(vector=d2)

        def st(sp, raw):
            L = []
            for j in range(3):
                L.append(raw.dma_start(out=g.crk[j].rearrange("h (s t) -> (h s) t", s=SEG), in_=rows[:, j, :]))
                L.append(raw.dma_start(out=g.crq[j].rearrange("h (s t) -> (h s) t", s=SEG), in_=rows[:, 3 + j, :]))
            sp.many(L, 16)
        S.run(sync=st)


def phase_attn(g):
    nc = g.nc
    NQ = SEQ // CH
    NKT = SEQ // 128
    with contextlib.ExitStack() as es:
        kTa = es.enter_context(nc.sbuf_tensor("kTa", [70, 2, SEQ], BF16))
        qTa = es.enter_context(nc.sbuf_tensor("qTa", [70, 2, SEQ], BF16))
        vsb = es.enter_context(nc.sbuf_tensor("vsb", [128, 2, NKT, 128], BF16))
        sga = es.enter_context(nc.sbuf_tensor("sga", [64, 2, SEQ], BF16))
        pT = es.enter_context(nc.sbuf_tensor("pT", [128, 3, 3, CH], BF16))
        rl = es.enter_context(nc.sbuf_tensor("rl", [64, CH], F32))
        yt = es.enter_context(nc.sbuf_tensor("yt", [64, CH], F32))
        ystage = es.enter_context(nc.sbuf_tensor("ystage", [64, 2, CH], BF16))
        sp_ps = es.enter_context(nc.psum_tensor("sp_ps", [128, 2, 3, CH], F32))
        o_ps = es.enter_context(nc.psum_tensor("o_ps", [128, 2, CH], F32))
        s_ms = nc.alloc_semaphore("p3_ms")
        s_ms2 = nc.alloc_semaphore("p3_ms2")
        s_pw = nc.alloc_semaphore("p3_pw")
        s_pc = nc.alloc_semaphore("p3_pc")
        s_pd = [nc.alloc_semaphore(f"p3_pd{i}") for i in range(2)]
        wtmpP = es.enter_context(nc.sbuf_tensor("wtmpP", [128, 2, D], F32))
        s_dv = nc.alloc_semaphore("p3_dv")
        s_ld = [nc.alloc_semaphore(f"p3_ld{i}") for i in range(2)]
        s_S = nc.alloc_semaphore("p3_S")
        s_exp = nc.alloc_semaphore("p3_exp")
        s_pv = nc.alloc_semaphore("p3_pv")
        s_fin = nc.alloc_semaphore("p3_fin")
        s_yo = [nc.alloc_semaphore(f"p3_yo{i}") for i in range(2)]

        GMAX = 3
        groups = []
        qlast = {}
        for h in range(H):
            for Q in range(NQ):
                nk = 4 * Q + 4
                full = [(kt, kt >= 4 * Q) for kt in range(4 * Q + 1)]
                cur = []
                glist = []
                for t in full:
                    cur.append(t)
                    if len(cur) == GMAX:
                        glist.append((0, cur)); cur = []
                if cur:
                    glist.append((0, cur))
                for kt in range(4 * Q + 1, nk):
                    glist.append(((kt - 4 * Q) * 128, [(kt, True)]))
                for gi_, (n0, tl) in enumerate(glist):
                    groups.append(dict(h=h, Q=Q, n0=n0, tiles=tl, first=(gi_ == 0), last=(gi_ == len(glist) - 1),
                                       i=len(groups)))
                qlast[(h, Q)] = len(groups) - 1
        head_first = {h: min(t["i"] for t in groups if t["h"] == h) for h in range(H)}
        head_last = {h: max(t["i"] for t in groups if t["h"] == h) for h in range(H)}
        NLD = 20

        with nc.Block() as b:
            @b.gpsimd
            def _(p):
                for k, ap in enumerate((kTa[64:70, 0, :], kTa[64:70, 1, :], vsb[:, 0, :, 64:128])):
                    p.memset(ap, 1.0).then_inc(s_ms, 1)
                    p.wait_ge(s_ms, k + 1)
                p.dma_start(out=g.bg[:], in_=g.b_glu.rearrange("o (j p) -> p (o j)", p=128),
                            allow_slow_non_contiguous=True).then_inc(s_pw, 16)
                p.dma_start(out=g.gpost[:], in_=bass.AP(g.norm_post.tensor, 0, [[0, 128], [1, D]])).then_inc(s_pw, 16)
                p.wait_ge(s_pw, 32)
                jobs = []
                for (src, dst, nk, nco) in ((g.w_glu, g.wg, 4, 512), (g.w_branch_a, g.wa, 4, D),
                                            (g.w_branch_s, g.wsr, 4, D), (g.w_out, g.wo, 8, D)):
                    for k in range(nk):
                        jobs.append((src[128 * k:128 * (k + 1), :], dst[:, k, :], nco))

                def pdma(i):
                    srcap, _, nco = jobs[i]
                    p.dma_start(out=wtmpP[:, i % 2, 0:nco], in_=srcap).then_inc(s_pd[i % 2], 16)
                pdma(0)
                pdma(1)
                for i, (srcap, dstap, nco) in enumerate(jobs):
                    p.wait_ge(s_pd[i % 2], 16 * (i // 2 + 1))
                    p.tensor_copy(out=dstap, in_=wtmpP[:, i % 2, 0:nco]).then_inc(s_pc, 1)
                    p.wait_ge(s_pc, i + 1)
                    if i + 2 < len(jobs):
                        pdma(i + 2)

            @b.sync
            def _(sp):
                def pieces(h):
                    sl = h % 2
                    P = []
                    QW = SEQ // 4
                    for c4 in range(4):
                        cs_ = slice(c4 * QW, (c4 + 1) * QW)
                        P.append(lambda cs_=cs_: sp.dma_start(out=kTa[0:64, sl, cs_], in_=g.kT[h * 64:(h + 1) * 64, cs_]
                                                              ).then_inc(s_ld[sl], 16))
                        P.append(lambda cs_=cs_: sp.dma_start(out=qTa[0:64, sl, cs_], in_=g.qT[h * 64:(h + 1) * 64, cs_]
                                                              ).then_inc(s_ld[sl], 16))
                    vsrc = g.vtok[:, h * 64:(h + 1) * 64].rearrange("(kt p) d -> p kt d", p=128)
                    for part in range(8):
                        P.append(lambda part=part: sp.dma_start(out=vsb[:, sl, part * 8:(part + 1) * 8, 0:64],
                                                                in_=vsrc[:, part * 8:(part + 1) * 8, :]
                                                                ).then_inc(s_ld[sl], 16))
                    for c2 in range(2):
                        cs_ = slice(c2 * (SEQ // 2), (c2 + 1) * (SEQ // 2))
                        P.append(lambda cs_=cs_: sp.dma_start(out=sga[:, sl, cs_], in_=g.sgaT[h * 64:(h + 1) * 64, cs_]
                                                              ).then_inc(s_ld[sl], 16))
                    C = [lambda: sp.dma_start(out=kTa[67:70, sl, :], in_=g.crk[:, h, :]).then_inc(s_ld[sl], 16),
                         lambda: sp.dma_start(out=qTa[64:67, sl, :], in_=g.crq[:, h, :]).then_inc(s_ld[sl], 16)]
                    return P, C

                for h0 in range(2):
                    P, C = pieces(h0)
                    for f in P:
                        f()
                    if h0 == 0:
                        sp.wait_ge(s_ms, 3)
                        sp.wait_ge(s_ms2, 3)
                    for f in C:
                        f()
                for h in range(H):
                    nxt = h + 1
                    pend = []
                    if 2 <= nxt < H:
                        P, C = pieces(nxt)
                        pend = P + C
                        sp.wait_ge(s_pv, head_last[nxt - 2] + 1)
                        sp.wait_ge(s_fin, NQ * (nxt - 1))
                    for Q in range(NQ):
                        qi = h * NQ + Q
                        sp.wait_ge(s_fin, qi + 1)
                        sp.dma_start(out=g.yaT[h * 64:(h + 1) * 64, Q * CH:(Q + 1) * CH],
                                     in_=ystage[:, qi % 2, :]).then_inc(s_yo[qi % 2], 16)
                        for _ in range(2):
                            if pend:
                                pend.pop(0)()
                    while pend:
                        pend.pop(0)()
                for i in range(2):
                    sp.wait_ge(s_yo[i], 16 * (H * NQ // 2))

            @b.tensor
            def _(pe):
                def S(t):
                    i = t["i"]
                    sl = t["h"] % 2
                    if i == head_first[t["h"]]:
                        pe.wait_ge(s_ld[sl], 16 * NLD * (t["h"] // 2 + 1))
                    if i >= 2:
                        pe.wait_ge(s_exp, i - 1)
                    n0 = t["n0"]
                    q0 = t["Q"] * CH
                    for j, (kt, diag) in enumerate(t["tiles"]):
                        ins = pe.matmul(sp_ps[:, i % 2, j, n0:CH], lhsT=kTa[0:70, sl, kt * 128:(kt + 1) * 128],
                                        rhs=qTa[0:70, sl, q0 + n0:q0 + CH], start=True, stop=not diag)
                        if diag:
                            ins = pe.matmul(sp_ps[:, i % 2, j, n0:n0 + 128], lhsT=g.ident[:], rhs=g.maskT[:],
                                            start=False, stop=True)
                    ins.then_inc(s_S, 1)

                def PV(t):
                    i = t["i"]
                    sl = t["h"] % 2
                    qi = t["h"] * NQ + t["Q"]
                    pe.wait_ge(s_exp, i + 1)
                    if t["first"] and qi >= 2:
                        pe.wait_ge(s_fin, qi - 1)
                    n0 = t["n0"]
                    nt = len(t["tiles"])
                    for j, (kt, diag) in enumerate(t["tiles"]):
                        ins = pe.matmul(o_ps[:, qi % 2, n0:CH], lhsT=vsb[:, sl, kt, :], rhs=pT[:, i % 3, j, n0:CH],
                                        start=(t["first"] and j == 0), stop=(t["last"] and j == nt - 1))
                    ins.then_inc(s_pv, 1)

                n = len(groups)
                S(groups[0])
                S(groups[1])
                for i in range(n):
                    if i + 2 < n:
                        S(groups[i + 2])
                    PV(groups[i])

            @b.scalar
            def _(act):
                for t in groups:
                    i = t["i"]
                    act.wait_ge(s_S, i + 1)
                    if i >= 3:
                        act.wait_ge(s_pv, i - 2)
                    n0 = t["n0"]
                    nt = len(t["tiles"])
                    act.activation(out=pT[:, i % 3, 0:nt, n0:CH], in_=sp_ps[:, i % 2, 0:nt, n0:CH], func=AF.Exp,
                                   scale=0.125).then_inc(s_exp, 1)

            @b.vector
            def _(dve):
                for k, ap in enumerate((qTa[64:70, 0, :], qTa[64:70, 1, :], vsb[:, 1, :, 64:128])):
                    dve.memset(ap, 1.0).then_inc(s_ms2, 1)
                    dve.wait_ge(s_ms2, k + 1)
                for h in range(H):
                    sl = h % 2
                    dve.wait_ge(s_ld[sl], 16 * NLD * (h // 2 + 1))
                    for Q in range(NQ):
                        qi = h * NQ + Q
                        dve.wait_ge(s_pv, qlast[(h, Q)] + 1)
                        if qi >= 1:
                            dve.wait_ge(s_fin, qi)
                        if qi >= 2:
                            dve.wait_ge(s_yo[qi % 2], 16 * (qi // 2))
                        dve.reciprocal(out=rl[:], in_=o_ps[64:128, qi % 2, :]).then_inc(s_dv, 1)
                        dve.wait_ge(s_dv, 2 * qi + 1)
                        dve.tensor_tensor(out=yt[:], in0=o_ps[0:64, qi % 2, :], in1=rl[:], op=ALU.mult).then_inc(s_dv, 1)
                        dve.wait_ge(s_dv, 2 * qi + 2)
                        dve.tensor_tensor(out=ystage[:, qi % 2, :], in0=yt[:], in1=sga[:, sl, Q * CH:(Q + 1) * CH],
                                          op=ALU.mult).then_inc(s_fin, 1)


class Ser:
    def __init__(self, eng, st):
        self.e = eng
        self.st = st

    def done(self, ins, inc=1):
        ins.then_inc(self.st["sem"], inc)
        self.st["n"] += inc
        self.e.wait_ge(self.st["sem"], self.st["n"])
        return ins

    def many(self, instrs, inc=1):
        for ins in instrs:
            ins.then_inc(self.st["sem"], inc)
            self.st["n"] += inc
        self.e.wait_ge(self.st["sem"], self.st["n"])

    def __getattr__(self, name):
        f = getattr(self.e, name)
        inc = 16 if name == "dma_start" else 1

        def w(*a, **k):
            return self.done(f(*a, **k), inc)
        return w


class Steps:
    def __init__(self, nc, tag):
        self.nc = nc
        self.st = {e: {"sem": nc.alloc_semaphore(f"{tag}_{e}"), "n": 0}
                   for e in ("sync", "vector", "scalar", "gpsimd", "tensor")}

    def run(self, **fns):
        with self.nc.Block() as b:
            for ename, fn in fns.items():
                def mk(fn, ename):
                    def body(e):
                        fn(Ser(e, self.st[ename]), e)
                    return body
                getattr(b, ename)(mk(fn, ename))


TWO_PI = 6.28318
HALF_PI = 1.5707963


def phase_ssm(g):
    nc = g.nc
    NK = SEQ // 16
    with contextlib.ExitStack() as es:
        def sb(name, shape, dt=F32):
            return es.enter_context(nc.sbuf_tensor("ssm_" + name, list(shape), dt))
        S = Steps(nc, "p4")
        identf = sb("identf", [128, 128])
        pm = sb("pm", [128, 2])
        bm = sb("bm", [128, 8])
        kidx_i = sb("kidx_i", [128, NK], I32)
        kidx = sb("kidx", [128, NK])
        midx = sb("midx", [128, 17])
        LR = sb("LR", [128, 4]); LI = sb("LI", [128, 4]); LDT = sb("LDT", [128, 4])
        BR = sb("BR", [128, 4, 16]); BI = sb("BI", [128, 4, 16])
        CNr = sb("CNr", [64, 128]); CNi = sb("CNi", [64, 128])
        Dcol = sb("Dcol", [128, 1])
        dt_ = sb("dt", [128, 4]); lrdt = sb("lrdt", [128, 4]); lidt = sb("lidt", [128, 4]); phi = sb("phi", [128, 4])
        tA = sb("tA", [128, 4, 17]); tB = sb("tB", [128, 4, 17]); tC = sb("tC", [128, 4, 17]); tI = sb("tI", [128, 4, 17], I32)
        EAr = sb("EAr", [128, 4, 17]); EAi = sb("EAi", [128, 4, 17])
        s1 = sb("s1", [128, 4]); s2 = sb("s2", [128, 4]); s3 = sb("s3", [128, 4]); s4 = sb("s4", [128, 4])
        fr = sb("fr", [128, 4]); fi = sb("fi", [128, 4]); sI = sb("sI", [128, 4], I32)
        Bbr = sb("Bbr", [128, 4, 16]); Bbi = sb("Bbi", [128, 4, 16]); b1 = sb("b1", [128, 4, 16])
        Gr = sb("Gr", [128, 4, 16, 16]); Gi = sb("Gi", [128, 4, 16, 16]); G1 = sb("G1", [128, 4, 16, 16])
        Gx = sb("Gx", [128, 16, 2, 4, 2, 16], BF16)
        CTr = sb("CTr", [128, 4, 16]); CTi = sb("CTi", [128, 4, 16])
        CTrb = sb("CTrb", [128, 4, 16], BF16); CTnib = sb("CTnib", [128, 4, 16], BF16)
        Wc = sb("Wc", [128, 4, 16, 2, 2, 16], BF16)
        Wb = sb("Wb", [128, 16, 2, 128], BF16)
        Kst = sb("Kst", [128, 16, 8, 16], BF16)
        cosT = sb("cosT", [128, 4, NK], BF16); sinT = sb("sinT", [128, 4, NK], BF16)
        rho = sb("rho", [128, 4]); phr = sb("phr", [128, 4])
        uN = sb("uN", [128, SEQ], BF16); uP = sb("uP", [128, 16, NK], BF16)
        Sp = sb("Sp", [128, 4, 2, NK], BF16)
        w1 = sb("w1", [128, 4, NK]); w2 = sb("w2", [128, 4, NK]); w3 = sb("w3", [128, 4, NK]); w4 = sb("w4", [128, 4, NK])
        kA = w1; kB = w2; kIv = w3[:].bitcast(I32)
        ystage = sb("ystage", [128, SEQ], BF16)
        yperm = sb("yperm", [128, 16, NK], BF16)

        def bc(ap, shape, axis):
            return ap.unsqueeze(axis).to_broadcast(list(shape))

        def c0(p, raw):
            p.memset(identf[:], 0.0)
            p.affine_select(out=identf[:], in_=g.ones_f[:], pattern=[[-1, 128]], compare_op=ALU.is_equal,
                            fill=0.0, base=0, channel_multiplier=1)
            p.affine_select(out=pm[:], in_=g.ones_f[:, 0:2], pattern=[[-64, 2]], compare_op=ALU.is_ge,
                            fill=0.0, base=0, channel_multiplier=1)
            p.affine_select(out=pm[:], in_=pm[:], pattern=[[64, 2]], compare_op=ALU.is_ge,
                            fill=0.0, base=63, channel_multiplier=-1)
            p.affine_select(out=bm[:], in_=g.ones_f[:, 0:8], pattern=[[-16, 8]], compare_op=ALU.is_ge,
                            fill=0.0, base=0, channel_multiplier=1)
            p.affine_select(out=bm[:], in_=bm[:], pattern=[[16, 8]], compare_op=ALU.is_ge,
                            fill=0.0, base=15, channel_multiplier=-1)
            p.iota(kidx_i[:], pattern=[[1, NK]], base=0, channel_multiplier=0)
            p.memset(Sp[:], 0.0)
        S.run(gpsimd=c0)

        def c1(v, raw):
            v.tensor_copy(out=kidx[:], in_=kidx_i[:])
            v.tensor_copy(out=midx[:], in_=kidx[:, 0:17])
        S.run(vector=c1)

        lamr = g.lam_re.rearrange("(q t) p -> (t p) q", t=2)
        lami = g.lam_im.rearrange("(q t) p -> (t p) q", t=2)
        bre = g.b_re.rearrange("(q t) p c -> (t p) q c", t=2)
        bim = g.b_im.rearrange("(q t) p c -> (t p) q c", t=2)

        for r in range(4):
            def ld(sp, raw, r=r):
                L = []
                L.append(raw.dma_start(out=LR[:], in_=lamr[:, 4 * r:4 * r + 4], allow_slow_non_contiguous=True))
                L.append(raw.dma_start(out=LI[:], in_=lami[:, 4 * r:4 * r + 4], allow_slow_non_contiguous=True))
                for t in range(2):
                    L.append(raw.dma_start(out=LDT[64 * t:64 * t + 64, :],
                                           in_=bass.AP(g.log_dt.tensor, t + 8 * r, [[0, 64], [2, 4]]),
                                           allow_slow_non_contiguous=True))
                L.append(raw.dma_start(out=BR[:], in_=bre[:, 4 * r:4 * r + 4, :]))
                L.append(raw.dma_start(out=BI[:], in_=bim[:, 4 * r:4 * r + 4, :]))
                for qq in range(4):
                    q = 4 * r + qq
                    L.append(raw.dma_start(out=CNr[16 * qq:16 * qq + 16, :].rearrange("c (t p) -> c t p", t=2),
                                           in_=g.c_re[2 * q:2 * q + 2, :, :].rearrange("t c p -> c t p")))
                    L.append(raw.dma_start(out=CNi[16 * qq:16 * qq + 16, :].rearrange("c (t p) -> c t p", t=2),
                                           in_=g.c_im[2 * q:2 * q + 2, :, :].rearrange("t c p -> c t p")))
                L.append(raw.dma_start(out=Dcol[:],
                                       in_=g.d_skip[8 * r:8 * r + 8, :].rearrange("g (c o) -> (g c) o", o=1)))
                L.append(raw.dma_start(out=uN[:], in_=g.uT[128 * r:128 * r + 128, :]))
                sp.many(L, 16)
            def make_ld(rr):
                return lambda sp, raw: ld(sp, raw, rr)
            if r == 0:
                S.run(sync=ld)

            if r == 0:
                S.run(scalar=lambda a, raw: a.activation(out=dt_[:], in_=LDT[:], func=AF.Exp))
            else:
                S.run(scalar=lambda a, raw: a.activation(out=dt_[:], in_=LDT[:], func=AF.Exp),
                      sync=lambda sp, raw: sp.dma_start(out=g.ys1T[128 * (r - 1):128 * r, :], in_=ystage[:]))

            def a1(v, raw):
                v.tensor_tensor(out=lrdt[:], in0=LR[:], in1=dt_[:], op=ALU.mult)
                v.tensor_tensor(out=lidt[:], in0=LI[:], in1=dt_[:], op=ALU.mult)
                v.tensor_scalar(out=phi[:], in0=lidt[:], scalar1=1.0 / (2 * math.pi), scalar2=None, op0=ALU.mult)
                v.tensor_tensor(out=tA[:], in0=bc(phi[:], [128, 4, 17], 2), in1=bc(midx[:], [128, 4, 17], 1), op=ALU.mult)
                v.tensor_tensor(out=tB[:], in0=bc(lrdt[:], [128, 4, 17], 2), in1=bc(midx[:], [128, 4, 17], 1), op=ALU.mult)
                v.tensor_copy(out=tI[:], in_=tA[:])
                v.tensor_copy(out=tC[:], in_=tI[:])
                v.tensor_tensor(out=tA[:], in0=tA[:], in1=tC[:], op=ALU.subtract)
                v.tensor_scalar(out=tC[:], in0=tA[:], scalar1=-1.0, scalar2=None, op0=ALU.mult)
                v.tensor_tensor(out=tC[:], in0=tA[:], in1=tC[:], op=ALU.max)
                v.tensor_scalar(out=s1[:], in0=phi[:], scalar1=16.0, scalar2=None, op0=ALU.mult)
                v.tensor_copy(out=sI[:], in_=s1[:])
                v.tensor_copy(out=s2[:], in_=sI[:])
                v.tensor_tensor(out=phr[:], in0=s1[:], in1=s2[:], op=ALU.subtract)
            S.run(vector=a1)

            def a2(a, raw):
                a.activation(out=EAi[:], in_=tA[:], func=AF.Sin, scale=TWO_PI)
                a.activation(out=EAr[:], in_=tC[:], func=AF.Sin, scale=-TWO_PI, bias=HALF_PI)
                a.activation(out=tB[:], in_=tB[:], func=AF.Exp)
                a.activation(out=rho[:], in_=lrdt[:], func=AF.Exp, scale=16.0)
            S.run(scalar=a2)

            def a3(v, raw):
                v.tensor_tensor(out=EAr[:], in0=EAr[:], in1=tB[:], op=ALU.mult)
                v.tensor_tensor(out=EAi[:], in0=EAi[:], in1=tB[:], op=ALU.mult)
                v.tensor_scalar(out=s1[:], in0=EAr[:, :, 1], scalar1=-1.0, scalar2=None, op0=ALU.add)
                v.tensor_tensor(out=s2[:], in0=LR[:], in1=LR[:], op=ALU.mult)
                v.tensor_tensor(out=s3[:], in0=LI[:], in1=LI[:], op=ALU.mult)
                v.tensor_tensor(out=s2[:], in0=s2[:], in1=s3[:], op=ALU.add)
                v.reciprocal(out=s2[:], in_=s2[:])
                v.tensor_tensor(out=s3[:], in0=s1[:], in1=LR[:], op=ALU.mult)
                v.tensor_tensor(out=s4[:], in0=EAi[:, :, 1], in1=LI[:], op=ALU.mult)
                v.tensor_tensor(out=s3[:], in0=s3[:], in1=s4[:], op=ALU.add)
                v.tensor_tensor(out=fr[:], in0=s3[:], in1=s2[:], op=ALU.mult)
                v.tensor_tensor(out=s3[:], in0=EAi[:, :, 1], in1=LR[:], op=ALU.mult)
                v.tensor_tensor(out=s4[:], in0=s1[:], in1=LI[:], op=ALU.mult)
                v.tensor_tensor(out=s3[:], in0=s3[:], in1=s4[:], op=ALU.subtract)
                v.tensor_tensor(out=fi[:], in0=s3[:], in1=s2[:], op=ALU.mult)
                frb = bc(fr[:], [128, 4, 16], 2); fib = bc(fi[:], [128, 4, 16], 2)
                v.tensor_tensor(out=Bbr[:], in0=BR[:], in1=frb, op=ALU.mult)
                v.tensor_tensor(out=b1[:], in0=BI[:], in1=fib, op=ALU.mult)
                v.tensor_tensor(out=Bbr[:], in0=Bbr[:], in1=b1[:], op=ALU.subtract)
                v.tensor_tensor(out=Bbi[:], in0=BI[:], in1=frb, op=ALU.mult)
                v.tensor_tensor(out=b1[:], in0=BR[:], in1=fib, op=ALU.mult)
                v.tensor_tensor(out=Bbi[:], in0=Bbi[:], in1=b1[:], op=ALU.add)
                sh = [128, 4, 16, 16]
                ear = bc(EAr[:, :, 0:16], sh, 3); eai = bc(EAi[:, :, 0:16], sh, 3)
                bbr = bc(Bbr[:], sh, 2); bbi = bc(Bbi[:], sh, 2)
                v.tensor_tensor(out=Gr[:], in0=ear, in1=bbr, op=ALU.mult)
                v.tensor_tensor(out=G1[:], in0=eai, in1=bbi, op=ALU.mult)
                v.tensor_tensor(out=Gr[:], in0=Gr[:], in1=G1[:], op=ALU.subtract)
                v.tensor_tensor(out=Gi[:], in0=ear, in1=bbi, op=ALU.mult)
                v.tensor_tensor(out=G1[:], in0=eai, in1=bbr, op=ALU.mult)
                v.tensor_tensor(out=Gi[:], in0=Gi[:], in1=G1[:], op=ALU.add)
                for x, Gsrc in enumerate((Gr, Gi)):
                    for g2 in range(2):
                        v.tensor_scalar(out=Gx[:, :, x, :, g2, :].rearrange("p m q c -> p q m c"), in0=Gsrc[:],
                                        scalar1=pm[:, g2:g2 + 1], scalar2=None, op0=ALU.mult)
            S.run(vector=a3)

            with nc.psum_tensor(f"ssm_ctp{r}", [128, 2, 64], F32) as ctp, \
                    nc.psum_tensor(f"ssm_wbp{r}", [128, 4, 8, 128], BF16) as wbp, \
                    nc.psum_tensor(f"ssm_kc{r}", [128, 16, 16], F32) as kc:
                def t1(pe, raw):
                    raw.transpose(out=ctp[:, 0, :], in_=CNr[:, :], identity=identf[0:64, 0:64])
                    raw.transpose(out=ctp[:, 1, :], in_=CNi[:, :], identity=identf[0:64, 0:64])
                    for m in range(16):
                        for x in range(2):
                            idx = m * 2 + x
                            ins = raw.transpose(out=wbp[:, idx // 8, idx % 8, :],
                                                in_=Gx[:, m, x, :, :, :].rearrange("p q t c -> p (q t c)"),
                                                identity=g.ident[:])
                    pe.done(ins)
                S.run(tensor=t1)

                def t2(v, raw):
                    v.tensor_copy(out=CTr[:].rearrange("p q c -> p (q c)"), in_=ctp[:, 0, :])
                    v.tensor_copy(out=CTi[:].rearrange("p q c -> p (q c)"), in_=ctp[:, 1, :])
                    v.tensor_copy(out=CTrb[:], in_=CTr[:])
                    v.tensor_scalar(out=CTnib[:], in0=CTi[:], scalar1=-1.0, scalar2=None, op0=ALU.mult)
                    for bk in range(4):
                        v.tensor_copy(out=Wb[:, 4 * bk:4 * bk + 4, :, :].rearrange("p m x c -> p (m x) c"),
                                      in_=wbp[:, bk, :, :])
                    sh = [128, 4, 16, 16]
                    ear = bc(EAr[:, :, 1:17], sh, 3); eai = bc(EAi[:, :, 1:17], sh, 3)
                    ctr = bc(CTr[:], sh, 2); cti = bc(CTi[:], sh, 2)
                    v.tensor_tensor(out=Gr[:], in0=ear, in1=ctr, op=ALU.mult)
                    v.tensor_tensor(out=G1[:], in0=eai, in1=cti, op=ALU.mult)
                    v.tensor_tensor(out=Gr[:], in0=Gr[:], in1=G1[:], op=ALU.subtract)
                    v.tensor_tensor(out=Gi[:], in0=eai, in1=ctr, op=ALU.mult)
                    v.tensor_tensor(out=G1[:], in0=ear, in1=cti, op=ALU.mult)
                    v.tensor_tensor(out=Gi[:], in0=Gi[:], in1=G1[:], op=ALU.add)
                    v.tensor_scalar(out=Gi[:], in0=Gi[:], scalar1=-1.0, scalar2=None, op0=ALU.mult)
                    for x, Csrc in enumerate((Gr, Gi)):
                        for g2 in range(2):
                            v.tensor_scalar(out=Wc[:, :, :, x, g2, :], in0=Csrc[:], scalar1=pm[:, g2:g2 + 1],
                                            scalar2=None, op0=ALU.mult)
                S.run(vector=t2)

                def t3(pe, raw):
                    for lag in range(16):
                        for qq in range(4):
                            raw.matmul(kc[32 * qq:32 * qq + 32, lag, :],
                                       lhsT=Gx[:, lag, 0, qq, :, :].rearrange("p t c -> p (t c)"), rhs=CTrb[:, qq, :],
                                       start=True, stop=False, tile_position=(0, 32 * qq), skip_group_check=True)
                            ins = raw.matmul(kc[32 * qq:32 * qq + 32, lag, :],
                                             lhsT=Gx[:, lag, 1, qq, :, :].rearrange("p t c -> p (t c)"),
                                             rhs=CTnib[:, qq, :], start=False, stop=True,
                                             tile_position=(0, 32 * qq), skip_group_check=True)
                    pe.done(ins)
                S.run(tensor=t3)

                def t4(v, raw):
                    for gi in range(8):
                        v.tensor_scalar(out=Kst[:, :, gi, :], in0=kc[:, :, :], scalar1=bm[:, gi:gi + 1], scalar2=None,
                                        op0=ALU.mult)
                    k0 = Kst[:, 0, :, :].rearrange("p a c -> p (a c)")
                    v.scalar_tensor_tensor(out=k0, in0=identf[:], scalar=Dcol[:, 0:1], in1=k0, op0=ALU.mult, op1=ALU.add)
                    shk = [128, 4, NK]
                    v.tensor_tensor(out=kA[:], in0=bc(phr[:], shk, 2), in1=bc(kidx[:], shk, 1), op=ALU.mult)
                    v.tensor_copy(out=kIv, in_=kA[:])
                    v.tensor_copy(out=kB[:], in_=kIv)
                    v.tensor_tensor(out=kA[:], in0=kA[:], in1=kB[:], op=ALU.subtract)
                    v.tensor_scalar(out=kB[:], in0=kA[:], scalar1=-1.0, scalar2=None, op0=ALU.mult)
                    v.tensor_tensor(out=kB[:], in0=kA[:], in1=kB[:], op=ALU.max)
                S.run(vector=t4)

            def t5(a, raw):
                a.activation(out=sinT[:], in_=kA[:], func=AF.Sin, scale=TWO_PI)
                a.activation(out=cosT[:], in_=kB[:], func=AF.Sin, scale=-TWO_PI, bias=HALF_PI)
            S.run(scalar=t5, vector=lambda p, raw: p.tensor_copy(
                out=uP[:], in_=uN[:].rearrange("p (k i) -> p i k", i=16)))

            with nc.psum_tensor(f"ssm_bb{r}", [128, 4, 2, NK], F32) as bb:
                def m1(pe, raw):
                    for x in range(2):
                        for j in range(16):
                            for qq in range(4):
                                ins = raw.matmul(bb[:, qq, x, :], lhsT=Wb[32 * qq:32 * qq + 32, 15 - j, x, :],
                                                 rhs=uP[32 * qq:32 * qq + 32, j, :], start=(j == 0), stop=(j == 15),
                                                 tile_position=(32 * qq, 0))
                    pe.done(ins)
                S.run(tensor=m1)

                def m2(v, raw):
                    br = bb[:, :, 0, :]; bi = bb[:, :, 1, :]
                    cs = cosT[:, :, :]; sn = sinT[:, :, :]
                    v.tensor_tensor(out=w1[:], in0=br, in1=cs, op=ALU.mult)
                    v.tensor_tensor(out=w2[:], in0=bi, in1=sn, op=ALU.mult)
                    v.tensor_tensor(out=w1[:], in0=w1[:], in1=w2[:], op=ALU.add)
                    v.tensor_tensor(out=w2[:], in0=bi, in1=cs, op=ALU.mult)
                    v.tensor_tensor(out=w3[:], in0=br, in1=sn, op=ALU.mult)
                    v.tensor_tensor(out=w2[:], in0=w2[:], in1=w3[:], op=ALU.subtract)
                    for qq in range(4):
                        rb = rho[:, qq:qq + 1].to_broadcast([128, NK])
                        v.tensor_tensor_scan(out=w3[:, qq, :], data0=rb, data1=w1[:, qq, :], initial=0.0,
                                             op0=ALU.mult, op1=ALU.add)
                        v.tensor_tensor_scan(out=w4[:, qq, :], data0=rb, data1=w2[:, qq, :], initial=0.0,
                                             op0=ALU.mult, op1=ALU.add)
                    v.tensor_tensor(out=w1[:], in0=w3[:], in1=cs, op=ALU.mult)
                    v.tensor_tensor(out=w2[:], in0=w4[:], in1=sn, op=ALU.mult)
                    v.tensor_tensor(out=Sp[:, :, 0, 1:NK], in0=w1[:, :, 0:NK - 1], in1=w2[:, :, 0:NK - 1],
                                    op=ALU.subtract)
                    v.tensor_tensor(out=w1[:], in0=w3[:], in1=sn, op=ALU.mult)
                    v.tensor_tensor(out=w2[:], in0=w4[:], in1=cs, op=ALU.mult)
                    v.tensor_tensor(out=Sp[:, :, 1, 1:NK], in0=w1[:, :, 0:NK - 1], in1=w2[:, :, 0:NK - 1], op=ALU.add)
                if r + 1 < 4:
                    S.run(vector=m2, sync=make_ld(r + 1))
                else:
                    S.run(vector=m2)

            ys = yperm
            with nc.psum_tensor(f"ssm_yb{r}", [128, 8, NK], F32) as yb:
                def mk_f1(qtr):
                    def f1(pe, raw):
                        for ii in range(4):
                            i = qtr * 4 + ii
                            bank = (qtr % 2) * 4 + ii
                            for lag in range(i + 1):
                                raw.matmul(yb[:, bank, :], lhsT=Kst[:, lag, :, :].rearrange("p a c -> p (a c)"),
                                           rhs=uP[:, i - lag, :], start=(lag == 0), stop=False)
                            for qq in range(4):
                                for x in range(2):
                                    ins = raw.matmul(yb[32 * qq:32 * qq + 32, bank, :],
                                                     lhsT=Wc[:, qq, i, x, :, :].rearrange("p t c -> p (t c)"),
                                                     rhs=Sp[:, qq, x, :], start=False, stop=(x == 1),
                                                     tile_position=(0, 32 * qq), skip_group_check=True)
                        pe.done(ins)
                    return f1

                def mk_f2(qtr):
                    def f2(a, raw):
                        for ii in range(4):
                            i = qtr * 4 + ii
                            bank = (qtr % 2) * 4 + ii
                            a.activation(out=ys[:, i, :], in_=yb[:, bank, :], func=AF.Gelu_apprx_tanh)
                    return f2

                S.run(tensor=mk_f1(0))
                for qtr in range(1, 4):
                    S.run(tensor=mk_f1(qtr), scalar=mk_f2(qtr - 1))
                S.run(scalar=mk_f2(3))
            S.run(vector=lambda v, raw: v.tensor_copy(out=ystage[:].rearrange("p (k i) -> p k i", i=16),
                                                      in_=yperm[:].rearrange("p i k -> p k i")))
        S.run(sync=lambda sp, raw: sp.dma_start(out=g.ys1T[128 * 3:128 * 4, :], in_=ystage[:]))


def phase_final(g):
    nc = g.nc
    with contextlib.ExitStack() as es:
        def sb(name, shape, dt=F32, stack=es):
            return stack.enter_context(nc.sbuf_tensor("fin_" + name, list(shape), dt))
        wg, wa, wsr, wo, bg, gpost = g.wg, g.wa, g.wsr, g.wo, g.bg, g.gpost
        ys1 = sb("ys1", [128, 2, 4, CH], BF16); sgs = sb("sgs", [128, 2, 4, CH], BF16); ya = sb("ya", [128, 2, 4, CH], BF16)
        sma = sb("sma", [128, 2, 8, CH], BF16); sms = sb("sms", [128, 2, 8, CH], BF16)
        xin = sb("xin", [128, 2, 4, D])
        sg = sb("sg", [128, 4, CH], BF16); y3 = sb("y3", [128, 4, CH], BF16)
        ma = sb("ma", [128, 8, CH]); mg = sb("mg", [128, 8, CH], BF16)
        tS = sb("tS", [128, 2, CH])
        tF = sb("tF", [128, 2, D])
        ost = sb("ost", [128, 2, D])
        junk = sb("junk", [128, 2, D], BF16)
        ss2 = sb("ss2", [128, 2]); sd2 = sb("sd2", [128, 2]); rs2 = sb("rs2", [128, 2])
        ps = es.enter_context(nc.psum_tensor("fin_ps", [128, 4, CH], F32))
        po = es.enter_context(nc.psum_tensor("fin_po", [128, 2, 2, CH], F32))

        s_in = [nc.alloc_semaphore(f"p5_in{i}") for i in range(2)]
        s_evA = nc.alloc_semaphore("p5_evA")
        s_evD = nc.alloc_semaphore("p5_evD")
        s_dd = nc.alloc_semaphore("p5_dd")
        s_y3 = nc.alloc_semaphore("p5_y3")
        s_mm = nc.alloc_semaphore("p5_mm")
        s_o = nc.alloc_semaphore("p5_o")
        s_sq = nc.alloc_semaphore("p5_sq")
        s_rs = nc.alloc_semaphore("p5_rs")
        s_fin = nc.alloc_semaphore("p5_fin")
        s_st = [nc.alloc_semaphore(f"p5_st{i}") for i in range(2)]

        fmseq = [("z", 0, j) for j in range(4)]
        for c in range(NCH):
            fmseq += [("a", c, j) for j in range(8)] + [("s", c, j) for j in range(8)]
            if c + 1 < NCH:
                fmseq += [("z", c + 1, j) for j in range(4)]
        nidx = {k: n for n, k in enumerate(fmseq)}

        def ev_info(n):
            kind, c, j = fmseq[n]
            if kind == "z":
                return "A", c * 4 + j + 1
            return "D", c * 16 + (j if kind == "a" else 8 + j) + 1

        dd = [0]

        with nc.Block() as b:
            @b.sync
            def _(sp):
                def load(c):
                    sl = c % 2
                    tok = slice(c * CH, (c + 1) * CH)
                    if c >= 2:
                        sp.wait_ge(s_fin, 4 * (c - 1))
                    for dst, src in ((ys1, g.ys1T), (sgs, g.sgsT), (ya, g.yaT), (sma, g.smaT), (sms, g.smsT)):
                        sp.dma_start(out=dst[:, sl], in_=src[:, tok].rearrange("(j p) t -> p j t", p=128)
                                     ).then_inc(s_in[sl], 16)
                    sp.dma_start(out=xin[:, sl], in_=g.x[tok, :].rearrange("(t p) d -> p t d", p=128)
                                 ).then_inc(s_in[sl], 16)
                load(0)
                load(1)
                for c in range(NCH):
                    for tt in range(4):
                        ti = 4 * c + tt
                        sp.wait_ge(s_fin, ti + 1)
                        r0 = c * CH + tt * 128
                        sp.dma_start(out=g.out[r0:r0 + 128, :], in_=ost[:, ti % 2, :]).then_inc(s_st[ti % 2], 16)
                    if c + 2 < NCH:
                        load(c + 2)
                for i in range(2):
                    sp.wait_ge(s_st[i], 16 * (4 * NCH // 2))

            @b.tensor
            def _(pe):
                loaded = set()

                def fmblock(n):
                    kind, c, j = fmseq[n]
                    sl = c % 2
                    if c not in loaded:
                        pe.wait_ge(s_in[sl], 96 * (c // 2 + 1))
                        loaded.add(c)
                    if n >= 4:
                        e, cnt = ev_info(n - 4)
                        pe.wait_ge(s_evA if e == "A" else s_evD, cnt)
                    if kind == "s" and j == 0:
                        pe.wait_ge(s_y3, c + 1)
                    w, src = {"z": (wg, ys1[:, sl]), "a": (wa, ya[:, sl]), "s": (wsr, y3)}[kind]
                    for kk in range(4):
                        ins = pe.matmul(ps[:, n % 4, :], lhsT=w[:, kk, 128 * j:128 * j + 128], rhs=src[:, kk, :],
                                        start=(kk == 0), stop=(kk == 3))
                    ins.then_inc(s_mm, 1)

                def outproj(c):
                    pe.wait_ge(s_evD, 16 * (c + 1))
                    for tt in range(4):
                        ti = 4 * c + tt
                        if ti >= 2:
                            pe.wait_ge(s_fin, ti - 1)
                        for hf in range(2):
                            for kk in range(8):
                                ins = pe.matmul(po[:, ti % 2, hf, :], lhsT=mg[:, kk, 128 * tt:128 * tt + 128],
                                                rhs=wo[:, kk, 512 * hf:512 * hf + 512], start=(kk == 0), stop=(kk == 7))
                        ins.then_inc(s_o, 1)

                for n, (kind, c, j) in enumerate(fmseq):
                    fmblock(n)
                    last_of_chunk = (kind == "z" and j == 3 and c >= 1) or (kind == "s" and j == 7 and c == NCH - 1)
                    if last_of_chunk:
                        outproj(c - 1 if kind == "z" else c)

            @b.scalar
            def _(act):
                def sig(c):
                    for j in range(4):
                        n = nidx[("z", c, j)]
                        act.wait_ge(s_mm, n + 1)
                        if j == 0 and c >= 1:
                            act.wait_ge(s_y3, c)
                        act.activation(out=sg[:, j, :], in_=ps[:, n % 4, :], func=AF.Sigmoid,
                                       bias=bg[:, j:j + 1]).then_inc(s_evA, 1)

                def stats(c):
                    for tt in range(4):
                        ti = 4 * c + tt
                        act.wait_ge(s_o, ti + 1)
                        if ti >= 2:
                            act.wait_ge(s_fin, ti - 1)
                        act.activation(out=junk[:, ti % 2, :], in_=po[:, ti % 2, :, :].rearrange("p a c -> p (a c)"),
                                       func=AF.Square, accum_out=ss2[:, ti % 2:ti % 2 + 1]).then_inc(s_sq, 1)
                        act.wait_ge(s_sq, 2 * ti + 1)
                        act.activation(out=sd2[:, ti % 2:ti % 2 + 1], in_=ss2[:, ti % 2:ti % 2 + 1], func=AF.Ln,
                                       scale=1.0 / D, bias=EPS).then_inc(s_sq, 1)
                        act.wait_ge(s_sq, 2 * ti + 2)
                        act.activation(out=rs2[:, ti % 2:ti % 2 + 1], in_=sd2[:, ti % 2:ti % 2 + 1], func=AF.Exp,
                                       scale=-0.5).then_inc(s_rs, 1)

                sig(0)
                for c in range(NCH):
                    if c + 1 < NCH:
                        sig(c + 1)
                    stats(c)

            @b.vector
            def _(dve):
                def chain(ins):
                    ins.then_inc(s_dd, 1)
                    dd[0] += 1
                    dve.wait_ge(s_dd, dd[0])

                def y3f(c):
                    sl = c % 2
                    dve.wait_ge(s_in[sl], 96 * (c // 2 + 1))
                    dve.wait_ge(s_evA, 4 * (c + 1))
                    if c >= 1:
                        dve.wait_ge(s_mm, nidx[("s", c - 1, 7)] + 1)
                    chain(dve.tensor_tensor(out=sg[:], in0=sg[:], in1=ys1[:, sl], op=ALU.mult))
                    dve.tensor_tensor(out=y3[:], in0=sg[:], in1=sgs[:, sl], op=ALU.mult).then_inc(s_y3, 1)

                def evacs(c):
                    sl = c % 2
                    dve.wait_ge(s_in[sl], 96 * (c // 2 + 1))
                    for kind in ("a", "s"):
                        for j in range(8):
                            n = nidx[(kind, c, j)]
                            dve.wait_ge(s_mm, n + 1)
                            if kind == "a":
                                if j == 0 and c >= 1:
                                    dve.wait_ge(s_evD, 16 * c)
                                dve.tensor_tensor(out=ma[:, j, :], in0=ps[:, n % 4, :], in1=sma[:, sl, j, :],
                                                  op=ALU.mult).then_inc(s_evD, 1)
                            else:
                                if j == 0 and c >= 1:
                                    dve.wait_ge(s_o, 4 * c)
                                chain(dve.tensor_tensor(out=tS[:, j % 2, :], in0=ps[:, n % 4, :],
                                                        in1=sms[:, sl, j, :], op=ALU.mult))
                                dve.wait_ge(s_evD, 16 * c + j + 1)
                                dve.tensor_tensor(out=mg[:, j, :], in0=tS[:, j % 2, :], in1=ma[:, j, :],
                                                  op=ALU.add).then_inc(s_evD, 1)

                def fin(c, tt):
                    sl = c % 2
                    ti = 4 * c + tt
                    dve.wait_ge(s_rs, ti + 1)
                    if ti >= 2:
                        dve.wait_ge(s_st[ti % 2], 16 * (ti // 2))
                    chain(dve.scalar_tensor_tensor(out=tF[:, ti % 2, :],
                                                   in0=po[:, ti % 2, :, :].rearrange("p a c -> p (a c)"),
                                                   scalar=rs2[:, ti % 2:ti % 2 + 1], in1=gpost[:],
                                                   op0=ALU.mult, op1=ALU.mult))
                    dve.tensor_tensor(out=ost[:, ti % 2, :], in0=tF[:, ti % 2, :], in1=xin[:, sl, tt, :],
                                      op=ALU.add).then_inc(s_fin, 1)

                y3f(0)
                for c in range(NCH):
                    evacs(c)
                    fin(c, 0)
                    fin(c, 1)
                    if c + 1 < NCH:
                        y3f(c + 1)
                    fin(c, 2)
                    fin(c, 3)


def kernel(**inputs):
    nc = build()
    x = np.ascontiguousarray(inputs["x"], dtype=np.float32)
    shared = {}
    for k, v in inputs.items():
        if k == "x":
            continue
        a = np.ascontiguousarray(v, dtype=np.float32)
        shared[k] = a.reshape(_SHAPES[k])
    in_maps = []
    for c in range(NCORES):
        m = dict(shared)
        m["x"] = x[c]
        in_maps.append(m)
    res = run_bass_kernel_spmd(nc, in_maps, core_ids=list(range(NCORES)))
    return np.stack([r["out"] for r in res.results], axis=0).astype(np.float32)


_SHAPES = {
    "norm_pre": (1, D), "w_in": (D, INC), "b_forget": (1, H), "lam_re": (32, 64), "lam_im": (32, 64),
    "log_dt": (1, 32), "b_re": (32, 64, 16), "b_im": (32, 64, 16), "c_re": (32, 16, 64),
    "c_im": (32, 16, 64), "d_skip": (32, 16), "w_glu": (512, 512), "b_glu": (1, 512),
    "w_branch_a": (512, D), "w_branch_s": (512, D), "w_out": (D, D), "norm_post": (1, D),
}
```

```python
import contextlib
import math
import numpy as np
import concourse.bass as bass
import concourse.mybir as mybir
from concourse.bass_utils import run_bass_kernel_spmd

F32 = mybir.dt.float32
BF16 = mybir.dt.bfloat16
I32 = mybir.dt.int32
AF = mybir.ActivationFunctionType
ALU = mybir.AluOpType

D = 1024
SEQ = 8192
NCORES = 8
INC = 5640
USED = 5128
H = 8
HD = 64
EPS = 1e-6
CH = 512
NCH = SEQ // CH


class Ctx:
    pass


def build(last_phase=99, debug=False):
    nc = bass.Bass("TRN2", target_bir_lowering=False)
    es = contextlib.ExitStack()
    g = Ctx()
    g.nc = nc
    g.debug = debug

    def din(name, shape):
        return nc.dram_tensor(name, list(shape), F32, kind="ExternalInput").ap()

    g.x = din("x", [SEQ, D])
    g.norm_pre = din("norm_pre", [1, D])
    g.w_in = din("w_in", [D, INC])
    g.b_forget = din("b_forget", [1, H])
    g.lam_re = din("lam_re", [32, 64])
    g.lam_im = din("lam_im", [32, 64])
    g.log_dt = din("log_dt", [1, 32])
    g.b_re = din("b_re", [32, 64, 16])
    g.b_im = din("b_im", [32, 64, 16])
    g.c_re = din("c_re", [32, 16, 64])
    g.c_im = din("c_im", [32, 16, 64])
    g.d_skip = din("d_skip", [32, 16])
    g.w_glu = din("w_glu", [512, 512])
    g.b_glu = din("b_glu", [1, 512])
    g.w_branch_a = din("w_branch_a", [512, D])
    g.w_branch_s = din("w_branch_s", [512, D])
    g.w_out = din("w_out", [D, D])
    g.norm_post = din("norm_post", [1, D])
    g.out = nc.dram_tensor("out", [SEQ, D], F32, kind="ExternalOutput").ap()

    def scratch(name, shape, dt=BF16):
        if debug:
            return nc.dram_tensor(name, list(shape), dt, kind="ExternalOutput").ap()
        return nc.dram_tensor(name, list(shape), dt).ap()

    g.qT = scratch("qT", [512, SEQ])
    g.kT = scratch("kT", [512, SEQ])
    g.vtok = scratch("vtok", [SEQ, 512])
    g.fT = scratch("fT", [8, SEQ], F32)
    g.sgaT = scratch("sgaT", [512, SEQ])
    g.uT = scratch("uT", [512, SEQ])
    g.sgsT = scratch("sgsT", [512, SEQ])
    g.smaT = scratch("smaT", [D, SEQ])
    g.smsT = scratch("smsT", [D, SEQ])

    g.ident = es.enter_context(nc.sbuf_tensor("ident", [128, 128], BF16))
    g.ones_f = es.enter_context(nc.sbuf_tensor("ones_f", [128, 128], F32))
    S0 = Steps(nc, "init")

    def i0(p, raw):
        p.memset(g.ones_f[:], 1.0)
        p.affine_select(out=g.ident[:], in_=g.ones_f[:], pattern=[[-1, 128]],
                        compare_op=ALU.is_equal, fill=0.0, base=0, channel_multiplier=1)
    S0.run(gpsimd=i0)

    g.crk = scratch("crk", [3, H, SEQ])
    g.crq = scratch("crq", [3, H, SEQ])
    g.yaT = scratch("yaT", [512, SEQ])
    g.zeros_b = es.enter_context(nc.sbuf_tensor("zeros_b", [128, 128], BF16))
    g.maskT = es.enter_context(nc.sbuf_tensor("maskT", [128, 128], BF16))
    def i1(p, raw):
        p.memset(g.zeros_b[:], 0.0)
        p.affine_select(out=g.maskT[:], in_=g.zeros_b[:], pattern=[[1, 128]],
                        compare_op=ALU.is_ge, fill=-65536.0, base=0, channel_multiplier=-1)
    S0.run(gpsimd=i1)

    if last_phase >= 1:
        phase_inproj(g)
    if last_phase >= 2:
        phase_forget(g)
    g.wg = es.enter_context(nc.sbuf_tensor("fin_wg", [128, 4, 512], BF16))
    g.wa = es.enter_context(nc.sbuf_tensor("fin_wa", [128, 4, D], BF16))
    g.wsr = es.enter_context(nc.sbuf_tensor("fin_wsr", [128, 4, D], BF16))
    g.wo = es.enter_context(nc.sbuf_tensor("fin_wo", [128, 8, D], BF16))
    g.bg = es.enter_context(nc.sbuf_tensor("fin_bg", [128, 4], F32))
    g.gpost = es.enter_context(nc.sbuf_tensor("fin_gpost", [128, D], F32))
    if last_phase >= 3:
        phase_attn(g)
    g.ys1T = scratch("ys1T", [512, SEQ])
    if last_phase >= 4:
        phase_ssm(g)
    if last_phase >= 5:
        phase_final(g)
    es.close()
    return nc


def phase_inproj(g):
    nc = g.nc
    with contextlib.ExitStack() as es:
        wsb = es.enter_context(nc.sbuf_tensor("wsb", [128, 8, USED], BF16))
        gain = es.enter_context(nc.sbuf_tensor("gain", [128, 8], F32))
        with contextlib.ExitStack() as es0:
            wtmp = es0.enter_context(nc.sbuf_tensor("wtmp", [128, 2, USED], F32))
            s_w = [nc.alloc_semaphore(f"p0_w{i}") for i in range(2)]
            s_g = nc.alloc_semaphore("p0_g")
            s_done = [nc.alloc_semaphore(f"p0_d{i}") for i in range(3)]
            cuts = [0, 1536, 4864, USED]
            with nc.Block() as b:
                @b.sync
                def _(sp):
                    sp.dma_start(out=gain[:], in_=g.norm_pre.rearrange("o (k p) -> p (o k)", p=128),
                                 allow_slow_non_contiguous=True).then_inc(s_g, 16)
                    for dk in range(8):
                        if dk >= 2:
                            for e in range(3):
                                sp.wait_ge(s_done[e], dk - 1)
                        sp.dma_start(out=wtmp[:, dk % 2, :],
                                     in_=g.w_in[dk * 128:(dk + 1) * 128, 0:USED]).then_inc(s_w[dk % 2], 16)

                def conv(eng, e, kind):
                    eng.wait_ge(s_g, 16)
                    for dk in range(8):
                        eng.wait_ge(s_w[dk % 2], 16 * (dk // 2 + 1))
                        c0, c1 = cuts[e], cuts[e + 1]
                        if kind == "act":
                            eng.activation(out=wsb[:, dk, c0:c1], in_=wtmp[:, dk % 2, c0:c1],
                                           func=AF.Copy, scale=gain[:, dk:dk + 1]).then_inc(s_done[e], 1)
                        else:
                            eng.tensor_scalar(out=wsb[:, dk, c0:c1], in0=wtmp[:, dk % 2, c0:c1],
                                              scalar1=gain[:, dk:dk + 1], scalar2=None,
                                              op0=ALU.mult).then_inc(s_done[e], 1)

                @b.vector
                def _(e):
                    conv(e, 0, "dve")

                @b.scalar
                def _(e):
                    conv(e, 1, "act")

                @b.gpsimd
                def _(e):
                    conv(e, 2, "pool")

        xs = es.enter_context(nc.sbuf_tensor("xs", [128, 2, 4, D], F32))
        hn = es.enter_context(nc.sbuf_tensor("hn", [128, 2, 4, D], BF16))
        hT = es.enter_context(nc.sbuf_tensor("hT", [128, 2, 8, CH], BF16))
        junk = es.enter_context(nc.sbuf_tensor("junk", [128, 4, D], BF16))
        ss = es.enter_context(nc.sbuf_tensor("ss", [128, 2, 4], F32))
        sd = es.enter_context(nc.sbuf_tensor("sd", [128, 2, 4], F32))
        rstd = es.enter_context(nc.sbuf_tensor("rstd", [128, 2, 4], F32))
        NS = 6
        stage = es.enter_context(nc.sbuf_tensor("stage", [128, NS, CH], BF16))
        stagef = es.enter_context(nc.sbuf_tensor("stagef", [8, 2, CH], F32))
        ps = es.enter_context(nc.psum_tensor("ps", [128, 4, CH], F32))
        tp = es.enter_context(nc.psum_tensor("tp", [128, 2, 1024], BF16))

        blks = []

        def add(kind, col0, m, func, eng, dest, row0):
            blks.append(dict(kind=kind, col0=col0, m=m, func=func, eng=eng, dest=dest, row0=row0))

        for j in range(4):
            add("fm", 0 + 128 * j, 128, AF.Copy, "dve", g.qT, 128 * j)
        for j in range(4):
            add("fm", 512 + 128 * j, 128, AF.Copy, "dve", g.kT, 128 * j)
        for j in range(4):
            add("fm", 2056 + 128 * j, 128, AF.Copy, "dve", g.uT, 128 * j)
        for tt in range(4):
            add("v", 1024, 128, AF.Copy, "dve", g.vtok, tt)
        add("f", 1536, 8, AF.Copy, "dve", g.fT, 0)
        for j in range(4):
            add("fm", 1544 + 128 * j, 128, AF.Silu, "act", g.sgaT, 128 * j)
        for j in range(4):
            add("fm", 2568 + 128 * j, 128, AF.Silu, "act", g.sgsT, 128 * j)
        for j in range(8):
            add("fm", 3080 + 128 * j, 128, AF.Sigmoid, "act", g.smaT, 128 * j)
        for j in range(8):
            add("fm", 4104 + 128 * j, 128, AF.Sigmoid, "act", g.smsT, 128 * j)
        NB = len(blks)
        seq = []
        cnt = {"act": 0, "dve": 0}
        nstage = 0
        for c in range(NCH):
            for j, bk in enumerate(blks):
                cnt[bk["eng"]] += 1
                d = dict(bk)
                d.update(c=c, n=len(seq), eidx=cnt[bk["eng"]])
                if bk["kind"] == "f":
                    d["slot"] = None
                else:
                    d["slot"] = nstage % NS
                    d["suse"] = nstage // NS
                    nstage += 1
                seq.append(d)

        s_x = [nc.alloc_semaphore(f"p1_x{i}") for i in range(2)]
        s_sd = nc.alloc_semaphore("p1_sd")
        s_ss = nc.alloc_semaphore("p1_ss")
        s_ln = nc.alloc_semaphore("p1_ln")
        s_rstd = nc.alloc_semaphore("p1_rstd")
        s_hn = nc.alloc_semaphore("p1_hn")
        s_tp = nc.alloc_semaphore("p1_tp")
        s_hT = nc.alloc_semaphore("p1_hT")
        s_mm = nc.alloc_semaphore("p1_mm")
        s_ev = {"act": nc.alloc_semaphore("p1_eva"), "dve": nc.alloc_semaphore("p1_evd")}
        s_out = [nc.alloc_semaphore(f"p1_o{i}") for i in range(NS)]
        s_outf = [nc.alloc_semaphore(f"p1_of{i}") for i in range(2)]

        def stage_ap(d):
            if d["kind"] == "f":
                return stagef[0:8, d["c"] % 2, :]
            return stage[:, d["slot"], :]

        def dest_ap(d):
            c = d["c"]
            if d["kind"] == "v":
                r0 = c * CH + d["row0"] * 128
                return d["dest"][r0:r0 + 128, :]
            if d["kind"] == "f":
                return d["dest"][0:8, c * CH:(c + 1) * CH]
            return d["dest"][d["row0"]:d["row0"] + 128, c * CH:(c + 1) * CH]

        with nc.Block() as b:
            @b.sync
            def _(sp):
                def load(c):
                    if c >= 2:
                        sp.wait_ge(s_hn, c - 1)
                    sp.dma_start(out=xs[:, c % 2, :, :],
                                 in_=g.x[c * CH:(c + 1) * CH, :].rearrange("(t p) d -> p t d", p=128)
                                 ).then_inc(s_x[c % 2], 16)
                load(0)
                load(1)
                for c in range(NCH):
                    if c + 2 < NCH:
                        load(c + 2)
                    for d in seq[c * NB:(c + 1) * NB]:
                        sp.wait_ge(s_ev[d["eng"]], d["eidx"])
                        so = s_outf[c % 2] if d["kind"] == "f" else s_out[d["slot"]]
                        sp.dma_start(out=dest_ap(d), in_=stage_ap(d)).then_inc(so, 16)
                for i in range(NS):
                    uses = len([d for d in seq if d["slot"] == i])
                    sp.wait_ge(s_out[i], 16 * uses)
                for i in range(2):
                    sp.wait_ge(s_outf[i], 16 * (NCH // 2))

            def evac(eng, d, is_act):
                eng.wait_ge(s_mm, d["n"] + 1)
                if d["kind"] == "f":
                    if d["c"] >= 2:
                        eng.wait_ge(s_outf[d["c"] % 2], 16 * (d["c"] // 2))
                    src = ps[0:8, d["n"] % 4, :]
                else:
                    if d["suse"] >= 1:
                        eng.wait_ge(s_out[d["slot"]], 16 * d["suse"])
                    src = ps[:, d["n"] % 4, :]
                if is_act:
                    eng.activation(out=stage_ap(d), in_=src, func=d["func"]).then_inc(s_ev["act"], 1)
                else:
                    eng.tensor_copy(out=stage_ap(d), in_=src).then_inc(s_ev["dve"], 1)

            @b.scalar
            def _(act):
                def stats(c):
                    sl = c % 2
                    act.wait_ge(s_x[sl], 16 * (c // 2 + 1))
                    for tt in range(4):
                        act.activation(out=junk[:, tt, :], in_=xs[:, sl, tt, :], func=AF.Square,
                                       accum_out=ss[:, sl, tt:tt + 1]).then_inc(s_ss, 1)
                    act.wait_ge(s_ss, 4 * (c + 1))
                    act.activation(out=ss[:, sl, :], in_=ss[:, sl, :], func=AF.Ln,
                                   scale=1.0 / D, bias=EPS).then_inc(s_ln, 1)
                    act.wait_ge(s_ln, c + 1)
                    act.activation(out=sd[:, sl, :], in_=ss[:, sl, :], func=AF.Exp,
                                   scale=-0.5).then_inc(s_sd, 1)
                    act.wait_ge(s_sd, c + 1)
                    if c >= 2:
                        act.wait_ge(s_tp, 8 * (c - 1))
                    for tt in range(4):
                        ins = act.activation(out=hn[:, sl, tt, :], in_=xs[:, sl, tt, :], func=AF.Copy,
                                             scale=sd[:, sl, tt:tt + 1])
                    ins.then_inc(s_hn, 1)
                stats(0)
                stats(1)
                for c in range(NCH):
                    if c + 2 < NCH:
                        stats(c + 2)
                    for d in seq[c * NB:(c + 1) * NB]:
                        if d["eng"] == "act":
                            evac(act, d, True)

            @b.vector
            def _(dve):
                def pro(c):
                    sl = c % 2
                    if c >= 2:
                        dve.wait_ge(s_mm, (c - 1) * NB)
                    for dk in range(8):
                        dve.wait_ge(s_tp, c * 8 + dk + 1)
                        dve.tensor_copy(out=hT[:, sl, dk, :], in_=tp[:, dk % 2, 0:CH]).then_inc(s_hT, 1)
                pro(0)
                for c in range(NCH):
                    if c + 1 < NCH:
                        pro(c + 1)
                    for d in seq[c * NB:(c + 1) * NB]:
                        if d["eng"] == "dve":
                            evac(dve, d, False)


            @b.tensor
            def _(pe):
                def trans(c):
                    sl = c % 2
                    pe.wait_ge(s_hn, c + 1)
                    for dk in range(8):
                        gi = c * 8 + dk
                        if gi >= 2:
                            pe.wait_ge(s_hT, gi - 1)
                        for tt in range(4):
                            ins = pe.transpose(out=tp[:, dk % 2, tt * 128:(tt + 1) * 128],
                                               in_=hn[:, sl, tt, dk * 128:(dk + 1) * 128], identity=g.ident[:])
                        ins.then_inc(s_tp, 1)
                trans(0)
                for c in range(NCH):
                    sl = c % 2
                    if c + 1 < NCH:
                        trans(c + 1)
                    pe.wait_ge(s_hT, 8 * (c + 1))
                    for d in seq[c * NB:(c + 1) * NB]:
                        n = d["n"]
                        if n >= 4:
                            pd = seq[n - 4]
                            pe.wait_ge(s_ev[pd["eng"]], pd["eidx"])
                        for dk in range(8):
                            if d["kind"] == "v":
                                tt = d["row0"]
                                ins = pe.matmul(ps[:, n % 4, :], lhsT=hT[:, sl, dk, tt * 128:(tt + 1) * 128],
                                                rhs=wsb[:, dk, 1024:1536], start=(dk == 0), stop=(dk == 7))
                            else:
                                m = d["m"]
                                ins = pe.matmul(ps[0:m, n % 4, :], lhsT=wsb[:, dk, d["col0"]:d["col0"] + m],
                                                rhs=hT[:, sl, dk, :], start=(dk == 0), stop=(dk == 7))
                        ins.then_inc(s_mm, 1)


def phase_forget(g):
    nc = g.nc
    SEG = 16
    SL = SEQ // SEG
    with contextlib.ExitStack() as es:
        ft = es.enter_context(nc.sbuf_tensor("ft", [128, SL], F32))
        t1 = es.enter_context(nc.sbuf_tensor("fg_t1", [128, SL], F32))
        t2 = es.enter_context(nc.sbuf_tensor("fg_t2", [128, SL], F32))
        ones = es.enter_context(nc.sbuf_tensor("fg_ones", [128, SL], F32))
        rows = es.enter_context(nc.sbuf_tensor("fg_rows", [128, 6, SL], BF16))
        bfn = es.enter_context(nc.sbuf_tensor("bfn", [128, 1], F32))
        M = es.enter_context(nc.sbuf_tensor("fg_M", [128, 128], F32))
        tot = es.enter_context(nc.sbuf_tensor("fg_tot", [128, 2], F32))
        off = es.enter_context(nc.sbuf_tensor("fg_off", [128, 2], F32))
        offp = es.enter_context(nc.psum_tensor("fg_offp", [128, 2], F32))
        S = Steps(nc, "p2")

        def ld(sp, raw):
            L = [raw.dma_start(out=ft[:], in_=g.fT.rearrange("h (s t) -> (h s) t", s=SEG))]
            for h in range(H):
                L.append(raw.dma_start(out=bfn[SEG * h:SEG * (h + 1), :],
                                       in_=bass.AP(g.b_forget.tensor, h, [[0, SEG], [1, 1]]),
                                       allow_slow_non_contiguous=True))
            sp.many(L, 16)

        def mk(p, raw):
            p.affine_select(out=M[:], in_=g.ones_f[:], pattern=[[1, 128]], compare_op=ALU.is_gt, fill=0.0,
                            base=0, channel_multiplier=-1)
            m3 = M[:].rearrange("p (h s) -> p h s", s=SEG)
            p.affine_select(out=m3, in_=m3, pattern=[[-SEG, H], [0, SEG]], compare_op=ALU.is_ge, fill=0.0,
                            base=0, channel_multiplier=1)
            p.affine_select(out=m3, in_=m3, pattern=[[SEG, H], [0, SEG]], compare_op=ALU.is_ge, fill=0.0,
                            base=SEG - 1, channel_multiplier=-1)
            p.memset(ones[:], 1.0)
        S.run(sync=ld, gpsimd=mk)
        S.run(vector=lambda v, raw: v.tensor_scalar(out=bfn[:], in0=bfn[:], scalar1=-1.0, scalar2=None, op0=ALU.mult))

        def a(act, raw):
            act.activation(out=t1[:], in_=ft[:], func=AF.Exp, scale=-1.0, bias=bfn[:, 0:1])
            act.activation(out=t2[:], in_=t1[:], func=AF.Ln, bias=1.0, scale=1.0)
        S.run(scalar=a)

        def d1(dve, raw):
            dve.tensor_tensor_scan(out=t1[:], data0=ones[:], data1=t2[:], initial=0.0, op0=ALU.mult, op1=ALU.add)
            dve.tensor_copy(out=tot[:, 0:1], in_=t1[:, SL - 1:SL])
            dve.tensor_copy(out=tot[:, 1:2], in_=t1[:, SL - 1:SL])
        S.run(vector=d1)
        S.run(tensor=lambda pe, raw: pe.matmul(offp[:, :], lhsT=M[:], rhs=tot[:], start=True, stop=True))

        def d2(dve, raw):
            dve.tensor_copy(out=off[:], in_=offp[:, :])
            dve.tensor_scalar(out=t1[:], in0=t1[:], scalar1=off[:, 0:1], scalar2=8.0, op0=ALU.add, op1=ALU.mult)
            dve.tensor_copy(out=rows[:, 0, :], in_=t1[:])
            dve.tensor_tensor(out=t2[:], in0=t1[:], in1=rows[:, 0, :], op=ALU.subtract)
            dve.tensor_copy(out=rows[:, 1, :], in_=t2[:])
            dve.tensor_tensor(out=t1[:], in0=t2[:], in1=rows[:, 1, :], op=ALU.subtract)
            dve.tensor_copy(out=rows[:, 2, :], in_=t1[:])
            dve.tensor_scalar(out=rows[:, 3:6, :], in0=rows[:, 0:3, :], scalar1=-1.0, scalar2=None, op0=ALU.mult)
        S.run(vector=d2)

        def st(sp, raw):
            L = []
            for j in range(3):
                L.append(raw.dma_start(out=g.crk[j].rearrange("h (s t) -> (h s) t", s=SEG), in_=rows[:, j, :]))
                L.append(raw.dma_start(out=g.crq[j].rearrange("h (s t) -> (h s) t", s=SEG), in_=rows[:, 3 + j, :]))
            sp.many(L, 16)
        S.run(sync=st)


def phase_attn(g):
    nc = g.nc
    NQ = SEQ // CH
    NKT = SEQ // 128
    with contextlib.ExitStack() as es:
        kTa = es.enter_context(nc.sbuf_tensor("kTa", [70, 2, SEQ], BF16))
        qTa = es.enter_context(nc.sbuf_tensor("qTa", [70, 2, SEQ], BF16))
        vsb = es.enter_context(nc.sbuf_tensor("vsb", [128, 2, NKT, 128], BF16))
        sga = es.enter_context(nc.sbuf_tensor("sga", [64, 2, SEQ], BF16))
        pT = es.enter_context(nc.sbuf_tensor("pT", [128, 3, 3, CH], BF16))
        rl = es.enter_context(nc.sbuf_tensor("rl", [64, CH], F32))
        yt = es.enter_context(nc.sbuf_tensor("yt", [64, CH], F32))
        ystage = es.enter_context(nc.sbuf_tensor("ystage", [64, 2, CH], BF16))
        sp_ps = es.enter_context(nc.psum_tensor("sp_ps", [128, 2, 3, CH], F32))
        o_ps = es.enter_context(nc.psum_tensor("o_ps", [128, 2, CH], F32))
        s_ms = nc.alloc_semaphore("p3_ms")
        s_ms2 = nc.alloc_semaphore("p3_ms2")
        s_pw = nc.alloc_semaphore("p3_pw")
        s_pc = nc.alloc_semaphore("p3_pc")
        s_pd = [nc.alloc_semaphore(f"p3_pd{i}") for i in range(2)]
        wtmpP = es.enter_context(nc.sbuf_tensor("wtmpP", [128, 2, D], F32))
        s_dv = nc.alloc_semaphore("p3_dv")
        s_ld = [nc.alloc_semaphore(f"p3_ld{i}") for i in range(2)]
        s_S = nc.alloc_semaphore("p3_S")
        s_exp = nc.alloc_semaphore("p3_exp")
        s_pv = nc.alloc_semaphore("p3_pv")
        s_fin = nc.alloc_semaphore("p3_fin")
        s_yo = [nc.alloc_semaphore(f"p3_yo{i}") for i in range(2)]

        GMAX = 3
        groups = []
        qlast = {}
        for h in range(H):
            for Q in range(NQ):
                nk = 4 * Q + 4
                full = [(kt, kt >= 4 * Q) for kt in range(4 * Q + 1)]
                cur = []
                glist = []
                for t in full:
                    cur.append(t)
                    if len(cur) == GMAX:
                        glist.append((0, cur)); cur = []
                if cur:
                    glist.append((0, cur))
                for kt in range(4 * Q + 1, nk):
                    glist.append(((kt - 4 * Q) * 128, [(kt, True)]))
                for gi_, (n0, tl) in enumerate(glist):
                    groups.append(dict(h=h, Q=Q, n0=n0, tiles=tl, first=(gi_ == 0), last=(gi_ == len(glist) - 1),
                                       i=len(groups)))
                qlast[(h, Q)] = len(groups) - 1
        head_first = {h: min(t["i"] for t in groups if t["h"] == h) for h in range(H)}
        head_last = {h: max(t["i"] for t in groups if t["h"] == h) for h in range(H)}
        NLD = 20

        with nc.Block() as b:
            @b.gpsimd
            def _(p):
                for k, ap in enumerate((kTa[64:70, 0, :], kTa[64:70, 1, :], vsb[:, 0, :, 64:128])):
                    p.memset(ap, 1.0).then_inc(s_ms, 1)
                    p.wait_ge(s_ms, k + 1)
                p.dma_start(out=g.bg[:], in_=g.b_glu.rearrange("o (j p) -> p (o j)", p=128),
                            allow_slow_non_contiguous=True).then_inc(s_pw, 16)
                p.dma_start(out=g.gpost[:], in_=bass.AP(g.norm_post.tensor, 0, [[0, 128], [1, D]])).then_inc(s_pw, 16)
                p.wait_ge(s_pw, 32)
                jobs = []
                for (src, dst, nk, nco) in ((g.w_glu, g.wg, 4, 512), (g.w_branch_a, g.wa, 4, D),
                                            (g.w_branch_s, g.wsr, 4, D), (g.w_out, g.wo, 8, D)):
                    for k in range(nk):
                        jobs.append((src[128 * k:128 * (k + 1), :], dst[:, k, :], nco))

                def pdma(i):
                    srcap, _, nco = jobs[i]
                    p.dma_start(out=wtmpP[:, i % 2, 0:nco], in_=srcap).then_inc(s_pd[i % 2], 16)
                pdma(0)
                pdma(1)
                for i, (srcap, dstap, nco) in enumerate(jobs):
                    p.wait_ge(s_pd[i % 2], 16 * (i // 2 + 1))
                    p.tensor_copy(out=dstap, in_=wtmpP[:, i % 2, 0:nco]).then_inc(s_pc, 1)
                    p.wait_ge(s_pc, i + 1)
                    if i + 2 < len(jobs):
                        pdma(i + 2)

            @b.sync
            def _(sp):
                def pieces(h):
                    sl = h % 2
                    P = []
                    QW = SEQ // 4
                    for c4 in range(4):
                        cs_ = slice(c4 * QW, (c4 + 1) * QW)
                        P.append(lambda cs_=cs_: sp.dma_start(out=kTa[0:64, sl, cs_], in_=g.kT[h * 64:(h + 1) * 64, cs_]
                                                              ).then_inc(s_ld[sl], 16))
                        P.append(lambda cs_=cs_: sp.dma_start(out=qTa[0:64, sl, cs_], in_=g.qT[h * 64:(h + 1) * 64, cs_]
                                                              ).then_inc(s_ld[sl], 16))
                    vsrc = g.vtok[:, h * 64:(h + 1) * 64].rearrange("(kt p) d -> p kt d", p=128)
                    for part in range(8):
                        P.append(lambda part=part: sp.dma_start(out=vsb[:, sl, part * 8:(part + 1) * 8, 0:64],
                                                                in_=vsrc[:, part * 8:(part + 1) * 8, :]
                                                                ).then_inc(s_ld[sl], 16))
                    for c2 in range(2):
                        cs_ = slice(c2 * (SEQ // 2), (c2 + 1) * (SEQ // 2))
                        P.append(lambda cs_=cs_: sp.dma_start(out=sga[:, sl, cs_], in_=g.sgaT[h * 64:(h + 1) * 64, cs_]
                                                              ).then_inc(s_ld[sl], 16))
                    C = [lambda: sp.dma_start(out=kTa[67:70, sl, :], in_=g.crk[:, h, :]).then_inc(s_ld[sl], 16),
                         lambda: sp.dma_start(out=qTa[64:67, sl, :], in_=g.crq[:, h, :]).then_inc(s_ld[sl], 16)]
                    return P, C

                for h0 in range(2):
                    P, C = pieces(h0)
                    for f in P:
                        f()
                    if h0 == 0:
                        sp.wait_ge(s_ms, 3)
                        sp.wait_ge(s_ms2, 3)
                    for f in C:
                        f()
                for h in range(H):
                    nxt = h + 1
                    pend = []
                    if 2 <= nxt < H:
                        P, C = pieces(nxt)
                        pend = P + C
                        sp.wait_ge(s_pv, head_last[nxt - 2] + 1)
                        sp.wait_ge(s_fin, NQ * (nxt - 1))
                    for Q in range(NQ):
                        qi = h * NQ + Q
                        sp.wait_ge(s_fin, qi + 1)
                        sp.dma_start(out=g.yaT[h * 64:(h + 1) * 64, Q * CH:(Q + 1) * CH],
                                     in_=ystage[:, qi % 2, :]).then_inc(s_yo[qi % 2], 16)
                        for _ in range(2):
                            if pend:
                                pend.pop(0)()
                    while pend:
                        pend.pop(0)()
                for i in range(2):
                    sp.wait_ge(s_yo[i], 16 * (H * NQ // 2))

            @b.tensor
            def _(pe):
                def S(t):
                    i = t["i"]
                    sl = t["h"] % 2
                    if i == head_first[t["h"]]:
                        pe.wait_ge(s_ld[sl], 16 * NLD * (t["h"] // 2 + 1))
                    if i >= 2:
                        pe.wait_ge(s_exp, i - 1)
                    n0 = t["n0"]
                    q0 = t["Q"] * CH
                    for j, (kt, diag) in enumerate(t["tiles"]):
                        ins = pe.matmul(sp_ps[:, i % 2, j, n0:CH], lhsT=kTa[0:70, sl, kt * 128:(kt + 1) * 128],
                                        rhs=qTa[0:70, sl, q0 + n0:q0 + CH], start=True, stop=not diag)
                        if diag:
                            ins = pe.matmul(sp_ps[:, i % 2, j, n0:n0 + 128], lhsT=g.ident[:], rhs=g.maskT[:],
                                            start=False, stop=True)
                    ins.then_inc(s_S, 1)

                def PV(t):
                    i = t["i"]
                    sl = t["h"] % 2
                    qi = t["h"] * NQ + t["Q"]
                    pe.wait_ge(s_exp, i + 1)
                    if t["first"] and qi >= 2:
                        pe.wait_ge(s_fin, qi - 1)
                    n0 = t["n0"]
                    nt = len(t["tiles"])
                    for j, (kt, diag) in enumerate(t["tiles"]):
                        ins = pe.matmul(o_ps[:, qi % 2, n0:CH], lhsT=vsb[:, sl, kt, :], rhs=pT[:, i % 3, j, n0:CH],
                                        start=(t["first"] and j == 0), stop=(t["last"] and j == nt - 1))
                    ins.then_inc(s_pv, 1)

                n = len(groups)
                S(groups[0])
                S(groups[1])
                for i in range(n):
                    if i + 2 < n:
                        S(groups[i + 2])
                    PV(groups[i])

            @b.scalar
            def _(act):
                for t in groups:
                    i = t["i"]
                    act.wait_ge(s_S, i + 1)
                    if i >= 3:
                        act.wait_ge(s_pv, i - 2)
                    n0 = t["n0"]
                    nt = len(t["tiles"])
                    act.activation(out=pT[:, i % 3, 0:nt, n0:CH], in_=sp_ps[:, i % 2, 0:nt, n0:CH], func=AF.Exp,
                                   scale=0.125).then_inc(s_exp, 1)

            @b.vector
            def _(dve):
                for k, ap in enumerate((qTa[64:70, 0, :], qTa[64:70, 1, :], vsb[:, 1, :, 64:128])):
                    dve.memset(ap, 1.0).then_inc(s_ms2, 1)
                    dve.wait_ge(s_ms2, k + 1)
                for h in range(H):
                    sl = h % 2
                    dve.wait_ge(s_ld[sl], 16 * NLD * (h // 2 + 1))
                    for Q in range(NQ):
                        qi = h * NQ + Q
                        dve.wait_ge(s_pv, qlast[(h, Q)] + 1)
                        if qi >= 1:
                            dve.wait_ge(s_fin, qi)
                        if qi >= 2:
                            dve.wait_ge(s_yo[qi % 2], 16 * (qi // 2))
                        dve.reciprocal(out=rl[:], in_=o_ps[64:128, qi % 2, :]).then_inc(s_dv, 1)
                        dve.wait_ge(s_dv, 2 * qi + 1)
                        dve.tensor_tensor(out=yt[:], in0=o_ps[0:64, qi % 2, :], in1=rl[:], op=ALU.mult).then_inc(s_dv, 1)
                        dve.wait_ge(s_dv, 2 * qi + 2)
                        dve.tensor_tensor(out=ystage[:, qi % 2, :], in0=yt[:], in1=sga[:, sl, Q * CH:(Q + 1) * CH],
                                          op=ALU.mult).then_inc(s_fin, 1)


class Ser:
    def __init__(self, eng, st):
        self.e = eng
        self.st = st

    def done(self, ins, inc=1):
        ins.then_inc(self.st["sem"], inc)
        self.st["n"] += inc
        self.e.wait_ge(self.st["sem"], self.st["n"])
        return ins

    def many(self, instrs, inc=1):
        for ins in instrs:
            ins.then_inc(self.st["sem"], inc)
            self.st["n"] += inc
        self.e.wait_ge(self.st["sem"], self.st["n"])

    def __getattr__(self, name):
        f = getattr(self.e, name)
        inc = 16 if name == "dma_start" else 1

        def w(*a, **k):
            return self.done(f(*a, **k), inc)
        return w


class Steps:
    def __init__(self, nc, tag):
        self.nc = nc
        self.st = {e: {"sem": nc.alloc_semaphore(f"{tag}_{e}"), "n": 0}
                   for e in ("sync", "vector", "scalar", "gpsimd", "tensor")}

    def run(self, **fns):
        with self.nc.Block() as b:
            for ename, fn in fns.items():
                def mk(fn, ename):
                    def body(e):
                        fn(Ser(e, self.st[ename]), e)
                    return body
                getattr(b, ename)(mk(fn, ename))


TWO_PI = 6.28318
HALF_PI = 1.5707963


def phase_ssm(g):
    nc = g.nc
    NK = SEQ // 16
    with contextlib.ExitStack() as es:
        def sb(name, shape, dt=F32):
            return es.enter_context(nc.sbuf_tensor("ssm_" + name, list(shape), dt))
        S = Steps(nc, "p4")
        identf = sb("identf", [128, 128])
        pm = sb("pm", [128, 2])
        bm = sb("bm", [128, 8])
        kidx_i = sb("kidx_i", [128, NK], I32)
        kidx = sb("kidx", [128, NK])
        midx = sb("midx", [128, 17])
        LR = sb("LR", [128, 4]); LI = sb("LI", [128, 4]); LDT = sb("LDT", [128, 4])
        BR = sb("BR", [128, 4, 16]); BI = sb("BI", [128, 4, 16])
        CNr = sb("CNr", [64, 128]); CNi = sb("CNi", [64, 128])
        Dcol = sb("Dcol", [128, 1])
        dt_ = sb("dt", [128, 4]); lrdt = sb("lrdt", [128, 4]); lidt = sb("lidt", [128, 4]); phi = sb("phi", [128, 4])
        tA = sb("tA", [128, 4, 17]); tB = sb("tB", [128, 4, 17]); tC = sb("tC", [128, 4, 17]); tI = sb("tI", [128, 4, 17], I32)
        EAr = sb("EAr", [128, 4, 17]); EAi = sb("EAi", [128, 4, 17])
        s1 = sb("s1", [128, 4]); s2 = sb("s2", [128, 4]); s3 = sb("s3", [128, 4]); s4 = sb("s4", [128, 4])
        fr = sb("fr", [128, 4]); fi = sb("fi", [128, 4]); sI = sb("sI", [128, 4], I32)
        Bbr = sb("Bbr", [128, 4, 16]); Bbi = sb("Bbi", [128, 4, 16]); b1 = sb("b1", [128, 4, 16])
        Gr = sb("Gr", [128, 4, 16, 16]); Gi = sb("Gi", [128, 4, 16, 16]); G1 = sb("G1", [128, 4, 16, 16])
        Gx = sb("Gx", [128, 16, 2, 4, 2, 16], BF16)
        CTr = sb("CTr", [128, 4, 16]); CTi = sb("CTi", [128, 4, 16])
        CTrb = sb("CTrb", [128, 4, 16], BF16); CTnib = sb("CTnib", [128, 4, 16], BF16)
        Wc = sb("Wc", [128, 4, 16, 2, 2, 16], BF16)
        Wb = sb("Wb", [128, 16, 2, 128], BF16)
        Kst = sb("Kst", [128, 16, 8, 16], BF16)
        cosT = sb("cosT", [128, 4, NK], BF16); sinT = sb("sinT", [128, 4, NK], BF16)
        rho = sb("rho", [128, 4]); phr = sb("phr", [128, 4])
        uN = sb("uN", [128, SEQ], BF16); uP = sb("uP", [128, 16, NK], BF16)
        Sp = sb("Sp", [128, 4, 2, NK], BF16)
        w1 = sb("w1", [128, 4, NK]); w2 = sb("w2", [128, 4, NK]); w3 = sb("w3", [128, 4, NK]); w4 = sb("w4", [128, 4, NK])
        kA = w1; kB = w2; kIv = w3[:].bitcast(I32)
        ystage = sb("ystage", [128, SEQ], BF16)
        yperm = sb("yperm", [128, 16, NK], BF16)

        def bc(ap, shape, axis):
            return ap.unsqueeze(axis).to_broadcast(list(shape))

        def c0(p, raw):
            p.memset(identf[:], 0.0)
            p.affine_select(out=identf[:], in_=g.ones_f[:], pattern=[[-1, 128]], compare_op=ALU.is_equal,
                            fill=0.0, base=0, channel_multiplier=1)
            p.affine_select(out=pm[:], in_=g.ones_f[:, 0:2], pattern=[[-64, 2]], compare_op=ALU.is_ge,
                            fill=0.0, base=0, channel_multiplier=1)
            p.affine_select(out=pm[:], in_=pm[:], pattern=[[64, 2]], compare_op=ALU.is_ge,
                            fill=0.0, base=63, channel_multiplier=-1)
            p.affine_select(out=bm[:], in_=g.ones_f[:, 0:8], pattern=[[-16, 8]], compare_op=ALU.is_ge,
                            fill=0.0, base=0, channel_multiplier=1)
            p.affine_select(out=bm[:], in_=bm[:], pattern=[[16, 8]], compare_op=ALU.is_ge,
                            fill=0.0, base=15, channel_multiplier=-1)
            p.iota(kidx_i[:], pattern=[[1, NK]], base=0, channel_multiplier=0)
            p.memset(Sp[:], 0.0)
        S.run(gpsimd=c0)

        def c1(v, raw):
            v.tensor_copy(out=kidx[:], in_=kidx_i[:])
            v.tensor_copy(out=midx[:], in_=kidx[:, 0:17])
        S.run(vector=c1)

        lamr = g.lam_re.rearrange("(q t) p -> (t p) q", t=2)
        lami = g.lam_im.rearrange("(q t) p -> (t p) q", t=2)
        bre = g.b_re.rearrange("(q t) p c -> (t p) q c", t=2)
        bim = g.b_im.rearrange("(q t) p c -> (t p) q c", t=2)

        for r in range(4):
            def ld(sp, raw, r=r):
                L = []
                L.append(raw.dma_start(out=LR[:], in_=lamr[:, 4 * r:4 * r + 4], allow_slow_non_contiguous=True))
                L.append(raw.dma_start(out=LI[:], in_=lami[:, 4 * r:4 * r + 4], allow_slow_non_contiguous=True))
                for t in range(2):
                    L.append(raw.dma_start(out=LDT[64 * t:64 * t + 64, :],
                                           in_=bass.AP(g.log_dt.tensor, t + 8 * r, [[0, 64], [2, 4]]),
                                           allow_slow_non_contiguous=True))
                L.append(raw.dma_start(out=BR[:], in_=bre[:, 4 * r:4 * r + 4, :]))
                L.append(raw.dma_start(out=BI[:], in_=bim[:, 4 * r:4 * r + 4, :]))
                for qq in range(4):
                    q = 4 * r + qq
                    L.append(raw.dma_start(out=CNr[16 * qq:16 * qq + 16, :].rearrange("c (t p) -> c t p", t=2),
                                           in_=g.c_re[2 * q:2 * q + 2, :, :].rearrange("t c p -> c t p")))
                    L.append(raw.dma_start(out=CNi[16 * qq:16 * qq + 16, :].rearrange("c (t p) -> c t p", t=2),
                                           in_=g.c_im[2 * q:2 * q + 2, :, :].rearrange("t c p -> c t p")))
                L.append(raw.dma_start(out=Dcol[:],
                                       in_=g.d_skip[8 * r:8 * r + 8, :].rearrange("g (c o) -> (g c) o", o=1)))
                L.append(raw.dma_start(out=uN[:], in_=g.uT[128 * r:128 * r + 128, :]))
                sp.many(L, 16)
            def make_ld(rr):
                return lambda sp, raw: ld(sp, raw, rr)
            if r == 0:
                S.run(sync=ld)

            def dtf(a, raw):
                a.activation(out=dt_[:], in_=LDT[:], func=AF.Exp)
            if r == 0:
                S.run(scalar=dtf)

            def a1(v, raw):
                v.tensor_tensor(out=lrdt[:], in0=LR[:], in1=dt_[:], op=ALU.mult)
                v.tensor_tensor(out=lidt[:], in0=LI[:], in1=dt_[:], op=ALU.mult)
                v.tensor_scalar(out=phi[:], in0=lidt[:], scalar1=1.0 / (2 * math.pi), scalar2=None, op0=ALU.mult)
                v.tensor_tensor(out=tA[:], in0=bc(phi[:], [128, 4, 17], 2), in1=bc(midx[:], [128, 4, 17], 1), op=ALU.mult)
                v.tensor_tensor(out=tB[:], in0=bc(lrdt[:], [128, 4, 17], 2), in1=bc(midx[:], [128, 4, 17], 1), op=ALU.mult)
                v.tensor_copy(out=tI[:], in_=tA[:])
                v.tensor_copy(out=tC[:], in_=tI[:])
                v.tensor_tensor(out=tA[:], in0=tA[:], in1=tC[:], op=ALU.subtract)
                v.tensor_scalar(out=tC[:], in0=tA[:], scalar1=-1.0, scalar2=None, op0=ALU.mult)
                v.tensor_tensor(out=tC[:], in0=tA[:], in1=tC[:], op=ALU.max)
                v.tensor_scalar(out=s1[:], in0=phi[:], scalar1=16.0, scalar2=None, op0=ALU.mult)
                v.tensor_copy(out=sI[:], in_=s1[:])
                v.tensor_copy(out=s2[:], in_=sI[:])
                v.tensor_tensor(out=phr[:], in0=s1[:], in1=s2[:], op=ALU.subtract)
            if r == 0:
                S.run(vector=a1)

            def a2(a, raw):
                a.activation(out=EAi[:], in_=tA[:], func=AF.Sin, scale=TWO_PI)
                a.activation(out=EAr[:], in_=tC[:], func=AF.Sin, scale=-TWO_PI, bias=HALF_PI)
                a.activation(out=tB[:], in_=tB[:], func=AF.Exp)
                a.activation(out=rho[:], in_=lrdt[:], func=AF.Exp, scale=16.0)
            if r == 0:
                S.run(scalar=a2)

            def a3(v, raw):
                v.tensor_tensor(out=EAr[:], in0=EAr[:], in1=tB[:], op=ALU.mult)
                v.tensor_tensor(out=EAi[:], in0=EAi[:], in1=tB[:], op=ALU.mult)
                v.tensor_scalar(out=s1[:], in0=EAr[:, :, 1], scalar1=-1.0, scalar2=None, op0=ALU.add)
                v.tensor_tensor(out=s2[:], in0=LR[:], in1=LR[:], op=ALU.mult)
                v.tensor_tensor(out=s3[:], in0=LI[:], in1=LI[:], op=ALU.mult)
                v.tensor_tensor(out=s2[:], in0=s2[:], in1=s3[:], op=ALU.add)
                v.reciprocal(out=s2[:], in_=s2[:])
                v.tensor_tensor(out=s3[:], in0=s1[:], in1=LR[:], op=ALU.mult)
                v.tensor_tensor(out=s4[:], in0=EAi[:, :, 1], in1=LI[:], op=ALU.mult)
                v.tensor_tensor(out=s3[:], in0=s3[:], in1=s4[:], op=ALU.add)
                v.tensor_tensor(out=fr[:], in0=s3[:], in1=s2[:], op=ALU.mult)
                v.tensor_tensor(out=s3[:], in0=EAi[:, :, 1], in1=LR[:], op=ALU.mult)
                v.tensor_tensor(out=s4[:], in0=s1[:], in1=LI[:], op=ALU.mult)
                v.tensor_tensor(out=s3[:], in0=s3[:], in1=s4[:], op=ALU.subtract)
                v.tensor_tensor(out=fi[:], in0=s3[:], in1=s2[:], op=ALU.mult)
                frb = bc(fr[:], [128, 4, 16], 2); fib = bc(fi[:], [128, 4, 16], 2)
                v.tensor_tensor(out=Bbr[:], in0=BR[:], in1=frb, op=ALU.mult)
                v.tensor_tensor(out=b1[:], in0=BI[:], in1=fib, op=ALU.mult)
                v.tensor_tensor(out=Bbr[:], in0=Bbr[:], in1=b1[:], op=ALU.subtract)
                v.tensor_tensor(out=Bbi[:], in0=BI[:], in1=frb, op=ALU.mult)
                v.tensor_tensor(out=b1[:], in0=BR[:], in1=fib, op=ALU.mult)
                v.tensor_tensor(out=Bbi[:], in0=Bbi[:], in1=b1[:], op=ALU.add)
                sh = [128, 4, 16, 16]
                ear = bc(EAr[:, :, 0:16], sh, 3); eai = bc(EAi[:, :, 0:16], sh, 3)
                bbr = bc(Bbr[:], sh, 2); bbi = bc(Bbi[:], sh, 2)
                v.tensor_tensor(out=Gr[:], in0=ear, in1=bbr, op=ALU.mult)
                v.tensor_tensor(out=G1[:], in0=eai, in1=bbi, op=ALU.mult)
                v.tensor_tensor(out=Gr[:], in0=Gr[:], in1=G1[:], op=ALU.subtract)
                v.tensor_tensor(out=Gi[:], in0=ear, in1=bbi, op=ALU.mult)
                v.tensor_tensor(out=G1[:], in0=eai, in1=bbr, op=ALU.mult)
                v.tensor_tensor(out=Gi[:], in0=Gi[:], in1=G1[:], op=ALU.add)
                for x, Gsrc in enumerate((Gr, Gi)):
                    for g2 in range(2):
                        v.tensor_scalar(out=Gx[:, :, x, :, g2, :].rearrange("p m q c -> p q m c"), in0=Gsrc[:],
                                        scalar1=pm[:, g2:g2 + 1], scalar2=None, op0=ALU.mult)
            if r == 0:
                S.run(vector=a3)

            with nc.psum_tensor(f"ssm_ctp{r}", [128, 2, 64], F32) as ctp, \
                    nc.psum_tensor(f"ssm_wbp{r}", [128, 4, 8, 128], BF16) as wbp, \
                    nc.psum_tensor(f"ssm_kc{r}", [128, 16, 16], F32) as kc:
                def t1(pe, raw):
                    raw.transpose(out=ctp[:, 0, :], in_=CNr[:, :], identity=identf[0:64, 0:64])
                    raw.transpose(out=ctp[:, 1, :], in_=CNi[:, :], identity=identf[0:64, 0:64])
                    for m in range(16):
                        for x in range(2):
                            idx = m * 2 + x
                            ins = raw.transpose(out=wbp[:, idx // 8, idx % 8, :],
                                                in_=Gx[:, m, x, :, :, :].rearrange("p q t c -> p (q t c)"),
                                                identity=g.ident[:])
                    pe.done(ins)
                if r == 0:
                    S.run(tensor=t1)
                else:
                    S.run(tensor=t1, sync=lambda sp, raw: sp.dma_start(out=g.ys1T[128 * (r - 1):128 * r, :],
                                                                      in_=ystage[:]))

                def t2(v, raw):
                    v.tensor_copy(out=CTr[:].rearrange("p q c -> p (q c)"), in_=ctp[:, 0, :])
                    v.tensor_copy(out=CTi[:].rearrange("p q c -> p (q c)"), in_=ctp[:, 1, :])
                    v.tensor_copy(out=CTrb[:], in_=CTr[:])
                    v.tensor_scalar(out=CTnib[:], in0=CTi[:], scalar1=-1.0, scalar2=None, op0=ALU.mult)
                    for bk in range(4):
                        v.tensor_copy(out=Wb[:, 4 * bk:4 * bk + 4, :, :].rearrange("p m x c -> p (m x) c"),
                                      in_=wbp[:, bk, :, :])
                    sh = [128, 4, 16, 16]
                    ear = bc(EAr[:, :, 1:17], sh, 3); eai = bc(EAi[:, :, 1:17], sh, 3)
                    ctr = bc(CTr[:], sh, 2); cti = bc(CTi[:], sh, 2)
                    v.tensor_tensor(out=Gr[:], in0=ear, in1=ctr, op=ALU.mult)
                    v.tensor_tensor(out=G1[:], in0=eai, in1=cti, op=ALU.mult)
                    v.tensor_tensor(out=Gr[:], in0=Gr[:], in1=G1[:], op=ALU.subtract)
                    v.tensor_tensor(out=Gi[:], in0=eai, in1=ctr, op=ALU.mult)
                    v.tensor_tensor(out=G1[:], in0=ear, in1=cti, op=ALU.mult)
                    v.tensor_tensor(out=Gi[:], in0=Gi[:], in1=G1[:], op=ALU.add)
                    v.tensor_scalar(out=Gi[:], in0=Gi[:], scalar1=-1.0, scalar2=None, op0=ALU.mult)
                    for x, Csrc in enumerate((Gr, Gi)):
                        for g2 in range(2):
                            v.tensor_scalar(out=Wc[:, :, :, x, g2, :], in0=Csrc[:], scalar1=pm[:, g2:g2 + 1],
                                            scalar2=None, op0=ALU.mult)
                S.run(vector=t2)

                def t3(pe, raw):
                    for lag in range(16):
                        for qq in range(4):
                            raw.matmul(kc[32 * qq:32 * qq + 32, lag, :],
                                       lhsT=Gx[:, lag, 0, qq, :, :].rearrange("p t c -> p (t c)"), rhs=CTrb[:, qq, :],
                                       start=True, stop=False, tile_position=(0, 32 * qq), skip_group_check=True)
                            ins = raw.matmul(kc[32 * qq:32 * qq + 32, lag, :],
                                             lhsT=Gx[:, lag, 1, qq, :, :].rearrange("p t c -> p (t c)"),
                                             rhs=CTnib[:, qq, :], start=False, stop=True,
                                             tile_position=(0, 32 * qq), skip_group_check=True)
                    pe.done(ins)
                S.run(tensor=t3)

                def t4(v, raw):
                    for gi in range(8):
                        v.tensor_scalar(out=Kst[:, :, gi, :], in0=kc[:, :, :], scalar1=bm[:, gi:gi + 1], scalar2=None,
                                        op0=ALU.mult)
                    k0 = Kst[:, 0, :, :].rearrange("p a c -> p (a c)")
                    v.scalar_tensor_tensor(out=k0, in0=identf[:], scalar=Dcol[:, 0:1], in1=k0, op0=ALU.mult, op1=ALU.add)
                    shk = [128, 4, NK]
                    v.tensor_tensor(out=kA[:], in0=bc(phr[:], shk, 2), in1=bc(kidx[:], shk, 1), op=ALU.mult)
                    v.tensor_copy(out=kIv, in_=kA[:])
                    v.tensor_copy(out=kB[:], in_=kIv)
                    v.tensor_tensor(out=kA[:], in0=kA[:], in1=kB[:], op=ALU.subtract)
                    v.tensor_scalar(out=kB[:], in0=kA[:], scalar1=-1.0, scalar2=None, op0=ALU.mult)
                    v.tensor_tensor(out=kB[:], in0=kA[:], in1=kB[:], op=ALU.max)
                S.run(vector=t4)

            def t5(a, raw):
                a.activation(out=sinT[:], in_=kA[:], func=AF.Sin, scale=TWO_PI)
                a.activation(out=cosT[:], in_=kB[:], func=AF.Sin, scale=-TWO_PI, bias=HALF_PI)
            S.run(scalar=t5, vector=lambda p, raw: p.tensor_copy(
                out=uP[:], in_=uN[:].rearrange("p (k i) -> p i k", i=16)))

            with nc.psum_tensor(f"ssm_bb{r}", [128, 4, 2, NK], F32) as bb:
                def m1(pe, raw):
                    for x in range(2):
                        for j in range(16):
                            for qq in range(4):
                                ins = raw.matmul(bb[:, qq, x, :], lhsT=Wb[32 * qq:32 * qq + 32, 15 - j, x, :],
                                                 rhs=uP[32 * qq:32 * qq + 32, j, :], start=(j == 0), stop=(j == 15),
                                                 tile_position=(32 * qq, 0))
                    pe.done(ins)
                S.run(tensor=m1)

                def m2(v, raw):
                    br = bb[:, :, 0, :]; bi = bb[:, :, 1, :]
                    cs = cosT[:, :, :]; sn = sinT[:, :, :]
                    v.tensor_tensor(out=w1[:], in0=br, in1=cs, op=ALU.mult)
                    v.tensor_tensor(out=w2[:], in0=bi, in1=sn, op=ALU.mult)
                    v.tensor_tensor(out=w1[:], in0=w1[:], in1=w2[:], op=ALU.add)
                    v.tensor_tensor(out=w2[:], in0=bi, in1=cs, op=ALU.mult)
                    v.tensor_tensor(out=w3[:], in0=br, in1=sn, op=ALU.mult)
                    v.tensor_tensor(out=w2[:], in0=w2[:], in1=w3[:], op=ALU.subtract)
                    for qq in range(4):
                        rb = rho[:, qq:qq + 1].to_broadcast([128, NK])
                        v.tensor_tensor_scan(out=w3[:, qq, :], data0=rb, data1=w1[:, qq, :], initial=0.0,
                                             op0=ALU.mult, op1=ALU.add)
                        v.tensor_tensor_scan(out=w4[:, qq, :], data0=rb, data1=w2[:, qq, :], initial=0.0,
                                             op0=ALU.mult, op1=ALU.add)
                    v.tensor_tensor(out=w1[:], in0=w3[:], in1=cs, op=ALU.mult)
                    v.tensor_tensor(out=w2[:], in0=w4[:], in1=sn, op=ALU.mult)
                    v.tensor_tensor(out=Sp[:, :, 0, 1:NK], in0=w1[:, :, 0:NK - 1], in1=w2[:, :, 0:NK - 1],
                                    op=ALU.subtract)
                    v.tensor_tensor(out=w1[:], in0=w3[:], in1=sn, op=ALU.mult)
                    v.tensor_tensor(out=w2[:], in0=w4[:], in1=cs, op=ALU.mult)
                    v.tensor_tensor(out=Sp[:, :, 1, 1:NK], in0=w1[:, :, 0:NK - 1], in1=w2[:, :, 0:NK - 1], op=ALU.add)
                if r + 1 < 4:
                    S.run(vector=m2, sync=make_ld(r + 1))
                else:
                    S.run(vector=m2)

            ys = yperm
            with nc.psum_tensor(f"ssm_yb{r}", [128, 8, NK], F32) as yb:
                def mk_f1(qtr):
                    def f1(pe, raw):
                        for ii in range(4):
                            i = qtr * 4 + ii
                            bank = (qtr % 2) * 4 + ii
                            for lag in range(i + 1):
                                raw.matmul(yb[:, bank, :], lhsT=Kst[:, lag, :, :].rearrange("p a c -> p (a c)"),
                                           rhs=uP[:, i - lag, :], start=(lag == 0), stop=False)
                            for qq in range(4):
                                for x in range(2):
                                    ins = raw.matmul(yb[32 * qq:32 * qq + 32, bank, :],
                                                     lhsT=Wc[:, qq, i, x, :, :].rearrange("p t c -> p (t c)"),
                                                     rhs=Sp[:, qq, x, :], start=False, stop=(x == 1),
                                                     tile_position=(0, 32 * qq), skip_group_check=True)
                        pe.done(ins)
                    return f1

                def mk_f2(qtr):
                    def f2(a, raw):
                        for ii in range(4):
                            i = qtr * 4 + ii
                            bank = (qtr % 2) * 4 + ii
                            a.activation(out=ys[:, i, :], in_=yb[:, bank, :], func=AF.Gelu_apprx_tanh)
                    return f2

                nxt = (r + 1 < 4)

                def both(f, g2):
                    def h(e, raw):
                        f(e, raw)
                        g2(e, raw)
                    return h
                S.run(tensor=mk_f1(0))
                S.run(tensor=mk_f1(1), scalar=both(mk_f2(0), dtf) if nxt else mk_f2(0))
                if nxt:
                    S.run(tensor=mk_f1(2), scalar=mk_f2(1), vector=a1)
                    S.run(tensor=mk_f1(3), scalar=both(mk_f2(2), a2))
                    S.run(scalar=mk_f2(3), vector=a3)
                else:
                    S.run(tensor=mk_f1(2), scalar=mk_f2(1))
                    S.run(tensor=mk_f1(3), scalar=mk_f2(2))
                    S.run(scalar=mk_f2(3))
            S.run(vector=lambda v, raw: v.tensor_copy(out=ystage[:].rearrange("p (k i) -> p k i", i=16),
                                                      in_=yperm[:].rearrange("p i k -> p k i")))
        S.run(sync=lambda sp, raw: sp.dma_start(out=g.ys1T[128 * 3:128 * 4, :], in_=ystage[:]))


def phase_final(g):
    nc = g.nc
    with contextlib.ExitStack() as es:
        def sb(name, shape, dt=F32, stack=es):
            return stack.enter_context(nc.sbuf_tensor("fin_" + name, list(shape), dt))
        wg, wa, wsr, wo, bg, gpost = g.wg, g.wa, g.wsr, g.wo, g.bg, g.gpost
        ys1 = sb("ys1", [128, 2, 4, CH], BF16); sgs = sb("sgs", [128, 2, 4, CH], BF16); ya = sb("ya", [128, 2, 4, CH], BF16)
        sma = sb("sma", [128, 2, 8, CH], BF16); sms = sb("sms", [128, 2, 8, CH], BF16)
        xin = sb("xin", [128, 2, 4, D])
        sg = sb("sg", [128, 4, CH], BF16); y3 = sb("y3", [128, 4, CH], BF16)
        ma = sb("ma", [128, 8, CH]); mg = sb("mg", [128, 8, CH], BF16)
        tS = sb("tS", [128, 2, CH])
        tF = sb("tF", [128, 2, D])
        ost = sb("ost", [128, 2, D])
        junk = sb("junk", [128, 2, D], BF16)
        ss2 = sb("ss2", [128, 2]); sd2 = sb("sd2", [128, 2]); rs2 = sb("rs2", [128, 2])
        ps = es.enter_context(nc.psum_tensor("fin_ps", [128, 4, CH], F32))
        po = es.enter_context(nc.psum_tensor("fin_po", [128, 2, 2, CH], F32))

        s_in = [nc.alloc_semaphore(f"p5_in{i}") for i in range(2)]
        s_evA = nc.alloc_semaphore("p5_evA")
        s_evD = nc.alloc_semaphore("p5_evD")
        s_dd = nc.alloc_semaphore("p5_dd")
        s_y3 = nc.alloc_semaphore("p5_y3")
        s_mm = nc.alloc_semaphore("p5_mm")
        s_o = nc.alloc_semaphore("p5_o")
        s_sq = nc.alloc_semaphore("p5_sq")
        s_rs = nc.alloc_semaphore("p5_rs")
        s_fin = nc.alloc_semaphore("p5_fin")
        s_st = [nc.alloc_semaphore(f"p5_st{i}") for i in range(2)]

        fmseq = [("z", 0, j) for j in range(4)]
        for c in range(NCH):
            fmseq += [("a", c, j) for j in range(8)] + [("s", c, j) for j in range(8)]
            if c + 1 < NCH:
                fmseq += [("z", c + 1, j) for j in range(4)]
        nidx = {k: n for n, k in enumerate(fmseq)}

        def ev_info(n):
            kind, c, j = fmseq[n]
            if kind == "z":
                return "A", c * 4 + j + 1
            return "D", c * 16 + (j if kind == "a" else 8 + j) + 1

        dd = [0]

        with nc.Block() as b:
            @b.sync
            def _(sp):
                def load(c):
                    sl = c % 2
                    tok = slice(c * CH, (c + 1) * CH)
                    if c >= 2:
                        sp.wait_ge(s_fin, 4 * (c - 1))
                    for dst, src in ((ys1, g.ys1T), (sgs, g.sgsT), (ya, g.yaT), (sma, g.smaT), (sms, g.smsT)):
                        sp.dma_start(out=dst[:, sl], in_=src[:, tok].rearrange("(j p) t -> p j t", p=128)
                                     ).then_inc(s_in[sl], 16)
                    sp.dma_start(out=xin[:, sl], in_=g.x[tok, :].rearrange("(t p) d -> p t d", p=128)
                                 ).then_inc(s_in[sl], 16)
                load(0)
                load(1)
                for c in range(NCH):
                    for tt in range(4):
                        ti = 4 * c + tt
                        sp.wait_ge(s_fin, ti + 1)
                        r0 = c * CH + tt * 128
                        sp.dma_start(out=g.out[r0:r0 + 128, :], in_=ost[:, ti % 2, :]).then_inc(s_st[ti % 2], 16)
                    if c + 2 < NCH:
                        load(c + 2)
                for i in range(2):
                    sp.wait_ge(s_st[i], 16 * (4 * NCH // 2))

            @b.tensor
            def _(pe):
                loaded = set()

                def fmblock(n):
                    kind, c, j = fmseq[n]
                    sl = c % 2
                    if c not in loaded:
                        pe.wait_ge(s_in[sl], 96 * (c // 2 + 1))
                        loaded.add(c)
                    if n >= 4:
                        e, cnt = ev_info(n - 4)
                        pe.wait_ge(s_evA if e == "A" else s_evD, cnt)
                    if kind == "s" and j == 0:
                        pe.wait_ge(s_y3, c + 1)
                    w, src = {"z": (wg, ys1[:, sl]), "a": (wa, ya[:, sl]), "s": (wsr, y3)}[kind]
                    for kk in range(4):
                        ins = pe.matmul(ps[:, n % 4, :], lhsT=w[:, kk, 128 * j:128 * j + 128], rhs=src[:, kk, :],
                                        start=(kk == 0), stop=(kk == 3))
                    ins.then_inc(s_mm, 1)

                def outproj(c):
                    pe.wait_ge(s_evD, 16 * (c + 1))
                    for tt in range(4):
                        ti = 4 * c + tt
                        if ti >= 2:
                            pe.wait_ge(s_fin, ti - 1)
                        for hf in range(2):
                            for kk in range(8):
                                ins = pe.matmul(po[:, ti % 2, hf, :], lhsT=mg[:, kk, 128 * tt:128 * tt + 128],
                                                rhs=wo[:, kk, 512 * hf:512 * hf + 512], start=(kk == 0), stop=(kk == 7))
                        ins.then_inc(s_o, 1)

                for n, (kind, c, j) in enumerate(fmseq):
                    fmblock(n)
                    last_of_chunk = (kind == "z" and j == 3 and c >= 1) or (kind == "s" and j == 7 and c == NCH - 1)
                    if last_of_chunk:
                        outproj(c - 1 if kind == "z" else c)

            @b.scalar
            def _(act):
                def sig(c):
                    for j in range(4):
                        n = nidx[("z", c, j)]
                        act.wait_ge(s_mm, n + 1)
                        if j == 0 and c >= 1:
                            act.wait_ge(s_y3, c)
                        act.activation(out=sg[:, j, :], in_=ps[:, n % 4, :], func=AF.Sigmoid,
                                       bias=bg[:, j:j + 1]).then_inc(s_evA, 1)

                def stats(c):
                    for tt in range(4):
                        ti = 4 * c + tt
                        act.wait_ge(s_o, ti + 1)
                        if ti >= 2:
                            act.wait_ge(s_fin, ti - 1)
                        act.activation(out=junk[:, ti % 2, :], in_=po[:, ti % 2, :, :].rearrange("p a c -> p (a c)"),
                                       func=AF.Square, accum_out=ss2[:, ti % 2:ti % 2 + 1]).then_inc(s_sq, 1)
                        act.wait_ge(s_sq, 2 * ti + 1)
                        act.activation(out=sd2[:, ti % 2:ti % 2 + 1], in_=ss2[:, ti % 2:ti % 2 + 1], func=AF.Ln,
                                       scale=1.0 / D, bias=EPS).then_inc(s_sq, 1)
                        act.wait_ge(s_sq, 2 * ti + 2)
                        act.activation(out=rs2[:, ti % 2:ti % 2 + 1], in_=sd2[:, ti % 2:ti % 2 + 1], func=AF.Exp,
                                       scale=-0.5).then_inc(s_rs, 1)

                sig(0)
                for c in range(NCH):
                    if c + 1 < NCH:
                        sig(c + 1)
                    stats(c)

            @b.vector
            def _(dve):
                def chain(ins):
                    ins.then_inc(s_dd, 1)
                    dd[0] += 1
                    dve.wait_ge(s_dd, dd[0])

                def y3f(c):
                    sl = c % 2
                    dve.wait_ge(s_in[sl], 96 * (c // 2 + 1))
                    dve.wait_ge(s_evA, 4 * (c + 1))
                    if c >= 1:
                        dve.wait_ge(s_mm, nidx[("s", c - 1, 7)] + 1)
                    chain(dve.tensor_tensor(out=sg[:], in0=sg[:], in1=ys1[:, sl], op=ALU.mult))
                    dve.tensor_tensor(out=y3[:], in0=sg[:], in1=sgs[:, sl], op=ALU.mult).then_inc(s_y3, 1)

                def evacs(c):
                    sl = c % 2
                    dve.wait_ge(s_in[sl], 96 * (c // 2 + 1))
                    for kind in ("a", "s"):
                        for j in range(8):
                            n = nidx[(kind, c, j)]
                            dve.wait_ge(s_mm, n + 1)
                            if kind == "a":
                                if j == 0 and c >= 1:
                                    dve.wait_ge(s_evD, 16 * c)
                                dve.tensor_tensor(out=ma[:, j, :], in0=ps[:, n % 4, :], in1=sma[:, sl, j, :],
                                                  op=ALU.mult).then_inc(s_evD, 1)
                            else:
                                if j == 0 and c >= 1:
                                    dve.wait_ge(s_o, 4 * c)
                                chain(dve.tensor_tensor(out=tS[:, j % 2, :], in0=ps[:, n % 4, :],
                                                        in1=sms[:, sl, j, :], op=ALU.mult))
                                dve.wait_ge(s_evD, 16 * c + j + 1)
                                dve.tensor_tensor(out=mg[:, j, :], in0=tS[:, j % 2, :], in1=ma[:, j, :],
                                                  op=ALU.add).then_inc(s_evD, 1)

                def fin(c, tt):
                    sl = c % 2
                    ti = 4 * c + tt
                    dve.wait_ge(s_rs, ti + 1)
                    if ti >= 2:
                        dve.wait_ge(s_st[ti % 2], 16 * (ti // 2))
                    chain(dve.scalar_tensor_tensor(out=tF[:, ti % 2, :],
                                                   in0=po[:, ti % 2, :, :].rearrange("p a c -> p (a c)"),
                                                   scalar=rs2[:, ti % 2:ti % 2 + 1], in1=gpost[:],
                                                   op0=ALU.mult, op1=ALU.mult))
                    dve.tensor_tensor(out=ost[:, ti % 2, :], in0=tF[:, ti % 2, :], in1=xin[:, sl, tt, :],
                                      op=ALU.add).then_inc(s_fin, 1)

                y3f(0)
                for c in range(NCH):
                    evacs(c)
                    fin(c, 0)
                    fin(c, 1)
                    if c + 1 < NCH:
                        y3f(c + 1)
                    fin(c, 2)
                    fin(c, 3)


def kernel(**inputs):
    nc = build()
    x = np.ascontiguousarray(inputs["x"], dtype=np.float32)
    shared = {}
    for k, v in inputs.items():
        if k == "x":
            continue
        a = np.ascontiguousarray(v, dtype=np.float32)
        shared[k] = a.reshape(_SHAPES[k])
    in_maps = []
    for c in range(NCORES):
        m = dict(shared)
        m["x"] = x[c]
        in_maps.append(m)
    res = run_bass_kernel_spmd(nc, in_maps, core_ids=list(range(NCORES)))
    return np.stack([r["out"] for r in res.results], axis=0).astype(np.float32)


_SHAPES = {
    "norm_pre": (1, D), "w_in": (D, INC), "b_forget": (1, H), "lam_re": (32, 64), "lam_im": (32, 64),
    "log_dt": (1, 32), "b_re": (32, 64, 16), "b_im": (32, 64, 16), "c_re": (32, 16, 64),
    "c_im": (32, 16, 64), "d_skip": (32, 16), "w_glu": (512, 512), "b_glu": (1, 512),
    "w_branch_a": (512, D), "w_branch_s": (512, D), "w_out": (D, D), "norm_post": (1, D),
}
```

```python
import contextlib
import math
import numpy as np
import concourse.bass as bass
import concourse.mybir as mybir
from concourse.bass_utils import run_bass_kernel_spmd

F32 = mybir.dt.float32
BF16 = mybir.dt.bfloat16
I32 = mybir.dt.int32
AF = mybir.ActivationFunctionType
ALU = mybir.AluOpType

D = 1024
SEQ = 8192
NCORES = 8
INC = 5640
USED = 5128
H = 8
HD = 64
EPS = 1e-6
CH = 512
NCH = SEQ // CH


class Ctx:
    pass


def build(last_phase=99, debug=False):
    nc = bass.Bass("TRN2", target_bir_lowering=False)
    es = contextlib.ExitStack()
    g = Ctx()
    g.nc = nc
    g.debug = debug

    def din(name, shape):
        return nc.dram_tensor(name, list(shape), F32, kind="ExternalInput").ap()

    g.x = din("x", [SEQ, D])
    g.norm_pre = din("norm_pre", [1, D])
    g.w_in = din("w_in", [D, INC])
    g.b_forget = din("b_forget", [1, H])
    g.lam_re = din("lam_re", [32, 64])
    g.lam_im = din("lam_im", [32, 64])
    g.log_dt = din("log_dt", [1, 32])
    g.b_re = din("b_re", [32, 64, 16])
    g.b_im = din("b_im", [32, 64, 16])
    g.c_re = din("c_re", [32, 16, 64])
    g.c_im = din("c_im", [32, 16, 64])
    g.d_skip = din("d_skip", [32, 16])
    g.w_glu = din("w_glu", [512, 512])
    g.b_glu = din("b_glu", [1, 512])
    g.w_branch_a = din("w_branch_a", [512, D])
    g.w_branch_s = din("w_branch_s", [512, D])
    g.w_out = din("w_out", [D, D])
    g.norm_post = din("norm_post", [1, D])
    g.out = nc.dram_tensor("out", [SEQ, D], F32, kind="ExternalOutput").ap()

    def scratch(name, shape, dt=BF16):
        if debug:
            return nc.dram_tensor(name, list(shape), dt, kind="ExternalOutput").ap()
        return nc.dram_tensor(name, list(shape), dt).ap()

    g.qT = scratch("qT", [512, SEQ])
    g.kT = scratch("kT", [512, SEQ])
    g.vtok = scratch("vtok", [SEQ, 512])
    g.fT = scratch("fT", [8, SEQ], F32)
    g.sgaT = scratch("sgaT", [512, SEQ])
    g.uT = scratch("uT", [512, SEQ])
    g.sgsT = scratch("sgsT", [512, SEQ])
    g.smaT = scratch("smaT", [D, SEQ])
    g.smsT = scratch("smsT", [D, SEQ])

    g.ident = es.enter_context(nc.sbuf_tensor("ident", [128, 128], BF16))
    g.ones_f = es.enter_context(nc.sbuf_tensor("ones_f", [128, 128], F32))
    S0 = Steps(nc, "init")

    def i0(p, raw):
        p.memset(g.ones_f[:], 1.0)
        p.affine_select(out=g.ident[:], in_=g.ones_f[:], pattern=[[-1, 128]],
                        compare_op=ALU.is_equal, fill=0.0, base=0, channel_multiplier=1)
    S0.run(gpsimd=i0)

    g.crk = scratch("crk", [3, H, SEQ])
    g.crq = scratch("crq", [3, H, SEQ])
    g.yaT = scratch("yaT", [512, SEQ])
    g.zeros_b = es.enter_context(nc.sbuf_tensor("zeros_b", [128, 128], BF16))
    g.maskT = es.enter_context(nc.sbuf_tensor("maskT", [128, 128], BF16))
    def i1(p, raw):
        p.memset(g.zeros_b[:], 0.0)
        p.affine_select(out=g.maskT[:], in_=g.zeros_b[:], pattern=[[1, 128]],
                        compare_op=ALU.is_ge, fill=-65536.0, base=0, channel_multiplier=-1)
    S0.run(gpsimd=i1)

    if last_phase >= 1:
        phase_inproj(g)
    if last_phase >= 2:
        phase_forget(g)
    g.wg = es.enter_context(nc.sbuf_tensor("fin_wg", [128, 4, 512], BF16))
    g.wa = es.enter_context(nc.sbuf_tensor("fin_wa", [128, 4, D], BF16))
    g.wsr = es.enter_context(nc.sbuf_tensor("fin_wsr", [128, 4, D], BF16))
    g.wo = es.enter_context(nc.sbuf_tensor("fin_wo", [128, 8, D], BF16))
    g.bg = es.enter_context(nc.sbuf_tensor("fin_bg", [128, 4], F32))
    g.gpost = es.enter_context(nc.sbuf_tensor("fin_gpost", [128, D], F32))
    if last_phase >= 3:
        phase_attn(g)
    g.ys1T = scratch("ys1T", [512, SEQ])
    if last_phase >= 4:
        phase_ssm(g)
    if last_phase >= 5:
        phase_final(g)
    es.close()
    return nc


def phase_inproj(g):
    nc = g.nc
    with contextlib.ExitStack() as es:
        wsb = es.enter_context(nc.sbuf_tensor("wsb", [128, 8, USED], BF16))
        gain = es.enter_context(nc.sbuf_tensor("gain", [128, 8], F32))
        with contextlib.ExitStack() as es0:
            wtmp = es0.enter_context(nc.sbuf_tensor("wtmp", [128, 2, USED], F32))
            s_w = [nc.alloc_semaphore(f"p0_w{i}") for i in range(2)]
            s_g = nc.alloc_semaphore("p0_g")
            s_done = [nc.alloc_semaphore(f"p0_d{i}") for i in range(3)]
            cuts = [0, 1536, 4864, USED]
            with nc.Block() as b:
                @b.sync
                def _(sp):
                    sp.dma_start(out=gain[:], in_=g.norm_pre.rearrange("o (k p) -> p (o k)", p=128),
                                 allow_slow_non_contiguous=True).then_inc(s_g, 16)
                    for dk in range(8):
                        if dk >= 2:
                            for e in range(3):
                                sp.wait_ge(s_done[e], dk - 1)
                        sp.dma_start(out=wtmp[:, dk % 2, :],
                                     in_=g.w_in[dk * 128:(dk + 1) * 128, 0:USED]).then_inc(s_w[dk % 2], 16)

                def conv(eng, e, kind):
                    eng.wait_ge(s_g, 16)
                    for dk in range(8):
                        eng.wait_ge(s_w[dk % 2], 16 * (dk // 2 + 1))
                        c0, c1 = cuts[e], cuts[e + 1]
                        if kind == "act":
                            eng.activation(out=wsb[:, dk, c0:c1], in_=wtmp[:, dk % 2, c0:c1],
                                           func=AF.Copy, scale=gain[:, dk:dk + 1]).then_inc(s_done[e], 1)
                        else:
                            eng.tensor_scalar(out=wsb[:, dk, c0:c1], in0=wtmp[:, dk % 2, c0:c1],
                                              scalar1=gain[:, dk:dk + 1], scalar2=None,
                                              op0=ALU.mult).then_inc(s_done[e], 1)

                @b.vector
                def _(e):
                    conv(e, 0, "dve")

                @b.scalar
                def _(e):
                    conv(e, 1, "act")

                @b.gpsimd
                def _(e):
                    conv(e, 2, "pool")

        xs = es.enter_context(nc.sbuf_tensor("xs", [128, 2, 4, D], F32))
        hn = es.enter_context(nc.sbuf_tensor("hn", [128, 2, 4, D], BF16))
        hT = es.enter_context(nc.sbuf_tensor("hT", [128, 2, 8, CH], BF16))
        junk = es.enter_context(nc.sbuf_tensor("junk", [128, 4, D], BF16))
        ss = es.enter_context(nc.sbuf_tensor("ss", [128, 2, 4], F32))
        sd = es.enter_context(nc.sbuf_tensor("sd", [128, 2, 4], F32))
        rstd = es.enter_context(nc.sbuf_tensor("rstd", [128, 2, 4], F32))
        NS = 6
        stage = es.enter_context(nc.sbuf_tensor("stage", [128, NS, CH], BF16))
        stagef = es.enter_context(nc.sbuf_tensor("stagef", [8, 2, CH], F32))
        ps = es.enter_context(nc.psum_tensor("ps", [128, 4, CH], F32))
        tp = es.enter_context(nc.psum_tensor("tp", [128, 2, 1024], BF16))

        blks = []

        def add(kind, col0, m, func, eng, dest, row0):
            blks.append(dict(kind=kind, col0=col0, m=m, func=func, eng=eng, dest=dest, row0=row0))

        for j in range(4):
            add("fm", 0 + 128 * j, 128, AF.Copy, "dve", g.qT, 128 * j)
        for j in range(4):
            add("fm", 512 + 128 * j, 128, AF.Copy, "dve", g.kT, 128 * j)
        for j in range(4):
            add("fm", 2056 + 128 * j, 128, AF.Copy, "dve", g.uT, 128 * j)
        for tt in range(4):
            add("v", 1024, 128, AF.Copy, "dve", g.vtok, tt)
        add("f", 1536, 8, AF.Copy, "dve", g.fT, 0)
        for j in range(4):
            add("fm", 1544 + 128 * j, 128, AF.Silu, "act", g.sgaT, 128 * j)
        for j in range(4):
            add("fm", 2568 + 128 * j, 128, AF.Silu, "act", g.sgsT, 128 * j)
        for j in range(8):
            add("fm", 3080 + 128 * j, 128, AF.Sigmoid, "act", g.smaT, 128 * j)
        for j in range(8):
            add("fm", 4104 + 128 * j, 128, AF.Sigmoid, "act", g.smsT, 128 * j)
        NB = len(blks)
        seq = []
        cnt = {"act": 0, "dve": 0}
        nstage = 0
        for c in range(NCH):
            for j, bk in enumerate(blks):
                cnt[bk["eng"]] += 1
                d = dict(bk)
                d.update(c=c, n=len(seq), eidx=cnt[bk["eng"]])
                if bk["kind"] == "f":
                    d["slot"] = None
                else:
                    d["slot"] = nstage % NS
                    d["suse"] = nstage // NS
                    nstage += 1
                seq.append(d)

        s_x = [nc.alloc_semaphore(f"p1_x{i}") for i in range(2)]
        s_sd = nc.alloc_semaphore("p1_sd")
        s_ss = nc.alloc_semaphore("p1_ss")
        s_ln = nc.alloc_semaphore("p1_ln")
        s_rstd = nc.alloc_semaphore("p1_rstd")
        s_hn = nc.alloc_semaphore("p1_hn")
        s_tp = nc.alloc_semaphore("p1_tp")
        s_hT = nc.alloc_semaphore("p1_hT")
        s_mm = nc.alloc_semaphore("p1_mm")
        s_ev = {"act": nc.alloc_semaphore("p1_eva"), "dve": nc.alloc_semaphore("p1_evd")}
        s_out = [nc.alloc_semaphore(f"p1_o{i}") for i in range(NS)]
        s_outf = [nc.alloc_semaphore(f"p1_of{i}") for i in range(2)]

        def stage_ap(d):
            if d["kind"] == "f":
                return stagef[0:8, d["c"] % 2, :]
            return stage[:, d["slot"], :]

        def dest_ap(d):
            c = d["c"]
            if d["kind"] == "v":
                r0 = c * CH + d["row0"] * 128
                return d["dest"][r0:r0 + 128, :]
            if d["kind"] == "f":
                return d["dest"][0:8, c * CH:(c + 1) * CH]
            return d["dest"][d["row0"]:d["row0"] + 128, c * CH:(c + 1) * CH]

        with nc.Block() as b:
            @b.sync
            def _(sp):
                def load(c):
                    if c >= 2:
                        sp.wait_ge(s_hn, c - 1)
                    sp.dma_start(out=xs[:, c % 2, :, :],
                                 in_=g.x[c * CH:(c + 1) * CH, :].rearrange("(t p) d -> p t d", p=128)
                                 ).then_inc(s_x[c % 2], 16)
                load(0)
                load(1)
                for c in range(NCH):
                    if c + 2 < NCH:
                        load(c + 2)
                    for d in seq[c * NB:(c + 1) * NB]:
                        sp.wait_ge(s_ev[d["eng"]], d["eidx"])
                        so = s_outf[c % 2] if d["kind"] == "f" else s_out[d["slot"]]
                        sp.dma_start(out=dest_ap(d), in_=stage_ap(d)).then_inc(so, 16)
                for i in range(NS):
                    uses = len([d for d in seq if d["slot"] == i])
                    sp.wait_ge(s_out[i], 16 * uses)
                for i in range(2):
                    sp.wait_ge(s_outf[i], 16 * (NCH // 2))

            def evac(eng, d, is_act):
                eng.wait_ge(s_mm, d["n"] + 1)
                if d["kind"] == "f":
                    if d["c"] >= 2:
                        eng.wait_ge(s_outf[d["c"] % 2], 16 * (d["c"] // 2))
                    src = ps[0:8, d["n"] % 4, :]
                else:
                    if d["suse"] >= 1:
                        eng.wait_ge(s_out[d["slot"]], 16 * d["suse"])
                    src = ps[:, d["n"] % 4, :]
                if is_act:
                    eng.activation(out=stage_ap(d), in_=src, func=d["func"]).then_inc(s_ev["act"], 1)
                else:
                    eng.tensor_copy(out=stage_ap(d), in_=src).then_inc(s_ev["dve"], 1)

            @b.scalar
            def _(act):
                def stats(c):
                    sl = c % 2
                    act.wait_ge(s_x[sl], 16 * (c // 2 + 1))
                    for tt in range(4):
                        act.activation(out=junk[:, tt, :], in_=xs[:, sl, tt, :], func=AF.Square,
                                       accum_out=ss[:, sl, tt:tt + 1]).then_inc(s_ss, 1)
                    act.wait_ge(s_ss, 4 * (c + 1))
                    act.activation(out=ss[:, sl, :], in_=ss[:, sl, :], func=AF.Ln,
                                   scale=1.0 / D, bias=EPS).then_inc(s_ln, 1)
                    act.wait_ge(s_ln, c + 1)
                    act.activation(out=sd[:, sl, :], in_=ss[:, sl, :], func=AF.Exp,
                                   scale=-0.5).then_inc(s_sd, 1)
                    act.wait_ge(s_sd, c + 1)
                    if c >= 2:
                        act.wait_ge(s_tp, 8 * (c - 1))
                    for tt in range(4):
                        ins = act.activation(out=hn[:, sl, tt, :], in_=xs[:, sl, tt, :], func=AF.Copy,
                                             scale=sd[:, sl, tt:tt + 1])
                    ins.then_inc(s_hn, 1)
                stats(0)
                stats(1)
                for c in range(NCH):
                    if c + 2 < NCH:
                        stats(c + 2)
                    for d in seq[c * NB:(c + 1) * NB]:
                        if d["eng"] == "act":
                            evac(act, d, True)

            @b.vector
            def _(dve):
                def pro(c):
                    sl = c % 2
                    if c >= 2:
                        dve.wait_ge(s_mm, (c - 1) * NB)
                    for dk in range(8):
                        dve.wait_ge(s_tp, c * 8 + dk + 1)
                        dve.tensor_copy(out=hT[:, sl, dk, :], in_=tp[:, dk % 2, 0:CH]).then_inc(s_hT, 1)
                pro(0)
                for c in range(NCH):
                    if c + 1 < NCH:
                        pro(c + 1)
                    for d in seq[c * NB:(c + 1) * NB]:
                        if d["eng"] == "dve":
                            evac(dve, d, False)


            @b.tensor
            def _(pe):
                def trans(c):
                    sl = c % 2
                    pe.wait_ge(s_hn, c + 1)
                    for dk in range(8):
                        gi = c * 8 + dk
                        if gi >= 2:
                            pe.wait_ge(s_hT, gi - 1)
                        for tt in range(4):
                            ins = pe.transpose(out=tp[:, dk % 2, tt * 128:(tt + 1) * 128],
                                               in_=hn[:, sl, tt, dk * 128:(dk + 1) * 128], identity=g.ident[:])
                        ins.then_inc(s_tp, 1)
                trans(0)
                for c in range(NCH):
                    sl = c % 2
                    if c + 1 < NCH:
                        trans(c + 1)
                    pe.wait_ge(s_hT, 8 * (c + 1))
                    for d in seq[c * NB:(c + 1) * NB]:
                        n = d["n"]
                        if n >= 4:
                            pd = seq[n - 4]
                            pe.wait_ge(s_ev[pd["eng"]], pd["eidx"])
                        for dk in range(8):
                            if d["kind"] == "v":
                                tt = d["row0"]
                                ins = pe.matmul(ps[:, n % 4, :], lhsT=hT[:, sl, dk, tt * 128:(tt + 1) * 128],
                                                rhs=wsb[:, dk, 1024:1536], start=(dk == 0), stop=(dk == 7))
                            else:
                                m = d["m"]
                                ins = pe.matmul(ps[0:m, n % 4, :], lhsT=wsb[:, dk, d["col0"]:d["col0"] + m],
                                                rhs=hT[:, sl, dk, :], start=(dk == 0), stop=(dk == 7))
                        ins.then_inc(s_mm, 1)


def phase_forget(g):
    nc = g.nc
    SEG = 16
    SL = SEQ // SEG
    with contextlib.ExitStack() as es:
        ft = es.enter_context(nc.sbuf_tensor("ft", [128, SL], F32))
        t1 = es.enter_context(nc.sbuf_tensor("fg_t1", [128, SL], F32))
        t2 = es.enter_context(nc.sbuf_tensor("fg_t2", [128, SL], F32))
        ones = es.enter_context(nc.sbuf_tensor("fg_ones", [128, SL], F32))
        rows = es.enter_context(nc.sbuf_tensor("fg_rows", [128, 6, SL], BF16))
        bfn = es.enter_context(nc.sbuf_tensor("bfn", [128, 1], F32))
        M = es.enter_context(nc.sbuf_tensor("fg_M", [128, 128], F32))
        tot = es.enter_context(nc.sbuf_tensor("fg_tot", [128, 2], F32))
        off = es.enter_context(nc.sbuf_tensor("fg_off", [128, 2], F32))
        offp = es.enter_context(nc.psum_tensor("fg_offp", [128, 2], F32))
        S = Steps(nc, "p2")

        def ld(sp, raw):
            L = [raw.dma_start(out=ft[:], in_=g.fT.rearrange("h (s t) -> (h s) t", s=SEG))]
            for h in range(H):
                L.append(raw.dma_start(out=bfn[SEG * h:SEG * (h + 1), :],
                                       in_=bass.AP(g.b_forget.tensor, h, [[0, SEG], [1, 1]]),
                                       allow_slow_non_contiguous=True))
            sp.many(L, 16)

        def mk(p, raw):
            p.affine_select(out=M[:], in_=g.ones_f[:], pattern=[[1, 128]], compare_op=ALU.is_gt, fill=0.0,
                            base=0, channel_multiplier=-1)
            m3 = M[:].rearrange("p (h s) -> p h s", s=SEG)
            p.affine_select(out=m3, in_=m3, pattern=[[-SEG, H], [0, SEG]], compare_op=ALU.is_ge, fill=0.0,
                            base=0, channel_multiplier=1)
            p.affine_select(out=m3, in_=m3, pattern=[[SEG, H], [0, SEG]], compare_op=ALU.is_ge, fill=0.0,
                            base=SEG - 1, channel_multiplier=-1)
            p.memset(ones[:], 1.0)
        S.run(sync=ld, gpsimd=mk)
        S.run(vector=lambda v, raw: v.tensor_scalar(out=bfn[:], in0=bfn[:], scalar1=-1.0, scalar2=None, op0=ALU.mult))

        def a(act, raw):
            act.activation(out=t1[:], in_=ft[:], func=AF.Exp, scale=-1.0, bias=bfn[:, 0:1])
            act.activation(out=t2[:], in_=t1[:], func=AF.Ln, bias=1.0, scale=1.0)
        S.run(scalar=a)

        def d1(dve, raw):
            dve.tensor_tensor_scan(out=t1[:], data0=ones[:], data1=t2[:], initial=0.0, op0=ALU.mult, op1=ALU.add)
            dve.tensor_copy(out=tot[:, 0:1], in_=t1[:, SL - 1:SL])
            dve.tensor_copy(out=tot[:, 1:2], in_=t1[:, SL - 1:SL])
        S.run(vector=d1)
        S.run(tensor=lambda pe, raw: pe.matmul(offp[:, :], lhsT=M[:], rhs=tot[:], start=True, stop=True))

        def d2(dve, raw):
            dve.tensor_copy(out=off[:], in_=offp[:, :])
            dve.tensor_scalar(out=t1[:], in0=t1[:], scalar1=off[:, 0:1], scalar2=8.0, op0=ALU.add, op1=ALU.mult)
            dve.tensor_copy(out=rows[:, 0, :], in_=t1[:])
            dve.tensor_tensor(out=t2[:], in0=t1[:], in1=rows[:, 0, :], op=ALU.subtract)
            dve.tensor_copy(out=rows[:, 1, :], in_=t2[:])
            dve.tensor_tensor(out=t1[:], in0=t2[:], in1=rows[:, 1, :], op=ALU.subtract)
            dve.tensor_copy(out=rows[:, 2, :], in_=t1[:])
            dve.tensor_scalar(out=rows[:, 3:6, :], in0=rows[:, 0:3, :], scalar1=-1.0, scalar2=None, op0=ALU.mult)
        S.run(vector=d2)

        def st(sp, raw):
            L = []
            for j in range(3):
                L.append(raw.dma_start(out=g.crk[j].rearrange("h (s t) -> (h s) t", s=SEG), in_=rows[:, j, :]))
                L.append(raw.dma_start(out=g.crq[j].rearrange("h (s t) -> (h s) t", s=SEG), in_=rows[:, 3 + j, :]))
            sp.many(L, 16)
        S.run(sync=st)


def phase_attn(g):
    nc = g.nc
    NQ = SEQ // CH
    NKT = SEQ // 128
    with contextlib.ExitStack() as es:
        kTa = es.enter_context(nc.sbuf_tensor("kTa", [70, 2, SEQ], BF16))
        qTa = es.enter_context(nc.sbuf_tensor("qTa", [70, 2, SEQ], BF16))
        vsb = es.enter_context(nc.sbuf_tensor("vsb", [128, 2, NKT, 128], BF16))
        sga = es.enter_context(nc.sbuf_tensor("sga", [64, 2, SEQ], BF16))
        pT = es.enter_context(nc.sbuf_tensor("pT", [128, 3, 3, CH], BF16))
        rl = es.enter_context(nc.sbuf_tensor("rl", [64, CH], F32))
        yt = es.enter_context(nc.sbuf_tensor("yt", [64, CH], F32))
        ystage = es.enter_context(nc.sbuf_tensor("ystage", [64, 2, CH], BF16))
        sp_ps = es.enter_context(nc.psum_tensor("sp_ps", [128, 2, 3, CH], F32))
        o_ps = es.enter_context(nc.psum_tensor("o_ps", [128, 2, CH], F32))
        s_ms = nc.alloc_semaphore("p3_ms")
        s_ms2 = nc.alloc_semaphore("p3_ms2")
        s_pw = nc.alloc_semaphore("p3_pw")
        s_pc = nc.alloc_semaphore("p3_pc")
        s_pd = [nc.alloc_semaphore(f"p3_pd{i}") for i in range(2)]
        wtmpP = es.enter_context(nc.sbuf_tensor("wtmpP", [128, 2, D], F32))
        s_dv = nc.alloc_semaphore("p3_dv")
        s_ld = [nc.alloc_semaphore(f"p3_ld{i}") for i in range(2)]
        s_S = nc.alloc_semaphore("p3_S")
        s_exp = nc.alloc_semaphore("p3_exp")
        s_pv = nc.alloc_semaphore("p3_pv")
        s_fin = nc.alloc_semaphore("p3_fin")
        s_yo = [nc.alloc_semaphore(f"p3_yo{i}") for i in range(2)]

        GMAX = 3
        groups = []
        qlast = {}
        for h in range(H):
            for Q in range(NQ):
                nk = 4 * Q + 4
                full = [(kt, kt >= 4 * Q) for kt in range(4 * Q + 1)]
                cur = []
                glist = []
                for t in full:
                    cur.append(t)
                    if len(cur) == GMAX:
                        glist.append((0, cur)); cur = []
                if cur:
                    glist.append((0, cur))
                for kt in range(4 * Q + 1, nk):
                    glist.append(((kt - 4 * Q) * 128, [(kt, True)]))
                for gi_, (n0, tl) in enumerate(glist):
                    groups.append(dict(h=h, Q=Q, n0=n0, tiles=tl, first=(gi_ == 0), last=(gi_ == len(glist) - 1),
                                       i=len(groups)))
                qlast[(h, Q)] = len(groups) - 1
        head_first = {h: min(t["i"] for t in groups if t["h"] == h) for h in range(H)}
        head_last = {h: max(t["i"] for t in groups if t["h"] == h) for h in range(H)}
        NLD = 20

        with nc.Block() as b:
            @b.gpsimd
            def _(p):
                for k, ap in enumerate((kTa[64:70, 0, :], kTa[64:70, 1, :], vsb[:, 0, :, 64:128])):
                    p.memset(ap, 1.0).then_inc(s_ms, 1)
                    p.wait_ge(s_ms, k + 1)
                p.dma_start(out=g.bg[:], in_=g.b_glu.rearrange("o (j p) -> p (o j)", p=128),
                            allow_slow_non_contiguous=True).then_inc(s_pw, 16)
                p.dma_start(out=g.gpost[:], in_=bass.AP(g.norm_post.tensor, 0, [[0, 128], [1, D]])).then_inc(s_pw, 16)
                p.wait_ge(s_pw, 32)
                jobs = []
                for (src, dst, nk, nco) in ((g.w_glu, g.wg, 4, 512), (g.w_branch_a, g.wa, 4, D),
                                            (g.w_branch_s, g.wsr, 4, D), (g.w_out, g.wo, 8, D)):
                    for k in range(nk):
                        jobs.append((src[128 * k:128 * (k + 1), :], dst[:, k, :], nco))

                def pdma(i):
                    srcap, _, nco = jobs[i]
                    p.dma_start(out=wtmpP[:, i % 2, 0:nco], in_=srcap).then_inc(s_pd[i % 2], 16)
                pdma(0)
                pdma(1)
                for i, (srcap, dstap, nco) in enumerate(jobs):
                    p.wait_ge(s_pd[i % 2], 16 * (i // 2 + 1))
                    p.tensor_copy(out=dstap, in_=wtmpP[:, i % 2, 0:nco]).then_inc(s_pc, 1)
                    p.wait_ge(s_pc, i + 1)
                    if i + 2 < len(jobs):
                        pdma(i + 2)

            @b.sync
            def _(sp):
                def pieces(h):
                    sl = h % 2
                    P = []
                    QW = SEQ // 4
                    for c4 in range(4):
                        cs_ = slice(c4 * QW, (c4 + 1) * QW)
                        P.append(lambda cs_=cs_: sp.dma_start(out=kTa[0:64, sl, cs_], in_=g.kT[h * 64:(h + 1) * 64, cs_]
                                                              ).then_inc(s_ld[sl], 16))
                        P.append(lambda cs_=cs_: sp.dma_start(out=qTa[0:64, sl, cs_], in_=g.qT[h * 64:(h + 1) * 64, cs_]
                                                              ).then_inc(s_ld[sl], 16))
                    vsrc = g.vtok[:, h * 64:(h + 1) * 64].rearrange("(kt p) d -> p kt d", p=128)
                    for part in range(8):
                        P.append(lambda part=part: sp.dma_start(out=vsb[:, sl, part * 8:(part + 1) * 8, 0:64],
                                                                in_=vsrc[:, part * 8:(part + 1) * 8, :]
                                                                ).then_inc(s_ld[sl], 16))
                    for c2 in range(2):
                        cs_ = slice(c2 * (SEQ // 2), (c2 + 1) * (SEQ // 2))
                        P.append(lambda cs_=cs_: sp.dma_start(out=sga[:, sl, cs_], in_=g.sgaT[h * 64:(h + 1) * 64, cs_]
                                                              ).then_inc(s_ld[sl], 16))
                    C = [lambda: sp.dma_start(out=kTa[67:70, sl, :], in_=g.crk[:, h, :]).then_inc(s_ld[sl], 16),
                         lambda: sp.dma_start(out=qTa[64:67, sl, :], in_=g.crq[:, h, :]).then_inc(s_ld[sl], 16)]
                    return P, C

                for h0 in range(2):
                    P, C = pieces(h0)
                    for f in P:
                        f()
                    if h0 == 0:
                        sp.wait_ge(s_ms, 3)
                        sp.wait_ge(s_ms2, 3)
                    for f in C:
                        f()
                for h in range(H):
                    nxt = h + 1
                    pend = []
                    if 2 <= nxt < H:
                        P, C = pieces(nxt)
                        pend = P + C
                        sp.wait_ge(s_pv, head_last[nxt - 2] + 1)
                        sp.wait_ge(s_fin, NQ * (nxt - 1))
                    for Q in range(NQ):
                        qi = h * NQ + Q
                        sp.wait_ge(s_fin, qi + 1)
                        sp.dma_start(out=g.yaT[h * 64:(h + 1) * 64, Q * CH:(Q + 1) * CH],
                                     in_=ystage[:, qi % 2, :]).then_inc(s_yo[qi % 2], 16)
                        for _ in range(2):
                            if pend:
                                pend.pop(0)()
                    while pend:
                        pend.pop(0)()
                for i in range(2):
                    sp.wait_ge(s_yo[i], 16 * (H * NQ // 2))

            @b.tensor
            def _(pe):
                def S(t):
                    i = t["i"]
                    sl = t["h"] % 2
                    if i == head_first[t["h"]]:
                        pe.wait_ge(s_ld[sl], 16 * NLD * (t["h"] // 2 + 1))
                    if i >= 2:
                        pe.wait_ge(s_exp, i - 1)
                    n0 = t["n0"]
                    q0 = t["Q"] * CH
                    for j, (kt, diag) in enumerate(t["tiles"]):
                        ins = pe.matmul(sp_ps[:, i % 2, j, n0:CH], lhsT=kTa[0:70, sl, kt * 128:(kt + 1) * 128],
                                        rhs=qTa[0:70, sl, q0 + n0:q0 + CH], start=True, stop=not diag)
                        if diag:
                            ins = pe.matmul(sp_ps[:, i % 2, j, n0:n0 + 128], lhsT=g.ident[:], rhs=g.maskT[:],
                                            start=False, stop=True)
                    ins.then_inc(s_S, 1)

                def PV(t):
                    i = t["i"]
                    sl = t["h"] % 2
                    qi = t["h"] * NQ + t["Q"]
                    pe.wait_ge(s_exp, i + 1)
                    if t["first"] and qi >= 2:
                        pe.wait_ge(s_fin, qi - 1)
                    n0 = t["n0"]
                    nt = len(t["tiles"])
                    for j, (kt, diag) in enumerate(t["tiles"]):
                        ins = pe.matmul(o_ps[:, qi % 2, n0:CH], lhsT=vsb[:, sl, kt, :], rhs=pT[:, i % 3, j, n0:CH],
                                        start=(t["first"] and j == 0), stop=(t["last"] and j == nt - 1))
                    ins.then_inc(s_pv, 1)

                n = len(groups)
                S(groups[0])
                S(groups[1])
                for i in range(n):
                    if i + 2 < n:
                        S(groups[i + 2])
                    PV(groups[i])

            @b.scalar
            def _(act):
                for t in groups:
                    i = t["i"]
                    act.wait_ge(s_S, i + 1)
                    if i >= 3:
                        act.wait_ge(s_pv, i - 2)
                    n0 = t["n0"]
                    nt = len(t["tiles"])
                    act.activation(out=pT[:, i % 3, 0:nt, n0:CH], in_=sp_ps[:, i % 2, 0:nt, n0:CH], func=AF.Exp,
                                   scale=0.125).then_inc(s_exp, 1)

            @b.vector
            def _(dve):
                for k, ap in enumerate((qTa[64:70, 0, :], qTa[64:70, 1, :], vsb[:, 1, :, 64:128])):
                    dve.memset(ap, 1.0).then_inc(s_ms2, 1)
                    dve.wait_ge(s_ms2, k + 1)
                for h in range(H):
                    sl = h % 2
                    dve.wait_ge(s_ld[sl], 16 * NLD * (h // 2 + 1))
                    for Q in range(NQ):
                        qi = h * NQ + Q
                        dve.wait_ge(s_pv, qlast[(h, Q)] + 1)
                        if qi >= 1:
                            dve.wait_ge(s_fin, qi)
                        if qi >= 2:
                            dve.wait_ge(s_yo[qi % 2], 16 * (qi // 2))
                        dve.reciprocal(out=rl[:], in_=o_ps[64:128, qi % 2, :]).then_inc(s_dv, 1)
                        dve.wait_ge(s_dv, 2 * qi + 1)
                        dve.tensor_tensor(out=yt[:], in0=o_ps[0:64, qi % 2, :], in1=rl[:], op=ALU.mult).then_inc(s_dv, 1)
                        dve.wait_ge(s_dv, 2 * qi + 2)
                        dve.tensor_tensor(out=ystage[:, qi % 2, :], in0=yt[:], in1=sga[:, sl, Q * CH:(Q + 1) * CH],
                                          op=ALU.mult).then_inc(s_fin, 1)


class Ser:
    def __init__(self, eng, st):
        self.e = eng
        self.st = st

    def done(self, ins, inc=1):
        ins.then_inc(self.st["sem"], inc)
        self.st["n"] += inc
        self.e.wait_ge(self.st["sem"], self.st["n"])
        return ins

    def many(self, instrs, inc=1):
        for ins in instrs:
            ins.then_inc(self.st["sem"], inc)
            self.st["n"] += inc
        self.e.wait_ge(self.st["sem"], self.st["n"])

    def __getattr__(self, name):
        f = getattr(self.e, name)
        inc = 16 if name == "dma_start" else 1

        def w(*a, **k):
            return self.done(f(*a, **k), inc)
        return w


class Steps:
    def __init__(self, nc, tag):
        self.nc = nc
        self.st = {e: {"sem": nc.alloc_semaphore(f"{tag}_{e}"), "n": 0}
                   for e in ("sync", "vector", "scalar", "gpsimd", "tensor")}

    def run(self, **fns):
        with self.nc.Block() as b:
            for ename, fn in fns.items():
                def mk(fn, ename):
                    def body(e):
                        fn(Ser(e, self.st[ename]), e)
                    return body
                getattr(b, ename)(mk(fn, ename))


TWO_PI = 6.28318
HALF_PI = 1.5707963


def phase_ssm(g):
    nc = g.nc
    NK = SEQ // 16
    with contextlib.ExitStack() as es:
        def sb(name, shape, dt=F32):
            return es.enter_context(nc.sbuf_tensor("ssm_" + name, list(shape), dt))
        S = Steps(nc, "p4")
        identf = sb("identf", [128, 128])
        pm = sb("pm", [128, 2])
        bm = sb("bm", [128, 8])
        kidx_i = sb("kidx_i", [128, NK], I32)
        kidx = sb("kidx", [128, NK])
        midx = sb("midx", [128, 17])
        LR = sb("LR", [128, 4]); LI = sb("LI", [128, 4]); LDT = sb("LDT", [128, 4])
        BR = sb("BR", [128, 4, 16]); BI = sb("BI", [128, 4, 16])
        CNr = sb("CNr", [64, 128]); CNi = sb("CNi", [64, 128])
        Dcol = sb("Dcol", [128, 1])
        dt_ = sb("dt", [128, 4]); lrdt = sb("lrdt", [128, 4]); lidt = sb("lidt", [128, 4]); phi = sb("phi", [128, 4])
        tA = sb("tA", [128, 4, 17]); tB = sb("tB", [128, 4, 17]); tC = sb("tC", [128, 4, 17]); tI = sb("tI", [128, 4, 17], I32)
        EAr = sb("EAr", [128, 4, 17]); EAi = sb("EAi", [128, 4, 17])
        s1 = sb("s1", [128, 4]); s2 = sb("s2", [128, 4]); s3 = sb("s3", [128, 4]); s4 = sb("s4", [128, 4])
        fr = sb("fr", [128, 4]); fi = sb("fi", [128, 4]); sI = sb("sI", [128, 4], I32)
        Bbr = sb("Bbr", [128, 4, 16]); Bbi = sb("Bbi", [128, 4, 16]); b1 = sb("b1", [128, 4, 16])
        Gr = sb("Gr", [128, 4, 16, 16]); Gi = sb("Gi", [128, 4, 16, 16]); G1 = sb("G1", [128, 4, 16, 16])
        Gx = sb("Gx", [128, 16, 2, 4, 2, 16], BF16)
        CTr = sb("CTr", [128, 4, 16]); CTi = sb("CTi", [128, 4, 16])
        CTrb = sb("CTrb", [128, 4, 16], BF16); CTnib = sb("CTnib", [128, 4, 16], BF16)
        Wc = sb("Wc", [128, 4, 16, 2, 2, 16], BF16)
        Wb = sb("Wb", [128, 16, 2, 128], BF16)
        Kst = sb("Kst", [128, 16, 8, 16], BF16)
        cosT = sb("cosT", [128, 4, NK], BF16); sinT = sb("sinT", [128, 4, NK], BF16)
        rho = sb("rho", [128, 4]); phr = sb("phr", [128, 4])
        uN = sb("uN", [128, SEQ], BF16); uP = sb("uP", [128, 16, NK], BF16)
        Sp = sb("Sp", [128, 4, 2, NK], BF16)
        w1 = sb("w1", [128, 4, NK]); w2 = sb("w2", [128, 4, NK]); w3 = sb("w3", [128, 4, NK]); w4 = sb("w4", [128, 4, NK])
        kA = w1; kB = w2; kIv = w3[:].bitcast(I32)
        ystage = sb("ystage", [128, SEQ], BF16)
        yperm = sb("yperm", [128, 16, NK], BF16)

        def bc(ap, shape, axis):
            return ap.unsqueeze(axis).to_broadcast(list(shape))

        def c0(p, raw):
            p.memset(identf[:], 0.0)
            p.affine_select(out=identf[:], in_=g.ones_f[:], pattern=[[-1, 128]], compare_op=ALU.is_equal,
                            fill=0.0, base=0, channel_multiplier=1)
            p.affine_select(out=pm[:], in_=g.ones_f[:, 0:2], pattern=[[-64, 2]], compare_op=ALU.is_ge,
                            fill=0.0, base=0, channel_multiplier=1)
            p.affine_select(out=pm[:], in_=pm[:], pattern=[[64, 2]], compare_op=ALU.is_ge,
                            fill=0.0, base=63, channel_multiplier=-1)
            p.affine_select(out=bm[:], in_=g.ones_f[:, 0:8], pattern=[[-16, 8]], compare_op=ALU.is_ge,
                            fill=0.0, base=0, channel_multiplier=1)
            p.affine_select(out=bm[:], in_=bm[:], pattern=[[16, 8]], compare_op=ALU.is_ge,
                            fill=0.0, base=15, channel_multiplier=-1)
            p.iota(kidx_i[:], pattern=[[1, NK]], base=0, channel_multiplier=0)
            p.memset(Sp[:], 0.0)
        S.run(gpsimd=c0)

        def c1(v, raw):
            v.tensor_copy(out=kidx[:], in_=kidx_i[:])
            v.tensor_copy(out=midx[:], in_=kidx[:, 0:17])
        S.run(vector=c1)

        lamr = g.lam_re.rearrange("(q t) p -> (t p) q", t=2)
        lami = g.lam_im.rearrange("(q t) p -> (t p) q", t=2)
        bre = g.b_re.rearrange("(q t) p c -> (t p) q c", t=2)
        bim = g.b_im.rearrange("(q t) p c -> (t p) q c", t=2)

        for r in range(4):
            def ld(sp, raw, r=r):
                L = []
                L.append(raw.dma_start(out=LR[:], in_=lamr[:, 4 * r:4 * r + 4], allow_slow_non_contiguous=True))
                L.append(raw.dma_start(out=LI[:], in_=lami[:, 4 * r:4 * r + 4], allow_slow_non_contiguous=True))
                for t in range(2):
                    L.append(raw.dma_start(out=LDT[64 * t:64 * t + 64, :],
                                           in_=bass.AP(g.log_dt.tensor, t + 8 * r, [[0, 64], [2, 4]]),
                                           allow_slow_non_contiguous=True))
                L.append(raw.dma_start(out=BR[:], in_=bre[:, 4 * r:4 * r + 4, :]))
                L.append(raw.dma_start(out=BI[:], in_=bim[:, 4 * r:4 * r + 4, :]))
                for qq in range(4):
                    q = 4 * r + qq
                    L.append(raw.dma_start(out=CNr[16 * qq:16 * qq + 16, :].rearrange("c (t p) -> c t p", t=2),
                                           in_=g.c_re[2 * q:2 * q + 2, :, :].rearrange("t c p -> c t p")))
                    L.append(raw.dma_start(out=CNi[16 * qq:16 * qq + 16, :].rearrange("c (t p) -> c t p", t=2),
                                           in_=g.c_im[2 * q:2 * q + 2, :, :].rearrange("t c p -> c t p")))
                L.append(raw.dma_start(out=Dcol[:],
                                       in_=g.d_skip[8 * r:8 * r + 8, :].rearrange("g (c o) -> (g c) o", o=1)))
                L.append(raw.dma_start(out=uN[:], in_=g.uT[128 * r:128 * r + 128, :]))
                sp.many(L, 16)
            def make_ld(rr):
                return lambda sp, raw: ld(sp, raw, rr)
            if r == 0:
                S.run(sync=ld)

            def dtf(a, raw):
                a.activation(out=dt_[:], in_=LDT[:], func=AF.Exp)

            def tbl(v, raw):
                shk = [128, 4, NK]
                v.tensor_tensor(out=kA[:], in0=bc(phr[:], shk, 2), in1=bc(kidx[:], shk, 1), op=ALU.mult)
                v.tensor_copy(out=kIv, in_=kA[:])
                v.tensor_copy(out=kB[:], in_=kIv)
                v.tensor_tensor(out=kA[:], in0=kA[:], in1=kB[:], op=ALU.subtract)
                v.tensor_scalar(out=kB[:], in0=kA[:], scalar1=-1.0, scalar2=None, op0=ALU.mult)
                v.tensor_tensor(out=kB[:], in0=kA[:], in1=kB[:], op=ALU.max)

            def t5(a, raw):
                a.activation(out=sinT[:], in_=kA[:], func=AF.Sin, scale=TWO_PI)
                a.activation(out=cosT[:], in_=kB[:], func=AF.Sin, scale=-TWO_PI, bias=HALF_PI)
            if r == 0:
                S.run(scalar=dtf)

            def a1(v, raw):
                v.tensor_tensor(out=lrdt[:], in0=LR[:], in1=dt_[:], op=ALU.mult)
                v.tensor_tensor(out=lidt[:], in0=LI[:], in1=dt_[:], op=ALU.mult)
                v.tensor_scalar(out=phi[:], in0=lidt[:], scalar1=1.0 / (2 * math.pi), scalar2=None, op0=ALU.mult)
                v.tensor_tensor(out=tA[:], in0=bc(phi[:], [128, 4, 17], 2), in1=bc(midx[:], [128, 4, 17], 1), op=ALU.mult)
                v.tensor_tensor(out=tB[:], in0=bc(lrdt[:], [128, 4, 17], 2), in1=bc(midx[:], [128, 4, 17], 1), op=ALU.mult)
                v.tensor_copy(out=tI[:], in_=tA[:])
                v.tensor_copy(out=tC[:], in_=tI[:])
                v.tensor_tensor(out=tA[:], in0=tA[:], in1=tC[:], op=ALU.subtract)
                v.tensor_scalar(out=tC[:], in0=tA[:], scalar1=-1.0, scalar2=None, op0=ALU.mult)
                v.tensor_tensor(out=tC[:], in0=tA[:], in1=tC[:], op=ALU.max)
                v.tensor_scalar(out=s1[:], in0=phi[:], scalar1=16.0, scalar2=None, op0=ALU.mult)
                v.tensor_copy(out=sI[:], in_=s1[:])
                v.tensor_copy(out=s2[:], in_=sI[:])
                v.tensor_tensor(out=phr[:], in0=s1[:], in1=s2[:], op=ALU.subtract)
            if r == 0:
                S.run(vector=a1)

            def a2(a, raw):
                a.activation(out=EAi[:], in_=tA[:], func=AF.Sin, scale=TWO_PI)
                a.activation(out=EAr[:], in_=tC[:], func=AF.Sin, scale=-TWO_PI, bias=HALF_PI)
                a.activation(out=tB[:], in_=tB[:], func=AF.Exp)
                a.activation(out=rho[:], in_=lrdt[:], func=AF.Exp, scale=16.0)
            if r == 0:
                S.run(scalar=a2)

            def a3(v, raw):
                v.tensor_tensor(out=EAr[:], in0=EAr[:], in1=tB[:], op=ALU.mult)
                v.tensor_tensor(out=EAi[:], in0=EAi[:], in1=tB[:], op=ALU.mult)
                v.tensor_scalar(out=s1[:], in0=EAr[:, :, 1], scalar1=-1.0, scalar2=None, op0=ALU.add)
                v.tensor_tensor(out=s2[:], in0=LR[:], in1=LR[:], op=ALU.mult)
                v.tensor_tensor(out=s3[:], in0=LI[:], in1=LI[:], op=ALU.mult)
                v.tensor_tensor(out=s2[:], in0=s2[:], in1=s3[:], op=ALU.add)
                v.reciprocal(out=s2[:], in_=s2[:])
                v.tensor_tensor(out=s3[:], in0=s1[:], in1=LR[:], op=ALU.mult)
                v.tensor_tensor(out=s4[:], in0=EAi[:, :, 1], in1=LI[:], op=ALU.mult)
                v.tensor_tensor(out=s3[:], in0=s3[:], in1=s4[:], op=ALU.add)
                v.tensor_tensor(out=fr[:], in0=s3[:], in1=s2[:], op=ALU.mult)
                v.tensor_tensor(out=s3[:], in0=EAi[:, :, 1], in1=LR[:], op=ALU.mult)
                v.tensor_tensor(out=s4[:], in0=s1[:], in1=LI[:], op=ALU.mult)
                v.tensor_tensor(out=s3[:], in0=s3[:], in1=s4[:], op=ALU.subtract)
                v.tensor_tensor(out=fi[:], in0=s3[:], in1=s2[:], op=ALU.mult)
                frb = bc(fr[:], [128, 4, 16], 2); fib = bc(fi[:], [128, 4, 16], 2)
                v.tensor_tensor(out=Bbr[:], in0=BR[:], in1=frb, op=ALU.mult)
                v.tensor_tensor(out=b1[:], in0=BI[:], in1=fib, op=ALU.mult)
                v.tensor_tensor(out=Bbr[:], in0=Bbr[:], in1=b1[:], op=ALU.subtract)
                v.tensor_tensor(out=Bbi[:], in0=BI[:], in1=frb, op=ALU.mult)
                v.tensor_tensor(out=b1[:], in0=BR[:], in1=fib, op=ALU.mult)
                v.tensor_tensor(out=Bbi[:], in0=Bbi[:], in1=b1[:], op=ALU.add)
                sh = [128, 4, 16, 16]
                ear = bc(EAr[:, :, 0:16], sh, 3); eai = bc(EAi[:, :, 0:16], sh, 3)
                bbr = bc(Bbr[:], sh, 2); bbi = bc(Bbi[:], sh, 2)
                v.tensor_tensor(out=Gr[:], in0=ear, in1=bbr, op=ALU.mult)
                v.tensor_tensor(out=G1[:], in0=eai, in1=bbi, op=ALU.mult)
                v.tensor_tensor(out=Gr[:], in0=Gr[:], in1=G1[:], op=ALU.subtract)
                v.tensor_tensor(out=Gi[:], in0=ear, in1=bbi, op=ALU.mult)
                v.tensor_tensor(out=G1[:], in0=eai, in1=bbr, op=ALU.mult)
                v.tensor_tensor(out=Gi[:], in0=Gi[:], in1=G1[:], op=ALU.add)
                for x, Gsrc in enumerate((Gr, Gi)):
                    for g2 in range(2):
                        v.tensor_scalar(out=Gx[:, :, x, :, g2, :].rearrange("p m q c -> p q m c"), in0=Gsrc[:],
                                        scalar1=pm[:, g2:g2 + 1], scalar2=None, op0=ALU.mult)
            if r == 0:
                S.run(vector=a3)

            with nc.psum_tensor(f"ssm_ctp{r}", [128, 2, 64], F32) as ctp, \
                    nc.psum_tensor(f"ssm_wbp{r}", [128, 4, 8, 128], BF16) as wbp, \
                    nc.psum_tensor(f"ssm_kc{r}", [128, 16, 16], F32) as kc:
                def t1(pe, raw):
                    raw.transpose(out=ctp[:, 0, :], in_=CNr[:, :], identity=identf[0:64, 0:64])
                    raw.transpose(out=ctp[:, 1, :], in_=CNi[:, :], identity=identf[0:64, 0:64])
                    for m in range(16):
                        for x in range(2):
                            idx = m * 2 + x
                            ins = raw.transpose(out=wbp[:, idx // 8, idx % 8, :],
                                                in_=Gx[:, m, x, :, :, :].rearrange("p q t c -> p (q t c)"),
                                                identity=g.ident[:])
                    pe.done(ins)
                if r == 0:
                    S.run(tensor=t1)
                else:
                    S.run(tensor=t1, sync=lambda sp, raw: sp.dma_start(out=g.ys1T[128 * (r - 1):128 * r, :],
                                                                      in_=ystage[:]))

                def t2(v, raw):
                    v.tensor_copy(out=CTr[:].rearrange("p q c -> p (q c)"), in_=ctp[:, 0, :])
                    v.tensor_copy(out=CTi[:].rearrange("p q c -> p (q c)"), in_=ctp[:, 1, :])
                    v.tensor_copy(out=CTrb[:], in_=CTr[:])
                    v.tensor_scalar(out=CTnib[:], in0=CTi[:], scalar1=-1.0, scalar2=None, op0=ALU.mult)
                    for bk in range(4):
                        v.tensor_copy(out=Wb[:, 4 * bk:4 * bk + 4, :, :].rearrange("p m x c -> p (m x) c"),
                                      in_=wbp[:, bk, :, :])
                    sh = [128, 4, 16, 16]
                    ear = bc(EAr[:, :, 1:17], sh, 3); eai = bc(EAi[:, :, 1:17], sh, 3)
                    ctr = bc(CTr[:], sh, 2); cti = bc(CTi[:], sh, 2)
                    v.tensor_tensor(out=Gr[:], in0=ear, in1=ctr, op=ALU.mult)
                    v.tensor_tensor(out=G1[:], in0=eai, in1=cti, op=ALU.mult)
                    v.tensor_tensor(out=Gr[:], in0=Gr[:], in1=G1[:], op=ALU.subtract)
                    v.tensor_tensor(out=Gi[:], in0=eai, in1=ctr, op=ALU.mult)
                    v.tensor_tensor(out=G1[:], in0=ear, in1=cti, op=ALU.mult)
                    v.tensor_tensor(out=Gi[:], in0=Gi[:], in1=G1[:], op=ALU.add)
                    v.tensor_scalar(out=Gi[:], in0=Gi[:], scalar1=-1.0, scalar2=None, op0=ALU.mult)
                    for x, Csrc in enumerate((Gr, Gi)):
                        for g2 in range(2):
                            v.tensor_scalar(out=Wc[:, :, :, x, g2, :], in0=Csrc[:], scalar1=pm[:, g2:g2 + 1],
                                            scalar2=None, op0=ALU.mult)
                S.run(vector=t2)

                def t3(pe, raw):
                    for lag in range(16):
                        for qq in range(4):
                            raw.matmul(kc[32 * qq:32 * qq + 32, lag, :],
                                       lhsT=Gx[:, lag, 0, qq, :, :].rearrange("p t c -> p (t c)"), rhs=CTrb[:, qq, :],
                                       start=True, stop=False, tile_position=(0, 32 * qq), skip_group_check=True)
                            ins = raw.matmul(kc[32 * qq:32 * qq + 32, lag, :],
                                             lhsT=Gx[:, lag, 1, qq, :, :].rearrange("p t c -> p (t c)"),
                                             rhs=CTnib[:, qq, :], start=False, stop=True,
                                             tile_position=(0, 32 * qq), skip_group_check=True)
                    pe.done(ins)
                S.run(tensor=t3)

                def t4(v, raw):
                    for gi in range(8):
                        v.tensor_scalar(out=Kst[:, :, gi, :], in0=kc[:, :, :], scalar1=bm[:, gi:gi + 1], scalar2=None,
                                        op0=ALU.mult)
                    k0 = Kst[:, 0, :, :].rearrange("p a c -> p (a c)")
                    v.scalar_tensor_tensor(out=k0, in0=identf[:], scalar=Dcol[:, 0:1], in1=k0, op0=ALU.mult, op1=ALU.add)
                    if r == 0:
                        tbl(v, raw)
                S.run(vector=t4)

            if r == 0:
                S.run(scalar=t5, vector=lambda p, raw: p.tensor_copy(
                    out=uP[:], in_=uN[:].rearrange("p (k i) -> p i k", i=16)))
            else:
                S.run(vector=lambda p, raw: p.tensor_copy(
                    out=uP[:], in_=uN[:].rearrange("p (k i) -> p i k", i=16)))

            with nc.psum_tensor(f"ssm_bb{r}", [128, 4, 2, NK], F32) as bb:
                def m1(pe, raw):
                    for x in range(2):
                        for j in range(16):
                            for qq in range(4):
                                ins = raw.matmul(bb[:, qq, x, :], lhsT=Wb[32 * qq:32 * qq + 32, 15 - j, x, :],
                                                 rhs=uP[32 * qq:32 * qq + 32, j, :], start=(j == 0), stop=(j == 15),
                                                 tile_position=(32 * qq, 0))
                    pe.done(ins)
                S.run(tensor=m1)

                def m2(v, raw):
                    br = bb[:, :, 0, :]; bi = bb[:, :, 1, :]
                    cs = cosT[:, :, :]; sn = sinT[:, :, :]
                    v.tensor_tensor(out=w1[:], in0=br, in1=cs, op=ALU.mult)
                    v.tensor_tensor(out=w2[:], in0=bi, in1=sn, op=ALU.mult)
                    v.tensor_tensor(out=w1[:], in0=w1[:], in1=w2[:], op=ALU.add)
                    v.tensor_tensor(out=w2[:], in0=bi, in1=cs, op=ALU.mult)
                    v.tensor_tensor(out=w3[:], in0=br, in1=sn, op=ALU.mult)
                    v.tensor_tensor(out=w2[:], in0=w2[:], in1=w3[:], op=ALU.subtract)
                    for qq in range(4):
                        rb = rho[:, qq:qq + 1].to_broadcast([128, NK])
                        v.tensor_tensor_scan(out=w3[:, qq, :], data0=rb, data1=w1[:, qq, :], initial=0.0,
                                             op0=ALU.mult, op1=ALU.add)
                        v.tensor_tensor_scan(out=w4[:, qq, :], data0=rb, data1=w2[:, qq, :], initial=0.0,
                                             op0=ALU.mult, op1=ALU.add)
                    v.tensor_tensor(out=w1[:], in0=w3[:], in1=cs, op=ALU.mult)
                    v.tensor_tensor(out=w2[:], in0=w4[:], in1=sn, op=ALU.mult)
                    v.tensor_tensor(out=Sp[:, :, 0, 1:NK], in0=w1[:, :, 0:NK - 1], in1=w2[:, :, 0:NK - 1],
                                    op=ALU.subtract)
                    v.tensor_tensor(out=w1[:], in0=w3[:], in1=sn, op=ALU.mult)
                    v.tensor_tensor(out=w2[:], in0=w4[:], in1=cs, op=ALU.mult)
                    v.tensor_tensor(out=Sp[:, :, 1, 1:NK], in0=w1[:, :, 0:NK - 1], in1=w2[:, :, 0:NK - 1], op=ALU.add)
                if r + 1 < 4:
                    S.run(vector=m2, sync=make_ld(r + 1))
                else:
                    S.run(vector=m2)

            ys = yperm
            with nc.psum_tensor(f"ssm_yb{r}", [128, 8, NK], F32) as yb:
                def mk_f1(qtr):
                    def f1(pe, raw):
                        for ii in range(4):
                            i = qtr * 4 + ii
                            bank = (qtr % 2) * 4 + ii
                            for lag in range(i + 1):
                                raw.matmul(yb[:, bank, :], lhsT=Kst[:, lag, :, :].rearrange("p a c -> p (a c)"),
                                           rhs=uP[:, i - lag, :], start=(lag == 0), stop=False)
                            for qq in range(4):
                                for x in range(2):
                                    ins = raw.matmul(yb[32 * qq:32 * qq + 32, bank, :],
                                                     lhsT=Wc[:, qq, i, x, :, :].rearrange("p t c -> p (t c)"),
                                                     rhs=Sp[:, qq, x, :], start=False, stop=(x == 1),
                                                     tile_position=(0, 32 * qq), skip_group_check=True)
                        pe.done(ins)
                    return f1

                def mk_f2(qtr):
                    def f2(a, raw):
                        for ii in range(4):
                            i = qtr * 4 + ii
                            bank = (qtr % 2) * 4 + ii
                            a.activation(out=ys[:, i, :], in_=yb[:, bank, :], func=AF.Gelu_apprx_tanh)
                    return f2

                nxt = (r + 1 < 4)

                def both(f, g2):
                    def h(e, raw):
                        f(e, raw)
                        g2(e, raw)
                    return h
                S.run(tensor=mk_f1(0))
                S.run(tensor=mk_f1(1), scalar=both(mk_f2(0), dtf) if nxt else mk_f2(0))
                if nxt:
                    S.run(tensor=mk_f1(2), scalar=mk_f2(1), vector=both(a1, tbl))
                    S.run(tensor=mk_f1(3), scalar=both(mk_f2(2), both(a2, t5)))
                    S.run(scalar=mk_f2(3), vector=a3)
                else:
                    S.run(tensor=mk_f1(2), scalar=mk_f2(1))
                    S.run(tensor=mk_f1(3), scalar=mk_f2(2))
                    S.run(scalar=mk_f2(3))
            S.run(vector=lambda v, raw: v.tensor_copy(out=ystage[:].rearrange("p (k i) -> p k i", i=16),
                                                      in_=yperm[:].rearrange("p i k -> p k i")))
        S.run(sync=lambda sp, raw: sp.dma_start(out=g.ys1T[128 * 3:128 * 4, :], in_=ystage[:]))


def phase_final(g):
    nc = g.nc
    with contextlib.ExitStack() as es:
        def sb(name, shape, dt=F32, stack=es):
            return stack.enter_context(nc.sbuf_tensor("fin_" + name, list(shape), dt))
        wg, wa, wsr, wo, bg, gpost = g.wg, g.wa, g.wsr, g.wo, g.bg, g.gpost
        ys1 = sb("ys1", [128, 2, 4, CH], BF16); sgs = sb("sgs", [128, 2, 4, CH], BF16); ya = sb("ya", [128, 2, 4, CH], BF16)
        sma = sb("sma", [128, 2, 8, CH], BF16); sms = sb("sms", [128, 2, 8, CH], BF16)
        xin = sb("xin", [128, 2, 4, D])
        sg = sb("sg", [128, 4, CH], BF16); y3 = sb("y3", [128, 4, CH], BF16)
        ma = sb("ma", [128, 8, CH]); mg = sb("mg", [128, 8, CH], BF16)
        tS = sb("tS", [128, 2, CH])
        tF = sb("tF", [128, 2, D])
        ost = sb("ost", [128, 2, D])
        junk = sb("junk", [128, 2, D], BF16)
        ss2 = sb("ss2", [128, 2]); sd2 = sb("sd2", [128, 2]); rs2 = sb("rs2", [128, 2])
        ps = es.enter_context(nc.psum_tensor("fin_ps", [128, 4, CH], F32))
        po = es.enter_context(nc.psum_tensor("fin_po", [128, 2, 2, CH], F32))

        s_in = [nc.alloc_semaphore(f"p5_in{i}") for i in range(2)]
        s_evA = nc.alloc_semaphore("p5_evA")
        s_evD = nc.alloc_semaphore("p5_evD")
        s_dd = nc.alloc_semaphore("p5_dd")
        s_y3 = nc.alloc_semaphore("p5_y3")
        s_mm = nc.alloc_semaphore("p5_mm")
        s_o = nc.alloc_semaphore("p5_o")
        s_sq = nc.alloc_semaphore("p5_sq")
        s_rs = nc.alloc_semaphore("p5_rs")
        s_fin = nc.alloc_semaphore("p5_fin")
        s_st = [nc.alloc_semaphore(f"p5_st{i}") for i in range(2)]

        fmseq = [("z", 0, j) for j in range(4)]
        for c in range(NCH):
            fmseq += [("a", c, j) for j in range(8)] + [("s", c, j) for j in range(8)]
            if c + 1 < NCH:
                fmseq += [("z", c + 1, j) for j in range(4)]
        nidx = {k: n for n, k in enumerate(fmseq)}

        def ev_info(n):
            kind, c, j = fmseq[n]
            if kind == "z":
                return "A", c * 4 + j + 1
            return "D", c * 16 + (j if kind == "a" else 8 + j) + 1

        dd = [0]

        with nc.Block() as b:
            @b.sync
            def _(sp):
                def load(c):
                    sl = c % 2
                    tok = slice(c * CH, (c + 1) * CH)
                    if c >= 2:
                        sp.wait_ge(s_fin, 4 * (c - 1))
                    for dst, src in ((ys1, g.ys1T), (sgs, g.sgsT), (ya, g.yaT), (sma, g.smaT), (sms, g.smsT)):
                        sp.dma_start(out=dst[:, sl], in_=src[:, tok].rearrange("(j p) t -> p j t", p=128)
                                     ).then_inc(s_in[sl], 16)
                    sp.dma_start(out=xin[:, sl], in_=g.x[tok, :].rearrange("(t p) d -> p t d", p=128)
                                 ).then_inc(s_in[sl], 16)
                load(0)
                load(1)
                for c in range(NCH):
                    for tt in range(4):
                        ti = 4 * c + tt
                        sp.wait_ge(s_fin, ti + 1)
                        r0 = c * CH + tt * 128
                        sp.dma_start(out=g.out[r0:r0 + 128, :], in_=ost[:, ti % 2, :]).then_inc(s_st[ti % 2], 16)
                    if c + 2 < NCH:
                        load(c + 2)
                for i in range(2):
                    sp.wait_ge(s_st[i], 16 * (4 * NCH // 2))

            @b.tensor
            def _(pe):
                loaded = set()

                def fmblock(n):
                    kind, c, j = fmseq[n]
                    sl = c % 2
                    if c not in loaded:
                        pe.wait_ge(s_in[sl], 96 * (c // 2 + 1))
                        loaded.add(c)
                    if n >= 4:
                        e, cnt = ev_info(n - 4)
                        pe.wait_ge(s_evA if e == "A" else s_evD, cnt)
                    if kind == "s" and j == 0:
                        pe.wait_ge(s_y3, c + 1)
                    w, src = {"z": (wg, ys1[:, sl]), "a": (wa, ya[:, sl]), "s": (wsr, y3)}[kind]
                    for kk in range(4):
                        ins = pe.matmul(ps[:, n % 4, :], lhsT=w[:, kk, 128 * j:128 * j + 128], rhs=src[:, kk, :],
                                        start=(kk == 0), stop=(kk == 3))
                    ins.then_inc(s_mm, 1)

                def outproj(c):
                    pe.wait_ge(s_evD, 16 * (c + 1))
                    for tt in range(4):
                        ti = 4 * c + tt
                        if ti >= 2:
                            pe.wait_ge(s_fin, ti - 1)
                        for hf in range(2):
                            for kk in range(8):
                                ins = pe.matmul(po[:, ti % 2, hf, :], lhsT=mg[:, kk, 128 * tt:128 * tt + 128],
                                                rhs=wo[:, kk, 512 * hf:512 * hf + 512], start=(kk == 0), stop=(kk == 7))
                        ins.then_inc(s_o, 1)

                for n, (kind, c, j) in enumerate(fmseq):
                    fmblock(n)
                    last_of_chunk = (kind == "z" and j == 3 and c >= 1) or (kind == "s" and j == 7 and c == NCH - 1)
                    if last_of_chunk:
                        outproj(c - 1 if kind == "z" else c)

            @b.scalar
            def _(act):
                def sig(c):
                    for j in range(4):
                        n = nidx[("z", c, j)]
                        act.wait_ge(s_mm, n + 1)
                        if j == 0 and c >= 1:
                            act.wait_ge(s_y3, c)
                        act.activation(out=sg[:, j, :], in_=ps[:, n % 4, :], func=AF.Sigmoid,
                                       bias=bg[:, j:j + 1]).then_inc(s_evA, 1)

                def stats(c):
                    for tt in range(4):
                        ti = 4 * c + tt
                        act.wait_ge(s_o, ti + 1)
                        if ti >= 2:
                            act.wait_ge(s_fin, ti - 1)
                        act.activation(out=junk[:, ti % 2, :], in_=po[:, ti % 2, :, :].rearrange("p a c -> p (a c)"),
                                       func=AF.Square, accum_out=ss2[:, ti % 2:ti % 2 + 1]).then_inc(s_sq, 1)
                        act.wait_ge(s_sq, 2 * ti + 1)
                        act.activation(out=sd2[:, ti % 2:ti % 2 + 1], in_=ss2[:, ti % 2:ti % 2 + 1], func=AF.Ln,
                                       scale=1.0 / D, bias=EPS).then_inc(s_sq, 1)
                        act.wait_ge(s_sq, 2 * ti + 2)
                        act.activation(out=rs2[:, ti % 2:ti % 2 + 1], in_=sd2[:, ti % 2:ti % 2 + 1], func=AF.Exp,
                                       scale=-0.5).then_inc(s_rs, 1)

                sig(0)
                for c in range(NCH):
                    if c + 1 < NCH:
                        sig(c + 1)
                    stats(c)

            @b.vector
            def _(dve):
                def chain(ins):
                    ins.then_inc(s_dd, 1)
                    dd[0] += 1
                    dve.wait_ge(s_dd, dd[0])

                def y3f(c):
                    sl = c % 2
                    dve.wait_ge(s_in[sl], 96 * (c // 2 + 1))
                    dve.wait_ge(s_evA, 4 * (c + 1))
                    if c >= 1:
                        dve.wait_ge(s_mm, nidx[("s", c - 1, 7)] + 1)
                    chain(dve.tensor_tensor(out=sg[:], in0=sg[:], in1=ys1[:, sl], op=ALU.mult))
                    dve.tensor_tensor(out=y3[:], in0=sg[:], in1=sgs[:, sl], op=ALU.mult).then_inc(s_y3, 1)

                def evacs(c):
                    sl = c % 2
                    dve.wait_ge(s_in[sl], 96 * (c // 2 + 1))
                    for kind in ("a", "s"):
                        for j in range(8):
                            n = nidx[(kind, c, j)]
                            dve.wait_ge(s_mm, n + 1)
                            if kind == "a":
                                if j == 0 and c >= 1:
                                    dve.wait_ge(s_evD, 16 * c)
                                dve.tensor_tensor(out=ma[:, j, :], in0=ps[:, n % 4, :], in1=sma[:, sl, j, :],
                                                  op=ALU.mult).then_inc(s_evD, 1)
                            else:
                                if j == 0 and c >= 1:
                                    dve.wait_ge(s_o, 4 * c)
                                chain(dve.tensor_tensor(out=tS[:, j % 2, :], in0=ps[:, n % 4, :],
                                                        in1=sms[:, sl, j, :], op=ALU.mult))
                                dve.wait_ge(s_evD, 16 * c + j + 1)
                                dve.tensor_tensor(out=mg[:, j, :], in0=tS[:, j % 2, :], in1=ma[:, j, :],
                                                  op=ALU.add).then_inc(s_evD, 1)

                def fin(c, tt):
                    sl = c % 2
                    ti = 4 * c + tt
                    dve.wait_ge(s_rs, ti + 1)
                    if ti >= 2:
                        dve.wait_ge(s_st[ti % 2], 16 * (ti // 2))
                    chain(dve.scalar_tensor_tensor(out=tF[:, ti % 2, :],
                                                   in0=po[:, ti % 2, :, :].rearrange("p a c -> p (a c)"),
                                                   scalar=rs2[:, ti % 2:ti % 2 + 1], in1=gpost[:],
                                                   op0=ALU.mult, op1=ALU.mult))
                    dve.tensor_tensor(out=ost[:, ti % 2, :], in0=tF[:, ti % 2, :], in1=xin[:, sl, tt, :],
                                      op=ALU.add).then_inc(s_fin, 1)

                y3f(0)
                for c in range(NCH):
                    evacs(c)
                    fin(c, 0)
                    fin(c, 1)
                    if c + 1 < NCH:
                        y3f(c + 1)
                    fin(c, 2)
                    fin(c, 3)


def kernel(**inputs):
    nc = build()
    x = np.ascontiguousarray(inputs["x"], dtype=np.float32)
    shared = {}
    for k, v in inputs.items():
        if k == "x":
            continue
        a = np.ascontiguousarray(v, dtype=np.float32)
        shared[k] = a.reshape(_SHAPES[k])
    in_maps = []
    for c in range(NCORES):
        m = dict(shared)
        m["x"] = x[c]
        in_maps.append(m)
    res = run_bass_kernel_spmd(nc, in_maps, core_ids=list(range(NCORES)))
    return np.stack([r["out"] for r in res.results], axis=0).astype(np.float32)


_SHAPES = {
    "norm_pre": (1, D), "w_in": (D, INC), "b_forget": (1, H), "lam_re": (32, 64), "lam_im": (32, 64),
    "log_dt": (1, 32), "b_re": (32, 64, 16), "b_im": (32, 64, 16), "c_re": (32, 16, 64),
    "c_im": (32, 16, 64), "d_skip": (32, 16), "w_glu": (512, 512), "b_glu": (1, 512),
    "w_branch_a": (512, D), "w_branch_s": (512, D), "w_out": (D, D), "norm_post": (1, D),
}
```

```python
import contextlib
import math
import numpy as np
import concourse.bass as bass
import concourse.mybir as mybir
from concourse.bass_utils import run_bass_kernel_spmd

F32 = mybir.dt.float32
BF16 = mybir.dt.bfloat16
I32 = mybir.dt.int32
AF = mybir.ActivationFunctionType
ALU = mybir.AluOpType

D = 1024
SEQ = 8192
NCORES = 8
INC = 5640
USED = 5128
H = 8
HD = 64
EPS = 1e-6
CH = 512
NCH = SEQ // CH


class Ctx:
    pass


def build(last_phase=99, debug=False):
    nc = bass.Bass("TRN2", target_bir_lowering=False)
    es = contextlib.ExitStack()
    g = Ctx()
    g.nc = nc
    g.debug = debug

    def din(name, shape):
        return nc.dram_tensor(name, list(shape), F32, kind="ExternalInput").ap()

    g.x = din("x", [SEQ, D])
    g.norm_pre = din("norm_pre", [1, D])
    g.w_in = din("w_in", [D, INC])
    g.b_forget = din("b_forget", [1, H])
    g.lam_re = din("lam_re", [32, 64])
    g.lam_im = din("lam_im", [32, 64])
    g.log_dt = din("log_dt", [1, 32])
    g.b_re = din("b_re", [32, 64, 16])
    g.b_im = din("b_im", [32, 64, 16])
    g.c_re = din("c_re", [32, 16, 64])
    g.c_im = din("c_im", [32, 16, 64])
    g.d_skip = din("d_skip", [32, 16])
    g.w_glu = din("w_glu", [512, 512])
    g.b_glu = din("b_glu", [1, 512])
    g.w_branch_a = din("w_branch_a", [512, D])
    g.w_branch_s = din("w_branch_s", [512, D])
    g.w_out = din("w_out", [D, D])
    g.norm_post = din("norm_post", [1, D])
    g.out = nc.dram_tensor("out", [SEQ, D], F32, kind="ExternalOutput").ap()

    def scratch(name, shape, dt=BF16):
        if debug:
            return nc.dram_tensor(name, list(shape), dt, kind="ExternalOutput").ap()
        return nc.dram_tensor(name, list(shape), dt).ap()

    g.qT = scratch("qT", [512, SEQ])
    g.kT = scratch("kT", [512, SEQ])
    g.vtok = scratch("vtok", [SEQ, 512])
    g.fT = scratch("fT", [8, SEQ], F32)
    g.sgaT = scratch("sgaT", [512, SEQ])
    g.uT = scratch("uT", [512, SEQ])
    g.sgsT = scratch("sgsT", [512, SEQ])
    g.smaT = scratch("smaT", [D, SEQ])
    g.smsT = scratch("smsT", [D, SEQ])

    g.ident = es.enter_context(nc.sbuf_tensor("ident", [128, 128], BF16))
    g.ones_f = es.enter_context(nc.sbuf_tensor("ones_f", [128, 128], F32))
    S0 = Steps(nc, "init")

    def i0(p, raw):
        p.memset(g.ones_f[:], 1.0)
        p.affine_select(out=g.ident[:], in_=g.ones_f[:], pattern=[[-1, 128]],
                        compare_op=ALU.is_equal, fill=0.0, base=0, channel_multiplier=1)
    S0.run(gpsimd=i0)

    g.crk = scratch("crk", [3, H, SEQ])
    g.crq = scratch("crq", [3, H, SEQ])
    g.yaT = scratch("yaT", [512, SEQ])
    g.zeros_b = es.enter_context(nc.sbuf_tensor("zeros_b", [128, 128], BF16))
    g.maskT = es.enter_context(nc.sbuf_tensor("maskT", [128, 128], BF16))
    def i1(p, raw):
        p.memset(g.zeros_b[:], 0.0)
        p.affine_select(out=g.maskT[:], in_=g.zeros_b[:], pattern=[[1, 128]],
                        compare_op=ALU.is_ge, fill=-65536.0, base=0, channel_multiplier=-1)
    S0.run(gpsimd=i1)

    if last_phase >= 1:
        phase_inproj(g)
    if last_phase >= 2:
        phase_forget(g)
    g.wg = es.enter_context(nc.sbuf_tensor("fin_wg", [128, 4, 512], BF16))
    g.wa = es.enter_context(nc.sbuf_tensor("fin_wa", [128, 4, D], BF16))
    g.wsr = es.enter_context(nc.sbuf_tensor("fin_wsr", [128, 4, D], BF16))
    g.wo = es.enter_context(nc.sbuf_tensor("fin_wo", [128, 8, D], BF16))
    g.bg = es.enter_context(nc.sbuf_tensor("fin_bg", [128, 4], F32))
    g.gpost = es.enter_context(nc.sbuf_tensor("fin_gpost", [128, D], F32))
    if last_phase >= 3:
        phase_attn(g)
    g.ys1T = scratch("ys1T", [512, SEQ])
    if last_phase >= 4:
        phase_ssm(g)
    if last_phase >= 5:
        phase_final(g)
    es.close()
    return nc


def phase_inproj(g):
    nc = g.nc
    with contextlib.ExitStack() as es:
        wsb = es.enter_context(nc.sbuf_tensor("wsb", [128, 8, USED], BF16))
        gain = es.enter_context(nc.sbuf_tensor("gain", [128, 8], F32))
        with contextlib.ExitStack() as es0:
            wtmp = es0.enter_context(nc.sbuf_tensor("wtmp", [128, 2, USED], F32))
            s_w = [nc.alloc_semaphore(f"p0_w{i}") for i in range(2)]
            s_g = nc.alloc_semaphore("p0_g")
            s_done = [nc.alloc_semaphore(f"p0_d{i}") for i in range(3)]
            cuts = [0, 1536, 4864, USED]
            with nc.Block() as b:
                @b.sync
                def _(sp):
                    sp.dma_start(out=gain[:], in_=g.norm_pre.rearrange("o (k p) -> p (o k)", p=128),
                                 allow_slow_non_contiguous=True).then_inc(s_g, 16)
                    for dk in range(8):
                        if dk >= 2:
                            for e in range(3):
                                sp.wait_ge(s_done[e], dk - 1)
                        sp.dma_start(out=wtmp[:, dk % 2, :],
                                     in_=g.w_in[dk * 128:(dk + 1) * 128, 0:USED]).then_inc(s_w[dk % 2], 16)

                def conv(eng, e, kind):
                    eng.wait_ge(s_g, 16)
                    for dk in range(8):
                        eng.wait_ge(s_w[dk % 2], 16 * (dk // 2 + 1))
                        c0, c1 = cuts[e], cuts[e + 1]
                        if kind == "act":
                            eng.activation(out=wsb[:, dk, c0:c1], in_=wtmp[:, dk % 2, c0:c1],
                                           func=AF.Copy, scale=gain[:, dk:dk + 1]).then_inc(s_done[e], 1)
                        else:
                            eng.tensor_scalar(out=wsb[:, dk, c0:c1], in0=wtmp[:, dk % 2, c0:c1],
                                              scalar1=gain[:, dk:dk + 1], scalar2=None,
                                              op0=ALU.mult).then_inc(s_done[e], 1)

                @b.vector
                def _(e):
                    conv(e, 0, "dve")

                @b.scalar
                def _(e):
                    conv(e, 1, "act")

                @b.gpsimd
                def _(e):
                    conv(e, 2, "pool")

        xs = es.enter_context(nc.sbuf_tensor("xs", [128, 2, 4, D], F32))
        hn = es.enter_context(nc.sbuf_tensor("hn", [128, 2, 4, D], BF16))
        hT = es.enter_context(nc.sbuf_tensor("hT", [128, 2, 8, CH], BF16))
        junk = es.enter_context(nc.sbuf_tensor("junk", [128, 4, D], BF16))
        ss = es.enter_context(nc.sbuf_tensor("ss", [128, 2, 4], F32))
        sd = es.enter_context(nc.sbuf_tensor("sd", [128, 2, 4], F32))
        rstd = es.enter_context(nc.sbuf_tensor("rstd", [128, 2, 4], F32))
        NS = 6
        stage = es.enter_context(nc.sbuf_tensor("stage", [128, NS, CH], BF16))
        stagef = es.enter_context(nc.sbuf_tensor("stagef", [8, 2, CH], F32))
        ps = es.enter_context(nc.psum_tensor("ps", [128, 4, CH], F32))
        tp = es.enter_context(nc.psum_tensor("tp", [128, 2, 1024], BF16))

        blks = []

        def add(kind, col0, m, func, eng, dest, row0):
            blks.append(dict(kind=kind, col0=col0, m=m, func=func, eng=eng, dest=dest, row0=row0))

        for j in range(4):
            add("fm", 0 + 128 * j, 128, AF.Copy, "dve", g.qT, 128 * j)
        for j in range(4):
            add("fm", 512 + 128 * j, 128, AF.Copy, "dve", g.kT, 128 * j)
        for j in range(4):
            add("fm", 2056 + 128 * j, 128, AF.Copy, "dve", g.uT, 128 * j)
        for tt in range(4):
            add("v", 1024, 128, AF.Copy, "dve", g.vtok, tt)
        add("f", 1536, 8, AF.Copy, "dve", g.fT, 0)
        for j in range(4):
            add("fm", 1544 + 128 * j, 128, AF.Silu, "act", g.sgaT, 128 * j)
        for j in range(4):
            add("fm", 2568 + 128 * j, 128, AF.Silu, "act", g.sgsT, 128 * j)
        for j in range(8):
            add("fm", 3080 + 128 * j, 128, AF.Sigmoid, "act", g.smaT, 128 * j)
        for j in range(8):
            add("fm", 4104 + 128 * j, 128, AF.Sigmoid, "act", g.smsT, 128 * j)
        NB = len(blks)
        seq = []
        cnt = {"act": 0, "dve": 0}
        nstage = 0
        for c in range(NCH):
            for j, bk in enumerate(blks):
                cnt[bk["eng"]] += 1
                d = dict(bk)
                d.update(c=c, n=len(seq), eidx=cnt[bk["eng"]])
                if bk["kind"] == "f":
                    d["slot"] = None
                else:
                    d["slot"] = nstage % NS
                    d["suse"] = nstage // NS
                    nstage += 1
                seq.append(d)

        s_x = [nc.alloc_semaphore(f"p1_x{i}") for i in range(2)]
        s_sd = nc.alloc_semaphore("p1_sd")
        s_ss = nc.alloc_semaphore("p1_ss")
        s_ln = nc.alloc_semaphore("p1_ln")
        s_rstd = nc.alloc_semaphore("p1_rstd")
        s_hn = nc.alloc_semaphore("p1_hn")
        s_tp = nc.alloc_semaphore("p1_tp")
        s_hT = nc.alloc_semaphore("p1_hT")
        s_mm = nc.alloc_semaphore("p1_mm")
        s_ev = {"act": nc.alloc_semaphore("p1_eva"), "dve": nc.alloc_semaphore("p1_evd")}
        s_out = [nc.alloc_semaphore(f"p1_o{i}") for i in range(NS)]
        s_outf = [nc.alloc_semaphore(f"p1_of{i}") for i in range(2)]

        def stage_ap(d):
            if d["kind"] == "f":
                return stagef[0:8, d["c"] % 2, :]
            return stage[:, d["slot"], :]

        def dest_ap(d):
            c = d["c"]
            if d["kind"] == "v":
                r0 = c * CH + d["row0"] * 128
                return d["dest"][r0:r0 + 128, :]
            if d["kind"] == "f":
                return d["dest"][0:8, c * CH:(c + 1) * CH]
            return d["dest"][d["row0"]:d["row0"] + 128, c * CH:(c + 1) * CH]

        with nc.Block() as b:
            @b.sync
            def _(sp):
                def load(c):
                    if c >= 2:
                        sp.wait_ge(s_hn, c - 1)
                    sp.dma_start(out=xs[:, c % 2, :, :],
                                 in_=g.x[c * CH:(c + 1) * CH, :].rearrange("(t p) d -> p t d", p=128)
                                 ).then_inc(s_x[c % 2], 16)
                load(0)
                load(1)
                for c in range(NCH):
                    if c + 2 < NCH:
                        load(c + 2)
                    for d in seq[c * NB:(c + 1) * NB]:
                        sp.wait_ge(s_ev[d["eng"]], d["eidx"])
                        so = s_outf[c % 2] if d["kind"] == "f" else s_out[d["slot"]]
                        sp.dma_start(out=dest_ap(d), in_=stage_ap(d)).then_inc(so, 16)
                for i in range(NS):
                    uses = len([d for d in seq if d["slot"] == i])
                    sp.wait_ge(s_out[i], 16 * uses)
                for i in range(2):
                    sp.wait_ge(s_outf[i], 16 * (NCH // 2))

            def evac(eng, d, is_act):
                eng.wait_ge(s_mm, d["n"] + 1)
                if d["kind"] == "f":
                    if d["c"] >= 2:
                        eng.wait_ge(s_outf[d["c"] % 2], 16 * (d["c"] // 2))
                    src = ps[0:8, d["n"] % 4, :]
                else:
                    if d["suse"] >= 1:
                        eng.wait_ge(s_out[d["slot"]], 16 * d["suse"])
                    src = ps[:, d["n"] % 4, :]
                if is_act:
                    eng.activation(out=stage_ap(d), in_=src, func=d["func"]).then_inc(s_ev["act"], 1)
                else:
                    eng.tensor_copy(out=stage_ap(d), in_=src).then_inc(s_ev["dve"], 1)

            @b.scalar
            def _(act):
                def stats(c):
                    sl = c % 2
                    act.wait_ge(s_x[sl], 16 * (c // 2 + 1))
                    for tt in range(4):
                        act.activation(out=junk[:, tt, :], in_=xs[:, sl, tt, :], func=AF.Square,
                                       accum_out=ss[:, sl, tt:tt + 1]).then_inc(s_ss, 1)
                    act.wait_ge(s_ss, 4 * (c + 1))
                    act.activation(out=ss[:, sl, :], in_=ss[:, sl, :], func=AF.Ln,
                                   scale=1.0 / D, bias=EPS).then_inc(s_ln, 1)
                    act.wait_ge(s_ln, c + 1)
                    act.activation(out=sd[:, sl, :], in_=ss[:, sl, :], func=AF.Exp,
                                   scale=-0.5).then_inc(s_sd, 1)
                    act.wait_ge(s_sd, c + 1)
                    if c >= 2:
                        act.wait_ge(s_tp, 8 * (c - 1))
                    for tt in range(4):
                        ins = act.activation(out=hn[:, sl, tt, :], in_=xs[:, sl, tt, :], func=AF.Copy,
                                             scale=sd[:, sl, tt:tt + 1])
                    ins.then_inc(s_hn, 1)
                stats(0)
                stats(1)
                for c in range(NCH):
                    if c + 2 < NCH:
                        stats(c + 2)
                    for d in seq[c * NB:(c + 1) * NB]:
                        if d["eng"] == "act":
                            evac(act, d, True)

            @b.vector
            def _(dve):
                def pro(c):
                    sl = c % 2
                    if c >= 2:
                        dve.wait_ge(s_mm, (c - 1) * NB)
                    for dk in range(8):
                        dve.wait_ge(s_tp, c * 8 + dk + 1)
                        dve.tensor_copy(out=hT[:, sl, dk, :], in_=tp[:, dk % 2, 0:CH]).then_inc(s_hT, 1)
                pro(0)
                for c in range(NCH):
                    if c + 1 < NCH:
                        pro(c + 1)
                    for d in seq[c * NB:(c + 1) * NB]:
                        if d["eng"] == "dve":
                            evac(dve, d, False)


            @b.tensor
            def _(pe):
                def trans(c):
                    sl = c % 2
                    pe.wait_ge(s_hn, c + 1)
                    for dk in range(8):
                        gi = c * 8 + dk
                        if gi >= 2:
                            pe.wait_ge(s_hT, gi - 1)
                        for tt in range(4):
                            ins = pe.transpose(out=tp[:, dk % 2, tt * 128:(tt + 1) * 128],
                                               in_=hn[:, sl, tt, dk * 128:(dk + 1) * 128], identity=g.ident[:])
                        ins.then_inc(s_tp, 1)
                trans(0)
                for c in range(NCH):
                    sl = c % 2
                    if c + 1 < NCH:
                        trans(c + 1)
                    pe.wait_ge(s_hT, 8 * (c + 1))
                    for d in seq[c * NB:(c + 1) * NB]:
                        n = d["n"]
                        if n >= 4:
                            pd = seq[n - 4]
                            pe.wait_ge(s_ev[pd["eng"]], pd["eidx"])
                        for dk in range(8):
                            if d["kind"] == "v":
                                tt = d["row0"]
                                ins = pe.matmul(ps[:, n % 4, :], lhsT=hT[:, sl, dk, tt * 128:(tt + 1) * 128],
                                                rhs=wsb[:, dk, 1024:1536], start=(dk == 0), stop=(dk == 7))
                            else:
                                m = d["m"]
                                ins = pe.matmul(ps[0:m, n % 4, :], lhsT=wsb[:, dk, d["col0"]:d["col0"] + m],
                                                rhs=hT[:, sl, dk, :], start=(dk == 0), stop=(dk == 7))
                        ins.then_inc(s_mm, 1)


def phase_forget(g):
    nc = g.nc
    SEG = 16
    SL = SEQ // SEG
    with contextlib.ExitStack() as es:
        ft = es.enter_context(nc.sbuf_tensor("ft", [128, SL], F32))
        t1 = es.enter_context(nc.sbuf_tensor("fg_t1", [128, SL], F32))
        t2 = es.enter_context(nc.sbuf_tensor("fg_t2", [128, SL], F32))
        ones = es.enter_context(nc.sbuf_tensor("fg_ones", [128, SL], F32))
        rows = es.enter_context(nc.sbuf_tensor("fg_rows", [128, 6, SL], BF16))
        bfn = es.enter_context(nc.sbuf_tensor("bfn", [128, 1], F32))
        M = es.enter_context(nc.sbuf_tensor("fg_M", [128, 128], F32))
        tot = es.enter_context(nc.sbuf_tensor("fg_tot", [128, 2], F32))
        off = es.enter_context(nc.sbuf_tensor("fg_off", [128, 2], F32))
        offp = es.enter_context(nc.psum_tensor("fg_offp", [128, 2], F32))
        S = Steps(nc, "p2")

        def ld(sp, raw):
            L = [raw.dma_start(out=ft[:], in_=g.fT.rearrange("h (s t) -> (h s) t", s=SEG))]
            for h in range(H):
                L.append(raw.dma_start(out=bfn[SEG * h:SEG * (h + 1), :],
                                       in_=bass.AP(g.b_forget.tensor, h, [[0, SEG], [1, 1]]),
                                       allow_slow_non_contiguous=True))
            sp.many(L, 16)

        def mk(p, raw):
            p.affine_select(out=M[:], in_=g.ones_f[:], pattern=[[1, 128]], compare_op=ALU.is_gt, fill=0.0,
                            base=0, channel_multiplier=-1)
            m3 = M[:].rearrange("p (h s) -> p h s", s=SEG)
            p.affine_select(out=m3, in_=m3, pattern=[[-SEG, H], [0, SEG]], compare_op=ALU.is_ge, fill=0.0,
                            base=0, channel_multiplier=1)
            p.affine_select(out=m3, in_=m3, pattern=[[SEG, H], [0, SEG]], compare_op=ALU.is_ge, fill=0.0,
                            base=SEG - 1, channel_multiplier=-1)
            p.memset(ones[:], 1.0)
        S.run(sync=ld, gpsimd=mk)
        S.run(vector=lambda v, raw: v.tensor_scalar(out=bfn[:], in0=bfn[:], scalar1=-1.0, scalar2=None, op0=ALU.mult))

        def a(act, raw):
            act.activation(out=t1[:], in_=ft[:], func=AF.Exp, scale=-1.0, bias=bfn[:, 0:1])
            act.activation(out=t2[:], in_=t1[:], func=AF.Ln, bias=1.0, scale=1.0)
        S.run(scalar=a)

        def d1(dve, raw):
            dve.tensor_tensor_scan(out=t1[:], data0=ones[:], data1=t2[:], initial=0.0, op0=ALU.mult, op1=ALU.add)
            dve.tensor_copy(out=tot[:, 0:1], in_=t1[:, SL - 1:SL])
            dve.tensor_copy(out=tot[:, 1:2], in_=t1[:, SL - 1:SL])
        S.run(vector=d1)
        S.run(tensor=lambda pe, raw: pe.matmul(offp[:, :], lhsT=M[:], rhs=tot[:], start=True, stop=True))

        def d2(dve, raw):
            dve.tensor_copy(out=off[:], in_=offp[:, :])
            dve.tensor_scalar(out=t1[:], in0=t1[:], scalar1=off[:, 0:1], scalar2=8.0, op0=ALU.add, op1=ALU.mult)
            dve.tensor_copy(out=rows[:, 0, :], in_=t1[:])
            dve.tensor_tensor(out=t2[:], in0=t1[:], in1=rows[:, 0, :], op=ALU.subtract)
            dve.tensor_copy(out=rows[:, 1, :], in_=t2[:])
            dve.tensor_tensor(out=t1[:], in0=t2[:], in1=rows[:, 1, :], op=ALU.subtract)
            dve.tensor_copy(out=rows[:, 2, :], in_=t1[:])
            dve.tensor_scalar(out=rows[:, 3:6, :], in0=rows[:, 0:3, :], scalar1=-1.0, scalar2=None, op0=ALU.mult)
        S.run(vector=d2)

        def st(sp, raw):
            L = []
            for j in range(3):
                L.append(raw.dma_start(out=g.crk[j].rearrange("h (s t) -> (h s) t", s=SEG), in_=rows[:, j, :]))
                L.append(raw.dma_start(out=g.crq[j].rearrange("h (s t) -> (h s) t", s=SEG), in_=rows[:, 3 + j, :]))
            sp.many(L, 16)
        S.run(sync=st)


def phase_attn(g):
    nc = g.nc
    NQ = SEQ // CH
    NKT = SEQ // 128
    with contextlib.ExitStack() as es:
        kTa = es.enter_context(nc.sbuf_tensor("kTa", [70, 2, SEQ], BF16))
        qTa = es.enter_context(nc.sbuf_tensor("qTa", [70, 2, SEQ], BF16))
        vsb = es.enter_context(nc.sbuf_tensor("vsb", [128, 2, NKT, 128], BF16))
        sga = es.enter_context(nc.sbuf_tensor("sga", [64, 2, SEQ], BF16))
        pT = es.enter_context(nc.sbuf_tensor("pT", [128, 3, 3, CH], BF16))
        rl = es.enter_context(nc.sbuf_tensor("rl", [64, CH], F32))
        yt = es.enter_context(nc.sbuf_tensor("yt", [64, CH], F32))
        ystage = es.enter_context(nc.sbuf_tensor("ystage", [64, 2, CH], BF16))
        sp_ps = es.enter_context(nc.psum_tensor("sp_ps", [128, 2, 3, CH], F32))
        o_ps = es.enter_context(nc.psum_tensor("o_ps", [128, 2, CH], F32))
        s_ms = nc.alloc_semaphore("p3_ms")
        s_ms2 = nc.alloc_semaphore("p3_ms2")
        s_pw = nc.alloc_semaphore("p3_pw")
        s_pc = nc.alloc_semaphore("p3_pc")
        s_pd = [nc.alloc_semaphore(f"p3_pd{i}") for i in range(2)]
        wtmpP = es.enter_context(nc.sbuf_tensor("wtmpP", [128, 2, D], F32))
        s_dv = nc.alloc_semaphore("p3_dv")
        s_ld = [nc.alloc_semaphore(f"p3_ld{i}") for i in range(2)]
        s_S = nc.alloc_semaphore("p3_S")
        s_exp = nc.alloc_semaphore("p3_exp")
        s_pv = nc.alloc_semaphore("p3_pv")
        s_fin = nc.alloc_semaphore("p3_fin")
        s_yo = [nc.alloc_semaphore(f"p3_yo{i}") for i in range(2)]

        GMAX = 3
        groups = []
        qlast = {}
        for h in range(H):
            for Q in range(NQ):
                nk = 4 * Q + 4
                full = [(kt, kt >= 4 * Q) for kt in range(4 * Q + 1)]
                cur = []
                glist = []
                for t in full:
                    cur.append(t)
                    if len(cur) == GMAX:
                        glist.append((0, cur)); cur = []
                if cur:
                    glist.append((0, cur))
                for kt in range(4 * Q + 1, nk):
                    glist.append(((kt - 4 * Q) * 128, [(kt, True)]))
                for gi_, (n0, tl) in enumerate(glist):
                    groups.append(dict(h=h, Q=Q, n0=n0, tiles=tl, first=(gi_ == 0), last=(gi_ == len(glist) - 1),
                                       i=len(groups)))
                qlast[(h, Q)] = len(groups) - 1
        head_first = {h: min(t["i"] for t in groups if t["h"] == h) for h in range(H)}
        head_last = {h: max(t["i"] for t in groups if t["h"] == h) for h in range(H)}
        NLD = 20

        with nc.Block() as b:
            @b.gpsimd
            def _(p):
                for k, ap in enumerate((kTa[64:70, 0, :], kTa[64:70, 1, :], vsb[:, 0, :, 64:128])):
                    p.memset(ap, 1.0).then_inc(s_ms, 1)
                    p.wait_ge(s_ms, k + 1)
                p.dma_start(out=g.bg[:], in_=g.b_glu.rearrange("o (j p) -> p (o j)", p=128),
                            allow_slow_non_contiguous=True).then_inc(s_pw, 16)
                p.dma_start(out=g.gpost[:], in_=bass.AP(g.norm_post.tensor, 0, [[0, 128], [1, D]])).then_inc(s_pw, 16)
                p.wait_ge(s_pw, 32)
                jobs = []
                for (src, dst, nk, nco) in ((g.w_glu, g.wg, 4, 512), (g.w_branch_a, g.wa, 4, D),
                                            (g.w_branch_s, g.wsr, 4, D), (g.w_out, g.wo, 8, D)):
                    for k in range(nk):
                        jobs.append((src[128 * k:128 * (k + 1), :], dst[:, k, :], nco))

                def pdma(i):
                    srcap, _, nco = jobs[i]
                    p.dma_start(out=wtmpP[:, i % 2, 0:nco], in_=srcap).then_inc(s_pd[i % 2], 16)
                pdma(0)
                pdma(1)
                for i, (srcap, dstap, nco) in enumerate(jobs):
                    p.wait_ge(s_pd[i % 2], 16 * (i // 2 + 1))
                    p.tensor_copy(out=dstap, in_=wtmpP[:, i % 2, 0:nco]).then_inc(s_pc, 1)
                    p.wait_ge(s_pc, i + 1)
                    if i + 2 < len(jobs):
                        pdma(i + 2)

            @b.sync
            def _(sp):
                def pieces(h):
                    sl = h % 2
                    P = []
                    QW = SEQ // 4
                    for c4 in range(4):
                        cs_ = slice(c4 * QW, (c4 + 1) * QW)
                        P.append(lambda cs_=cs_: sp.dma_start(out=kTa[0:64, sl, cs_], in_=g.kT[h * 64:(h + 1) * 64, cs_]
                                                              ).then_inc(s_ld[sl], 16))
                        P.append(lambda cs_=cs_: sp.dma_start(out=qTa[0:64, sl, cs_], in_=g.qT[h * 64:(h + 1) * 64, cs_]
                                                              ).then_inc(s_ld[sl], 16))
                    vsrc = g.vtok[:, h * 64:(h + 1) * 64].rearrange("(kt p) d -> p kt d", p=128)
                    for part in range(8):
                        P.append(lambda part=part: sp.dma_start(out=vsb[:, sl, part * 8:(part + 1) * 8, 0:64],
                                                                in_=vsrc[:, part * 8:(part + 1) * 8, :]
                                                                ).then_inc(s_ld[sl], 16))
                    for c2 in range(2):
                        cs_ = slice(c2 * (SEQ // 2), (c2 + 1) * (SEQ // 2))
                        P.append(lambda cs_=cs_: sp.dma_start(out=sga[:, sl, cs_], in_=g.sgaT[h * 64:(h + 1) * 64, cs_]
                                                              ).then_inc(s_ld[sl], 16))
                    C = [lambda: sp.dma_start(out=kTa[67:70, sl, :], in_=g.crk[:, h, :]).then_inc(s_ld[sl], 16),
                         lambda: sp.dma_start(out=qTa[64:67, sl, :], in_=g.crq[:, h, :]).then_inc(s_ld[sl], 16)]
                    return P, C

                for h0 in range(2):
                    P, C = pieces(h0)
                    for f in P:
                        f()
                    if h0 == 0:
                        sp.wait_ge(s_ms, 3)
                        sp.wait_ge(s_ms2, 3)
                    for f in C:
                        f()
                for h in range(H):
                    nxt = h + 1
                    pend = []
                    if 2 <= nxt < H:
                        P, C = pieces(nxt)
                        pend = P + C
                        sp.wait_ge(s_pv, head_last[nxt - 2] + 1)
                        sp.wait_ge(s_fin, NQ * (nxt - 1))
                    for Q in range(NQ):
                        qi = h * NQ + Q
                        sp.wait_ge(s_fin, qi + 1)
                        sp.dma_start(out=g.yaT[h * 64:(h + 1) * 64, Q * CH:(Q + 1) * CH],
                                     in_=ystage[:, qi % 2, :]).then_inc(s_yo[qi % 2], 16)
                        for _ in range(2):
                            if pend:
                                pend.pop(0)()
                    while pend:
                        pend.pop(0)()
                for i in range(2):
                    sp.wait_ge(s_yo[i], 16 * (H * NQ // 2))

            @b.tensor
            def _(pe):
                def S(t):
                    i = t["i"]
                    sl = t["h"] % 2
                    if i == head_first[t["h"]]:
                        pe.wait_ge(s_ld[sl], 16 * NLD * (t["h"] // 2 + 1))
                    if i >= 2:
                        pe.wait_ge(s_exp, i - 1)
                    n0 = t["n0"]
                    q0 = t["Q"] * CH
                    for j, (kt, diag) in enumerate(t["tiles"]):
                        ins = pe.matmul(sp_ps[:, i % 2, j, n0:CH], lhsT=kTa[0:70, sl, kt * 128:(kt + 1) * 128],
                                        rhs=qTa[0:70, sl, q0 + n0:q0 + CH], start=True, stop=not diag)
                        if diag:
                            ins = pe.matmul(sp_ps[:, i % 2, j, n0:n0 + 128], lhsT=g.ident[:], rhs=g.maskT[:],
                                            start=False, stop=True)
                    ins.then_inc(s_S, 1)

                def PV(t):
                    i = t["i"]
                    sl = t["h"] % 2
                    qi = t["h"] * NQ + t["Q"]
                    pe.wait_ge(s_exp, i + 1)
                    if t["first"] and qi >= 2:
                        pe.wait_ge(s_fin, qi - 1)
                    n0 = t["n0"]
                    nt = len(t["tiles"])
                    for j, (kt, diag) in enumerate(t["tiles"]):
                        ins = pe.matmul(o_ps[:, qi % 2, n0:CH], lhsT=vsb[:, sl, kt, :], rhs=pT[:, i % 3, j, n0:CH],
                                        start=(t["first"] and j == 0), stop=(t["last"] and j == nt - 1))
                    ins.then_inc(s_pv, 1)

                n = len(groups)
                S(groups[0])
                S(groups[1])
                for i in range(n):
                    if i + 2 < n:
                        S(groups[i + 2])
                    PV(groups[i])

            @b.scalar
            def _(act):
                for t in groups:
                    i = t["i"]
                    act.wait_ge(s_S, i + 1)
                    if i >= 3:
                        act.wait_ge(s_pv, i - 2)
                    n0 = t["n0"]
                    nt = len(t["tiles"])
                    act.activation(out=pT[:, i % 3, 0:nt, n0:CH], in_=sp_ps[:, i % 2, 0:nt, n0:CH], func=AF.Exp,
                                   scale=0.125).then_inc(s_exp, 1)

            @b.vector
            def _(dve):
                for k, ap in enumerate((qTa[64:70, 0, :], qTa[64:70, 1, :], vsb[:, 1, :, 64:128])):
                    dve.memset(ap, 1.0).then_inc(s_ms2, 1)
                    dve.wait_ge(s_ms2, k + 1)
                for h in range(H):
                    sl = h % 2
                    dve.wait_ge(s_ld[sl], 16 * NLD * (h // 2 + 1))
                    for Q in range(NQ):
                        qi = h * NQ + Q
                        dve.wait_ge(s_pv, qlast[(h, Q)] + 1)
                        if qi >= 1:
                            dve.wait_ge(s_fin, qi)
                        if qi >= 2:
                            dve.wait_ge(s_yo[qi % 2], 16 * (qi // 2))
                        dve.reciprocal(out=rl[:], in_=o_ps[64:128, qi % 2, :]).then_inc(s_dv, 1)
                        dve.wait_ge(s_dv, 2 * qi + 1)
                        dve.tensor_tensor(out=yt[:], in0=o_ps[0:64, qi % 2, :], in1=rl[:], op=ALU.mult).then_inc(s_dv, 1)
                        dve.wait_ge(s_dv, 2 * qi + 2)
                        dve.tensor_tensor(out=ystage[:, qi % 2, :], in0=yt[:], in1=sga[:, sl, Q * CH:(Q + 1) * CH],
                                          op=ALU.mult).then_inc(s_fin, 1)


class Ser:
    def __init__(self, eng, st):
        self.e = eng
        self.st = st

    def done(self, ins, inc=1):
        ins.then_inc(self.st["sem"], inc)
        self.st["n"] += inc
        self.e.wait_ge(self.st["sem"], self.st["n"])
        return ins

    def many(self, instrs, inc=1):
        for ins in instrs:
            ins.then_inc(self.st["sem"], inc)
            self.st["n"] += inc
        self.e.wait_ge(self.st["sem"], self.st["n"])

    def __getattr__(self, name):
        f = getattr(self.e, name)
        inc = 16 if name == "dma_start" else 1

        def w(*a, **k):
            return self.done(f(*a, **k), inc)
        return w


class Steps:
    def __init__(self, nc, tag):
        self.nc = nc
        self.st = {e: {"sem": nc.alloc_semaphore(f"{tag}_{e}"), "n": 0}
                   for e in ("sync", "vector", "scalar", "gpsimd", "tensor")}

    def run(self, **fns):
        with self.nc.Block() as b:
            for ename, fn in fns.items():
                def mk(fn, ename):
                    def body(e):
                        fn(Ser(e, self.st[ename]), e)
                    return body
                getattr(b, ename)(mk(fn, ename))


TWO_PI = 6.28318
HALF_PI = 1.5707963


def phase_ssm(g):
    nc = g.nc
    NK = SEQ // 16
    with contextlib.ExitStack() as es:
        def sb(name, shape, dt=F32):
            return es.enter_context(nc.sbuf_tensor("ssm_" + name, list(shape), dt))
        S = Steps(nc, "p4")
        identf = sb("identf", [128, 128])
        pm = sb("pm", [128, 2])
        bm = sb("bm", [128, 8])
        kidx_i = sb("kidx_i", [128, NK], I32)
        kidx = sb("kidx", [128, NK])
        midx = sb("midx", [128, 17])
        LR = sb("LR", [128, 4]); LI = sb("LI", [128, 4]); LDT = sb("LDT", [128, 4])
        BR = sb("BR", [128, 4, 16]); BI = sb("BI", [128, 4, 16])
        CNr = sb("CNr", [64, 128]); CNi = sb("CNi", [64, 128])
        Dcol = sb("Dcol", [128, 1])
        dt_ = sb("dt", [128, 4]); lrdt = sb("lrdt", [128, 4]); lidt = sb("lidt", [128, 4]); phi = sb("phi", [128, 4])
        tA = sb("tA", [128, 4, 17]); tB = sb("tB", [128, 4, 17]); tC = sb("tC", [128, 4, 17]); tI = sb("tI", [128, 4, 17], I32)
        EAr = sb("EAr", [128, 4, 17]); EAi = sb("EAi", [128, 4, 17])
        s1 = sb("s1", [128, 4]); s2 = sb("s2", [128, 4]); s3 = sb("s3", [128, 4]); s4 = sb("s4", [128, 4])
        fr = sb("fr", [128, 4]); fi = sb("fi", [128, 4]); sI = sb("sI", [128, 4], I32)
        Bbr = sb("Bbr", [128, 4, 16]); Bbi = sb("Bbi", [128, 4, 16]); b1 = sb("b1", [128, 4, 16])
        Gr = sb("Gr", [128, 4, 16, 16]); Gi = sb("Gi", [128, 4, 16, 16]); G1 = sb("G1", [128, 4, 16, 16])
        Gx = sb("Gx", [128, 16, 2, 4, 2, 16], BF16)
        CTr = sb("CTr", [128, 4, 16]); CTi = sb("CTi", [128, 4, 16])
        CTrb = sb("CTrb", [128, 4, 16], BF16); CTnib = sb("CTnib", [128, 4, 16], BF16)
        Wc = sb("Wc", [128, 4, 16, 2, 2, 16], BF16)
        Wb = sb("Wb", [128, 16, 2, 128], BF16)
        Kst = sb("Kst", [128, 16, 8, 16], BF16)
        cosT = sb("cosT", [128, 4, NK], BF16); sinT = sb("sinT", [128, 4, NK], BF16)
        rho = sb("rho", [128, 4]); phr = sb("phr", [128, 4])
        uN = sb("uN", [128, SEQ], BF16); uP = sb("uP", [128, 16, NK], BF16)
        Sp = sb("Sp", [128, 4, 2, NK], BF16)
        w1 = sb("w1", [128, 4, NK]); w2 = sb("w2", [128, 4, NK]); w3 = sb("w3", [128, 4, NK]); w4 = sb("w4", [128, 4, NK])
        kA = w1; kB = w2; kIv = w3[:].bitcast(I32)
        ystage = sb("ystage", [128, SEQ], BF16)
        yperm = sb("yperm", [128, 16, NK], BF16)

        def bc(ap, shape, axis):
            return ap.unsqueeze(axis).to_broadcast(list(shape))

        def c0(p, raw):
            p.memset(identf[:], 0.0)
            p.affine_select(out=identf[:], in_=g.ones_f[:], pattern=[[-1, 128]], compare_op=ALU.is_equal,
                            fill=0.0, base=0, channel_multiplier=1)
            p.affine_select(out=pm[:], in_=g.ones_f[:, 0:2], pattern=[[-64, 2]], compare_op=ALU.is_ge,
                            fill=0.0, base=0, channel_multiplier=1)
            p.affine_select(out=pm[:], in_=pm[:], pattern=[[64, 2]], compare_op=ALU.is_ge,
                            fill=0.0, base=63, channel_multiplier=-1)
            p.affine_select(out=bm[:], in_=g.ones_f[:, 0:8], pattern=[[-16, 8]], compare_op=ALU.is_ge,
                            fill=0.0, base=0, channel_multiplier=1)
            p.affine_select(out=bm[:], in_=bm[:], pattern=[[16, 8]], compare_op=ALU.is_ge,
                            fill=0.0, base=15, channel_multiplier=-1)
            p.iota(kidx_i[:], pattern=[[1, NK]], base=0, channel_multiplier=0)
            p.memset(Sp[:], 0.0)
        S.run(gpsimd=c0)

        def c1(v, raw):
            v.tensor_copy(out=kidx[:], in_=kidx_i[:])
            v.tensor_copy(out=midx[:], in_=kidx[:, 0:17])
        S.run(vector=c1)

        lamr = g.lam_re.rearrange("(q t) p -> (t p) q", t=2)
        lami = g.lam_im.rearrange("(q t) p -> (t p) q", t=2)
        bre = g.b_re.rearrange("(q t) p c -> (t p) q c", t=2)
        bim = g.b_im.rearrange("(q t) p c -> (t p) q c", t=2)

        for r in range(4):
            def ld(sp, raw, r=r):
                L = []
                L.append(raw.dma_start(out=LR[:], in_=lamr[:, 4 * r:4 * r + 4], allow_slow_non_contiguous=True))
                L.append(raw.dma_start(out=LI[:], in_=lami[:, 4 * r:4 * r + 4], allow_slow_non_contiguous=True))
                for t in range(2):
                    L.append(raw.dma_start(out=LDT[64 * t:64 * t + 64, :],
                                           in_=bass.AP(g.log_dt.tensor, t + 8 * r, [[0, 64], [2, 4]]),
                                           allow_slow_non_contiguous=True))
                L.append(raw.dma_start(out=BR[:], in_=bre[:, 4 * r:4 * r + 4, :]))
                L.append(raw.dma_start(out=BI[:], in_=bim[:, 4 * r:4 * r + 4, :]))
                for qq in range(4):
                    q = 4 * r + qq
                    L.append(raw.dma_start(out=CNr[16 * qq:16 * qq + 16, :].rearrange("c (t p) -> c t p", t=2),
                                           in_=g.c_re[2 * q:2 * q + 2, :, :].rearrange("t c p -> c t p")))
                    L.append(raw.dma_start(out=CNi[16 * qq:16 * qq + 16, :].rearrange("c (t p) -> c t p", t=2),
                                           in_=g.c_im[2 * q:2 * q + 2, :, :].rearrange("t c p -> c t p")))
                L.append(raw.dma_start(out=Dcol[:],
                                       in_=g.d_skip[8 * r:8 * r + 8, :].rearrange("g (c o) -> (g c) o", o=1)))
                L.append(raw.dma_start(out=uN[:], in_=g.uT[128 * r:128 * r + 128, :]))
                sp.many(L, 16)
            def make_ld(rr):
                return lambda sp, raw: ld(sp, raw, rr)
            if r == 0:
                S.run(sync=ld)

            def dtf(a, raw):
                a.activation(out=dt_[:], in_=LDT[:], func=AF.Exp)

            def tbl(v, raw):
                shk = [128, 4, NK]
                v.tensor_tensor(out=kA[:], in0=bc(phr[:], shk, 2), in1=bc(kidx[:], shk, 1), op=ALU.mult)
                v.tensor_copy(out=kIv, in_=kA[:])
                v.tensor_copy(out=kB[:], in_=kIv)
                v.tensor_tensor(out=kA[:], in0=kA[:], in1=kB[:], op=ALU.subtract)
                v.tensor_scalar(out=kB[:], in0=kA[:], scalar1=-1.0, scalar2=None, op0=ALU.mult)
                v.tensor_tensor(out=kB[:], in0=kA[:], in1=kB[:], op=ALU.max)

            def t5(a, raw):
                a.activation(out=sinT[:], in_=kA[:], func=AF.Sin, scale=TWO_PI)
                a.activation(out=cosT[:], in_=kB[:], func=AF.Sin, scale=-TWO_PI, bias=HALF_PI)
            if r == 0:
                S.run(scalar=dtf)

            def a1(v, raw):
                v.tensor_tensor(out=lrdt[:], in0=LR[:], in1=dt_[:], op=ALU.mult)
                v.tensor_tensor(out=lidt[:], in0=LI[:], in1=dt_[:], op=ALU.mult)
                v.tensor_scalar(out=phi[:], in0=lidt[:], scalar1=1.0 / (2 * math.pi), scalar2=None, op0=ALU.mult)
                v.tensor_tensor(out=tA[:], in0=bc(phi[:], [128, 4, 17], 2), in1=bc(midx[:], [128, 4, 17], 1), op=ALU.mult)
                v.tensor_tensor(out=tB[:], in0=bc(lrdt[:], [128, 4, 17], 2), in1=bc(midx[:], [128, 4, 17], 1), op=ALU.mult)
                v.tensor_copy(out=tI[:], in_=tA[:])
                v.tensor_copy(out=tC[:], in_=tI[:])
                v.tensor_tensor(out=tA[:], in0=tA[:], in1=tC[:], op=ALU.subtract)
                v.tensor_scalar(out=tC[:], in0=tA[:], scalar1=-1.0, scalar2=None, op0=ALU.mult)
                v.tensor_tensor(out=tC[:], in0=tA[:], in1=tC[:], op=ALU.max)
                v.tensor_scalar(out=s1[:], in0=phi[:], scalar1=16.0, scalar2=None, op0=ALU.mult)
                v.tensor_copy(out=sI[:], in_=s1[:])
                v.tensor_copy(out=s2[:], in_=sI[:])
                v.tensor_tensor(out=phr[:], in0=s1[:], in1=s2[:], op=ALU.subtract)
            if r == 0:
                S.run(vector=a1)

            def a2(a, raw):
                a.activation(out=EAi[:], in_=tA[:], func=AF.Sin, scale=TWO_PI)
                a.activation(out=EAr[:], in_=tC[:], func=AF.Sin, scale=-TWO_PI, bias=HALF_PI)
                a.activation(out=tB[:], in_=tB[:], func=AF.Exp)
                a.activation(out=rho[:], in_=lrdt[:], func=AF.Exp, scale=16.0)
            if r == 0:
                S.run(scalar=a2)

            def a3(v, raw):
                v.tensor_tensor(out=EAr[:], in0=EAr[:], in1=tB[:], op=ALU.mult)
                v.tensor_tensor(out=EAi[:], in0=EAi[:], in1=tB[:], op=ALU.mult)
                v.tensor_scalar(out=s1[:], in0=EAr[:, :, 1], scalar1=-1.0, scalar2=None, op0=ALU.add)
                v.tensor_tensor(out=s2[:], in0=LR[:], in1=LR[:], op=ALU.mult)
                v.tensor_tensor(out=s3[:], in0=LI[:], in1=LI[:], op=ALU.mult)
                v.tensor_tensor(out=s2[:], in0=s2[:], in1=s3[:], op=ALU.add)
                v.reciprocal(out=s2[:], in_=s2[:])
                v.tensor_tensor(out=s3[:], in0=s1[:], in1=LR[:], op=ALU.mult)
                v.tensor_tensor(out=s4[:], in0=EAi[:, :, 1], in1=LI[:], op=ALU.mult)
                v.tensor_tensor(out=s3[:], in0=s3[:], in1=s4[:], op=ALU.add)
                v.tensor_tensor(out=fr[:], in0=s3[:], in1=s2[:], op=ALU.mult)
                v.tensor_tensor(out=s3[:], in0=EAi[:, :, 1], in1=LR[:], op=ALU.mult)
                v.tensor_tensor(out=s4[:], in0=s1[:], in1=LI[:], op=ALU.mult)
                v.tensor_tensor(out=s3[:], in0=s3[:], in1=s4[:], op=ALU.subtract)
                v.tensor_tensor(out=fi[:], in0=s3[:], in1=s2[:], op=ALU.mult)
                frb = bc(fr[:], [128, 4, 16], 2); fib = bc(fi[:], [128, 4, 16], 2)
                v.tensor_tensor(out=Bbr[:], in0=BR[:], in1=frb, op=ALU.mult)
                v.tensor_tensor(out=b1[:], in0=BI[:], in1=fib, op=ALU.mult)
                v.tensor_tensor(out=Bbr[:], in0=Bbr[:], in1=b1[:], op=ALU.subtract)
                v.tensor_tensor(out=Bbi[:], in0=BI[:], in1=frb, op=ALU.mult)
                v.tensor_tensor(out=b1[:], in0=BR[:], in1=fib, op=ALU.mult)
                v.tensor_tensor(out=Bbi[:], in0=Bbi[:], in1=b1[:], op=ALU.add)
                sh = [128, 4, 16, 16]
                ear = bc(EAr[:, :, 0:16], sh, 3); eai = bc(EAi[:, :, 0:16], sh, 3)
                bbr = bc(Bbr[:], sh, 2); bbi = bc(Bbi[:], sh, 2)
                v.tensor_tensor(out=Gr[:], in0=ear, in1=bbr, op=ALU.mult)
                v.tensor_tensor(out=G1[:], in0=eai, in1=bbi, op=ALU.mult)
                v.tensor_tensor(out=Gr[:], in0=Gr[:], in1=G1[:], op=ALU.subtract)
                v.tensor_tensor(out=Gi[:], in0=ear, in1=bbi, op=ALU.mult)
                v.tensor_tensor(out=G1[:], in0=eai, in1=bbr, op=ALU.mult)
                v.tensor_tensor(out=Gi[:], in0=Gi[:], in1=G1[:], op=ALU.add)
                for x, Gsrc in enumerate((Gr, Gi)):
                    for g2 in range(2):
                        v.tensor_scalar(out=Gx[:, :, x, :, g2, :].rearrange("p m q c -> p q m c"), in0=Gsrc[:],
                                        scalar1=pm[:, g2:g2 + 1], scalar2=None, op0=ALU.mult)
            if r == 0:
                S.run(vector=a3)

            with nc.psum_tensor(f"ssm_ctp{r}", [128, 2, 64], F32) as ctp, \
                    nc.psum_tensor(f"ssm_wbp{r}", [128, 4, 8, 128], BF16) as wbp, \
                    nc.psum_tensor(f"ssm_kc{r}", [128, 16, 16], F32) as kc:
                def t1(pe, raw):
                    raw.transpose(out=ctp[:, 0, :], in_=CNr[:, :], identity=identf[0:64, 0:64])
                    raw.transpose(out=ctp[:, 1, :], in_=CNi[:, :], identity=identf[0:64, 0:64])
                    for m in range(16):
                        for x in range(2):
                            idx = m * 2 + x
                            ins = raw.transpose(out=wbp[:, idx // 8, idx % 8, :],
                                                in_=Gx[:, m, x, :, :, :].rearrange("p q t c -> p (q t c)"),
                                                identity=g.ident[:])
                    pe.done(ins)
                if r == 0:
                    S.run(tensor=t1)
                else:
                    S.run(tensor=t1, sync=lambda sp, raw: sp.dma_start(out=g.ys1T[128 * (r - 1):128 * r, :],
                                                                      in_=ystage[:]))

                def t2(v, raw):
                    v.tensor_copy(out=CTr[:].rearrange("p q c -> p (q c)"), in_=ctp[:, 0, :])
                    v.tensor_copy(out=CTi[:].rearrange("p q c -> p (q c)"), in_=ctp[:, 1, :])
                    v.tensor_copy(out=CTrb[:], in_=CTr[:])
                    v.tensor_scalar(out=CTnib[:], in0=CTi[:], scalar1=-1.0, scalar2=None, op0=ALU.mult)
                    for bk in range(4):
                        v.tensor_copy(out=Wb[:, 4 * bk:4 * bk + 4, :, :].rearrange("p m x c -> p (m x) c"),
                                      in_=wbp[:, bk, :, :])
                    sh = [128, 4, 16, 16]
                    ear = bc(EAr[:, :, 1:17], sh, 3); eai = bc(EAi[:, :, 1:17], sh, 3)
                    ctr = bc(CTr[:], sh, 2); cti = bc(CTi[:], sh, 2)
                    v.tensor_tensor(out=Gr[:], in0=ear, in1=ctr, op=ALU.mult)
                    v.tensor_tensor(out=G1[:], in0=eai, in1=cti, op=ALU.mult)
                    v.tensor_tensor(out=Gr[:], in0=Gr[:], in1=G1[:], op=ALU.subtract)
                    v.tensor_tensor(out=Gi[:], in0=eai, in1=ctr, op=ALU.mult)
                    v.tensor_tensor(out=G1[:], in0=ear, in1=cti, op=ALU.mult)
                    v.tensor_tensor(out=Gi[:], in0=Gi[:], in1=G1[:], op=ALU.add)
                    v.tensor_scalar(out=Gi[:], in0=Gi[:], scalar1=-1.0, scalar2=None, op0=ALU.mult)
                    for x, Csrc in enumerate((Gr, Gi)):
                        for g2 in range(2):
                            v.tensor_scalar(out=Wc[:, :, :, x, g2, :], in0=Csrc[:], scalar1=pm[:, g2:g2 + 1],
                                            scalar2=None, op0=ALU.mult)
                S.run(vector=t2)

                def t3(pe, raw):
                    for lag in range(16):
                        for qq in range(4):
                            raw.matmul(kc[32 * qq:32 * qq + 32, lag, :],
                                       lhsT=Gx[:, lag, 0, qq, :, :].rearrange("p t c -> p (t c)"), rhs=CTrb[:, qq, :],
                                       start=True, stop=False, tile_position=(0, 32 * qq), skip_group_check=True)
                            ins = raw.matmul(kc[32 * qq:32 * qq + 32, lag, :],
                                             lhsT=Gx[:, lag, 1, qq, :, :].rearrange("p t c -> p (t c)"),
                                             rhs=CTnib[:, qq, :], start=False, stop=True,
                                             tile_position=(0, 32 * qq), skip_group_check=True)
                    pe.done(ins)
                S.run(tensor=t3)

                def t4(v, raw):
                    for gi in range(8):
                        v.tensor_scalar(out=Kst[:, :, gi, :], in0=kc[:, :, :], scalar1=bm[:, gi:gi + 1], scalar2=None,
                                        op0=ALU.mult)
                    k0 = Kst[:, 0, :, :].rearrange("p a c -> p (a c)")
                    v.scalar_tensor_tensor(out=k0, in0=identf[:], scalar=Dcol[:, 0:1], in1=k0, op0=ALU.mult, op1=ALU.add)
                    if r == 0:
                        tbl(v, raw)
                    else:
                        v.tensor_copy(out=uP[:], in_=uN[:].rearrange("p (k i) -> p i k", i=16))
                S.run(vector=t4)

            if r == 0:
                S.run(scalar=t5, vector=lambda p, raw: p.tensor_copy(
                    out=uP[:], in_=uN[:].rearrange("p (k i) -> p i k", i=16)))

            with nc.psum_tensor(f"ssm_bb{r}", [128, 4, 2, NK], F32) as bb:
                def m1(pe, raw):
                    for x in range(2):
                        for j in range(16):
                            for qq in range(4):
                                ins = raw.matmul(bb[:, qq, x, :], lhsT=Wb[32 * qq:32 * qq + 32, 15 - j, x, :],
                                                 rhs=uP[32 * qq:32 * qq + 32, j, :], start=(j == 0), stop=(j == 15),
                                                 tile_position=(32 * qq, 0))
                    pe.done(ins)
                S.run(tensor=m1)

                def m2(v, raw):
                    br = bb[:, :, 0, :]; bi = bb[:, :, 1, :]
                    cs = cosT[:, :, :]; sn = sinT[:, :, :]
                    v.tensor_tensor(out=w1[:], in0=br, in1=cs, op=ALU.mult)
                    v.tensor_tensor(out=w2[:], in0=bi, in1=sn, op=ALU.mult)
                    v.tensor_tensor(out=w1[:], in0=w1[:], in1=w2[:], op=ALU.add)
                    v.tensor_tensor(out=w2[:], in0=bi, in1=cs, op=ALU.mult)
                    v.tensor_tensor(out=w3[:], in0=br, in1=sn, op=ALU.mult)
                    v.tensor_tensor(out=w2[:], in0=w2[:], in1=w3[:], op=ALU.subtract)
                    for qq in range(4):
                        rb = rho[:, qq:qq + 1].to_broadcast([128, NK])
                        v.tensor_tensor_scan(out=w3[:, qq, :], data0=rb, data1=w1[:, qq, :], initial=0.0,
                                             op0=ALU.mult, op1=ALU.add)
                        v.tensor_tensor_scan(out=w4[:, qq, :], data0=rb, data1=w2[:, qq, :], initial=0.0,
                                             op0=ALU.mult, op1=ALU.add)
                    v.tensor_tensor(out=w1[:], in0=w3[:], in1=cs, op=ALU.mult)
                    v.tensor_tensor(out=w2[:], in0=w4[:], in1=sn, op=ALU.mult)
                    v.tensor_tensor(out=Sp[:, :, 0, 1:NK], in0=w1[:, :, 0:NK - 1], in1=w2[:, :, 0:NK - 1],
                                    op=ALU.subtract)
                    v.tensor_tensor(out=w1[:], in0=w3[:], in1=sn, op=ALU.mult)
                    v.tensor_tensor(out=w2[:], in0=w4[:], in1=cs, op=ALU.mult)
                    v.tensor_tensor(out=Sp[:, :, 1, 1:NK], in0=w1[:, :, 0:NK - 1], in1=w2[:, :, 0:NK - 1], op=ALU.add)
                if r + 1 < 4:
                    S.run(vector=m2, sync=make_ld(r + 1))
                else:
                    S.run(vector=m2)

            ys = yperm
            with nc.psum_tensor(f"ssm_yb{r}", [128, 8, NK], F32) as yb:
                def mk_f1(qtr):
                    def f1(pe, raw):
                        for ii in range(4):
                            i = qtr * 4 + ii
                            bank = (qtr % 2) * 4 + ii
                            for lag in range(i + 1):
                                raw.matmul(yb[:, bank, :], lhsT=Kst[:, lag, :, :].rearrange("p a c -> p (a c)"),
                                           rhs=uP[:, i - lag, :], start=(lag == 0), stop=(lag == i))
                            for qq in range(4):
                                for x in range(2):
                                    ins = raw.matmul(yb[32 * qq:32 * qq + 32, bank, :],
                                                     lhsT=Wc[:, qq, i, x, :, :].rearrange("p t c -> p (t c)"),
                                                     rhs=Sp[:, qq, x, :], start=False, stop=(x == 1),
                                                     tile_position=(0, 32 * qq), skip_group_check=True)
                        pe.done(ins)
                    return f1

                def mk_f2(qtr):
                    def f2(a, raw):
                        for ii in range(4):
                            i = qtr * 4 + ii
                            bank = (qtr % 2) * 4 + ii
                            a.activation(out=ys[:, i, :], in_=yb[:, bank, :], func=AF.Gelu_apprx_tanh)
                    return f2

                nxt = (r + 1 < 4)

                def both(f, g2):
                    def h(e, raw):
                        f(e, raw)
                        g2(e, raw)
                    return h
                S.run(tensor=mk_f1(0))
                S.run(tensor=mk_f1(1), scalar=both(mk_f2(0), dtf) if nxt else mk_f2(0))
                if nxt:
                    S.run(tensor=mk_f1(2), scalar=mk_f2(1), vector=both(a1, tbl))
                    S.run(tensor=mk_f1(3), scalar=both(mk_f2(2), both(a2, t5)))
                    S.run(scalar=mk_f2(3), vector=a3)
                else:
                    S.run(tensor=mk_f1(2), scalar=mk_f2(1))
                    S.run(tensor=mk_f1(3), scalar=mk_f2(2))
                    S.run(scalar=mk_f2(3))
            S.run(vector=lambda v, raw: v.tensor_copy(out=ystage[:].rearrange("p (k i) -> p k i", i=16),
                                                      in_=yperm[:].rearrange("p i k -> p k i")))
        S.run(sync=lambda sp, raw: sp.dma_start(out=g.ys1T[128 * 3:128 * 4, :], in_=ystage[:]))


def phase_final(g):
    nc = g.nc
    with contextlib.ExitStack() as es:
        def sb(name, shape, dt=F32, stack=es):
            return stack.enter_context(nc.sbuf_tensor("fin_" + name, list(shape), dt))
        wg, wa, wsr, wo, bg, gpost = g.wg, g.wa, g.wsr, g.wo, g.bg, g.gpost
        ys1 = sb("ys1", [128, 2, 4, CH], BF16); sgs = sb("sgs", [128, 2, 4, CH], BF16); ya = sb("ya", [128, 2, 4, CH], BF16)
        sma = sb("sma", [128, 2, 8, CH], BF16); sms = sb("sms", [128, 2, 8, CH], BF16)
        xin = sb("xin", [128, 2, 4, D])
        sg = sb("sg", [128, 4, CH], BF16); y3 = sb("y3", [128, 4, CH], BF16)
        ma = sb("ma", [128, 8, CH]); mg = sb("mg", [128, 8, CH], BF16)
        tS = sb("tS", [128, 2, CH])
        tF = sb("tF", [128, 2, D])
        ost = sb("ost", [128, 2, D])
        junk = sb("junk", [128, 2, D], BF16)
        ss2 = sb("ss2", [128, 2]); sd2 = sb("sd2", [128, 2]); rs2 = sb("rs2", [128, 2])
        ps = es.enter_context(nc.psum_tensor("fin_ps", [128, 4, CH], F32))
        po = es.enter_context(nc.psum_tensor("fin_po", [128, 2, 2, CH], F32))

        s_in = [nc.alloc_semaphore(f"p5_in{i}") for i in range(2)]
        s_evA = nc.alloc_semaphore("p5_evA")
        s_evD = nc.alloc_semaphore("p5_evD")
        s_dd = nc.alloc_semaphore("p5_dd")
        s_y3 = nc.alloc_semaphore("p5_y3")
        s_mm = nc.alloc_semaphore("p5_mm")
        s_o = nc.alloc_semaphore("p5_o")
        s_sq = nc.alloc_semaphore("p5_sq")
        s_rs = nc.alloc_semaphore("p5_rs")
        s_fin = nc.alloc_semaphore("p5_fin")
        s_st = [nc.alloc_semaphore(f"p5_st{i}") for i in range(2)]

        fmseq = [("z", 0, j) for j in range(4)]
        for c in range(NCH):
            fmseq += [("a", c, j) for j in range(8)] + [("s", c, j) for j in range(8)]
            if c + 1 < NCH:
                fmseq += [("z", c + 1, j) for j in range(4)]
        nidx = {k: n for n, k in enumerate(fmseq)}

        def ev_info(n):
            kind, c, j = fmseq[n]
            if kind == "z":
                return "A", c * 4 + j + 1
            return "D", c * 16 + (j if kind == "a" else 8 + j) + 1

        dd = [0]

        with nc.Block() as b:
            @b.sync
            def _(sp):
                def load(c):
                    sl = c % 2
                    tok = slice(c * CH, (c + 1) * CH)
                    if c >= 2:
                        sp.wait_ge(s_fin, 4 * (c - 1))
                    for dst, src in ((ys1, g.ys1T), (sgs, g.sgsT), (ya, g.yaT), (sma, g.smaT), (sms, g.smsT)):
                        sp.dma_start(out=dst[:, sl], in_=src[:, tok].rearrange("(j p) t -> p j t", p=128)
                                     ).then_inc(s_in[sl], 16)
                    sp.dma_start(out=xin[:, sl], in_=g.x[tok, :].rearrange("(t p) d -> p t d", p=128)
                                 ).then_inc(s_in[sl], 16)
                load(0)
                load(1)
                for c in range(NCH):
                    for tt in range(4):
                        ti = 4 * c + tt
                        sp.wait_ge(s_fin, ti + 1)
                        r0 = c * CH + tt * 128
                        sp.dma_start(out=g.out[r0:r0 + 128, :], in_=ost[:, ti % 2, :]).then_inc(s_st[ti % 2], 16)
                    if c + 2 < NCH:
                        load(c + 2)
                for i in range(2):
                    sp.wait_ge(s_st[i], 16 * (4 * NCH // 2))

            @b.tensor
            def _(pe):
                loaded = set()

                def fmblock(n):
                    kind, c, j = fmseq[n]
                    sl = c % 2
                    if c not in loaded:
                        pe.wait_ge(s_in[sl], 96 * (c // 2 + 1))
                        loaded.add(c)
                    if n >= 4:
                        e, cnt = ev_info(n - 4)
                        pe.wait_ge(s_evA if e == "A" else s_evD, cnt)
                    if kind == "s" and j == 0:
                        pe.wait_ge(s_y3, c + 1)
                    w, src = {"z": (wg, ys1[:, sl]), "a": (wa, ya[:, sl]), "s": (wsr, y3)}[kind]
                    for kk in range(4):
                        ins = pe.matmul(ps[:, n % 4, :], lhsT=w[:, kk, 128 * j:128 * j + 128], rhs=src[:, kk, :],
                                        start=(kk == 0), stop=(kk == 3))
                    ins.then_inc(s_mm, 1)

                def outproj(c):
                    pe.wait_ge(s_evD, 16 * (c + 1))
                    for tt in range(4):
                        ti = 4 * c + tt
                        if ti >= 2:
                            pe.wait_ge(s_fin, ti - 1)
                        for hf in range(2):
                            for kk in range(8):
                                ins = pe.matmul(po[:, ti % 2, hf, :], lhsT=mg[:, kk, 128 * tt:128 * tt + 128],
                                                rhs=wo[:, kk, 512 * hf:512 * hf + 512], start=(kk == 0), stop=(kk == 7))
                        ins.then_inc(s_o, 1)

                for n, (kind, c, j) in enumerate(fmseq):
                    fmblock(n)
                    last_of_chunk = (kind == "z" and j == 3 and c >= 1) or (kind == "s" and j == 7 and c == NCH - 1)
                    if last_of_chunk:
                        outproj(c - 1 if kind == "z" else c)

            @b.scalar
            def _(act):
                def sig(c):
                    for j in range(4):
                        n = nidx[("z", c, j)]
                        act.wait_ge(s_mm, n + 1)
                        if j == 0 and c >= 1:
                            act.wait_ge(s_y3, c)
                        act.activation(out=sg[:, j, :], in_=ps[:, n % 4, :], func=AF.Sigmoid,
                                       bias=bg[:, j:j + 1]).then_inc(s_evA, 1)

                def stats(c):
                    for tt in range(4):
                        ti = 4 * c + tt
                        act.wait_ge(s_o, ti + 1)
                        if ti >= 2:
                            act.wait_ge(s_fin, ti - 1)
                        act.activation(out=junk[:, ti % 2, :], in_=po[:, ti % 2, :, :].rearrange("p a c -> p (a c)"),
                                       func=AF.Square, accum_out=ss2[:, ti % 2:ti % 2 + 1]).then_inc(s_sq, 1)
                        act.wait_ge(s_sq, 2 * ti + 1)
                        act.activation(out=sd2[:, ti % 2:ti % 2 + 1], in_=ss2[:, ti % 2:ti % 2 + 1], func=AF.Ln,
                                       scale=1.0 / D, bias=EPS).then_inc(s_sq, 1)
                        act.wait_ge(s_sq, 2 * ti + 2)
                        act.activation(out=rs2[:, ti % 2:ti % 2 + 1], in_=sd2[:, ti % 2:ti % 2 + 1], func=AF.Exp,
                                       scale=-0.5).then_inc(s_rs, 1)

                sig(0)
                for c in range(NCH):
                    if c + 1 < NCH:
                        sig(c + 1)
                    stats(c)

            @b.vector
            def _(dve):
                def chain(ins):
                    ins.then_inc(s_dd, 1)
                    dd[0] += 1
                    dve.wait_ge(s_dd, dd[0])

                def y3f(c):
                    sl = c % 2
                    dve.wait_ge(s_in[sl], 96 * (c // 2 + 1))
                    dve.wait_ge(s_evA, 4 * (c + 1))
                    if c >= 1:
                        dve.wait_ge(s_mm, nidx[("s", c - 1, 7)] + 1)
                    chain(dve.tensor_tensor(out=sg[:], in0=sg[:], in1=ys1[:, sl], op=ALU.mult))
                    dve.tensor_tensor(out=y3[:], in0=sg[:], in1=sgs[:, sl], op=ALU.mult).then_inc(s_y3, 1)

                def evacs(c):
                    sl = c % 2
                    dve.wait_ge(s_in[sl], 96 * (c // 2 + 1))
                    for kind in ("a", "s"):
                        for j in range(8):
                            n = nidx[(kind, c, j)]
                            dve.wait_ge(s_mm, n + 1)
                            if kind == "a":
                                if j == 0 and c >= 1:
                                    dve.wait_ge(s_evD, 16 * c)
                                dve.tensor_tensor(out=ma[:, j, :], in0=ps[:, n % 4, :], in1=sma[:, sl, j, :],
                                                  op=ALU.mult).then_inc(s_evD, 1)
                            else:
                                if j == 0 and c >= 1:
                                    dve.wait_ge(s_o, 4 * c)
                                chain(dve.tensor_tensor(out=tS[:, j % 2, :], in0=ps[:, n % 4, :],
                                                        in1=sms[:, sl, j, :], op=ALU.mult))
                                dve.wait_ge(s_evD, 16 * c + j + 1)
                                dve.tensor_tensor(out=mg[:, j, :], in0=tS[:, j % 2, :], in1=ma[:, j, :],
                                                  op=ALU.add).then_inc(s_evD, 1)

                def fin(c, tt):
                    sl = c % 2
                    ti = 4 * c + tt
                    dve.wait_ge(s_rs, ti + 1)
                    if ti >= 2:
                        dve.wait_ge(s_st[ti % 2], 16 * (ti // 2))
                    chain(dve.scalar_tensor_tensor(out=tF[:, ti % 2, :],
                                                   in0=po[:, ti % 2, :, :].rearrange("p a c -> p (a c)"),
                                                   scalar=rs2[:, ti % 2:ti % 2 + 1], in1=gpost[:],
                                                   op0=ALU.mult, op1=ALU.mult))
                    dve.tensor_tensor(out=ost[:, ti % 2, :], in0=tF[:, ti % 2, :], in1=xin[:, sl, tt, :],
                                      op=ALU.add).then_inc(s_fin, 1)

                y3f(0)
                for c in range(NCH):
                    evacs(c)
                    fin(c, 0)
                    fin(c, 1)
                    if c + 1 < NCH:
                        y3f(c + 1)
                    fin(c, 2)
                    fin(c, 3)


def kernel(**inputs):
    nc = build()
    x = np.ascontiguousarray(inputs["x"], dtype=np.float32)
    shared = {}
    for k, v in inputs.items():
        if k == "x":
            continue
        a = np.ascontiguousarray(v, dtype=np.float32)
        shared[k] = a.reshape(_SHAPES[k])
    in_maps = []
    for c in range(NCORES):
        m = dict(shared)
        m["x"] = x[c]
        in_maps.append(m)
    res = run_bass_kernel_spmd(nc, in_maps, core_ids=list(range(NCORES)))
    return np.stack([r["out"] for r in res.results], axis=0).astype(np.float32)


_SHAPES = {
    "norm_pre": (1, D), "w_in": (D, INC), "b_forget": (1, H), "lam_re": (32, 64), "lam_im": (32, 64),
    "log_dt": (1, 32), "b_re": (32, 64, 16), "b_im": (32, 64, 16), "c_re": (32, 16, 64),
    "c_im": (32, 16, 64), "d_skip": (32, 16), "w_glu": (512, 512), "b_glu": (1, 512),
    "w_branch_a": (512, D), "w_branch_s": (512, D), "w_out": (D, D), "norm_post": (1, D),
}
```
